# Optimizing a Trainium2 kernel written in Bass

```python
import math
import jax
import jax.numpy as jnp
from jax import lax
import numpy as np

D_MODEL = 2048
BATCH = 8
SEQ = 4096
DEPTH = 4

CTX_LEN = 256
GRID_W = 64
N_DIR = 2
D_MIX = D_MODEL
SSD_WIDTH = D_MIX // 2
GDN_WIDTH = D_MIX - SSD_WIDTH
SSD_HEAD_DIM = 64
SSD_HEADS = SSD_WIDTH // SSD_HEAD_DIM
SSD_GROUPS = 2
SSD_HEADS_PER_GROUP = SSD_HEADS // SSD_GROUPS
SSD_STATE = 128
SSD_CONV_DIM = SSD_WIDTH + 2 * SSD_GROUPS * SSD_STATE
GDN_HEAD_DIM = 128
GDN_HEADS = GDN_WIDTH // GDN_HEAD_DIM
GDN_CONV_DIM = 3 * GDN_WIDTH
CONV_K = 5
CHUNK = 128
EPS = 1e-6

OFF_XBC = D_MIX
OFF_DT = OFF_XBC + SSD_CONV_DIM
OFF_QKV = OFF_DT + N_DIR * SSD_HEADS
OFF_BETA = OFF_QKV + GDN_CONV_DIM
OFF_ALPHA = OFF_BETA + N_DIR * GDN_HEADS
IN_DIM = OFF_ALPHA + N_DIR * GDN_HEADS

kernel_name = 'hybrid_ssd_gdn_prefix_dit'


def rms_norm(x, w):
    xf = x.astype(jnp.float32)
    y = xf * lax.rsqrt(jnp.mean(xf * xf, axis=-1, keepdims=True) + EPS)
    return (y * w.astype(jnp.float32)).astype(x.dtype)


def l2_norm(t):
    return t * lax.rsqrt(jnp.sum(t * t, axis=-1, keepdims=True) + EPS)


def _flip(t):
    return t[:, ::-1]


def _same(t):
    return t


def centred_dwconv(x, w):
    pad = CONV_K // 2
    L = x.shape[1]
    xp = jnp.pad(x, ((0, 0), (pad, pad), (0, 0)))
    out = xp[:, 0:L] * w[0]
    for k in range(1, CONV_K):
        out = out + xp[:, k:k + L] * w[k]
    return out


def conv_latent(x, w):
    bsz, L, ch = x.shape
    rows = L // GRID_W
    y = centred_dwconv(x.reshape(bsz * rows, GRID_W, ch), w)
    return y.reshape(bsz, L, ch)


def _chunks(t):
    return t.reshape(t.shape[0], t.shape[1] // CHUNK, CHUNK, *t.shape[2:])


def ssd_scan(xs, dt, a_neg, bm, cm, h0):
    loga = _chunks(dt * a_neg)
    xdt = _chunks(xs * dt[..., None])
    bmc, cmc = _chunks(bm), _chunks(cm)
    acum = jnp.cumsum(loga, axis=2)
    causal = jnp.tril(jnp.ones((CHUNK, CHUNK), bool))
    a_t = jnp.moveaxis(acum, 2, -1)
    seg = a_t[..., :, None] - a_t[..., None, :]
    lmat = jnp.exp(jnp.where(causal, seg, -jnp.inf))
    cb = jnp.einsum('bcqgn,bcsgn->bcgqs', cmc, bmc)
    y_diag = jnp.einsum('bcgrqs,bcsgrp->bcqgrp', cb[:, :, :, None] * lmat, xdt)
    decay_end = jnp.exp(acum[:, :, -1:] - acum)
    states = jnp.einsum('bcsgn,bcsgrp->bcgrpn', bmc, xdt * decay_end[..., None])
    chunk_decay = jnp.exp(acum[:, :, -1])

    def step(h, inp):
        st, dec = inp
        return h * dec[..., None, None] + st, h

    h_last, h_prev = lax.scan(step, h0, (jnp.moveaxis(states, 1, 0), jnp.moveaxis(chunk_decay, 1, 0)))
    h_prev = jnp.moveaxis(h_prev, 0, 1)
    y_off = jnp.einsum('bcqgn,bcgrpn->bcqgrp', cmc, h_prev) * jnp.exp(acum)[..., None]
    return (y_diag + y_off).reshape(xs.shape), h_last


def gdn_scan(q, k, v, g, beta, s0):
    bsz, L, H, V = v.shape

    def heads_first(t):
        return jnp.moveaxis(_chunks(t), 3, 1)

    q, k, v, g, beta = (heads_first(t) for t in (q, k, v, g, beta))
    gcum = jnp.cumsum(g, axis=-1)
    incl = jnp.tril(jnp.ones((CHUNK, CHUNK), bool))
    strict = jnp.tril(jnp.ones((CHUNK, CHUNK), bool), -1)
    decay = jnp.exp(jnp.where(incl, gcum[..., :, None] - gcum[..., None, :], -jnp.inf))
    k_beta = k * beta[..., None]
    a_strict = jnp.where(strict, jnp.einsum('bhcik,bhcjk->bhcij', k_beta, k) * decay, 0.0)
    rhs = jnp.concatenate([v * beta[..., None], k_beta * jnp.exp(gcum)[..., None]], axis=-1)
    sol = lax.linalg.triangular_solve(a_strict, rhs, left_side=True, lower=True, unit_diagonal=True)
    u, w = sol[..., :V], sol[..., V:]
    attn = jnp.einsum('bhcik,bhcjk->bhcij', q, k) * decay
    q_dec = q * jnp.exp(gcum)[..., None]
    k_end = k * jnp.exp(gcum[..., -1:] - gcum)[..., None]
    chunk_decay = jnp.exp(gcum[..., -1])

    def step(S, inp):
        qd, ke, ui, wi, ai, dec = inp
        v_new = ui - jnp.einsum('bhqk,bhkv->bhqv', wi, S)
        o = jnp.einsum('bhqk,bhkv->bhqv', qd, S) + jnp.einsum('bhqs,bhsv->bhqv', ai, v_new)
        S = S * dec[..., None, None] + jnp.einsum('bhsk,bhsv->bhkv', ke, v_new)
        return S, o

    xs = tuple(jnp.moveaxis(t, 2, 0) for t in (q_dec, k_end, u, w, attn, chunk_decay))
    s_last, o = lax.scan(step, s0, xs)
    o = jnp.moveaxis(jnp.moveaxis(o, 0, 2), 1, 3).reshape(bsz, L, H, V)
    return o, s_last


def ssd_mixer(xbc_c, dt_c, xbc_l, dt_l, a_log, dt_bias, d_skip):
    G, R, P, N = SSD_GROUPS, SSD_HEADS_PER_GROUP, SSD_HEAD_DIM, SSD_STATE
    a_neg = -jnp.exp(a_log.astype(jnp.float32)).reshape(N_DIR, G, R)
    bias = dt_bias.astype(jnp.float32).reshape(N_DIR, G, R)

    def unpack(xbc, dt_raw):
        bsz, L = xbc.shape[:2]
        xbc = xbc.astype(jnp.float32)
        xs = xbc[..., :SSD_WIDTH].reshape(bsz, L, G, R, P)
        bm = xbc[..., SSD_WIDTH:SSD_WIDTH + G * N].reshape(bsz, L, G, N)
        cm = xbc[..., SSD_WIDTH + G * N:].reshape(bsz, L, G, N)
        dt = jax.nn.softplus(dt_raw.astype(jnp.float32).reshape(bsz, L, N_DIR, G, R) + bias)
        return xs, bm, cm, dt

    xc, bc, cc, dtc = unpack(xbc_c, dt_c)
    xl, bl, cl, dtl = unpack(xbc_l, dt_l)
    dsk = d_skip.astype(jnp.float32).reshape(G, R)[..., None]
    y_c, y_l = xc * dsk, xl * dsk
    h0 = jnp.zeros((xc.shape[0], G, R, P, N), jnp.float32)
    for d in range(N_DIR):
        f = _flip if d else _same
        yc_d, h_ctx = ssd_scan(f(xc), f(dtc[:, :, d]), a_neg[d], f(bc), f(cc), h0)
        yl_d, _ = ssd_scan(f(xl), f(dtl[:, :, d]), a_neg[d], f(bl), f(cl), h_ctx)
        y_c = y_c + f(yc_d)
        y_l = y_l + f(yl_d)
    return y_c.reshape(*xc.shape[:2], SSD_WIDTH), y_l.reshape(*xl.shape[:2], SSD_WIDTH)


def gdn_mixer(qkv_c, b_c, a_c, qkv_l, b_l, a_l, a_log, dt_bias):
    H, K = GDN_HEADS, GDN_HEAD_DIM
    rate = jnp.exp(a_log.astype(jnp.float32))
    bias = dt_bias.astype(jnp.float32)

    def unpack(qkv, b_raw, a_raw):
        bsz, L = qkv.shape[:2]
        qkv = qkv.astype(jnp.float32).reshape(bsz, L, 3, H, K)
        q = l2_norm(qkv[:, :, 0]) * (K ** -0.5)
        k = l2_norm(qkv[:, :, 1])
        v = qkv[:, :, 2]
        beta = jax.nn.sigmoid(b_raw.astype(jnp.float32).reshape(bsz, L, N_DIR, H))
        g = -rate * jax.nn.softplus(a_raw.astype(jnp.float32).reshape(bsz, L, N_DIR, H) + bias)
        return q, k, v, beta, g

    qc, kc, vc, bec, gc = unpack(qkv_c, b_c, a_c)
    ql, kl, vl, bel, gl = unpack(qkv_l, b_l, a_l)
    s0 = jnp.zeros((qc.shape[0], H, K, K), jnp.float32)
    o_c = jnp.zeros_like(vc)
    o_l = jnp.zeros_like(vl)
    for d in range(N_DIR):
        f = _flip if d else _same
        oc_d, s_ctx = gdn_scan(f(qc), f(kc), f(vc), f(gc[:, :, d]), f(bec[:, :, d]), s0)
        ol_d, _ = gdn_scan(f(ql), f(kl), f(vl), f(gl[:, :, d]), f(bel[:, :, d]), s_ctx)
        o_c = o_c + f(oc_d)
        o_l = o_l + f(ol_d)
    return o_c, o_l


def ssd_out_norm(y, z, w):
    bsz, L = y.shape[:2]
    yg = (y * jax.nn.silu(z.astype(jnp.float32))).reshape(bsz, L, SSD_GROUPS, SSD_WIDTH // SSD_GROUPS)
    yg = yg * lax.rsqrt(jnp.mean(yg * yg, axis=-1, keepdims=True) + EPS)
    return yg.reshape(bsz, L, SSD_WIDTH) * w.astype(jnp.float32)


def gdn_out_norm(o, z, w):
    bsz, L = o.shape[:2]
    o = o * lax.rsqrt(jnp.mean(o * o, axis=-1, keepdims=True) + EPS) * w.astype(jnp.float32)
    return o.reshape(bsz, L, GDN_WIDTH) * jax.nn.silu(z.astype(jnp.float32))


def _split_proj(p):
    return (p[..., :OFF_XBC], p[..., OFF_XBC:OFF_DT], p[..., OFF_DT:OFF_QKV],
            p[..., OFF_QKV:OFF_BETA], p[..., OFF_BETA:OFF_ALPHA], p[..., OFF_ALPHA:IN_DIM])


def hybrid_layer(h_lat, h_ctx, mod_lat, mod_ctx, pre_w, post_w, w_in, conv_ssd_w, conv_ssd_b,
                 conv_gdn_w, ssd_a_log, ssd_dt_bias, ssd_d, ssd_norm_w, gdn_a_log, gdn_dt_bias,
                 gdn_norm_w, w_out, update_ctx):
    shift_l, scale_l, gate_l = jnp.split(mod_lat, 3, axis=-1)
    shift_c, scale_c, gate_c = jnp.split(mod_ctx, 3, axis=-1)
    u_l = rms_norm(h_lat, pre_w) * (1 + scale_l[:, None]) + shift_l[:, None]
    u_c = rms_norm(h_ctx, pre_w) * (1 + scale_c) + shift_c
    z_l, xbc_l, dt_l, qkv_l, b_l, a_l = _split_proj(u_l @ w_in)
    z_c, xbc_c, dt_c, qkv_c, b_c, a_c = _split_proj(u_c @ w_in)
    xbc_l = jax.nn.silu(conv_latent(xbc_l, conv_ssd_w) + conv_ssd_b)
    xbc_c = jax.nn.silu(centred_dwconv(xbc_c, conv_ssd_w) + conv_ssd_b)
    qkv_l = jax.nn.silu(conv_latent(qkv_l, conv_gdn_w))
    qkv_c = jax.nn.silu(centred_dwconv(qkv_c, conv_gdn_w))
    ssd_c, ssd_l = ssd_mixer(xbc_c, dt_c, xbc_l, dt_l, ssd_a_log, ssd_dt_bias, ssd_d)
    gdn_c, gdn_l = gdn_mixer(qkv_c, b_c, a_c, qkv_l, b_l, a_l, gdn_a_log, gdn_dt_bias)

    def merge(ssd_y, gdn_o, z, dtype):
        y = jnp.concatenate([ssd_out_norm(ssd_y, z[..., :SSD_WIDTH], ssd_norm_w),
                             gdn_out_norm(gdn_o, z[..., SSD_WIDTH:], gdn_norm_w)], axis=-1)
        return rms_norm(y.astype(dtype) @ w_out, post_w)

    h_lat = h_lat + gate_l[:, None] * merge(ssd_l, gdn_l, z_l, h_lat.dtype)
    if update_ctx:
        h_ctx = h_ctx + gate_c * merge(ssd_c, gdn_c, z_c, h_ctx.dtype)
    return h_lat, h_ctx


def _dt_bias_init(key, shape):
    dt = jnp.exp(jax.random.uniform(key, shape, jnp.float32, math.log(1e-3), math.log(1e-1)))
    return dt + jnp.log(-jnp.expm1(-dt))


def setup_inputs(seed: int = 0) -> dict:
    key = jax.random.key(seed)
    ks = jax.random.split(key, 20)
    f32 = jnp.float32

    def nrm(k, shape, s):
        return jax.random.normal(k, shape, f32) * s

    return {
        'x': nrm(ks[0], (BATCH, SEQ, D_MODEL), 1.0),
        'c': nrm(ks[1], (BATCH, D_MODEL), 1.0),
        'ctx': nrm(ks[2], (BATCH, CTX_LEN, D_MODEL), 1.0),
        'c_ctx': nrm(ks[3], (D_MODEL,), 1.0),
        'w_ada': nrm(ks[4], (DEPTH, D_MODEL, 3 * D_MODEL), 0.5 * D_MODEL ** -0.5),
        'b_ada': nrm(ks[5], (DEPTH, 3 * D_MODEL), 0.01),
        'pre_norm_w': 1.0 + nrm(ks[6], (DEPTH, D_MODEL), 0.05),
        'post_norm_w': 1.0 + nrm(ks[7], (DEPTH, D_MODEL), 0.05),
        'w_in': nrm(ks[8], (DEPTH, D_MODEL, IN_DIM), D_MODEL ** -0.5),
        'conv_ssd_w': nrm(ks[9], (DEPTH, CONV_K, SSD_CONV_DIM), CONV_K ** -0.5),
        'conv_ssd_b': nrm(ks[10], (DEPTH, SSD_CONV_DIM), 0.02),
        'conv_gdn_w': nrm(ks[11], (DEPTH, CONV_K, GDN_CONV_DIM), CONV_K ** -0.5),
        'ssd_a_log': jnp.log(jax.random.uniform(ks[12], (DEPTH, N_DIR, SSD_HEADS), f32, 1.0, 16.0)),
        'ssd_dt_bias': _dt_bias_init(ks[13], (DEPTH, N_DIR, SSD_HEADS)),
        'ssd_d': 1.0 + nrm(ks[14], (DEPTH, SSD_HEADS), 0.1),
        'ssd_norm_w': 1.0 + nrm(ks[15], (DEPTH, SSD_WIDTH), 0.05),
        'gdn_a_log': jnp.log(jax.random.uniform(ks[16], (DEPTH, N_DIR, GDN_HEADS), f32, 1.0, 16.0)),
        'gdn_dt_bias': _dt_bias_init(ks[17], (DEPTH, N_DIR, GDN_HEADS)),
        'gdn_norm_w': 1.0 + nrm(ks[18], (DEPTH, GDN_HEAD_DIM), 0.05),
        'w_out': nrm(ks[19], (DEPTH, D_MIX, D_MODEL), D_MIX ** -0.5),
    }


def reference(x, c, ctx, c_ctx, w_ada, b_ada, pre_norm_w, post_norm_w, w_in, conv_ssd_w, conv_ssd_b,
              conv_gdn_w, ssd_a_log, ssd_dt_bias, ssd_d, ssd_norm_w, gdn_a_log, gdn_dt_bias,
              gdn_norm_w, w_out):
    h_lat, h_ctx = x, ctx
    c_act = jax.nn.silu(c)
    cc_act = jax.nn.silu(c_ctx)
    for l in range(DEPTH):
        mod_lat = c_act @ w_ada[l] + b_ada[l]
        mod_ctx = cc_act @ w_ada[l] + b_ada[l]
        h_lat, h_ctx = hybrid_layer(
            h_lat, h_ctx, mod_lat, mod_ctx, pre_norm_w[l], post_norm_w[l], w_in[l],
            conv_ssd_w[l], conv_ssd_b[l], conv_gdn_w[l], ssd_a_log[l], ssd_dt_bias[l], ssd_d[l],
            ssd_norm_w[l], gdn_a_log[l], gdn_dt_bias[l], gdn_norm_w[l], w_out[l],
            update_ctx=(l < DEPTH - 1))
    return h_lat
```

```python
import numpy as np
from contextlib import ExitStack
import concourse.bass as bass
import concourse.mybir as mybir
from concourse.bass_utils import run_bass_kernel_spmd

F32, BF16 = mybir.dt.float32, mybir.dt.bfloat16
AF = mybir.ActivationFunctionType
ALU = mybir.AluOpType
AX = mybir.AxisListType

D = 2048
T = 4352
NCH = 34
DEPTH = 4
IN_DIM = 6720
EPS = 1e-6
NPV = 208
C_ID, C_INC0, C_INC1, C_EXC0, C_EXC1, C_ONES, C_BD = 0, 128, 256, 384, 512, 640, 768
NCONST = 896
FM_B, FM_C, FM_Q, FM_K, FM_ROWS = 0, 256, 512, 1536, 2560
TM_X, TM_B, TM_K, TM_V, TM_COLS = 0, 1024, 1280, 2304, 3328


class Buf:
    __slots__ = ('name', 'w', 'r', 't', 'g')

    def __init__(self, name, t=None):
        self.name = name
        self.w = None
        self.r = {}
        self.t = t

    def __getitem__(self, k):
        return self.t[k]


class HalfBuf(Buf):
    __slots__ = ('off',)

    def __init__(self, name, t, off):
        Buf.__init__(self, name, t)
        self.off = off

    def __getitem__(self, k):
        if isinstance(k, tuple):
            p, c = k[0], k[1]
            start = (c.start or 0) + self.off
            stop = (c.stop if c.stop is not None else 512) + self.off
            return self.t[p, start:stop]
        return self.t[:, self.off:self.off + 512]


def run_pipeline(gens, depth):
    active = []
    it = iter(gens)
    done = False
    while True:
        if not done and len(active) < depth:
            try:
                active.append(next(it))
            except StopIteration:
                done = True
        if not active:
            if done:
                break
            continue
        for g in list(active):
            try:
                next(g)
            except StopIteration:
                active.remove(g)


class Ctx:
    NS = 8

    def __init__(self, nc):
        self.nc = nc
        self.eng = {'pe': nc.tensor, 'act': nc.scalar, 'dve': nc.vector, 'sp': nc.sync, 'pool': nc.gpsimd}
        self.inorder = ('pe', 'act', 'dve')
        self.inorder_skip = ('pe',)
        self.dmaq = ('sp', 'pool')
        self.sem = {e: nc.alloc_semaphore('s_' + e) for e in self.inorder}
        self.cnt = {e: 0 for e in self.inorder}
        self.dsem = {q: [nc.alloc_semaphore('d_%s%d' % (q, i)) for i in range(self.NS)] for q in self.dmaq}
        self.dcnt = {q: 0 for q in self.dmaq}
        self.waited = {}
        self.nins = 0

    def _wait(self, e, tok):
        if tok is None:
            return
        sem, val, owner = tok
        if owner == e and e in self.inorder_skip:
            return
        key = (e, id(sem))
        if self.waited.get(key, 0) >= val:
            return
        self.eng[e].wait_ge(sem, val)
        self.waited[key] = val

    def op(self, e, fns, reads=(), writes=()):
        if not isinstance(fns, (list, tuple)):
            fns = [fns]
        for b in reads:
            self._wait(e, b.w)
        for b in writes:
            self._wait(e, b.w)
            for t in b.r.values():
                self._wait(e, t)
        self.nins += len(fns)
        if e in self.dmaq:
            n = self.dcnt[e]
            slot = n % self.NS
            rnd = n // self.NS
            sem = self.dsem[e][slot]
            if rnd > 0:
                self._wait(e, (sem, 16 * rnd, None))
            for f in fns[:-1]:
                f()
            fns[-1]().then_inc(sem, 16)
            tok = (sem, 16 * (rnd + 1), e)
            self.dcnt[e] = n + 1
        else:
            for f in fns[:-1]:
                f()
            fns[-1]().then_inc(self.sem[e], 1)
            self.cnt[e] += 1
            tok = (self.sem[e], self.cnt[e], e)
        for b in reads:
            old = b.r.get(id(tok[0]))
            if old is None or old[1] < tok[1]:
                b.r[id(tok[0])] = tok
        for b in writes:
            b.w = tok
            b.r = {}
        return tok

    def all_tokens(self):
        toks = [(self.sem[e], self.cnt[e], e) for e in self.inorder if self.cnt[e] > 0]
        for q in self.dmaq:
            n = self.dcnt[q]
            for slot in range(self.NS):
                uses = (n - slot + self.NS - 1) // self.NS if n > slot else 0
                if uses > 0:
                    toks.append((self.dsem[q][slot], 16 * uses, q))
        return toks

    def barrier(self):
        toks = self.all_tokens()
        for e in self.eng:
            for t in toks:
                if t[2] == e and e in self.inorder:
                    continue
                self._wait(e, t)


def dram_bcast(ap, nparts):
    n = 1
    for s in ap.shape:
        n *= s
    return bass.AP(ap.tensor, ap.offset, [[0, nparts], [1, n]])


def tok_blocks():
    blks = [(0, 256, 256)]
    for i in range(8):
        blks.append((256 + 512 * i, 512, 64))
    return blks


class Builder:
    def __init__(self, n_layers=DEPTH, debug=False, stop_after=None, force_last=False):
        self.force_last = force_last
        self.n_layers = n_layers
        self.debug = debug
        self.stop_after = stop_after
        nc = self.nc = bass.Bass("TRN2", target_bir_lowering=False)
        self.K = Ctx(nc)
        di = lambda name, shape: nc.dram_tensor(name, shape, F32, kind="ExternalInput").ap()
        self.h0 = di("h0", [T, D])
        self.c2t = di("c2t", [128, 32])
        self.pv = di("pv", [DEPTH, 128, NPV])
        self.consts = di("consts", [128, NCONST])
        self.w_ada = di("w_ada", [DEPTH, D, 3 * D])
        self.b_ada = di("b_ada", [DEPTH, 3 * D])
        self.post_w = di("post_norm_w", [DEPTH, D])
        self.w_in = di("w_in", [DEPTH, D, IN_DIM])
        self.ssd_a_log = di("ssd_a_log", [DEPTH, 32])
        self.ssd_dt_bias = di("ssd_dt_bias", [DEPTH, 32])
        self.ssd_d = di("ssd_d", [DEPTH, 16])
        self.ssd_norm_w = di("ssd_norm_w", [DEPTH, 1024])
        self.gdn_a_log = di("gdn_a_log", [DEPTH, 16])
        self.gdn_dt_bias = di("gdn_dt_bias", [DEPTH, 16])
        self.gdn_norm_w = di("gdn_norm_w", [DEPTH, 128])
        self.w_out = di("w_out", [DEPTH, D, D])
        self.out = nc.dram_tensor("out", [4096, D], F32, kind="ExternalOutput").ap()
        self.ext_set = getattr(Builder, 'EXT_SET', ())
        ds = lambda name, shape, dt: nc.dram_tensor(name, shape, dt, kind=("ExternalOutput" if (debug or name in self.ext_set) else "Internal")).ap()
        self.modv = ds("modv", [2, 3 * D], F32)
        self.FM = ds("FM", [FM_ROWS, T], BF16)
        self.TM = ds("TM", [T, TM_COLS], BF16)
        self.ZS = ds("ZS", [T, D], BF16)
        self.SMALL = ds("SMALL", [T, 64], F32)
        self.YO = ds("YO", [T, D], F32)
        self.YM = ds("YM", [T, D], BF16)
        self.H = ds("H", [T, D], F32)
        if debug:
            self.UT = ds("UT", [D, T], BF16)

    def sb(self, es, name, shape, dt=F32):
        self.uid = getattr(self, 'uid', 0) + 1
        name = "%s_%d" % (name, self.uid)
        return Buf(name, es.enter_context(self.nc.sbuf_tensor(name, shape, dt)))

    def load(self, q, dst, dst_ap, src_ap, **kw):
        eng = self.K.eng[q]
        return self.K.op(q, lambda: eng.dma_start(out=dst_ap, in_=src_ap, **kw), writes=[dst])

    def store(self, q, src, dst_ap, src_ap, **kw):
        eng = self.K.eng[q]
        return self.K.op(q, lambda: eng.dma_start(out=dst_ap, in_=src_ap, **kw), reads=[src])

    def build(self):
        nc, K = self.nc, self.K
        with ExitStack() as es:
            self.ps = [Buf("ps%d" % i, es.enter_context(nc.psum_tensor("ps%d" % i, [128, 1024], F32))) for i in range(4)]
            self.psi = 0
            self.psh = [HalfBuf("psh%d" % i, self.ps[i // 2].t, (i % 2) * 512) for i in range(8)]
            self.psfree = list(self.psh)
            self.cst = self.sb(es, "cst", [128, NCONST])
            self.load('sp', self.cst, self.cst[:], self.consts)
            self.idb = self.sb(es, "idb", [128, 128], BF16)
            K.op('dve', lambda: nc.vector.tensor_copy(out=self.idb[:], in_=self.cst[:, C_ID:C_ID + 128]), reads=[self.cst], writes=[self.idb])
            self.onesb = self.sb(es, "onesb", [128, 128], BF16)
            K.op('dve', lambda: nc.vector.tensor_copy(out=self.onesb[:], in_=self.cst[:, C_ONES:C_ONES + 128]), reads=[self.cst], writes=[self.onesb])
            self.epsb = self.sb(es, "epsb", [128, 1])
            K.op('dve', lambda: nc.vector.memset(self.epsb[:], EPS), writes=[self.epsb])
            self.lnqs = self.sb(es, "lnqs", [128, 1])
            K.op('dve', lambda: nc.vector.memset(self.lnqs[:], float(np.log(128.0 ** -0.5))), writes=[self.lnqs])
            for l in range(self.n_layers):
                self.layer(l)
            K.barrier()
        return nc

    def next_ps(self):
        p = self.ps[self.psi % 4]
        self.psi += 1
        return p

    def next_half(self):
        assert self.psfree, "PSUM half-slots exhausted"
        return self.psfree.pop(0)

    def free_half(self, p):
        self.psfree.append(p)

    def layer(self, l):
        K = self.K
        last = (l == DEPTH - 1) or (self.force_last and l == self.n_layers - 1)
        hsrc = self.h0 if l == 0 else self.H
        with ExitStack() as es:
            self.pvt = self.sb(es, "pvt", [128, NPV])
            self.load('sp', self.pvt, self.pvt[:], self.pv[l])
            self.phase0(l)
            K.barrier()
            if self.stop_after == 'p0':
                return
            self.phaseA(l, hsrc)
            K.barrier()
            if self.stop_after == 'A':
                return
            self.phaseB(l, last)
            K.barrier()
            if self.stop_after == 'B':
                return
            self.phaseC(l, hsrc, last)
            K.barrier()

    def phase0(self, l):
        nc, K = self.nc, self.K
        with ExitStack() as es:
            sct = self.sb(es, "sct", [128, 32])
            self.load('sp', sct, sct[:], self.c2t)
            K.op('act', lambda: nc.scalar.activation(out=sct[:], in_=sct[:], func=AF.Silu), reads=[sct], writes=[sct])
            bia = self.sb(es, "bia", [2, 3 * D])
            self.load('sp', bia, bia[:], dram_bcast(self.b_ada[l], 2))
            modsb = self.sb(es, "modsb", [2, 3 * D])
            wts = [self.sb(es, "wada%d" % i, [128, 16, 512]) for i in range(2)]
            sct3 = sct[:].rearrange("p (k r) -> p k r", r=2)
            for cb in range(12):
                wt = wts[cb % 2]
                src = self.w_ada[l][:, cb * 512:(cb + 1) * 512].rearrange("(k p) f -> p k f", p=128)
                self.load('sp', wt, wt[:], src)
                ps = self.next_ps()
                fns = []
                for kc in range(16):
                    fns.append(lambda kc=kc, ps=ps, wt=wt: nc.tensor.matmul(ps[0:2, 0:512], lhsT=sct3[:, kc, :], rhs=wt[:, kc, :], start=(kc == 0), stop=(kc == 15)))
                K.op('pe', fns, reads=[sct, wt], writes=[ps])
                K.op('dve', lambda cb=cb, ps=ps: nc.vector.tensor_tensor(out=modsb[:, cb * 512:(cb + 1) * 512], in0=ps[0:2, 0:512], in1=bia[:, cb * 512:(cb + 1) * 512], op=ALU.add),
                     reads=[ps, bia], writes=[modsb])
            self.store('sp', modsb, self.modv, modsb[:])

    def phaseA(self, l, hsrc):
        nc, K = self.nc, self.K
        with ExitStack() as es:
            uT = self.sb(es, "uT", [128, 16, T], BF16)
            mraw = self.sb(es, "mraw", [128, 2, 2, 16])
            for r in range(2):
                for w in range(2):
                    src = bass.AP(self.modv.tensor, self.modv.offset + r * 3 * D + w * D, [[1, 128], [128, 16]])
                    self.load('sp', mraw, mraw[:, r, w, :], src, allow_slow_non_contiguous=True)
            mA = self.sb(es, "mA", [128, 2, 16])
            for r in range(2):
                K.op('dve', lambda r=r: nc.vector.scalar_tensor_tensor(out=mA[:, r, :], in0=mraw[:, r, 1, :], scalar=1.0, in1=self.pvt[:, 0:16], op0=ALU.add, op1=ALU.mult),
                     reads=[mraw, self.pvt], writes=[mA])
            with ExitStack() as es0:
                NH = 3
                hx = [self.sb(es0, "hx%d" % i, [128, D]) for i in range(NH)]
                junk = self.sb(es0, "junk", [128, D], BF16)
                xn = [self.sb(es0, "xn%d" % i, [128, D], BF16) for i in range(NH)]
                st = [self.sb(es0, "st%d" % i, [128, 4]) for i in range(NH)]

                def a0_load(c):
                    h_ = hx[c % NH]
                    self.load('sp', h_, h_[:], hsrc[c * 128:(c + 1) * 128, :])

                def a0_iter(c):
                    r = 1 if c < 2 else 0
                    h_, x_, s_ = hx[c % NH], xn[c % NH], st[c % NH]
                    K.op('act', lambda: nc.scalar.activation(out=junk[:], in_=h_[:], func=AF.Square, accum_out=s_[:, 0:1]), reads=[h_], writes=[junk, s_])
                    yield
                    K.op('act', lambda: nc.scalar.activation(out=s_[:, 1:2], in_=s_[:, 0:1], func=AF.Ln, bias=self.epsb[:], scale=1.0 / D), reads=[s_, self.epsb], writes=[s_])
                    K.op('act', lambda: nc.scalar.activation(out=s_[:, 2:3], in_=s_[:, 1:2], func=AF.Exp, scale=-0.5), reads=[s_], writes=[s_])
                    yield
                    K.op('dve', lambda: nc.vector.tensor_scalar(out=x_[:], in0=h_[:], scalar1=s_[:, 2:3], scalar2=None, op0=ALU.mult), reads=[h_, s_], writes=[x_])
                    if c + NH < NCH:
                        a0_load(c + NH)
                    yield
                    for half in range(2):
                        ps = self.next_half()
                        pb = ps[:].bitcast(BF16)
                        fns = [lambda j=j, pb=pb, half=half: nc.tensor.transpose(out=pb[:, j * 128:(j + 1) * 128], in_=x_[:, (half * 8 + j) * 128:(half * 8 + j + 1) * 128], identity=self.idb[:]) for j in range(8)]
                        K.op('pe', fns, reads=[x_, self.idb], writes=[ps])
                        for j in range(8):
                            kc = half * 8 + j
                            K.op('act', lambda j=j, kc=kc, pb=pb: nc.scalar.activation(out=uT[:, kc, c * 128:(c + 1) * 128], in_=pb[:, j * 128:(j + 1) * 128], func=AF.Identity,
                                                                                     bias=mraw[:, r, 0, kc:kc + 1], scale=mA[:, r, kc:kc + 1]),
                                 reads=[ps, mraw, mA], writes=[uT])
                        self.free_half(ps)
                        yield

                for c in range(NH):
                    a0_load(c)
                run_pipeline((a0_iter(c) for c in range(NCH)), 2)
                K.barrier()
            if self.debug:
                for kc in range(16):
                    self.store('pool', uT, self.UT[kc * 128:(kc + 1) * 128, :], uT[:, kc, :])
            with ExitStack() as es1:
                NB = 6
                wf = [self.sb(es1, "wf%d" % i, [128, 16, 128], BF16) for i in range(2)]
                acc = [self.sb(es1, "acc%d" % i, [128, 512]) for i in range(NB)]
                cv = [self.sb(es1, "cv%d" % i, [128, 512]) for i in range(NB)]
                ob = [self.sb(es1, "ob%d" % i, [128, 512], BF16) for i in range(NB)]
                sq = [self.sb(es1, "sq%d" % i, [128, 512], BF16) for i in range(NB)]
                lnv = [self.sb(es1, "lnv%d" % i, [128, 512]) for i in range(NB)]
                tmo = [self.sb(es1, "tmo%d" % i, [128, 4, 128], BF16) for i in range(NB)]
                blks = tok_blocks()

                def fm_iter(it, fc, w_, kind, fm_row, tm_col, t0, nt, rl):
                    a_, c_, o_, s_, l_, m_ = acc[it % NB], cv[it % NB], ob[it % NB], sq[it % NB], lnv[it % NB], tmo[it % NB]
                    ps = self.next_half()
                    fns = [lambda kc=kc: nc.tensor.matmul(ps[:, 0:nt], lhsT=w_[:, kc, :], rhs=uT[:, kc, t0:t0 + nt], start=(kc == 0), stop=(kc == 15)) for kc in range(16)]
                    K.op('pe', fns, reads=[w_, uT], writes=[ps])
                    yield
                    cw = lambda k: self.pvt[:, 16 + fc * 5 + k:16 + fc * 5 + k + 1]
                    K.op('act', lambda: nc.scalar.activation(out=a_[:, 0:nt], in_=ps[:, 0:nt], func=AF.Copy, scale=cw(2)), reads=[ps, self.pvt], writes=[a_])
                    pv3 = ps[:, 0:nt].rearrange("p (r j) -> p r j", j=rl)
                    av3 = a_[:, 0:nt].rearrange("p (r j) -> p r j", j=rl)
                    for k in (0, 1, 3, 4):
                        sft = k - 2
                        j0, j1 = max(0, -sft), rl - max(0, sft)
                        K.op('dve', lambda k=k, sft=sft, j0=j0, j1=j1: nc.vector.scalar_tensor_tensor(out=av3[:, :, j0:j1], in0=pv3[:, :, j0 + sft:j1 + sft], scalar=cw(k), in1=av3[:, :, j0:j1], op0=ALU.mult, op1=ALU.add),
                             reads=[ps, a_, self.pvt], writes=[a_])
                        if k == 1:
                            yield
                    self.free_half(ps)
                    yield
                    if kind in ('q', 'k'):
                        K.op('act', lambda: nc.scalar.activation(out=c_[:, 0:nt], in_=a_[:, 0:nt], func=AF.Silu), reads=[a_], writes=[c_])
                        K.op('dve', lambda: nc.vector.tensor_tensor(out=s_[:, 0:nt], in0=c_[:, 0:nt], in1=c_[:, 0:nt], op=ALU.mult), reads=[c_], writes=[s_])
                        yield
                        ps2 = self.next_half()
                        K.op('pe', lambda: nc.tensor.matmul(ps2[:, 0:nt], lhsT=self.onesb[:], rhs=s_[:, 0:nt], start=True, stop=True), reads=[s_, self.onesb], writes=[ps2])
                        K.op('act', lambda: nc.scalar.activation(out=l_[:, 0:nt], in_=ps2[:, 0:nt], func=AF.Ln, bias=self.epsb[:], scale=1.0), reads=[ps2, self.epsb], writes=[l_])
                        self.free_half(ps2)
                        yield
                        if kind == 'q':
                            K.op('act', lambda: nc.scalar.activation(out=l_[:, 0:nt], in_=l_[:, 0:nt], func=AF.Exp, bias=self.lnqs[:], scale=-0.5), reads=[l_, self.lnqs], writes=[l_])
                        else:
                            K.op('act', lambda: nc.scalar.activation(out=l_[:, 0:nt], in_=l_[:, 0:nt], func=AF.Exp, scale=-0.5), reads=[l_], writes=[l_])
                        yield
                        K.op('dve', lambda: nc.vector.tensor_tensor(out=o_[:, 0:nt], in0=c_[:, 0:nt], in1=l_[:, 0:nt], op=ALU.mult), reads=[c_, l_], writes=[o_])
                    elif fc < 12:
                        K.op('act', lambda: nc.scalar.activation(out=o_[:, 0:nt], in_=a_[:, 0:nt], func=AF.Silu, bias=self.pvt[:, 196 + fc:197 + fc], scale=1.0), reads=[a_, self.pvt], writes=[o_])
                    else:
                        K.op('act', lambda: nc.scalar.activation(out=o_[:, 0:nt], in_=a_[:, 0:nt], func=AF.Silu), reads=[a_], writes=[o_])
                    yield
                    if fm_row is not None:
                        self.store('pool', o_, self.FM[fm_row:fm_row + 128, t0:t0 + nt], o_[:, 0:nt])
                    if tm_col is not None:
                        nj = nt // 128
                        ps3 = self.next_half()
                        pb = ps3[:].bitcast(BF16)
                        fns = [lambda j=j: nc.tensor.transpose(out=pb[:, j * 128:(j + 1) * 128], in_=o_[:, j * 128:(j + 1) * 128], identity=self.idb[:]) for j in range(nj)]
                        K.op('pe', fns, reads=[o_, self.idb], writes=[ps3])
                        K.op('act', lambda: nc.scalar.copy(out=m_[:, 0:nj, :], in_=pb[:, 0:nj * 128].rearrange("p (j f) -> p j f", f=128)), reads=[ps3], writes=[m_])
                        self.free_half(ps3)
                        dst = self.TM[t0:t0 + nt, tm_col:tm_col + 128].rearrange("(j p) f -> p j f", p=128)
                        self.store('pool', m_, dst, m_[:, 0:nj, :])

                def fm_gens():
                    it = 0
                    for fc in range(36):
                        col0 = (2048 + fc * 128) if fc < 12 else (3616 + (fc - 12) * 128)
                        w_ = wf[fc % 2]
                        src = self.w_in[l][:, col0:col0 + 128].rearrange("(k p) f -> p k f", p=128)
                        self.load('pool', w_, w_[:], src)
                        if fc < 8:
                            kind, fm_row, tm_col = 'x', None, TM_X + fc * 128
                        elif fc < 10:
                            kind, fm_row, tm_col = 'B', FM_B + (fc - 8) * 128, TM_B + (fc - 8) * 128
                        elif fc < 12:
                            kind, fm_row, tm_col = 'C', FM_C + (fc - 10) * 128, None
                        elif fc < 20:
                            kind, fm_row, tm_col = 'q', FM_Q + (fc - 12) * 128, None
                        elif fc < 28:
                            kind, fm_row, tm_col = 'k', FM_K + (fc - 20) * 128, TM_K + (fc - 20) * 128
                        else:
                            kind, fm_row, tm_col = 'v', None, TM_V + (fc - 28) * 128
                        for (t0, nt, rl) in blks:
                            yield fm_iter(it, fc, w_, kind, fm_row, tm_col, t0, nt, rl)
                            it += 1

                run_pipeline(fm_gens(), 5)
                K.barrier()
            with ExitStack() as es2:
                NZ = 4
                wz = [self.sb(es2, "wz%d" % i, [128, 16, 256], BF16) for i in range(2)]
                zo = [self.sb(es2, "zo%d" % i, [128, 256], BF16) for i in range(NZ)]
                so = [self.sb(es2, "so%d" % i, [128, 64]) for i in range(NZ)]

                def z_iter(it, cb, c, w_, ncol):
                    ps = self.next_half()
                    fns = [lambda kc=kc: nc.tensor.matmul(ps[:, 0:ncol], lhsT=uT[:, kc, c * 128:(c + 1) * 128], rhs=w_[:, kc, 0:ncol], start=(kc == 0), stop=(kc == 15)) for kc in range(16)]
                    K.op('pe', fns, reads=[w_, uT], writes=[ps])
                    yield
                    if cb < 8:
                        z_ = zo[it % NZ]
                        K.op('act', lambda: nc.scalar.activation(out=z_[:], in_=ps[:, 0:256], func=AF.Silu), reads=[ps], writes=[z_])
                        self.free_half(ps)
                        self.store('sp', z_, self.ZS[c * 128:(c + 1) * 128, cb * 256:(cb + 1) * 256], z_[:])
                    else:
                        s_ = so[it % NZ]
                        K.op('act', lambda: nc.scalar.copy(out=s_[:], in_=ps[:, 0:64]), reads=[ps], writes=[s_])
                        self.free_half(ps)
                        self.store('sp', s_, self.SMALL[c * 128:(c + 1) * 128, :], s_[:])

                def z_gens():
                    it = 0
                    for cb in range(9):
                        w_ = wz[cb % 2]
                        if cb < 8:
                            src = self.w_in[l][:, cb * 256:(cb + 1) * 256].rearrange("(k p) f -> p k f", p=128)
                            self.load('pool', w_, w_[:], src)
                            ncol = 256
                        else:
                            self.load('pool', w_, w_[:, :, 0:32], self.w_in[l][:, 3584:3616].rearrange("(k p) f -> p k f", p=128))
                            self.load('pool', w_, w_[:, :, 32:64], self.w_in[l][:, 6688:6720].rearrange("(k p) f -> p k f", p=128))
                            ncol = 64
                        for c in range(NCH):
                            yield z_iter(it, cb, c, w_, ncol)
                            it += 1

                run_pipeline(z_gens(), 3)

    def phaseS(self, l, es):
        nc, K = self.nc, self.K
        sb = lambda name, shape, dt=F32: self.sb(es, name, shape, dt)
        self.dt_all = sb("dt_all", [128, NCH, 32])
        self.loga_all = sb("loga_all", [128, NCH, 32])
        self.beta_all = sb("beta_all", [128, NCH, 16])
        self.g_all = sb("g_all", [128, NCH, 16])
        self.dsk = sb("dsk", [128, 16])
        self.onecol = sb("onecol", [128, 1])
        K.op('dve', lambda: nc.vector.memset(self.onecol[:], 1.0), writes=[self.onecol])
        self.load('sp', self.dsk, self.dsk[:], dram_bcast(self.ssd_d[l], 128))
        with ExitStack() as e2:
            sm = self.sb(e2, "sm", [128, NCH, 64])
            self.load('sp', sm, sm[:], self.SMALL.rearrange("(c p) f -> p c f", p=128))
            bias = self.sb(e2, "sbias", [128, 48])
            alog = self.sb(e2, "salog", [128, 48])
            self.load('sp', bias, bias[:, 0:32], dram_bcast(self.ssd_dt_bias[l], 128))
            self.load('sp', bias, bias[:, 32:48], dram_bcast(self.gdn_dt_bias[l], 128))
            self.load('sp', alog, alog[:, 0:32], dram_bcast(self.ssd_a_log[l], 128))
            self.load('sp', alog, alog[:, 32:48], dram_bcast(self.gdn_a_log[l], 128))
            K.op('act', lambda: nc.scalar.activation(out=alog[:], in_=alog[:], func=AF.Exp), reads=[alog], writes=[alog])
            K.op('dve', lambda: nc.vector.tensor_scalar(out=alog[:], in0=alog[:], scalar1=-1.0, scalar2=None, op0=ALU.mult), reads=[alog], writes=[alog])
            tmp = self.sb(e2, "stmp", [128, NCH, 48])
            K.op('dve', lambda: nc.vector.tensor_tensor(out=tmp[:, :, 0:32], in0=sm[:, :, 0:32], in1=bias[:, 0:32].unsqueeze(1).broadcast_to([128, NCH, 32]), op=ALU.add), reads=[sm, bias], writes=[tmp])
            K.op('dve', lambda: nc.vector.tensor_tensor(out=tmp[:, :, 32:48], in0=sm[:, :, 48:64], in1=bias[:, 32:48].unsqueeze(1).broadcast_to([128, NCH, 16]), op=ALU.add), reads=[sm, bias], writes=[tmp])
            K.op('act', lambda: nc.scalar.activation(out=tmp[:], in_=tmp[:], func=AF.Exp), reads=[tmp], writes=[tmp])
            K.op('act', lambda: nc.scalar.activation(out=tmp[:], in_=tmp[:], func=AF.Ln, bias=self.onecol[:], scale=1.0), reads=[tmp, self.onecol], writes=[tmp])
            K.op('dve', lambda: nc.vector.tensor_copy(out=self.dt_all[:], in_=tmp[:, :, 0:32]), reads=[tmp], writes=[self.dt_all])
            K.op('dve', lambda: nc.vector.tensor_tensor(out=self.loga_all[:], in0=tmp[:, :, 0:32], in1=alog[:, 0:32].unsqueeze(1).broadcast_to([128, NCH, 32]), op=ALU.mult), reads=[tmp, alog], writes=[self.loga_all])
            K.op('dve', lambda: nc.vector.tensor_tensor(out=self.g_all[:], in0=tmp[:, :, 32:48], in1=alog[:, 32:48].unsqueeze(1).broadcast_to([128, NCH, 16]), op=ALU.mult), reads=[tmp, alog], writes=[self.g_all])
            K.op('act', lambda: nc.scalar.activation(out=self.beta_all[:], in_=sm[:, :, 32:48], func=AF.Sigmoid), reads=[sm], writes=[self.beta_all])
            K.barrier()

    def phaseB(self, l, last):
        nc, K = self.nc, self.K
        with ExitStack() as es:
            self.phaseS(l, es)
            sb = lambda name, shape, dt=F32: self.sb(es, name, shape, dt)
            B = self.Bt = type('T', (), {})()
            B.tm = [sb("tm%d" % i, [128, TM_COLS], BF16) for i in range(3)]
            B.fm = [sb("fm%d" % i, [128, 20, 128], BF16) for i in range(3)]
            B.yo1 = sb("yo1", [128, D])
            B.zs = sb("zs", [128, D], BF16)
            B.ex = sb("ex", [128, 48]); B.rhsL = sb("rhsL", [128, 8, 128]); B.E = sb("E", [128, 16, 128], BF16)
            B.cbm = sb("cbm", [128, 2, 128], BF16); B.MT = sb("MT", [128, 16, 128], BF16)
            B.xdt = sb("xdt", [128, 16, 64], BF16); B.xdd = sb("xdd", [128, 16, 64], BF16)
            B.ytmp = sb("ytmp", [128, 1024]); B.ysum = sb("ysum", [128, 1024]); B.yfin = sb("yfin", [128, 1024])
            B.hT = sb("hT", [128, 16, 64]); B.hTb = sb("hTb", [128, 16, 64], BF16)
            B.gex = [sb("gex%d" % i, [128, 3, 8]) for i in range(2)]; B.rhsG = sb("rhsG", [128, 8, 128]); B.DT = sb("DT", [128, 8, 128])
            B.tA = sb("tA", [128, 8, 128]); B.tB = sb("tB", [128, 8, 128])
            B.nm = sb("nm", [128, 128], BF16); B.bdS = sb("bdS", [128, 128])
            g16 = lambda name: sb(name, [128, 8, 128], BF16)
            B.UD = g16("UD"); B.UO = g16("UO"); B.attnT = [g16("attnT0"), g16("attnT1")]; B.AD = g16("AD")
            B.P = [g16("P0"), g16("P1")]; B.X = [g16("X0"), g16("X1")]; B.XT = [g16("XT0"), g16("XT1")]
            B.TD = g16("TD"); B.VT = g16("VT"); B.V = g16("V"); B.V2T = g16("V2T"); B.Z = g16("Z"); B.Sf = g16("Sf")
            B.kg = g16("kg"); B.kendk = [g16("kendk0"), g16("kendk1")]; B.wT = [g16("wT0"), g16("wT1")]; B.vnew = g16("vnew")
            B.u_sb = [sb("u_sb%d" % i, [128, 8, 128]) for i in range(2)]; B.o = sb("o", [128, 8, 128]); B.ofin = sb("ofin", [128, 8, 128])
            B.Sg = sb("Sg", [128, 8, 128]); B.Sgb = g16("Sgb")
            NG = self.NG = 1
            for t_ in [B.rhsG, B.DT, B.tA, B.UD, B.UO, B.AD, B.TD, B.VT, B.V, B.V2T, B.Z, B.Sf, B.kg] + B.P + B.X + B.XT + B.gex + B.attnT + B.u_sb + B.wT + B.kendk:
                t_.g = [Buf(t_.name + "_g%d" % j, t_.t) for j in range(NG)]
            B.mT = sb("mT", [128, 128]); B.nbd = sb("nbd", [128, 128])
            K.op('dve', lambda: nc.vector.tensor_scalar(out=B.nbd[:], in0=self.cst[:, C_BD:C_BD + 128], scalar1=-1.0, scalar2=1.0, op0=ALU.mult, op1=ALU.add), reads=[self.cst], writes=[B.nbd])
            B.snw = sb("snw", [128, 1024]); B.gnw = sb("gnw", [128, 128])
            self.load('sp', B.snw, B.snw[:], dram_bcast(self.ssd_norm_w[l], 128))
            self.load('sp', B.gnw, B.gnw[:], dram_bcast(self.gdn_norm_w[l], 128))
            B.yg = sb("yg", [128, 1024]); B.ejunk = sb("ejunk", [128, 1024], BF16); B.est = sb("est", [128, 16]); B.est2 = sb("est2", [128, 16])
            B.ym = sb("ym", [128, D], BF16); B.wz = sb("wzg", [128, 8, 128])
            for d in range(2):
                self.scan_pass(l, d, last)
                K.barrier()

    def scan_pass(self, l, d, last):
        nc, K, B = self.nc, self.K, self.Bt
        order = list(range(NCH)) if d == 0 else [1, 0] + list(range(NCH - 1, 1, -1))
        B.inc = self.cst[:, (C_INC0 if d == 0 else C_INC1):(C_INC0 if d == 0 else C_INC1) + 128]
        B.exc = self.cst[:, (C_EXC0 if d == 0 else C_EXC1):(C_EXC0 if d == 0 else C_EXC1) + 128]
        B.strictT = self.cst[:, (C_EXC1 if d == 0 else C_EXC0):(C_EXC1 if d == 0 else C_EXC0) + 128]
        B.ones = self.cst[:, C_ONES:C_ONES + 128]
        B.bd = self.cst[:, C_BD:C_BD + 128]
        K.op('dve', lambda: nc.vector.memset(B.hT[:], 0.0), writes=[B.hT])
        K.op('dve', lambda: nc.vector.memset(B.hTb[:], 0.0), writes=[B.hTb])
        K.op('dve', lambda: nc.vector.memset(B.Sg[:], 0.0), writes=[B.Sg])
        K.op('dve', lambda: nc.vector.memset(B.Sgb[:], 0.0), writes=[B.Sgb])

        K.op('dve', lambda: nc.vector.tensor_scalar(out=B.nm[:], in0=B.inc, scalar1=30000.0, scalar2=-30000.0, op0=ALU.mult, op1=ALU.add), reads=[self.cst], writes=[B.nm])
        K.op('dve', lambda: nc.vector.tensor_tensor(out=B.bdS[:], in0=B.strictT, in1=B.bd, op=ALU.mult), reads=[self.cst], writes=[B.bdS])

        def issue_loads(i):
            c = order[i]
            tm_, fm_ = B.tm[i % 3], B.fm[i % 3]
            self.load('sp', tm_, tm_[:], self.TM[c * 128:(c + 1) * 128, :])
            self.load('sp', fm_, fm_[:], self.FM[:, c * 128:(c + 1) * 128].rearrange("(f p) t -> p f t", p=128))

        def run_weighted(gens):
            st = [[g, n, 0] for g, n in gens]
            while st:
                st.sort(key=lambda x: x[2] / float(x[1]))
                g = st[0]
                try:
                    next(g[0])
                    g[2] += 1
                except StopIteration:
                    st.remove(g)

        n = len(order)
        issue_loads(0)
        if n > 1:
            issue_loads(1)
        run_weighted([(self.gdn_pre(order[0], d, B.tm[0], B.fm[0], 0, hg, self.NG), 20) for hg in range(self.NG)])
        for i, c in enumerate(order):
            if i + 2 < n:
                issue_loads(i + 2)
            tm_, fm_ = B.tm[i % 3], B.fm[i % 3]
            need_out = not (last and c < 2)
            if d == 1 and need_out:
                self.load('sp', B.yo1, B.yo1[:], self.YO[c * 128:(c + 1) * 128, :])
                self.load('sp', B.zs, B.zs[:], self.ZS[c * 128:(c + 1) * 128, :])
            gens = [(self.ssd_gen(c, d, tm_, fm_, need_out), 7), (self.gdn_rec(c, d, fm_, need_out, i % 2), 3)]
            if i + 1 < n:
                for hg in range(self.NG):
                    gens.append((self.gdn_pre(order[i + 1], d, B.tm[(i + 1) % 3], B.fm[(i + 1) % 3], (i + 1) % 2, hg, self.NG), 20))
            run_weighted(gens)
            if d == 1 and need_out:
                self.epilogue(c)

    def ssd_gen(self, c, d, tm_, fm_, need_out):
        nc, K, B = self.nc, self.K, self.Bt
        cst = self.cst
        loga = self.loga_all[:, c, d * 16:(d + 1) * 16]
        dtc = self.dt_all[:, c, d * 16:(d + 1) * 16]
        xs3 = tm_[:, TM_X:TM_X + 1024].rearrange("p (h q) -> p h q", q=64)
        ps = self.next_ps()
        K.op('pe', [lambda: nc.tensor.matmul(ps[:, 0:16], lhsT=B.inc, rhs=loga, start=True, stop=True),
                    lambda: nc.tensor.matmul(ps[:, 16:32], lhsT=B.exc, rhs=loga, start=True, stop=True),
                    lambda: nc.tensor.matmul(ps[:, 32:48], lhsT=B.ones, rhs=loga, start=True, stop=True)], reads=[cst, self.loga_all], writes=[ps])
        K.op('act', lambda: nc.scalar.activation(out=B.ex[:], in_=ps[:, 0:48], func=AF.Exp), reads=[ps], writes=[B.ex])
        expA, dend, cd = B.ex[:, 0:16], B.ex[:, 16:32], B.ex[:, 32:48]
        yield
        for hh in range(2):
            K.op('dve', lambda hh=hh: nc.vector.tensor_tensor(out=B.rhsL[:], in0=B.inc.unsqueeze(1).broadcast_to([128, 8, 128]),
                                                             in1=loga[:, hh * 8:(hh + 1) * 8].unsqueeze(2).broadcast_to([128, 8, 128]), op=ALU.mult),
                 reads=[cst, self.loga_all], writes=[B.rhsL])
            ps = self.next_ps()
            rl2 = B.rhsL[:].rearrange("p h q -> p (h q)")
            nm4 = B.nm[:].unsqueeze(1).broadcast_to([128, 4, 128])
            fns = []
            for j in range(2):
                fns.append(lambda ps=ps, rl2=rl2, j=j: nc.tensor.matmul(ps[:, j * 512:(j + 1) * 512], lhsT=B.exc, rhs=rl2[:, j * 512:(j + 1) * 512], start=True, stop=False))
                fns.append(lambda ps=ps, j=j: nc.tensor.matmul(ps[:, j * 512:(j + 1) * 512].rearrange("p (h q) -> p h q", h=4), lhsT=self.idb[:], rhs=nm4, start=False, stop=True))
            K.op('pe', fns, reads=[cst, B.rhsL, B.nm, self.idb], writes=[ps])
            K.op('act', lambda ps=ps, hh=hh: nc.scalar.activation(out=B.E[:, hh * 8:(hh + 1) * 8, :].rearrange("p h q -> p (h q)"), in_=ps[:], func=AF.Exp), reads=[ps], writes=[B.E])
            yield
        ps = self.next_ps()
        K.op('pe', [lambda ps=ps, g=g: nc.tensor.matmul(ps[:, g * 128:(g + 1) * 128], lhsT=fm_[:, FM_B // 128 + g, :], rhs=fm_[:, FM_C // 128 + g, :], start=True, stop=True) for g in range(2)],
             reads=[fm_], writes=[ps])
        K.op('act', lambda ps=ps: nc.scalar.copy(out=B.cbm[:].rearrange("p g q -> p (g q)"), in_=ps[:, 0:256]), reads=[ps], writes=[B.cbm])
        yield
        K.op('dve', lambda: nc.vector.tensor_tensor(out=B.MT[:].rearrange("p (g r) q -> p g r q", g=2), in0=B.E[:].rearrange("p (g r) q -> p g r q", g=2),
                                                    in1=B.cbm[:].unsqueeze(2).broadcast_to([128, 2, 8, 128]), op=ALU.mult), reads=[B.E, B.cbm], writes=[B.MT])
        K.op('dve', lambda: nc.vector.tensor_tensor(out=B.xdt[:], in0=xs3, in1=dtc.unsqueeze(2).broadcast_to([128, 16, 64]), op=ALU.mult), reads=[tm_, self.dt_all], writes=[B.xdt])
        K.op('dve', lambda: nc.vector.tensor_tensor(out=B.xdd[:], in0=B.xdt[:], in1=dend.unsqueeze(2).broadcast_to([128, 16, 64]), op=ALU.mult), reads=[B.xdt, B.ex], writes=[B.xdd])
        yield
        psd = self.next_ps()
        K.op('pe', [lambda h=h: nc.tensor.matmul(psd[:, h * 64:(h + 1) * 64], lhsT=B.MT[:, h, :], rhs=B.xdt[:, h, :], start=True, stop=True) for h in range(16)], reads=[B.MT, B.xdt], writes=[psd])
        pso = self.next_ps()
        K.op('pe', [lambda g=g: nc.tensor.matmul(pso[:, g * 512:(g + 1) * 512], lhsT=fm_[:, FM_C // 128 + g, :], rhs=B.hTb[:, g * 8:(g + 1) * 8, :].rearrange("p h q -> p (h q)"), start=True, stop=True) for g in range(2)],
             reads=[fm_, B.hTb], writes=[pso])
        K.op('dve', lambda: nc.vector.tensor_tensor(out=B.ytmp[:].rearrange("p (h q) -> p h q", q=64), in0=pso[:].rearrange("p (h q) -> p h q", q=64), in1=expA.unsqueeze(2).broadcast_to([128, 16, 64]), op=ALU.mult),
             reads=[pso, B.ex], writes=[B.ytmp])
        K.op('dve', lambda: nc.vector.tensor_tensor(out=B.ysum[:], in0=B.ytmp[:], in1=psd[:], op=ALU.add), reads=[B.ytmp, psd], writes=[B.ysum])
        if need_out:
            if d == 0:
                K.op('dve', lambda: nc.vector.tensor_tensor(out=B.ytmp[:].rearrange("p (h q) -> p h q", q=64), in0=xs3, in1=self.dsk[:].unsqueeze(2).broadcast_to([128, 16, 64]), op=ALU.mult),
                     reads=[tm_, self.dsk], writes=[B.ytmp])
                K.op('dve', lambda: nc.vector.tensor_tensor(out=B.yfin[:], in0=B.ytmp[:], in1=B.ysum[:], op=ALU.add), reads=[B.ytmp, B.ysum], writes=[B.yfin])
                self.store('pool', B.yfin, self.YO[c * 128:(c + 1) * 128, 0:1024], B.yfin[:])
            else:
                K.op('dve', lambda: nc.vector.tensor_tensor(out=B.yfin[:], in0=B.yo1[:, 0:1024], in1=B.ysum[:], op=ALU.add), reads=[B.yo1, B.ysum], writes=[B.yfin])
        yield
        pss = self.next_ps()
        K.op('pe', [lambda g=g: nc.tensor.matmul(pss[:, g * 512:(g + 1) * 512], lhsT=tm_[:, TM_B + g * 128:TM_B + (g + 1) * 128], rhs=B.xdd[:, g * 8:(g + 1) * 8, :].rearrange("p h q -> p (h q)"), start=True, stop=True) for g in range(2)],
             reads=[tm_, B.xdd], writes=[pss])
        K.op('dve', lambda: nc.vector.tensor_tensor(out=B.hT[:], in0=B.hT[:], in1=cd.unsqueeze(2).broadcast_to([128, 16, 64]), op=ALU.mult), reads=[B.hT, B.ex], writes=[B.hT])
        K.op('dve', lambda: nc.vector.tensor_tensor(out=B.hT[:].rearrange("p h q -> p (h q)"), in0=B.hT[:].rearrange("p h q -> p (h q)"), in1=pss[:], op=ALU.add), reads=[B.hT, pss], writes=[B.hT])
        K.op('act', lambda: nc.scalar.copy(out=B.hTb[:], in_=B.hT[:]), reads=[B.hT], writes=[B.hTb])
        yield

    def mm8(self, ps, lhs, rhs, lhs_bufs, rhs_bufs, bf16_out=False, transpose=False, acc=None, hs=None):
        nc, K = self.nc, self.K
        if hs is None:
            hs = range(8)
        if bf16_out:
            pv = ps[:].bitcast(BF16)
        else:
            pv = ps[:]
        if transpose:
            fns = [lambda j=j, H=H: nc.tensor.transpose(out=pv[:, j * 128:(j + 1) * 128], in_=lhs(H), identity=self.idb[:]) for j, H in enumerate(hs)]
        elif acc is None:
            fns = [lambda j=j, H=H: nc.tensor.matmul(pv[:, j * 128:(j + 1) * 128], lhsT=lhs(H), rhs=rhs(H), start=True, stop=True) for j, H in enumerate(hs)]
        else:
            al, ar = acc
            fns = []
            for j, H in enumerate(hs):
                fns.append(lambda j=j, H=H: nc.tensor.matmul(pv[:, j * 128:(j + 1) * 128], lhsT=lhs(H), rhs=rhs(H), start=True, stop=False))
                fns.append(lambda j=j, H=H: nc.tensor.matmul(pv[:, j * 128:(j + 1) * 128], lhsT=al(H), rhs=ar(H), start=False, stop=True))
        K.op('pe', fns, reads=list(lhs_bufs) + list(rhs_bufs), writes=[ps])
        return pv

    def gdn_pre(self, c, d, tm_, fm_, slot, hg, ng):
        nc, K, B = self.nc, self.K, self.Bt
        cst = self.cst
        nh = 8 // ng
        H0 = hg * nh
        hs = range(H0, H0 + nh)
        W = nh * 128
        gc = self.g_all[:, c, d * 8 + H0:d * 8 + H0 + nh]
        bc = self.beta_all[:, c, d * 8 + H0:d * 8 + H0 + nh]
        k3 = tm_[:, TM_K + H0 * 128:TM_K + (H0 + nh) * 128].rearrange("p (h q) -> p h q", q=128)
        v3 = tm_[:, TM_V:TM_V + 1024].rearrange("p (h q) -> p h q", q=128)
        kT = lambda H: fm_[:, FM_K // 128 + H, :]
        qT = lambda H: fm_[:, FM_Q // 128 + H, :]
        G = lambda buf: buf.g[hg]
        hd = lambda buf: (lambda H: buf[:, H, :])
        idH = lambda H: self.idb[:]
        sl = lambda buf: buf[:, H0:H0 + nh, :]
        fl = lambda buf: buf[:, H0:H0 + nh, :].rearrange("p h q -> p (h q)")
        bcl = lambda ap2: ap2.unsqueeze(2).broadcast_to([128, nh, 128])
        bcm = lambda ap2: ap2.unsqueeze(1).broadcast_to([128, nh, 128])
        gex, attnT, u_sb, wT, kendk = B.gex[slot], B.attnT[slot], B.u_sb[slot], B.wT[slot], B.kendk[slot]
        ps = self.next_ps()
        K.op('pe', [lambda: nc.tensor.matmul(ps[:, 0:nh], lhsT=B.inc, rhs=gc, start=True, stop=True),
                    lambda: nc.tensor.matmul(ps[:, 8:8 + nh], lhsT=B.exc, rhs=gc, start=True, stop=True),
                    lambda: nc.tensor.matmul(ps[:, 16:16 + nh], lhsT=B.ones, rhs=gc, start=True, stop=True)], reads=[cst, self.g_all], writes=[ps])
        K.op('act', lambda: nc.scalar.activation(out=gex[:, :, H0:H0 + nh], in_=ps[:, 0:24].rearrange("p (a h) -> p a h", a=3)[:, :, 0:nh], func=AF.Exp), reads=[ps], writes=[G(gex)])
        expG, kend = gex[:, 0, H0:H0 + nh], gex[:, 1, H0:H0 + nh]
        yield
        K.op('dve', lambda: nc.vector.tensor_tensor(out=sl(B.rhsG), in0=bcm(B.inc), in1=bcl(gc), op=ALU.mult), reads=[cst, self.g_all], writes=[G(B.rhsG)])
        ps = self.next_ps()
        rg2 = fl(B.rhsG)
        nm4 = B.nm[:].unsqueeze(1).broadcast_to([128, 4, 128])
        fns = []
        for j in range(W // 512):
            fns.append(lambda j=j: nc.tensor.matmul(ps[:, j * 512:(j + 1) * 512], lhsT=B.exc, rhs=rg2[:, j * 512:(j + 1) * 512], start=True, stop=False))
            fns.append(lambda j=j: nc.tensor.matmul(ps[:, j * 512:(j + 1) * 512].rearrange("p (h q) -> p h q", h=4), lhsT=self.idb[:], rhs=nm4, start=False, stop=True))
        K.op('pe', fns, reads=[cst, G(B.rhsG), B.nm, self.idb], writes=[ps])
        K.op('act', lambda: nc.scalar.activation(out=fl(B.DT), in_=ps[:, 0:W], func=AF.Exp), reads=[ps], writes=[G(B.DT)])
        yield
        ps = self.next_ps()
        self.mm8(ps, kT, kT, [fm_], [], hs=hs)
        K.op('dve', lambda: nc.vector.tensor_tensor(out=fl(B.tA), in0=ps[:, 0:W], in1=fl(B.DT), op=ALU.mult), reads=[ps, G(B.DT)], writes=[G(B.tA)])
        K.op('dve', lambda: nc.vector.tensor_tensor(out=sl(B.tA), in0=sl(B.tA), in1=bcl(bc), op=ALU.mult), reads=[G(B.tA), self.beta_all], writes=[G(B.tA)])
        yield
        K.op('dve', lambda: nc.vector.tensor_tensor(out=sl(B.UD), in0=sl(B.tA), in1=bcm(B.bdS[:]), op=ALU.mult), reads=[G(B.tA), B.bdS], writes=[G(B.UD)])
        K.op('dve', lambda: nc.vector.tensor_tensor(out=sl(B.UO), in0=sl(B.tA), in1=bcm(B.nbd[:]), op=ALU.mult), reads=[G(B.tA), B.nbd], writes=[G(B.UO)])
        yield
        ps = self.next_ps()
        self.mm8(ps, kT, qT, [fm_], [], hs=hs)
        K.op('dve', lambda: nc.vector.tensor_tensor(out=fl(attnT), in0=ps[:, 0:W], in1=fl(B.DT), op=ALU.mult), reads=[ps, G(B.DT)], writes=[G(attnT)])
        yield
        ps = self.next_ps()
        pv = self.mm8(ps, hd(B.UD), None, [G(B.UD), self.idb], [], bf16_out=True, transpose=True, hs=hs)
        K.op('act', lambda: nc.scalar.copy(out=fl(B.AD), in_=pv[:, 0:W]), reads=[ps], writes=[G(B.AD)])
        yield
        X, XT = B.UD, B.AD
        P = B.P[0]
        K.op('dve', lambda: nc.vector.tensor_tensor(out=sl(P), in0=bcm(self.idb[:]), in1=sl(X), op=ALU.subtract), reads=[self.idb, G(X)], writes=[G(P)])

        def square(k, X, XT):
            XTn = B.XT[k % 2]
            ps = self.next_ps()
            self.mm8(ps, hd(X), hd(XT), [G(X)], [G(XT)], hs=hs)
            K.op('act', lambda: nc.scalar.copy(out=fl(XTn), in_=ps[:, 0:W]), reads=[ps], writes=[G(XTn)])
            Xn = None
            if k < 4:
                Xn = B.X[k % 2]
                ps2 = self.next_ps()
                self.mm8(ps2, hd(XT), hd(X), [G(XT)], [G(X)], hs=hs)
                K.op('act', lambda: nc.scalar.copy(out=fl(Xn), in_=ps2[:, 0:W]), reads=[ps2], writes=[G(Xn)])
            return Xn, XTn

        Xn, XTn = square(1, X, XT)
        yield
        for k in range(1, 5):
            ps3 = self.next_ps()
            self.mm8(ps3, hd(XTn), hd(P), [G(XTn), self.idb], [G(P)], acc=(idH, hd(P)), hs=hs)
            Pn = B.P[k % 2]
            K.op('act', lambda ps3=ps3, Pn=Pn: nc.scalar.copy(out=fl(Pn), in_=ps3[:, 0:W]), reads=[ps3], writes=[G(Pn)])
            P = Pn
            if k < 4:
                yield
                Xn, XTn = square(k + 1, Xn, XTn)
            yield
        SD = P
        ps = self.next_ps()
        self.mm8(ps, hd(SD), idH, [G(SD)], [self.idb], hs=hs)
        K.op('act', lambda ps=ps: nc.scalar.copy(out=fl(B.TD), in_=ps[:, 0:W]), reads=[ps], writes=[G(B.TD)])
        yield
        ps = self.next_ps()
        self.mm8(ps, hd(B.UO), hd(B.TD), [G(B.UO)], [G(B.TD)], hs=hs)
        K.op('act', lambda ps=ps: nc.scalar.activation(out=fl(B.VT), in_=ps[:, 0:W], func=AF.Copy, scale=-1.0), reads=[ps], writes=[G(B.VT)])
        ps2 = self.next_ps()
        self.mm8(ps2, hd(B.TD), hd(B.UO), [G(B.TD)], [G(B.UO)], hs=hs)
        K.op('act', lambda ps2=ps2: nc.scalar.copy(out=fl(B.V), in_=ps2[:, 0:W]), reads=[ps2], writes=[G(B.V)])
        yield
        ps = self.next_ps()
        self.mm8(ps, hd(B.V), hd(B.VT), [G(B.V)], [G(B.VT)], hs=hs)
        K.op('act', lambda ps=ps: nc.scalar.activation(out=fl(B.V2T), in_=ps[:, 0:W], func=AF.Copy, scale=-1.0), reads=[ps], writes=[G(B.V2T)])
        ps2 = self.next_ps()
        self.mm8(ps2, hd(B.VT), hd(SD), [G(B.VT), self.idb], [G(SD)], acc=(idH, hd(SD)), hs=hs)
        K.op('act', lambda ps2=ps2: nc.scalar.copy(out=fl(B.Z), in_=ps2[:, 0:W]), reads=[ps2], writes=[G(B.Z)])
        yield
        ps = self.next_ps()
        self.mm8(ps, hd(B.V2T), hd(B.Z), [G(B.V2T), self.idb], [G(B.Z)], acc=(idH, hd(B.Z)), hs=hs)
        K.op('act', lambda ps=ps: nc.scalar.copy(out=fl(B.Sf), in_=ps[:, 0:W]), reads=[ps], writes=[G(B.Sf)])
        yield
        K.op('dve', lambda: nc.vector.tensor_tensor(out=sl(B.kg), in0=k3, in1=bcl(expG), op=ALU.mult), reads=[tm_, G(gex)], writes=[G(B.kg)])
        K.op('dve', lambda: nc.vector.tensor_tensor(out=sl(kendk), in0=k3, in1=bcl(kend), op=ALU.mult), reads=[tm_, G(gex)], writes=[G(kendk)])
        yield
        ps = self.next_ps()
        self.mm8(ps, hd(B.Sf), lambda H: v3[:, H, :], [G(B.Sf)], [tm_], hs=hs)
        K.op('act', lambda ps=ps: nc.scalar.copy(out=fl(u_sb), in_=ps[:, 0:W]), reads=[ps], writes=[G(u_sb)])
        ps2 = self.next_ps()
        self.mm8(ps2, hd(B.kg), hd(B.Sf), [G(B.kg)], [G(B.Sf)], hs=hs)
        K.op('act', lambda ps2=ps2: nc.scalar.copy(out=fl(wT), in_=ps2[:, 0:W]), reads=[ps2], writes=[G(wT)])
        yield

    def gdn_rec(self, c, d, fm_, need_out, slot):
        nc, K, B = self.nc, self.K, self.Bt
        bc = self.beta_all[:, c, d * 8:(d + 1) * 8]
        qT = lambda H: fm_[:, FM_Q // 128 + H, :]
        hd = lambda buf: (lambda H: buf[:, H, :])
        fl = lambda buf: buf[:].rearrange("p h q -> p (h q)")
        f3 = lambda ap: ap.rearrange("p (h q) -> p h q", q=128)
        gex_, attnT_, u_sb_, wT_, kendk_ = B.gex[slot], B.attnT[slot], B.u_sb[slot], B.wT[slot], B.kendk[slot]
        expG, cdG = gex_[:, 0, :], gex_[:, 2, :]

        class _Multi:
            def __init__(self, buf):
                self.buf = buf
            def __getitem__(self, k):
                return self.buf[k]
        gex, attnT, u_sb, wT, kendk = gex_, attnT_, u_sb_, wT_, kendk_
        GG = lambda buf: list(buf.g)
        ps = self.next_ps()
        self.mm8(ps, hd(wT), hd(B.Sgb), GG(wT), [B.Sgb])
        K.op('dve', lambda: nc.vector.tensor_tensor(out=fl(B.tB), in0=fl(u_sb), in1=ps[:], op=ALU.subtract), reads=GG(u_sb) + [ps], writes=[B.tB])
        K.op('dve', lambda: nc.vector.tensor_tensor(out=B.vnew[:], in0=B.tB[:], in1=bc.unsqueeze(2).broadcast_to([128, 8, 128]), op=ALU.mult), reads=[B.tB, self.beta_all], writes=[B.vnew])
        yield
        ps = self.next_ps()
        self.mm8(ps, qT, hd(B.Sgb), [fm_], [B.Sgb])
        K.op('dve', lambda: nc.vector.tensor_tensor(out=B.tB[:], in0=f3(ps[:]), in1=expG.unsqueeze(2).broadcast_to([128, 8, 128]), op=ALU.mult), reads=[ps] + GG(gex), writes=[B.tB])
        ps2 = self.next_ps()
        self.mm8(ps2, hd(attnT), hd(B.vnew), GG(attnT), [B.vnew])
        K.op('dve', lambda: nc.vector.tensor_tensor(out=fl(B.o), in0=fl(B.tB), in1=ps2[:], op=ALU.add), reads=[B.tB, ps2], writes=[B.o])
        if need_out:
            if d == 0:
                self.store('pool', B.o, self.YO[c * 128:(c + 1) * 128, 1024:2048], fl(B.o))
            else:
                K.op('dve', lambda: nc.vector.tensor_tensor(out=fl(B.ofin), in0=fl(B.o), in1=B.yo1[:, 1024:2048], op=ALU.add), reads=[B.o, B.yo1], writes=[B.ofin])
        yield
        ps = self.next_ps()
        self.mm8(ps, hd(kendk), hd(B.vnew), GG(kendk), [B.vnew])
        K.op('dve', lambda: nc.vector.tensor_tensor(out=B.Sg[:], in0=B.Sg[:], in1=cdG.unsqueeze(2).broadcast_to([128, 8, 128]), op=ALU.mult), reads=[B.Sg] + GG(gex), writes=[B.Sg])
        K.op('dve', lambda: nc.vector.tensor_tensor(out=fl(B.Sg), in0=fl(B.Sg), in1=ps[:], op=ALU.add), reads=[B.Sg, ps], writes=[B.Sg])
        K.op('act', lambda: nc.scalar.copy(out=B.Sgb[:], in_=B.Sg[:]), reads=[B.Sg], writes=[B.Sgb])
        yield

    def epilogue(self, c):
        nc, K, B = self.nc, self.K, self.Bt
        K.op('dve', lambda: nc.vector.tensor_tensor(out=B.yg[:], in0=B.yfin[:], in1=B.zs[:, 0:1024], op=ALU.mult), reads=[B.yfin, B.zs], writes=[B.yg])
        for g in range(2):
            K.op('act', lambda g=g: nc.scalar.activation(out=B.ejunk[:, 0:512], in_=B.yg[:, g * 512:(g + 1) * 512], func=AF.Square, accum_out=B.est[:, g:g + 1]), reads=[B.yg], writes=[B.ejunk, B.est])
        K.op('dve', lambda: nc.vector.tensor_tensor(out=B.wz[:], in0=B.ofin[:], in1=B.ofin[:], op=ALU.mult), reads=[B.ofin], writes=[B.wz])
        K.op('dve', lambda: nc.vector.tensor_reduce(out=B.est[:, 2:10], in_=B.wz[:], axis=AX.X, op=ALU.add), reads=[B.wz], writes=[B.est])
        K.op('act', lambda: nc.scalar.activation(out=B.est2[:, 0:2], in_=B.est[:, 0:2], func=AF.Ln, bias=self.epsb[:], scale=1.0 / 512), reads=[B.est, self.epsb], writes=[B.est2])
        K.op('act', lambda: nc.scalar.activation(out=B.est2[:, 2:10], in_=B.est[:, 2:10], func=AF.Ln, bias=self.epsb[:], scale=1.0 / 128), reads=[B.est, self.epsb], writes=[B.est2])
        K.op('act', lambda: nc.scalar.activation(out=B.est2[:, 0:10], in_=B.est2[:, 0:10], func=AF.Exp, scale=-0.5), reads=[B.est2], writes=[B.est2])
        for g in range(2):
            K.op('dve', lambda g=g: nc.vector.scalar_tensor_tensor(out=B.ym[:, g * 512:(g + 1) * 512], in0=B.yg[:, g * 512:(g + 1) * 512], scalar=B.est2[:, g:g + 1], in1=B.snw[:, g * 512:(g + 1) * 512], op0=ALU.mult, op1=ALU.mult),
                 reads=[B.yg, B.est2, B.snw], writes=[B.ym])
        K.op('dve', lambda: nc.vector.tensor_tensor(out=B.wz[:], in0=B.zs[:, 1024:2048].rearrange("p (h q) -> p h q", q=128), in1=B.gnw[:].unsqueeze(1).broadcast_to([128, 8, 128]), op=ALU.mult), reads=[B.zs, B.gnw], writes=[B.wz])
        K.op('dve', lambda: nc.vector.tensor_tensor(out=B.ofin[:], in0=B.ofin[:], in1=B.est2[:, 2:10].unsqueeze(2).broadcast_to([128, 8, 128]), op=ALU.mult), reads=[B.ofin, B.est2], writes=[B.ofin])
        K.op('dve', lambda: nc.vector.tensor_tensor(out=B.ym[:, 1024:2048].rearrange("p (h q) -> p h q", q=128), in0=B.ofin[:], in1=B.wz[:], op=ALU.mult), reads=[B.ofin, B.wz], writes=[B.ym])
        self.store('pool', B.ym, self.YM[c * 128:(c + 1) * 128, :], B.ym[:])

    def phaseC(self, l, hsrc, last):
        nc, K = self.nc, self.K
        with ExitStack() as es:
            sb = lambda name, shape, dt=F32: self.sb(es, name, shape, dt)
            wo = sb("wo", [128, 16, D], BF16)
            for kc in range(16):
                self.load('pool', wo, wo[:, kc, :], self.w_out[l][kc * 128:(kc + 1) * 128, :])
            gp = [sb("gp%d" % r, [128, D]) for r in range(2)]
            pw = sb("pwb", [128, D])
            self.load('sp', pw, pw[:], dram_bcast(self.post_w[l], 128))
            for r in range(2):
                self.load('sp', gp[r], gp[r][:], dram_bcast(self.modv[r, 2 * D:3 * D], 128))
                K.op('dve', lambda r=r: nc.vector.tensor_tensor(out=gp[r][:], in0=gp[r][:], in1=pw[:], op=ALU.mult), reads=[gp[r], pw], writes=[gp[r]])
            ymc = [sb("ymc%d" % i, [128, D], BF16) for i in range(2)]
            hc = [sb("hc%d" % i, [128, D]) for i in range(2)]
            ymT = sb("ymT", [128, 16, 128], BF16)
            cj = sb("cjunk", [128, D], BF16)
            cst_ = sb("cst_", [128, 8])
            res = [sb("res%d" % i, [128, D]) for i in range(2)]
            chunks = list(range(2, NCH)) if last else list(range(NCH))
            ymTs = [ymT, sb("ymT2", [128, 16, 128], BF16)]
            csts = [cst_, sb("cst2_", [128, 8])]

            def issue(i):
                c = chunks[i]
                self.load('sp', ymc[i % 2], ymc[i % 2][:], self.YM[c * 128:(c + 1) * 128, :])
                self.load('sp', hc[i % 2], hc[i % 2][:], hsrc[c * 128:(c + 1) * 128, :])

            pos = [sb("po%d" % i, [128, D]) for i in range(2)]

            def c_iter(i, c):
                y_, h_, r_, yT, st_, po = ymc[i % 2], hc[i % 2], res[i % 2], ymTs[i % 2], csts[i % 2], pos[i % 2]
                r = 1 if c < 2 else 0
                for half in range(2):
                    ps = self.next_half()
                    pb = ps[:].bitcast(BF16)
                    K.op('pe', [lambda j=j, pb=pb, half=half: nc.tensor.transpose(out=pb[:, j * 128:(j + 1) * 128], in_=y_[:, (half * 8 + j) * 128:(half * 8 + j + 1) * 128], identity=self.idb[:]) for j in range(8)],
                         reads=[y_, self.idb], writes=[ps])
                    K.op('act', lambda pb=pb, half=half: nc.scalar.copy(out=yT[:, half * 8:(half + 1) * 8, :].rearrange("p k t -> p (k t)"), in_=pb[:, 0:1024]), reads=[ps], writes=[yT])
                    self.free_half(ps)
                yield
                for cb in range(4):
                    ps = self.next_half()
                    fns = [lambda ps=ps, cb=cb, kc=kc: nc.tensor.matmul(ps[:, 0:512], lhsT=yT[:, kc, :], rhs=wo[:, kc, cb * 512:(cb + 1) * 512], start=(kc == 0), stop=(kc == 15)) for kc in range(16)]
                    K.op('pe', fns, reads=[yT, wo], writes=[ps])
                    K.op('act', lambda ps=ps, cb=cb: nc.scalar.activation(out=cj[:, cb * 512:(cb + 1) * 512], in_=ps[:, 0:512], func=AF.Square, accum_out=st_[:, cb:cb + 1]), reads=[ps], writes=[cj, st_])
                    K.op('act', lambda ps=ps, cb=cb: nc.scalar.copy(out=po[:, cb * 512:(cb + 1) * 512], in_=ps[:, 0:512]), reads=[ps], writes=[po])
                    self.free_half(ps)
                    yield
                K.op('dve', lambda: nc.vector.tensor_reduce(out=st_[:, 4:5], in_=st_[:, 0:4], axis=AX.X, op=ALU.add), reads=[st_], writes=[st_])
                K.op('act', lambda: nc.scalar.activation(out=st_[:, 5:6], in_=st_[:, 4:5], func=AF.Ln, bias=self.epsb[:], scale=1.0 / D), reads=[st_, self.epsb], writes=[st_])
                K.op('act', lambda: nc.scalar.activation(out=st_[:, 6:7], in_=st_[:, 5:6], func=AF.Exp, scale=-0.5), reads=[st_], writes=[st_])
                yield
                K.op('dve', lambda: nc.vector.scalar_tensor_tensor(out=r_[:], in0=po[:], scalar=st_[:, 6:7], in1=gp[r][:], op0=ALU.mult, op1=ALU.mult), reads=[po, st_, gp[r]], writes=[r_])
                yield
                K.op('dve', lambda: nc.vector.tensor_tensor(out=r_[:], in0=r_[:], in1=h_[:], op=ALU.add), reads=[r_, h_], writes=[r_])
                if last:
                    dst = self.out[(c - 2) * 128:(c - 1) * 128, :]
                else:
                    dst = self.H[c * 128:(c + 1) * 128, :]
                self.store('pool', r_, dst, r_[:])
                if i + 2 < len(chunks):
                    issue(i + 2)

            def c_gens():
                for i, c in enumerate(chunks):
                    yield c_iter(i, c)

            issue(0)
            issue(1)
            run_pipeline(c_gens(), 2)


def make_consts():
    i = np.arange(128)
    c = np.zeros((128, NCONST), np.float32)
    c[:, C_ID:C_ID + 128] = np.eye(128)
    c[:, C_INC0:C_INC0 + 128] = (i[:, None] <= i[None, :])
    c[:, C_INC1:C_INC1 + 128] = (i[:, None] >= i[None, :])
    c[:, C_EXC0:C_EXC0 + 128] = (i[:, None] > i[None, :])
    c[:, C_EXC1:C_EXC1 + 128] = (i[:, None] < i[None, :])
    c[:, C_ONES:C_ONES + 128] = 1.0
    c[:, C_BD:C_BD + 128] = (i[:, None] // 32 == i[None, :] // 32)
    return c


def make_pv(inp):
    pv = np.zeros((DEPTH, 128, NPV), np.float32)
    for l in range(DEPTH):
        pv[l, :, 0:16] = inp['pre_norm_w'][l].reshape(16, 128).T
        cw = np.concatenate([inp['conv_ssd_w'][l], inp['conv_gdn_w'][l]], axis=1)
        pv[l, :, 16:196] = cw.reshape(5, 36, 128).transpose(2, 1, 0).reshape(128, 180)
        pv[l, :, 196:208] = inp['conv_ssd_b'][l].reshape(12, 128).T
    return pv


def make_in_maps(inp, cores):
    shared = {
        'pv': make_pv(inp), 'consts': make_consts(),
        'w_ada': np.ascontiguousarray(inp['w_ada']), 'b_ada': np.ascontiguousarray(inp['b_ada']),
        'post_norm_w': np.ascontiguousarray(inp['post_norm_w']), 'w_in': np.ascontiguousarray(inp['w_in']),
        'ssd_a_log': np.ascontiguousarray(inp['ssd_a_log']).reshape(DEPTH, 32),
        'ssd_dt_bias': np.ascontiguousarray(inp['ssd_dt_bias']).reshape(DEPTH, 32),
        'ssd_d': np.ascontiguousarray(inp['ssd_d']), 'ssd_norm_w': np.ascontiguousarray(inp['ssd_norm_w']),
        'gdn_a_log': np.ascontiguousarray(inp['gdn_a_log']).reshape(DEPTH, 16),
        'gdn_dt_bias': np.ascontiguousarray(inp['gdn_dt_bias']).reshape(DEPTH, 16),
        'gdn_norm_w': np.ascontiguousarray(inp['gdn_norm_w']), 'w_out': np.ascontiguousarray(inp['w_out']),
    }
    maps = []
    for b in cores:
        m = dict(shared)
        m['h0'] = np.ascontiguousarray(np.concatenate([inp['ctx'][b], inp['x'][b]], axis=0))
        c2 = np.stack([inp['c'][b], inp['c_ctx']])
        m['c2t'] = np.ascontiguousarray(c2.reshape(2, 16, 128).transpose(2, 1, 0).reshape(128, 32))
        maps.append(m)
    return maps


def kernel(**inputs):
    inp = {k: np.asarray(v) for k, v in inputs.items()}
    nc = Builder().build()
    maps = make_in_maps(inp, list(range(8)))
    res = run_bass_kernel_spmd(nc, maps, core_ids=list(range(8)))
    return np.stack([r['out'] for r in res.results], axis=0).astype(np.float32)
```

```python
import numpy as np
from contextlib import ExitStack
import concourse.bass as bass
import concourse.mybir as mybir
from concourse.bass_utils import run_bass_kernel_spmd

F32, BF16 = mybir.dt.float32, mybir.dt.bfloat16
AF = mybir.ActivationFunctionType
ALU = mybir.AluOpType
AX = mybir.AxisListType

D = 2048
T = 4352
NCH = 34
DEPTH = 4
IN_DIM = 6720
EPS = 1e-6
NPV = 208
C_ID, C_INC0, C_INC1, C_EXC0, C_EXC1, C_ONES, C_BD = 0, 128, 256, 384, 512, 640, 768
NCONST = 896
FM_B, FM_C, FM_Q, FM_K, FM_ROWS = 0, 256, 512, 1536, 2560
TM_X, TM_B, TM_K, TM_V, TM_COLS = 0, 1024, 1280, 2304, 3328


class Buf:
    __slots__ = ('name', 'w', 'r', 't', 'g')

    def __init__(self, name, t=None):
        self.name = name
        self.w = None
        self.r = {}
        self.t = t

    def __getitem__(self, k):
        return self.t[k]


class HalfBuf(Buf):
    __slots__ = ('off',)

    def __init__(self, name, t, off):
        Buf.__init__(self, name, t)
        self.off = off

    def __getitem__(self, k):
        if isinstance(k, tuple):
            p, c = k[0], k[1]
            start = (c.start or 0) + self.off
            stop = (c.stop if c.stop is not None else 512) + self.off
            return self.t[p, start:stop]
        return self.t[:, self.off:self.off + 512]


def run_pipeline(gens, depth):
    active = []
    it = iter(gens)
    done = False
    while True:
        if not done and len(active) < depth:
            try:
                active.append(next(it))
            except StopIteration:
                done = True
        if not active:
            if done:
                break
            continue
        for g in list(active):
            try:
                next(g)
            except StopIteration:
                active.remove(g)


class Ctx:
    NS = 8

    def __init__(self, nc):
        self.nc = nc
        self.eng = {'pe': nc.tensor, 'act': nc.scalar, 'dve': nc.vector, 'sp': nc.sync, 'pool': nc.gpsimd}
        self.inorder = ('pe', 'act', 'dve')
        self.inorder_skip = ('pe',)
        self.dmaq = ('sp', 'pool')
        self.sem = {e: nc.alloc_semaphore('s_' + e) for e in self.inorder}
        self.cnt = {e: 0 for e in self.inorder}
        self.dsem = {q: [nc.alloc_semaphore('d_%s%d' % (q, i)) for i in range(self.NS)] for q in self.dmaq}
        self.dcnt = {q: 0 for q in self.dmaq}
        self.waited = {}
        self.nins = 0

    def _wait(self, e, tok):
        if tok is None:
            return
        sem, val, owner = tok
        if owner == e and e in self.inorder_skip:
            return
        key = (e, id(sem))
        if self.waited.get(key, 0) >= val:
            return
        self.eng[e].wait_ge(sem, val)
        self.waited[key] = val

    def op(self, e, fns, reads=(), writes=()):
        if not isinstance(fns, (list, tuple)):
            fns = [fns]
        for b in reads:
            self._wait(e, b.w)
        for b in writes:
            self._wait(e, b.w)
            for t in b.r.values():
                self._wait(e, t)
        self.nins += len(fns)
        if e in self.dmaq:
            n = self.dcnt[e]
            slot = n % self.NS
            rnd = n // self.NS
            sem = self.dsem[e][slot]
            if rnd > 0:
                self._wait(e, (sem, 16 * rnd, None))
            for f in fns[:-1]:
                f()
            fns[-1]().then_inc(sem, 16)
            tok = (sem, 16 * (rnd + 1), e)
            self.dcnt[e] = n + 1
        else:
            for f in fns[:-1]:
                f()
            fns[-1]().then_inc(self.sem[e], 1)
            self.cnt[e] += 1
            tok = (self.sem[e], self.cnt[e], e)
        for b in reads:
            old = b.r.get(id(tok[0]))
            if old is None or old[1] < tok[1]:
                b.r[id(tok[0])] = tok
        for b in writes:
            b.w = tok
            b.r = {}
        return tok

    def all_tokens(self):
        toks = [(self.sem[e], self.cnt[e], e) for e in self.inorder if self.cnt[e] > 0]
        for q in self.dmaq:
            n = self.dcnt[q]
            for slot in range(self.NS):
                uses = (n - slot + self.NS - 1) // self.NS if n > slot else 0
                if uses > 0:
                    toks.append((self.dsem[q][slot], 16 * uses, q))
        return toks

    def barrier(self):
        toks = self.all_tokens()
        for e in self.eng:
            for t in toks:
                if t[2] == e and e in self.inorder:
                    continue
                self._wait(e, t)


def dram_bcast(ap, nparts):
    n = 1
    for s in ap.shape:
        n *= s
    return bass.AP(ap.tensor, ap.offset, [[0, nparts], [1, n]])


def tok_blocks():
    blks = [(0, 256, 256)]
    for i in range(8):
        blks.append((256 + 512 * i, 512, 64))
    return blks


class Builder:
    def __init__(self, n_layers=DEPTH, debug=False, stop_after=None, force_last=False):
        self.force_last = force_last
        self.n_layers = n_layers
        self.debug = debug
        self.stop_after = stop_after
        nc = self.nc = bass.Bass("TRN2", target_bir_lowering=False)
        self.K = Ctx(nc)
        di = lambda name, shape: nc.dram_tensor(name, shape, F32, kind="ExternalInput").ap()
        self.h0 = di("h0", [T, D])
        self.c2t = di("c2t", [128, 32])
        self.pv = di("pv", [DEPTH, 128, NPV])
        self.consts = di("consts", [128, NCONST])
        self.w_ada = di("w_ada", [DEPTH, D, 3 * D])
        self.b_ada = di("b_ada", [DEPTH, 3 * D])
        self.post_w = di("post_norm_w", [DEPTH, D])
        self.w_in = di("w_in", [DEPTH, D, IN_DIM])
        self.ssd_a_log = di("ssd_a_log", [DEPTH, 32])
        self.ssd_dt_bias = di("ssd_dt_bias", [DEPTH, 32])
        self.ssd_d = di("ssd_d", [DEPTH, 16])
        self.ssd_norm_w = di("ssd_norm_w", [DEPTH, 1024])
        self.gdn_a_log = di("gdn_a_log", [DEPTH, 16])
        self.gdn_dt_bias = di("gdn_dt_bias", [DEPTH, 16])
        self.gdn_norm_w = di("gdn_norm_w", [DEPTH, 128])
        self.w_out = di("w_out", [DEPTH, D, D])
        self.out = nc.dram_tensor("out", [4096, D], F32, kind="ExternalOutput").ap()
        self.ext_set = getattr(Builder, 'EXT_SET', ())
        ds = lambda name, shape, dt: nc.dram_tensor(name, shape, dt, kind=("ExternalOutput" if (debug or name in self.ext_set) else "Internal")).ap()
        self.modv = ds("modv", [2, 3 * D], F32)
        self.FM = ds("FM", [FM_ROWS, T], BF16)
        self.TM = ds("TM", [T, TM_COLS], BF16)
        self.ZS = ds("ZS", [T, D], BF16)
        self.SMALL = ds("SMALL", [T, 64], F32)
        self.YO = ds("YO", [T, D], F32)
        self.YM = ds("YM", [T, D], BF16)
        self.H = ds("H", [T, D], F32)
        if debug:
            self.UT = ds("UT", [D, T], BF16)

    def sb(self, es, name, shape, dt=F32):
        self.uid = getattr(self, 'uid', 0) + 1
        name = "%s_%d" % (name, self.uid)
        return Buf(name, es.enter_context(self.nc.sbuf_tensor(name, shape, dt)))

    def load(self, q, dst, dst_ap, src_ap, **kw):
        eng = self.K.eng[q]
        return self.K.op(q, lambda: eng.dma_start(out=dst_ap, in_=src_ap, **kw), writes=[dst])

    def store(self, q, src, dst_ap, src_ap, **kw):
        eng = self.K.eng[q]
        return self.K.op(q, lambda: eng.dma_start(out=dst_ap, in_=src_ap, **kw), reads=[src])

    def build(self):
        nc, K = self.nc, self.K
        with ExitStack() as es:
            self.ps = [Buf("ps%d" % i, es.enter_context(nc.psum_tensor("ps%d" % i, [128, 1024], F32))) for i in range(4)]
            self.psi = 0
            self.psh = [HalfBuf("psh%d" % i, self.ps[i // 2].t, (i % 2) * 512) for i in range(8)]
            self.psfree = list(self.psh)
            self.cst = self.sb(es, "cst", [128, NCONST])
            self.load('sp', self.cst, self.cst[:], self.consts)
            self.idb = self.sb(es, "idb", [128, 128], BF16)
            K.op('dve', lambda: nc.vector.tensor_copy(out=self.idb[:], in_=self.cst[:, C_ID:C_ID + 128]), reads=[self.cst], writes=[self.idb])
            self.onesb = self.sb(es, "onesb", [128, 128], BF16)
            K.op('dve', lambda: nc.vector.tensor_copy(out=self.onesb[:], in_=self.cst[:, C_ONES:C_ONES + 128]), reads=[self.cst], writes=[self.onesb])
            self.epsb = self.sb(es, "epsb", [128, 1])
            K.op('dve', lambda: nc.vector.memset(self.epsb[:], EPS), writes=[self.epsb])
            self.lnqs = self.sb(es, "lnqs", [128, 1])
            K.op('dve', lambda: nc.vector.memset(self.lnqs[:], float(np.log(128.0 ** -0.5))), writes=[self.lnqs])
            for l in range(self.n_layers):
                self.layer(l)
            K.barrier()
        return nc

    def next_ps(self):
        p = self.ps[self.psi % 4]
        self.psi += 1
        return p

    def next_half(self):
        assert self.psfree, "PSUM half-slots exhausted"
        return self.psfree.pop(0)

    def free_half(self, p):
        self.psfree.append(p)

    def layer(self, l):
        K = self.K
        last = (l == DEPTH - 1) or (self.force_last and l == self.n_layers - 1)
        hsrc = self.h0 if l == 0 else self.H
        with ExitStack() as es:
            self.pvt = self.sb(es, "pvt", [128, NPV])
            self.load('sp', self.pvt, self.pvt[:], self.pv[l])
            self.phase0(l)
            K.barrier()
            if self.stop_after == 'p0':
                return
            self.phaseA(l, hsrc)
            K.barrier()
            if self.stop_after == 'A':
                return
            self.phaseB(l, last)
            K.barrier()
            if self.stop_after == 'B':
                return
            self.phaseC(l, hsrc, last)
            K.barrier()

    def phase0(self, l):
        nc, K = self.nc, self.K
        with ExitStack() as es:
            sct = self.sb(es, "sct", [128, 32])
            self.load('sp', sct, sct[:], self.c2t)
            K.op('act', lambda: nc.scalar.activation(out=sct[:], in_=sct[:], func=AF.Silu), reads=[sct], writes=[sct])
            bia = self.sb(es, "bia", [2, 3 * D])
            self.load('sp', bia, bia[:], dram_bcast(self.b_ada[l], 2))
            modsb = self.sb(es, "modsb", [2, 3 * D])
            wts = [self.sb(es, "wada%d" % i, [128, 16, 512]) for i in range(2)]
            sct3 = sct[:].rearrange("p (k r) -> p k r", r=2)
            for cb in range(12):
                wt = wts[cb % 2]
                src = self.w_ada[l][:, cb * 512:(cb + 1) * 512].rearrange("(k p) f -> p k f", p=128)
                self.load('sp', wt, wt[:], src)
                ps = self.next_ps()
                fns = []
                for kc in range(16):
                    fns.append(lambda kc=kc, ps=ps, wt=wt: nc.tensor.matmul(ps[0:2, 0:512], lhsT=sct3[:, kc, :], rhs=wt[:, kc, :], start=(kc == 0), stop=(kc == 15)))
                K.op('pe', fns, reads=[sct, wt], writes=[ps])
                K.op('dve', lambda cb=cb, ps=ps: nc.vector.tensor_tensor(out=modsb[:, cb * 512:(cb + 1) * 512], in0=ps[0:2, 0:512], in1=bia[:, cb * 512:(cb + 1) * 512], op=ALU.add),
                     reads=[ps, bia], writes=[modsb])
            self.store('sp', modsb, self.modv, modsb[:])

    def phaseA(self, l, hsrc):
        nc, K = self.nc, self.K
        with ExitStack() as es:
            uT = self.sb(es, "uT", [128, 16, T], BF16)
            mraw = self.sb(es, "mraw", [128, 2, 2, 16])
            for r in range(2):
                for w in range(2):
                    src = bass.AP(self.modv.tensor, self.modv.offset + r * 3 * D + w * D, [[1, 128], [128, 16]])
                    self.load('sp', mraw, mraw[:, r, w, :], src, allow_slow_non_contiguous=True)
            mA = self.sb(es, "mA", [128, 2, 16])
            for r in range(2):
                K.op('dve', lambda r=r: nc.vector.scalar_tensor_tensor(out=mA[:, r, :], in0=mraw[:, r, 1, :], scalar=1.0, in1=self.pvt[:, 0:16], op0=ALU.add, op1=ALU.mult),
                     reads=[mraw, self.pvt], writes=[mA])
            with ExitStack() as es0:
                NH = 3
                hx = [self.sb(es0, "hx%d" % i, [128, D]) for i in range(NH)]
                junk = self.sb(es0, "junk", [128, D], BF16)
                xn = [self.sb(es0, "xn%d" % i, [128, D], BF16) for i in range(NH)]
                st = [self.sb(es0, "st%d" % i, [128, 4]) for i in range(NH)]

                def a0_load(c):
                    h_ = hx[c % NH]
                    self.load('sp', h_, h_[:], hsrc[c * 128:(c + 1) * 128, :])

                def a0_iter(c):
                    r = 1 if c < 2 else 0
                    h_, x_, s_ = hx[c % NH], xn[c % NH], st[c % NH]
                    K.op('act', lambda: nc.scalar.activation(out=junk[:], in_=h_[:], func=AF.Square, accum_out=s_[:, 0:1]), reads=[h_], writes=[junk, s_])
                    yield
                    K.op('act', lambda: nc.scalar.activation(out=s_[:, 1:2], in_=s_[:, 0:1], func=AF.Ln, bias=self.epsb[:], scale=1.0 / D), reads=[s_, self.epsb], writes=[s_])
                    K.op('act', lambda: nc.scalar.activation(out=s_[:, 2:3], in_=s_[:, 1:2], func=AF.Exp, scale=-0.5), reads=[s_], writes=[s_])
                    yield
                    K.op('dve', lambda: nc.vector.tensor_scalar(out=x_[:], in0=h_[:], scalar1=s_[:, 2:3], scalar2=None, op0=ALU.mult), reads=[h_, s_], writes=[x_])
                    if c + NH < NCH:
                        a0_load(c + NH)
                    yield
                    for half in range(2):
                        ps = self.next_half()
                        pb = ps[:].bitcast(BF16)
                        fns = [lambda j=j, pb=pb, half=half: nc.tensor.transpose(out=pb[:, j * 128:(j + 1) * 128], in_=x_[:, (half * 8 + j) * 128:(half * 8 + j + 1) * 128], identity=self.idb[:]) for j in range(8)]
                        K.op('pe', fns, reads=[x_, self.idb], writes=[ps])
                        for j in range(8):
                            kc = half * 8 + j
                            K.op('act', lambda j=j, kc=kc, pb=pb: nc.scalar.activation(out=uT[:, kc, c * 128:(c + 1) * 128], in_=pb[:, j * 128:(j + 1) * 128], func=AF.Identity,
                                                                                     bias=mraw[:, r, 0, kc:kc + 1], scale=mA[:, r, kc:kc + 1]),
                                 reads=[ps, mraw, mA], writes=[uT])
                        self.free_half(ps)
                        yield

                for c in range(NH):
                    a0_load(c)
                run_pipeline((a0_iter(c) for c in range(NCH)), 2)
                K.barrier()
            if self.debug:
                for kc in range(16):
                    self.store('pool', uT, self.UT[kc * 128:(kc + 1) * 128, :], uT[:, kc, :])
            with ExitStack() as es1:
                NB = 6
                wf = [self.sb(es1, "wf%d" % i, [128, 16, 128], BF16) for i in range(2)]
                acc = [self.sb(es1, "acc%d" % i, [128, 512]) for i in range(NB)]
                cv = [self.sb(es1, "cv%d" % i, [128, 512]) for i in range(NB)]
                ob = [self.sb(es1, "ob%d" % i, [128, 512], BF16) for i in range(NB)]
                sq = [self.sb(es1, "sq%d" % i, [128, 512], BF16) for i in range(NB)]
                lnv = [self.sb(es1, "lnv%d" % i, [128, 512]) for i in range(NB)]
                tmo = [self.sb(es1, "tmo%d" % i, [128, 4, 128], BF16) for i in range(NB)]
                blks = tok_blocks()

                def fm_iter(it, fc, w_, kind, fm_row, tm_col, t0, nt, rl):
                    a_, c_, o_, s_, l_, m_ = acc[it % NB], cv[it % NB], ob[it % NB], sq[it % NB], lnv[it % NB], tmo[it % NB]
                    ps = self.next_half()
                    fns = [lambda kc=kc: nc.tensor.matmul(ps[:, 0:nt], lhsT=w_[:, kc, :], rhs=uT[:, kc, t0:t0 + nt], start=(kc == 0), stop=(kc == 15)) for kc in range(16)]
                    K.op('pe', fns, reads=[w_, uT], writes=[ps])
                    yield
                    cw = lambda k: self.pvt[:, 16 + fc * 5 + k:16 + fc * 5 + k + 1]
                    K.op('act', lambda: nc.scalar.activation(out=a_[:, 0:nt], in_=ps[:, 0:nt], func=AF.Copy, scale=cw(2)), reads=[ps, self.pvt], writes=[a_])
                    pv3 = ps[:, 0:nt].rearrange("p (r j) -> p r j", j=rl)
                    av3 = a_[:, 0:nt].rearrange("p (r j) -> p r j", j=rl)
                    for k in (0, 1, 3, 4):
                        sft = k - 2
                        j0, j1 = max(0, -sft), rl - max(0, sft)
                        K.op('dve', lambda k=k, sft=sft, j0=j0, j1=j1: nc.vector.scalar_tensor_tensor(out=av3[:, :, j0:j1], in0=pv3[:, :, j0 + sft:j1 + sft], scalar=cw(k), in1=av3[:, :, j0:j1], op0=ALU.mult, op1=ALU.add),
                             reads=[ps, a_, self.pvt], writes=[a_])
                        if k == 1:
                            yield
                    self.free_half(ps)
                    yield
                    if kind in ('q', 'k'):
                        K.op('act', lambda: nc.scalar.activation(out=c_[:, 0:nt], in_=a_[:, 0:nt], func=AF.Silu), reads=[a_], writes=[c_])
                        K.op('dve', lambda: nc.vector.tensor_tensor(out=s_[:, 0:nt], in0=c_[:, 0:nt], in1=c_[:, 0:nt], op=ALU.mult), reads=[c_], writes=[s_])
                        yield
                        ps2 = self.next_half()
                        K.op('pe', lambda: nc.tensor.matmul(ps2[:, 0:nt], lhsT=self.onesb[:], rhs=s_[:, 0:nt], start=True, stop=True), reads=[s_, self.onesb], writes=[ps2])
                        K.op('act', lambda: nc.scalar.activation(out=l_[:, 0:nt], in_=ps2[:, 0:nt], func=AF.Ln, bias=self.epsb[:], scale=1.0), reads=[ps2, self.epsb], writes=[l_])
                        self.free_half(ps2)
                        yield
                        if kind == 'q':
                            K.op('act', lambda: nc.scalar.activation(out=l_[:, 0:nt], in_=l_[:, 0:nt], func=AF.Exp, bias=self.lnqs[:], scale=-0.5), reads=[l_, self.lnqs], writes=[l_])
                        else:
                            K.op('act', lambda: nc.scalar.activation(out=l_[:, 0:nt], in_=l_[:, 0:nt], func=AF.Exp, scale=-0.5), reads=[l_], writes=[l_])
                        yield
                        K.op('dve', lambda: nc.vector.tensor_tensor(out=o_[:, 0:nt], in0=c_[:, 0:nt], in1=l_[:, 0:nt], op=ALU.mult), reads=[c_, l_], writes=[o_])
                    elif fc < 12:
                        K.op('act', lambda: nc.scalar.activation(out=o_[:, 0:nt], in_=a_[:, 0:nt], func=AF.Silu, bias=self.pvt[:, 196 + fc:197 + fc], scale=1.0), reads=[a_, self.pvt], writes=[o_])
                    else:
                        K.op('act', lambda: nc.scalar.activation(out=o_[:, 0:nt], in_=a_[:, 0:nt], func=AF.Silu), reads=[a_], writes=[o_])
                    yield
                    if fm_row is not None:
                        self.store('pool', o_, self.FM[fm_row:fm_row + 128, t0:t0 + nt], o_[:, 0:nt])
                    if tm_col is not None:
                        nj = nt // 128
                        ps3 = self.next_half()
                        pb = ps3[:].bitcast(BF16)
                        fns = [lambda j=j: nc.tensor.transpose(out=pb[:, j * 128:(j + 1) * 128], in_=o_[:, j * 128:(j + 1) * 128], identity=self.idb[:]) for j in range(nj)]
                        K.op('pe', fns, reads=[o_, self.idb], writes=[ps3])
                        K.op('act', lambda: nc.scalar.copy(out=m_[:, 0:nj, :], in_=pb[:, 0:nj * 128].rearrange("p (j f) -> p j f", f=128)), reads=[ps3], writes=[m_])
                        self.free_half(ps3)
                        dst = self.TM[t0:t0 + nt, tm_col:tm_col + 128].rearrange("(j p) f -> p j f", p=128)
                        self.store('pool', m_, dst, m_[:, 0:nj, :])

                def fm_gens():
                    it = 0
                    for fc in range(36):
                        col0 = (2048 + fc * 128) if fc < 12 else (3616 + (fc - 12) * 128)
                        w_ = wf[fc % 2]
                        src = self.w_in[l][:, col0:col0 + 128].rearrange("(k p) f -> p k f", p=128)
                        self.load('pool', w_, w_[:], src)
                        if fc < 8:
                            kind, fm_row, tm_col = 'x', None, TM_X + fc * 128
                        elif fc < 10:
                            kind, fm_row, tm_col = 'B', FM_B + (fc - 8) * 128, TM_B + (fc - 8) * 128
                        elif fc < 12:
                            kind, fm_row, tm_col = 'C', FM_C + (fc - 10) * 128, None
                        elif fc < 20:
                            kind, fm_row, tm_col = 'q', FM_Q + (fc - 12) * 128, None
                        elif fc < 28:
                            kind, fm_row, tm_col = 'k', FM_K + (fc - 20) * 128, TM_K + (fc - 20) * 128
                        else:
                            kind, fm_row, tm_col = 'v', None, TM_V + (fc - 28) * 128
                        for (t0, nt, rl) in blks:
                            yield fm_iter(it, fc, w_, kind, fm_row, tm_col, t0, nt, rl)
                            it += 1

                run_pipeline(fm_gens(), 6)
                K.barrier()
            with ExitStack() as es2:
                NZ = 6
                wz = [self.sb(es2, "wz%d" % i, [128, 16, 256], BF16) for i in range(2)]
                zo = [self.sb(es2, "zo%d" % i, [128, 256], BF16) for i in range(NZ)]
                so = [self.sb(es2, "so%d" % i, [128, 64]) for i in range(NZ)]

                def z_iter(it, cb, c, w_, ncol):
                    ps = self.next_half()
                    fns = [lambda kc=kc: nc.tensor.matmul(ps[:, 0:ncol], lhsT=uT[:, kc, c * 128:(c + 1) * 128], rhs=w_[:, kc, 0:ncol], start=(kc == 0), stop=(kc == 15)) for kc in range(16)]
                    K.op('pe', fns, reads=[w_, uT], writes=[ps])
                    yield
                    if cb < 8:
                        z_ = zo[it % NZ]
                        K.op('act', lambda: nc.scalar.activation(out=z_[:], in_=ps[:, 0:256], func=AF.Silu), reads=[ps], writes=[z_])
                        self.free_half(ps)
                        self.store('sp', z_, self.ZS[c * 128:(c + 1) * 128, cb * 256:(cb + 1) * 256], z_[:])
                    else:
                        s_ = so[it % NZ]
                        K.op('act', lambda: nc.scalar.copy(out=s_[:], in_=ps[:, 0:64]), reads=[ps], writes=[s_])
                        self.free_half(ps)
                        self.store('sp', s_, self.SMALL[c * 128:(c + 1) * 128, :], s_[:])

                def z_gens():
                    it = 0
                    for cb in range(9):
                        w_ = wz[cb % 2]
                        if cb < 8:
                            src = self.w_in[l][:, cb * 256:(cb + 1) * 256].rearrange("(k p) f -> p k f", p=128)
                            self.load('pool', w_, w_[:], src)
                            ncol = 256
                        else:
                            self.load('pool', w_, w_[:, :, 0:32], self.w_in[l][:, 3584:3616].rearrange("(k p) f -> p k f", p=128))
                            self.load('pool', w_, w_[:, :, 32:64], self.w_in[l][:, 6688:6720].rearrange("(k p) f -> p k f", p=128))
                            ncol = 64
                        for c in range(NCH):
                            yield z_iter(it, cb, c, w_, ncol)
                            it += 1

                run_pipeline(z_gens(), 5)

    def phaseS(self, l, es):
        nc, K = self.nc, self.K
        sb = lambda name, shape, dt=F32: self.sb(es, name, shape, dt)
        self.dt_all = sb("dt_all", [128, NCH, 32])
        self.loga_all = sb("loga_all", [128, NCH, 32])
        self.beta_all = sb("beta_all", [128, NCH, 16])
        self.g_all = sb("g_all", [128, NCH, 16])
        self.dsk = sb("dsk", [128, 16])
        self.onecol = sb("onecol", [128, 1])
        K.op('dve', lambda: nc.vector.memset(self.onecol[:], 1.0), writes=[self.onecol])
        self.load('sp', self.dsk, self.dsk[:], dram_bcast(self.ssd_d[l], 128))
        with ExitStack() as e2:
            sm = self.sb(e2, "sm", [128, NCH, 64])
            self.load('sp', sm, sm[:], self.SMALL.rearrange("(c p) f -> p c f", p=128))
            bias = self.sb(e2, "sbias", [128, 48])
            alog = self.sb(e2, "salog", [128, 48])
            self.load('sp', bias, bias[:, 0:32], dram_bcast(self.ssd_dt_bias[l], 128))
            self.load('sp', bias, bias[:, 32:48], dram_bcast(self.gdn_dt_bias[l], 128))
            self.load('sp', alog, alog[:, 0:32], dram_bcast(self.ssd_a_log[l], 128))
            self.load('sp', alog, alog[:, 32:48], dram_bcast(self.gdn_a_log[l], 128))
            K.op('act', lambda: nc.scalar.activation(out=alog[:], in_=alog[:], func=AF.Exp), reads=[alog], writes=[alog])
            K.op('dve', lambda: nc.vector.tensor_scalar(out=alog[:], in0=alog[:], scalar1=-1.0, scalar2=None, op0=ALU.mult), reads=[alog], writes=[alog])
            tmp = self.sb(e2, "stmp", [128, NCH, 48])
            K.op('dve', lambda: nc.vector.tensor_tensor(out=tmp[:, :, 0:32], in0=sm[:, :, 0:32], in1=bias[:, 0:32].unsqueeze(1).broadcast_to([128, NCH, 32]), op=ALU.add), reads=[sm, bias], writes=[tmp])
            K.op('dve', lambda: nc.vector.tensor_tensor(out=tmp[:, :, 32:48], in0=sm[:, :, 48:64], in1=bias[:, 32:48].unsqueeze(1).broadcast_to([128, NCH, 16]), op=ALU.add), reads=[sm, bias], writes=[tmp])
            K.op('act', lambda: nc.scalar.activation(out=tmp[:], in_=tmp[:], func=AF.Exp), reads=[tmp], writes=[tmp])
            K.op('act', lambda: nc.scalar.activation(out=tmp[:], in_=tmp[:], func=AF.Ln, bias=self.onecol[:], scale=1.0), reads=[tmp, self.onecol], writes=[tmp])
            K.op('dve', lambda: nc.vector.tensor_copy(out=self.dt_all[:], in_=tmp[:, :, 0:32]), reads=[tmp], writes=[self.dt_all])
            K.op('dve', lambda: nc.vector.tensor_tensor(out=self.loga_all[:], in0=tmp[:, :, 0:32], in1=alog[:, 0:32].unsqueeze(1).broadcast_to([128, NCH, 32]), op=ALU.mult), reads=[tmp, alog], writes=[self.loga_all])
            K.op('dve', lambda: nc.vector.tensor_tensor(out=self.g_all[:], in0=tmp[:, :, 32:48], in1=alog[:, 32:48].unsqueeze(1).broadcast_to([128, NCH, 16]), op=ALU.mult), reads=[tmp, alog], writes=[self.g_all])
            K.op('act', lambda: nc.scalar.activation(out=self.beta_all[:], in_=sm[:, :, 32:48], func=AF.Sigmoid), reads=[sm], writes=[self.beta_all])
            K.barrier()

    def phaseB(self, l, last):
        nc, K = self.nc, self.K
        with ExitStack() as es:
            self.phaseS(l, es)
            sb = lambda name, shape, dt=F32: self.sb(es, name, shape, dt)
            B = self.Bt = type('T', (), {})()
            B.tm = [sb("tm%d" % i, [128, TM_COLS], BF16) for i in range(3)]
            B.fm = [sb("fm%d" % i, [128, 20, 128], BF16) for i in range(3)]
            B.yo1 = sb("yo1", [128, D])
            B.zs = sb("zs", [128, D], BF16)
            B.ex = sb("ex", [128, 48]); B.rhsL = sb("rhsL", [128, 8, 128]); B.E = sb("E", [128, 16, 128], BF16)
            B.cbm = sb("cbm", [128, 2, 128], BF16); B.MT = sb("MT", [128, 16, 128], BF16)
            B.xdt = sb("xdt", [128, 16, 64], BF16); B.xdd = sb("xdd", [128, 16, 64], BF16)
            B.ytmp = sb("ytmp", [128, 1024]); B.ysum = sb("ysum", [128, 1024]); B.yfin = sb("yfin", [128, 1024])
            B.hT = sb("hT", [128, 16, 64]); B.hTb = sb("hTb", [128, 16, 64], BF16)
            B.gex = [sb("gex%d" % i, [128, 3, 8]) for i in range(2)]; B.rhsG = sb("rhsG", [128, 8, 128]); B.DT = sb("DT", [128, 8, 128])
            B.tA = sb("tA", [128, 8, 128]); B.tB = sb("tB", [128, 8, 128])
            B.nm = sb("nm", [128, 128], BF16); B.bdS = sb("bdS", [128, 128])
            g16 = lambda name: sb(name, [128, 8, 128], BF16)
            B.UD = g16("UD"); B.UO = g16("UO"); B.attnT = [g16("attnT0"), g16("attnT1")]; B.AD = g16("AD")
            B.P = [g16("P0"), g16("P1")]; B.X = [g16("X0"), g16("X1")]; B.XT = [g16("XT0"), g16("XT1")]
            B.TD = g16("TD"); B.VT = g16("VT"); B.V = g16("V"); B.V2T = g16("V2T"); B.Z = g16("Z"); B.Sf = g16("Sf")
            B.kg = g16("kg"); B.kendk = [g16("kendk0"), g16("kendk1")]; B.wT = [g16("wT0"), g16("wT1")]; B.vnew = g16("vnew")
            B.u_sb = [sb("u_sb%d" % i, [128, 8, 128]) for i in range(2)]; B.o = sb("o", [128, 8, 128]); B.ofin = sb("ofin", [128, 8, 128])
            B.Sg = sb("Sg", [128, 8, 128]); B.Sgb = g16("Sgb")
            NG = self.NG = 1
            self.PREW = getattr(Builder, "PREW", 20)
            for t_ in [B.rhsG, B.DT, B.tA, B.UD, B.UO, B.AD, B.TD, B.VT, B.V, B.V2T, B.Z, B.Sf, B.kg] + B.P + B.X + B.XT + B.gex + B.attnT + B.u_sb + B.wT + B.kendk:
                t_.g = [Buf(t_.name + "_g%d" % j, t_.t) for j in range(NG)]
            B.mT = sb("mT", [128, 128]); B.nbd = sb("nbd", [128, 128])
            K.op('dve', lambda: nc.vector.tensor_scalar(out=B.nbd[:], in0=self.cst[:, C_BD:C_BD + 128], scalar1=-1.0, scalar2=1.0, op0=ALU.mult, op1=ALU.add), reads=[self.cst], writes=[B.nbd])
            B.snw = sb("snw", [128, 1024]); B.gnw = sb("gnw", [128, 128])
            self.load('sp', B.snw, B.snw[:], dram_bcast(self.ssd_norm_w[l], 128))
            self.load('sp', B.gnw, B.gnw[:], dram_bcast(self.gdn_norm_w[l], 128))
            B.yg = sb("yg", [128, 1024]); B.ejunk = sb("ejunk", [128, 1024], BF16); B.est = sb("est", [128, 16]); B.est2 = sb("est2", [128, 16])
            B.ym = sb("ym", [128, D], BF16); B.wz = sb("wzg", [128, 8, 128])
            for d in range(2):
                self.scan_pass(l, d, last)
                K.barrier()

    def scan_pass(self, l, d, last):
        nc, K, B = self.nc, self.K, self.Bt
        order = list(range(NCH)) if d == 0 else [1, 0] + list(range(NCH - 1, 1, -1))
        B.inc = self.cst[:, (C_INC0 if d == 0 else C_INC1):(C_INC0 if d == 0 else C_INC1) + 128]
        B.exc = self.cst[:, (C_EXC0 if d == 0 else C_EXC1):(C_EXC0 if d == 0 else C_EXC1) + 128]
        B.strictT = self.cst[:, (C_EXC1 if d == 0 else C_EXC0):(C_EXC1 if d == 0 else C_EXC0) + 128]
        B.ones = self.cst[:, C_ONES:C_ONES + 128]
        B.bd = self.cst[:, C_BD:C_BD + 128]
        K.op('dve', lambda: nc.vector.memset(B.hT[:], 0.0), writes=[B.hT])
        K.op('dve', lambda: nc.vector.memset(B.hTb[:], 0.0), writes=[B.hTb])
        K.op('dve', lambda: nc.vector.memset(B.Sg[:], 0.0), writes=[B.Sg])
        K.op('dve', lambda: nc.vector.memset(B.Sgb[:], 0.0), writes=[B.Sgb])

        K.op('dve', lambda: nc.vector.tensor_scalar(out=B.nm[:], in0=B.inc, scalar1=30000.0, scalar2=-30000.0, op0=ALU.mult, op1=ALU.add), reads=[self.cst], writes=[B.nm])
        K.op('dve', lambda: nc.vector.tensor_tensor(out=B.bdS[:], in0=B.strictT, in1=B.bd, op=ALU.mult), reads=[self.cst], writes=[B.bdS])

        def issue_loads(i):
            c = order[i]
            tm_, fm_ = B.tm[i % 3], B.fm[i % 3]
            self.load('sp', tm_, tm_[:], self.TM[c * 128:(c + 1) * 128, :])
            self.load('sp', fm_, fm_[:], self.FM[:, c * 128:(c + 1) * 128].rearrange("(f p) t -> p f t", p=128))

        def run_weighted(gens):
            st = [[g, n, 0] for g, n in gens]
            while st:
                st.sort(key=lambda x: x[2] / float(x[1]))
                g = st[0]
                try:
                    next(g[0])
                    g[2] += 1
                except StopIteration:
                    st.remove(g)

        n = len(order)
        issue_loads(0)
        if n > 1:
            issue_loads(1)
        run_weighted([(self.gdn_pre(order[0], d, B.tm[0], B.fm[0], 0, hg, self.NG), 20) for hg in range(self.NG)])
        for i, c in enumerate(order):
            if i + 2 < n:
                issue_loads(i + 2)
            tm_, fm_ = B.tm[i % 3], B.fm[i % 3]
            need_out = not (last and c < 2)
            if d == 1 and need_out:
                self.load('sp', B.yo1, B.yo1[:], self.YO[c * 128:(c + 1) * 128, :])
                self.load('sp', B.zs, B.zs[:], self.ZS[c * 128:(c + 1) * 128, :])
            gens = [(self.ssd_gen(c, d, tm_, fm_, need_out), 7), (self.gdn_rec(c, d, fm_, need_out, i % 2), 3)]
            if i + 1 < n:
                for hg in range(self.NG):
                    gens.append((self.gdn_pre(order[i + 1], d, B.tm[(i + 1) % 3], B.fm[(i + 1) % 3], (i + 1) % 2, hg, self.NG), self.PREW))
            run_weighted(gens)
            if d == 1 and need_out:
                self.epilogue(c)

    def ssd_gen(self, c, d, tm_, fm_, need_out):
        nc, K, B = self.nc, self.K, self.Bt
        cst = self.cst
        loga = self.loga_all[:, c, d * 16:(d + 1) * 16]
        dtc = self.dt_all[:, c, d * 16:(d + 1) * 16]
        xs3 = tm_[:, TM_X:TM_X + 1024].rearrange("p (h q) -> p h q", q=64)
        ps = self.next_ps()
        K.op('pe', [lambda: nc.tensor.matmul(ps[:, 0:16], lhsT=B.inc, rhs=loga, start=True, stop=True),
                    lambda: nc.tensor.matmul(ps[:, 16:32], lhsT=B.exc, rhs=loga, start=True, stop=True),
                    lambda: nc.tensor.matmul(ps[:, 32:48], lhsT=B.ones, rhs=loga, start=True, stop=True)], reads=[cst, self.loga_all], writes=[ps])
        K.op('act', lambda: nc.scalar.activation(out=B.ex[:], in_=ps[:, 0:48], func=AF.Exp), reads=[ps], writes=[B.ex])
        expA, dend, cd = B.ex[:, 0:16], B.ex[:, 16:32], B.ex[:, 32:48]
        yield
        for hh in range(2):
            K.op('dve', lambda hh=hh: nc.vector.tensor_tensor(out=B.rhsL[:], in0=B.inc.unsqueeze(1).broadcast_to([128, 8, 128]),
                                                             in1=loga[:, hh * 8:(hh + 1) * 8].unsqueeze(2).broadcast_to([128, 8, 128]), op=ALU.mult),
                 reads=[cst, self.loga_all], writes=[B.rhsL])
            ps = self.next_ps()
            rl2 = B.rhsL[:].rearrange("p h q -> p (h q)")
            nm4 = B.nm[:].unsqueeze(1).broadcast_to([128, 4, 128])
            fns = []
            for j in range(2):
                fns.append(lambda ps=ps, rl2=rl2, j=j: nc.tensor.matmul(ps[:, j * 512:(j + 1) * 512], lhsT=B.exc, rhs=rl2[:, j * 512:(j + 1) * 512], start=True, stop=False))
                fns.append(lambda ps=ps, j=j: nc.tensor.matmul(ps[:, j * 512:(j + 1) * 512].rearrange("p (h q) -> p h q", h=4), lhsT=self.idb[:], rhs=nm4, start=False, stop=True))
            K.op('pe', fns, reads=[cst, B.rhsL, B.nm, self.idb], writes=[ps])
            K.op('act', lambda ps=ps, hh=hh: nc.scalar.activation(out=B.E[:, hh * 8:(hh + 1) * 8, :].rearrange("p h q -> p (h q)"), in_=ps[:], func=AF.Exp), reads=[ps], writes=[B.E])
            yield
        ps = self.next_ps()
        K.op('pe', [lambda ps=ps, g=g: nc.tensor.matmul(ps[:, g * 128:(g + 1) * 128], lhsT=fm_[:, FM_B // 128 + g, :], rhs=fm_[:, FM_C // 128 + g, :], start=True, stop=True) for g in range(2)],
             reads=[fm_], writes=[ps])
        K.op('act', lambda ps=ps: nc.scalar.copy(out=B.cbm[:].rearrange("p g q -> p (g q)"), in_=ps[:, 0:256]), reads=[ps], writes=[B.cbm])
        yield
        K.op('dve', lambda: nc.vector.tensor_tensor(out=B.MT[:].rearrange("p (g r) q -> p g r q", g=2), in0=B.E[:].rearrange("p (g r) q -> p g r q", g=2),
                                                    in1=B.cbm[:].unsqueeze(2).broadcast_to([128, 2, 8, 128]), op=ALU.mult), reads=[B.E, B.cbm], writes=[B.MT])
        K.op('dve', lambda: nc.vector.tensor_tensor(out=B.xdt[:], in0=xs3, in1=dtc.unsqueeze(2).broadcast_to([128, 16, 64]), op=ALU.mult), reads=[tm_, self.dt_all], writes=[B.xdt])
        K.op('dve', lambda: nc.vector.tensor_tensor(out=B.xdd[:], in0=B.xdt[:], in1=dend.unsqueeze(2).broadcast_to([128, 16, 64]), op=ALU.mult), reads=[B.xdt, B.ex], writes=[B.xdd])
        yield
        psd = self.next_ps()
        K.op('pe', [lambda h=h: nc.tensor.matmul(psd[:, h * 64:(h + 1) * 64], lhsT=B.MT[:, h, :], rhs=B.xdt[:, h, :], start=True, stop=True) for h in range(16)], reads=[B.MT, B.xdt], writes=[psd])
        pso = self.next_ps()
        K.op('pe', [lambda g=g: nc.tensor.matmul(pso[:, g * 512:(g + 1) * 512], lhsT=fm_[:, FM_C // 128 + g, :], rhs=B.hTb[:, g * 8:(g + 1) * 8, :].rearrange("p h q -> p (h q)"), start=True, stop=True) for g in range(2)],
             reads=[fm_, B.hTb], writes=[pso])
        K.op('dve', lambda: nc.vector.tensor_tensor(out=B.ytmp[:].rearrange("p (h q) -> p h q", q=64), in0=pso[:].rearrange("p (h q) -> p h q", q=64), in1=expA.unsqueeze(2).broadcast_to([128, 16, 64]), op=ALU.mult),
             reads=[pso, B.ex], writes=[B.ytmp])
        K.op('dve', lambda: nc.vector.tensor_tensor(out=B.ysum[:], in0=B.ytmp[:], in1=psd[:], op=ALU.add), reads=[B.ytmp, psd], writes=[B.ysum])
        if need_out:
            if d == 0:
                K.op('dve', lambda: nc.vector.tensor_tensor(out=B.ytmp[:].rearrange("p (h q) -> p h q", q=64), in0=xs3, in1=self.dsk[:].unsqueeze(2).broadcast_to([128, 16, 64]), op=ALU.mult),
                     reads=[tm_, self.dsk], writes=[B.ytmp])
                K.op('dve', lambda: nc.vector.tensor_tensor(out=B.yfin[:], in0=B.ytmp[:], in1=B.ysum[:], op=ALU.add), reads=[B.ytmp, B.ysum], writes=[B.yfin])
                self.store('pool', B.yfin, self.YO[c * 128:(c + 1) * 128, 0:1024], B.yfin[:])
            else:
                K.op('dve', lambda: nc.vector.tensor_tensor(out=B.yfin[:], in0=B.yo1[:, 0:1024], in1=B.ysum[:], op=ALU.add), reads=[B.yo1, B.ysum], writes=[B.yfin])
        yield
        pss = self.next_ps()
        K.op('pe', [lambda g=g: nc.tensor.matmul(pss[:, g * 512:(g + 1) * 512], lhsT=tm_[:, TM_B + g * 128:TM_B + (g + 1) * 128], rhs=B.xdd[:, g * 8:(g + 1) * 8, :].rearrange("p h q -> p (h q)"), start=True, stop=True) for g in range(2)],
             reads=[tm_, B.xdd], writes=[pss])
        K.op('dve', lambda: nc.vector.tensor_tensor(out=B.hT[:], in0=B.hT[:], in1=cd.unsqueeze(2).broadcast_to([128, 16, 64]), op=ALU.mult), reads=[B.hT, B.ex], writes=[B.hT])
        K.op('dve', lambda: nc.vector.tensor_tensor(out=B.hT[:].rearrange("p h q -> p (h q)"), in0=B.hT[:].rearrange("p h q -> p (h q)"), in1=pss[:], op=ALU.add), reads=[B.hT, pss], writes=[B.hT])
        K.op('act', lambda: nc.scalar.copy(out=B.hTb[:], in_=B.hT[:]), reads=[B.hT], writes=[B.hTb])
        yield

    def mm8(self, ps, lhs, rhs, lhs_bufs, rhs_bufs, bf16_out=False, transpose=False, acc=None, hs=None):
        nc, K = self.nc, self.K
        if hs is None:
            hs = range(8)
        if bf16_out:
            pv = ps[:].bitcast(BF16)
        else:
            pv = ps[:]
        if transpose:
            fns = [lambda j=j, H=H: nc.tensor.transpose(out=pv[:, j * 128:(j + 1) * 128], in_=lhs(H), identity=self.idb[:]) for j, H in enumerate(hs)]
        elif acc is None:
            fns = [lambda j=j, H=H: nc.tensor.matmul(pv[:, j * 128:(j + 1) * 128], lhsT=lhs(H), rhs=rhs(H), start=True, stop=True) for j, H in enumerate(hs)]
        else:
            al, ar = acc
            fns = []
            for j, H in enumerate(hs):
                fns.append(lambda j=j, H=H: nc.tensor.matmul(pv[:, j * 128:(j + 1) * 128], lhsT=lhs(H), rhs=rhs(H), start=True, stop=False))
                fns.append(lambda j=j, H=H: nc.tensor.matmul(pv[:, j * 128:(j + 1) * 128], lhsT=al(H), rhs=ar(H), start=False, stop=True))
        K.op('pe', fns, reads=list(lhs_bufs) + list(rhs_bufs), writes=[ps])
        return pv

    def gdn_pre(self, c, d, tm_, fm_, slot, hg, ng):
        nc, K, B = self.nc, self.K, self.Bt
        cst = self.cst
        nh = 8 // ng
        H0 = hg * nh
        hs = range(H0, H0 + nh)
        W = nh * 128
        gc = self.g_all[:, c, d * 8 + H0:d * 8 + H0 + nh]
        bc = self.beta_all[:, c, d * 8 + H0:d * 8 + H0 + nh]
        k3 = tm_[:, TM_K + H0 * 128:TM_K + (H0 + nh) * 128].rearrange("p (h q) -> p h q", q=128)
        v3 = tm_[:, TM_V:TM_V + 1024].rearrange("p (h q) -> p h q", q=128)
        kT = lambda H: fm_[:, FM_K // 128 + H, :]
        qT = lambda H: fm_[:, FM_Q // 128 + H, :]
        G = lambda buf: buf.g[hg]
        hd = lambda buf: (lambda H: buf[:, H, :])
        idH = lambda H: self.idb[:]
        sl = lambda buf: buf[:, H0:H0 + nh, :]
        fl = lambda buf: buf[:, H0:H0 + nh, :].rearrange("p h q -> p (h q)")
        bcl = lambda ap2: ap2.unsqueeze(2).broadcast_to([128, nh, 128])
        bcm = lambda ap2: ap2.unsqueeze(1).broadcast_to([128, nh, 128])
        gex, attnT, u_sb, wT, kendk = B.gex[slot], B.attnT[slot], B.u_sb[slot], B.wT[slot], B.kendk[slot]
        ps = self.next_ps()
        K.op('pe', [lambda: nc.tensor.matmul(ps[:, 0:nh], lhsT=B.inc, rhs=gc, start=True, stop=True),
                    lambda: nc.tensor.matmul(ps[:, 8:8 + nh], lhsT=B.exc, rhs=gc, start=True, stop=True),
                    lambda: nc.tensor.matmul(ps[:, 16:16 + nh], lhsT=B.ones, rhs=gc, start=True, stop=True)], reads=[cst, self.g_all], writes=[ps])
        K.op('act', lambda: nc.scalar.activation(out=gex[:, :, H0:H0 + nh], in_=ps[:, 0:24].rearrange("p (a h) -> p a h", a=3)[:, :, 0:nh], func=AF.Exp), reads=[ps], writes=[G(gex)])
        expG, kend = gex[:, 0, H0:H0 + nh], gex[:, 1, H0:H0 + nh]
        yield
        K.op('dve', lambda: nc.vector.tensor_tensor(out=sl(B.rhsG), in0=bcm(B.inc), in1=bcl(gc), op=ALU.mult), reads=[cst, self.g_all], writes=[G(B.rhsG)])
        ps = self.next_ps()
        rg2 = fl(B.rhsG)
        nm4 = B.nm[:].unsqueeze(1).broadcast_to([128, 4, 128])
        fns = []
        for j in range(W // 512):
            fns.append(lambda j=j: nc.tensor.matmul(ps[:, j * 512:(j + 1) * 512], lhsT=B.exc, rhs=rg2[:, j * 512:(j + 1) * 512], start=True, stop=False))
            fns.append(lambda j=j: nc.tensor.matmul(ps[:, j * 512:(j + 1) * 512].rearrange("p (h q) -> p h q", h=4), lhsT=self.idb[:], rhs=nm4, start=False, stop=True))
        K.op('pe', fns, reads=[cst, G(B.rhsG), B.nm, self.idb], writes=[ps])
        K.op('act', lambda: nc.scalar.activation(out=fl(B.DT), in_=ps[:, 0:W], func=AF.Exp), reads=[ps], writes=[G(B.DT)])
        yield
        ps = self.next_ps()
        self.mm8(ps, kT, kT, [fm_], [], hs=hs)
        K.op('dve', lambda: nc.vector.tensor_tensor(out=fl(B.tA), in0=ps[:, 0:W], in1=fl(B.DT), op=ALU.mult), reads=[ps, G(B.DT)], writes=[G(B.tA)])
        K.op('dve', lambda: nc.vector.tensor_tensor(out=sl(B.tA), in0=sl(B.tA), in1=bcl(bc), op=ALU.mult), reads=[G(B.tA), self.beta_all], writes=[G(B.tA)])
        yield
        K.op('dve', lambda: nc.vector.tensor_tensor(out=sl(B.UD), in0=sl(B.tA), in1=bcm(B.bdS[:]), op=ALU.mult), reads=[G(B.tA), B.bdS], writes=[G(B.UD)])
        K.op('dve', lambda: nc.vector.tensor_tensor(out=sl(B.UO), in0=sl(B.tA), in1=bcm(B.nbd[:]), op=ALU.mult), reads=[G(B.tA), B.nbd], writes=[G(B.UO)])
        yield
        ps = self.next_ps()
        self.mm8(ps, kT, qT, [fm_], [], hs=hs)
        K.op('dve', lambda: nc.vector.tensor_tensor(out=fl(attnT), in0=ps[:, 0:W], in1=fl(B.DT), op=ALU.mult), reads=[ps, G(B.DT)], writes=[G(attnT)])
        yield
        ps = self.next_ps()
        pv = self.mm8(ps, hd(B.UD), None, [G(B.UD), self.idb], [], bf16_out=True, transpose=True, hs=hs)
        K.op('act', lambda: nc.scalar.copy(out=fl(B.AD), in_=pv[:, 0:W]), reads=[ps], writes=[G(B.AD)])
        yield
        X, XT = B.UD, B.AD
        P = B.P[0]
        K.op('dve', lambda: nc.vector.tensor_tensor(out=sl(P), in0=bcm(self.idb[:]), in1=sl(X), op=ALU.subtract), reads=[self.idb, G(X)], writes=[G(P)])

        def square(k, X, XT):
            XTn = B.XT[k % 2]
            ps = self.next_ps()
            self.mm8(ps, hd(X), hd(XT), [G(X)], [G(XT)], hs=hs)
            K.op('act', lambda: nc.scalar.copy(out=fl(XTn), in_=ps[:, 0:W]), reads=[ps], writes=[G(XTn)])
            Xn = None
            if k < 4:
                Xn = B.X[k % 2]
                ps2 = self.next_ps()
                self.mm8(ps2, hd(XT), hd(X), [G(XT)], [G(X)], hs=hs)
                K.op('act', lambda: nc.scalar.copy(out=fl(Xn), in_=ps2[:, 0:W]), reads=[ps2], writes=[G(Xn)])
            return Xn, XTn

        Xn, XTn = square(1, X, XT)
        yield
        for k in range(1, 5):
            ps3 = self.next_ps()
            self.mm8(ps3, hd(XTn), hd(P), [G(XTn), self.idb], [G(P)], acc=(idH, hd(P)), hs=hs)
            Pn = B.P[k % 2]
            K.op('act', lambda ps3=ps3, Pn=Pn: nc.scalar.copy(out=fl(Pn), in_=ps3[:, 0:W]), reads=[ps3], writes=[G(Pn)])
            P = Pn
            if k < 4:
                yield
                Xn, XTn = square(k + 1, Xn, XTn)
            yield
        SD = P
        ps = self.next_ps()
        self.mm8(ps, hd(SD), idH, [G(SD)], [self.idb], hs=hs)
        K.op('act', lambda ps=ps: nc.scalar.copy(out=fl(B.TD), in_=ps[:, 0:W]), reads=[ps], writes=[G(B.TD)])
        yield
        ps = self.next_ps()
        self.mm8(ps, hd(B.UO), hd(B.TD), [G(B.UO)], [G(B.TD)], hs=hs)
        K.op('act', lambda ps=ps: nc.scalar.activation(out=fl(B.VT), in_=ps[:, 0:W], func=AF.Copy, scale=-1.0), reads=[ps], writes=[G(B.VT)])
        ps2 = self.next_ps()
        self.mm8(ps2, hd(B.TD), hd(B.UO), [G(B.TD)], [G(B.UO)], hs=hs)
        K.op('act', lambda ps2=ps2: nc.scalar.copy(out=fl(B.V), in_=ps2[:, 0:W]), reads=[ps2], writes=[G(B.V)])
        yield
        ps = self.next_ps()
        self.mm8(ps, hd(B.V), hd(B.VT), [G(B.V)], [G(B.VT)], hs=hs)
        K.op('act', lambda ps=ps: nc.scalar.activation(out=fl(B.V2T), in_=ps[:, 0:W], func=AF.Copy, scale=-1.0), reads=[ps], writes=[G(B.V2T)])
        ps2 = self.next_ps()
        self.mm8(ps2, hd(B.VT), hd(SD), [G(B.VT), self.idb], [G(SD)], acc=(idH, hd(SD)), hs=hs)
        K.op('act', lambda ps2=ps2: nc.scalar.copy(out=fl(B.Z), in_=ps2[:, 0:W]), reads=[ps2], writes=[G(B.Z)])
        yield
        ps = self.next_ps()
        self.mm8(ps, hd(B.V2T), hd(B.Z), [G(B.V2T), self.idb], [G(B.Z)], acc=(idH, hd(B.Z)), hs=hs)
        K.op('act', lambda ps=ps: nc.scalar.copy(out=fl(B.Sf), in_=ps[:, 0:W]), reads=[ps], writes=[G(B.Sf)])
        yield
        K.op('dve', lambda: nc.vector.tensor_tensor(out=sl(B.kg), in0=k3, in1=bcl(expG), op=ALU.mult), reads=[tm_, G(gex)], writes=[G(B.kg)])
        K.op('dve', lambda: nc.vector.tensor_tensor(out=sl(kendk), in0=k3, in1=bcl(kend), op=ALU.mult), reads=[tm_, G(gex)], writes=[G(kendk)])
        yield
        ps = self.next_ps()
        self.mm8(ps, hd(B.Sf), lambda H: v3[:, H, :], [G(B.Sf)], [tm_], hs=hs)
        K.op('act', lambda ps=ps: nc.scalar.copy(out=fl(u_sb), in_=ps[:, 0:W]), reads=[ps], writes=[G(u_sb)])
        ps2 = self.next_ps()
        self.mm8(ps2, hd(B.kg), hd(B.Sf), [G(B.kg)], [G(B.Sf)], hs=hs)
        K.op('act', lambda ps2=ps2: nc.scalar.copy(out=fl(wT), in_=ps2[:, 0:W]), reads=[ps2], writes=[G(wT)])
        yield

    def gdn_rec(self, c, d, fm_, need_out, slot):
        nc, K, B = self.nc, self.K, self.Bt
        bc = self.beta_all[:, c, d * 8:(d + 1) * 8]
        qT = lambda H: fm_[:, FM_Q // 128 + H, :]
        hd = lambda buf: (lambda H: buf[:, H, :])
        fl = lambda buf: buf[:].rearrange("p h q -> p (h q)")
        f3 = lambda ap: ap.rearrange("p (h q) -> p h q", q=128)
        gex_, attnT_, u_sb_, wT_, kendk_ = B.gex[slot], B.attnT[slot], B.u_sb[slot], B.wT[slot], B.kendk[slot]
        expG, cdG = gex_[:, 0, :], gex_[:, 2, :]

        class _Multi:
            def __init__(self, buf):
                self.buf = buf
            def __getitem__(self, k):
                return self.buf[k]
        gex, attnT, u_sb, wT, kendk = gex_, attnT_, u_sb_, wT_, kendk_
        GG = lambda buf: list(buf.g)
        ps = self.next_ps()
        self.mm8(ps, hd(wT), hd(B.Sgb), GG(wT), [B.Sgb])
        K.op('dve', lambda: nc.vector.tensor_tensor(out=fl(B.tB), in0=fl(u_sb), in1=ps[:], op=ALU.subtract), reads=GG(u_sb) + [ps], writes=[B.tB])
        K.op('dve', lambda: nc.vector.tensor_tensor(out=B.vnew[:], in0=B.tB[:], in1=bc.unsqueeze(2).broadcast_to([128, 8, 128]), op=ALU.mult), reads=[B.tB, self.beta_all], writes=[B.vnew])
        yield
        ps = self.next_ps()
        self.mm8(ps, qT, hd(B.Sgb), [fm_], [B.Sgb])
        K.op('dve', lambda: nc.vector.tensor_tensor(out=B.tB[:], in0=f3(ps[:]), in1=expG.unsqueeze(2).broadcast_to([128, 8, 128]), op=ALU.mult), reads=[ps] + GG(gex), writes=[B.tB])
        ps2 = self.next_ps()
        self.mm8(ps2, hd(attnT), hd(B.vnew), GG(attnT), [B.vnew])
        K.op('dve', lambda: nc.vector.tensor_tensor(out=fl(B.o), in0=fl(B.tB), in1=ps2[:], op=ALU.add), reads=[B.tB, ps2], writes=[B.o])
        if need_out:
            if d == 0:
                self.store('pool', B.o, self.YO[c * 128:(c + 1) * 128, 1024:2048], fl(B.o))
            else:
                K.op('dve', lambda: nc.vector.tensor_tensor(out=fl(B.ofin), in0=fl(B.o), in1=B.yo1[:, 1024:2048], op=ALU.add), reads=[B.o, B.yo1], writes=[B.ofin])
        yield
        ps = self.next_ps()
        self.mm8(ps, hd(kendk), hd(B.vnew), GG(kendk), [B.vnew])
        K.op('dve', lambda: nc.vector.tensor_tensor(out=B.Sg[:], in0=B.Sg[:], in1=cdG.unsqueeze(2).broadcast_to([128, 8, 128]), op=ALU.mult), reads=[B.Sg] + GG(gex), writes=[B.Sg])
        K.op('dve', lambda: nc.vector.tensor_tensor(out=fl(B.Sg), in0=fl(B.Sg), in1=ps[:], op=ALU.add), reads=[B.Sg, ps], writes=[B.Sg])
        K.op('act', lambda: nc.scalar.copy(out=B.Sgb[:], in_=B.Sg[:]), reads=[B.Sg], writes=[B.Sgb])
        yield

    def epilogue(self, c):
        nc, K, B = self.nc, self.K, self.Bt
        K.op('dve', lambda: nc.vector.tensor_tensor(out=B.yg[:], in0=B.yfin[:], in1=B.zs[:, 0:1024], op=ALU.mult), reads=[B.yfin, B.zs], writes=[B.yg])
        for g in range(2):
            K.op('act', lambda g=g: nc.scalar.activation(out=B.ejunk[:, 0:512], in_=B.yg[:, g * 512:(g + 1) * 512], func=AF.Square, accum_out=B.est[:, g:g + 1]), reads=[B.yg], writes=[B.ejunk, B.est])
        K.op('dve', lambda: nc.vector.tensor_tensor(out=B.wz[:], in0=B.ofin[:], in1=B.ofin[:], op=ALU.mult), reads=[B.ofin], writes=[B.wz])
        K.op('dve', lambda: nc.vector.tensor_reduce(out=B.est[:, 2:10], in_=B.wz[:], axis=AX.X, op=ALU.add), reads=[B.wz], writes=[B.est])
        K.op('act', lambda: nc.scalar.activation(out=B.est2[:, 0:2], in_=B.est[:, 0:2], func=AF.Ln, bias=self.epsb[:], scale=1.0 / 512), reads=[B.est, self.epsb], writes=[B.est2])
        K.op('act', lambda: nc.scalar.activation(out=B.est2[:, 2:10], in_=B.est[:, 2:10], func=AF.Ln, bias=self.epsb[:], scale=1.0 / 128), reads=[B.est, self.epsb], writes=[B.est2])
        K.op('act', lambda: nc.scalar.activation(out=B.est2[:, 0:10], in_=B.est2[:, 0:10], func=AF.Exp, scale=-0.5), reads=[B.est2], writes=[B.est2])
        for g in range(2):
            K.op('dve', lambda g=g: nc.vector.scalar_tensor_tensor(out=B.ym[:, g * 512:(g + 1) * 512], in0=B.yg[:, g * 512:(g + 1) * 512], scalar=B.est2[:, g:g + 1], in1=B.snw[:, g * 512:(g + 1) * 512], op0=ALU.mult, op1=ALU.mult),
                 reads=[B.yg, B.est2, B.snw], writes=[B.ym])
        K.op('dve', lambda: nc.vector.tensor_tensor(out=B.wz[:], in0=B.zs[:, 1024:2048].rearrange("p (h q) -> p h q", q=128), in1=B.gnw[:].unsqueeze(1).broadcast_to([128, 8, 128]), op=ALU.mult), reads=[B.zs, B.gnw], writes=[B.wz])
        K.op('dve', lambda: nc.vector.tensor_tensor(out=B.ofin[:], in0=B.ofin[:], in1=B.est2[:, 2:10].unsqueeze(2).broadcast_to([128, 8, 128]), op=ALU.mult), reads=[B.ofin, B.est2], writes=[B.ofin])
        K.op('dve', lambda: nc.vector.tensor_tensor(out=B.ym[:, 1024:2048].rearrange("p (h q) -> p h q", q=128), in0=B.ofin[:], in1=B.wz[:], op=ALU.mult), reads=[B.ofin, B.wz], writes=[B.ym])
        self.store('pool', B.ym, self.YM[c * 128:(c + 1) * 128, :], B.ym[:])

    def phaseC(self, l, hsrc, last):
        nc, K = self.nc, self.K
        with ExitStack() as es:
            sb = lambda name, shape, dt=F32: self.sb(es, name, shape, dt)
            wo = sb("wo", [128, 16, D], BF16)
            for kc in range(16):
                self.load('pool', wo, wo[:, kc, :], self.w_out[l][kc * 128:(kc + 1) * 128, :])
            gp = [sb("gp%d" % r, [128, D]) for r in range(2)]
            pw = sb("pwb", [128, D])
            self.load('sp', pw, pw[:], dram_bcast(self.post_w[l], 128))
            for r in range(2):
                self.load('sp', gp[r], gp[r][:], dram_bcast(self.modv[r, 2 * D:3 * D], 128))
                K.op('dve', lambda r=r: nc.vector.tensor_tensor(out=gp[r][:], in0=gp[r][:], in1=pw[:], op=ALU.mult), reads=[gp[r], pw], writes=[gp[r]])
            ymc = [sb("ymc%d" % i, [128, D], BF16) for i in range(2)]
            hc = [sb("hc%d" % i, [128, D]) for i in range(2)]
            ymT = sb("ymT", [128, 16, 128], BF16)
            cj = sb("cjunk", [128, D], BF16)
            cst_ = sb("cst_", [128, 8])
            res = [sb("res%d" % i, [128, D]) for i in range(2)]
            chunks = list(range(2, NCH)) if last else list(range(NCH))
            ymTs = [ymT, sb("ymT2", [128, 16, 128], BF16)]
            csts = [cst_, sb("cst2_", [128, 8])]

            def issue(i):
                c = chunks[i]
                self.load('sp', ymc[i % 2], ymc[i % 2][:], self.YM[c * 128:(c + 1) * 128, :])
                self.load('sp', hc[i % 2], hc[i % 2][:], hsrc[c * 128:(c + 1) * 128, :])

            pos = [sb("po%d" % i, [128, D]) for i in range(2)]

            def c_iter(i, c):
                y_, h_, r_, yT, st_, po = ymc[i % 2], hc[i % 2], res[i % 2], ymTs[i % 2], csts[i % 2], pos[i % 2]
                r = 1 if c < 2 else 0
                for half in range(2):
                    ps = self.next_half()
                    pb = ps[:].bitcast(BF16)
                    K.op('pe', [lambda j=j, pb=pb, half=half: nc.tensor.transpose(out=pb[:, j * 128:(j + 1) * 128], in_=y_[:, (half * 8 + j) * 128:(half * 8 + j + 1) * 128], identity=self.idb[:]) for j in range(8)],
                         reads=[y_, self.idb], writes=[ps])
                    K.op('act', lambda pb=pb, half=half: nc.scalar.copy(out=yT[:, half * 8:(half + 1) * 8, :].rearrange("p k t -> p (k t)"), in_=pb[:, 0:1024]), reads=[ps], writes=[yT])
                    self.free_half(ps)
                yield
                for cb in range(4):
                    ps = self.next_half()
                    fns = [lambda ps=ps, cb=cb, kc=kc: nc.tensor.matmul(ps[:, 0:512], lhsT=yT[:, kc, :], rhs=wo[:, kc, cb * 512:(cb + 1) * 512], start=(kc == 0), stop=(kc == 15)) for kc in range(16)]
                    K.op('pe', fns, reads=[yT, wo], writes=[ps])
                    K.op('act', lambda ps=ps, cb=cb: nc.scalar.activation(out=cj[:, cb * 512:(cb + 1) * 512], in_=ps[:, 0:512], func=AF.Square, accum_out=st_[:, cb:cb + 1]), reads=[ps], writes=[cj, st_])
                    K.op('act', lambda ps=ps, cb=cb: nc.scalar.copy(out=po[:, cb * 512:(cb + 1) * 512], in_=ps[:, 0:512]), reads=[ps], writes=[po])
                    self.free_half(ps)
                    yield
                K.op('dve', lambda: nc.vector.tensor_reduce(out=st_[:, 4:5], in_=st_[:, 0:4], axis=AX.X, op=ALU.add), reads=[st_], writes=[st_])
                K.op('act', lambda: nc.scalar.activation(out=st_[:, 5:6], in_=st_[:, 4:5], func=AF.Ln, bias=self.epsb[:], scale=1.0 / D), reads=[st_, self.epsb], writes=[st_])
                K.op('act', lambda: nc.scalar.activation(out=st_[:, 6:7], in_=st_[:, 5:6], func=AF.Exp, scale=-0.5), reads=[st_], writes=[st_])
                yield
                K.op('dve', lambda: nc.vector.scalar_tensor_tensor(out=r_[:], in0=po[:], scalar=st_[:, 6:7], in1=gp[r][:], op0=ALU.mult, op1=ALU.mult), reads=[po, st_, gp[r]], writes=[r_])
                yield
                K.op('dve', lambda: nc.vector.tensor_tensor(out=r_[:], in0=r_[:], in1=h_[:], op=ALU.add), reads=[r_, h_], writes=[r_])
                if last:
                    dst = self.out[(c - 2) * 128:(c - 1) * 128, :]
                else:
                    dst = self.H[c * 128:(c + 1) * 128, :]
                self.store('pool', r_, dst, r_[:])
                if i + 2 < len(chunks):
                    issue(i + 2)

            def c_gens():
                for i, c in enumerate(chunks):
                    yield c_iter(i, c)

            issue(0)
            issue(1)
            run_pipeline(c_gens(), 2)


def make_consts():
    i = np.arange(128)
    c = np.zeros((128, NCONST), np.float32)
    c[:, C_ID:C_ID + 128] = np.eye(128)
    c[:, C_INC0:C_INC0 + 128] = (i[:, None] <= i[None, :])
    c[:, C_INC1:C_INC1 + 128] = (i[:, None] >= i[None, :])
    c[:, C_EXC0:C_EXC0 + 128] = (i[:, None] > i[None, :])
    c[:, C_EXC1:C_EXC1 + 128] = (i[:, None] < i[None, :])
    c[:, C_ONES:C_ONES + 128] = 1.0
    c[:, C_BD:C_BD + 128] = (i[:, None] // 32 == i[None, :] // 32)
    return c


def make_pv(inp):
    pv = np.zeros((DEPTH, 128, NPV), np.float32)
    for l in range(DEPTH):
        pv[l, :, 0:16] = inp['pre_norm_w'][l].reshape(16, 128).T
        cw = np.concatenate([inp['conv_ssd_w'][l], inp['conv_gdn_w'][l]], axis=1)
        pv[l, :, 16:196] = cw.reshape(5, 36, 128).transpose(2, 1, 0).reshape(128, 180)
        pv[l, :, 196:208] = inp['conv_ssd_b'][l].reshape(12, 128).T
    return pv


def make_in_maps(inp, cores):
    shared = {
        'pv': make_pv(inp), 'consts': make_consts(),
        'w_ada': np.ascontiguousarray(inp['w_ada']), 'b_ada': np.ascontiguousarray(inp['b_ada']),
        'post_norm_w': np.ascontiguousarray(inp['post_norm_w']), 'w_in': np.ascontiguousarray(inp['w_in']),
        'ssd_a_log': np.ascontiguousarray(inp['ssd_a_log']).reshape(DEPTH, 32),
        'ssd_dt_bias': np.ascontiguousarray(inp['ssd_dt_bias']).reshape(DEPTH, 32),
        'ssd_d': np.ascontiguousarray(inp['ssd_d']), 'ssd_norm_w': np.ascontiguousarray(inp['ssd_norm_w']),
        'gdn_a_log': np.ascontiguousarray(inp['gdn_a_log']).reshape(DEPTH, 16),
        'gdn_dt_bias': np.ascontiguousarray(inp['gdn_dt_bias']).reshape(DEPTH, 16),
        'gdn_norm_w': np.ascontiguousarray(inp['gdn_norm_w']), 'w_out': np.ascontiguousarray(inp['w_out']),
    }
    maps = []
    for b in cores:
        m = dict(shared)
        m['h0'] = np.ascontiguousarray(np.concatenate([inp['ctx'][b], inp['x'][b]], axis=0))
        c2 = np.stack([inp['c'][b], inp['c_ctx']])
        m['c2t'] = np.ascontiguousarray(c2.reshape(2, 16, 128).transpose(2, 1, 0).reshape(128, 32))
        maps.append(m)
    return maps


def kernel(**inputs):
    inp = {k: np.asarray(v) for k, v in inputs.items()}
    nc = Builder().build()
    maps = make_in_maps(inp, list(range(8)))
    res = run_bass_kernel_spmd(nc, maps, core_ids=list(range(8)))
    return np.stack([r['out'] for r in res.results], axis=0).astype(np.float32)
```

```python
import numpy as np
from contextlib import ExitStack
import concourse.bass as bass
import concourse.mybir as mybir
from concourse.bass_utils import run_bass_kernel_spmd

F32, BF16 = mybir.dt.float32, mybir.dt.bfloat16
AF = mybir.ActivationFunctionType
ALU = mybir.AluOpType
AX = mybir.AxisListType

D = 2048
T = 4352
NCH = 34
DEPTH = 4
IN_DIM = 6720
EPS = 1e-6
NPV = 208
C_ID, C_INC0, C_INC1, C_EXC0, C_EXC1, C_ONES, C_BD = 0, 128, 256, 384, 512, 640, 768
NCONST = 896
FM_B, FM_C, FM_Q, FM_K, FM_ROWS = 0, 256, 512, 1536, 2560
TM_X, TM_B, TM_K, TM_V, TM_COLS = 0, 1024, 1280, 2304, 3328


class Buf:
    __slots__ = ('name', 'w', 'r', 't', 'g')

    def __init__(self, name, t=None):
        self.name = name
        self.w = None
        self.r = {}
        self.t = t

    def __getitem__(self, k):
        return self.t[k]


class HalfBuf(Buf):
    __slots__ = ('off',)

    def __init__(self, name, t, off):
        Buf.__init__(self, name, t)
        self.off = off

    def __getitem__(self, k):
        if isinstance(k, tuple):
            p, c = k[0], k[1]
            start = (c.start or 0) + self.off
            stop = (c.stop if c.stop is not None else 512) + self.off
            return self.t[p, start:stop]
        return self.t[:, self.off:self.off + 512]


def run_pipeline(gens, depth):
    active = []
    it = iter(gens)
    done = False
    while True:
        if not done and len(active) < depth:
            try:
                active.append(next(it))
            except StopIteration:
                done = True
        if not active:
            if done:
                break
            continue
        for g in list(active):
            try:
                next(g)
            except StopIteration:
                active.remove(g)


class Ctx:
    NS = 8

    def __init__(self, nc):
        self.nc = nc
        self.eng = {'pe': nc.tensor, 'act': nc.scalar, 'dve': nc.vector, 'gp': nc.gpsimd, 'sp': nc.sync, 'pool': nc.gpsimd}
        self.inorder = ('pe', 'act', 'dve', 'gp')
        self.inorder_skip = ('pe',)
        self.dmaq = ('sp', 'pool')
        self.sem = {e: nc.alloc_semaphore('s_' + e) for e in self.inorder}
        self.cnt = {e: 0 for e in self.inorder}
        self.dsem = {q: [nc.alloc_semaphore('d_%s%d' % (q, i)) for i in range(self.NS)] for q in self.dmaq}
        self.dcnt = {q: 0 for q in self.dmaq}
        self.waited = {}
        self.nins = 0

    def _wait(self, e, tok):
        if tok is None:
            return
        sem, val, owner = tok
        if owner == e and e in self.inorder_skip:
            return
        key = (e, id(sem))
        if self.waited.get(key, 0) >= val:
            return
        self.eng[e].wait_ge(sem, val)
        self.waited[key] = val

    def op(self, e, fns, reads=(), writes=()):
        if not isinstance(fns, (list, tuple)):
            fns = [fns]
        for b in reads:
            self._wait(e, b.w)
        for b in writes:
            self._wait(e, b.w)
            for t in b.r.values():
                self._wait(e, t)
        self.nins += len(fns)
        if e in self.dmaq:
            n = self.dcnt[e]
            slot = n % self.NS
            rnd = n // self.NS
            sem = self.dsem[e][slot]
            if rnd > 0:
                self._wait(e, (sem, 16 * rnd, None))
            for f in fns[:-1]:
                f()
            fns[-1]().then_inc(sem, 16)
            tok = (sem, 16 * (rnd + 1), e)
            self.dcnt[e] = n + 1
        else:
            for f in fns[:-1]:
                f()
            fns[-1]().then_inc(self.sem[e], 1)
            self.cnt[e] += 1
            tok = (self.sem[e], self.cnt[e], e)
        for b in reads:
            old = b.r.get(id(tok[0]))
            if old is None or old[1] < tok[1]:
                b.r[id(tok[0])] = tok
        for b in writes:
            b.w = tok
            b.r = {}
        return tok

    def all_tokens(self):
        toks = [(self.sem[e], self.cnt[e], e) for e in self.inorder if self.cnt[e] > 0]
        for q in self.dmaq:
            n = self.dcnt[q]
            for slot in range(self.NS):
                uses = (n - slot + self.NS - 1) // self.NS if n > slot else 0
                if uses > 0:
                    toks.append((self.dsem[q][slot], 16 * uses, q))
        return toks

    def barrier(self):
        toks = self.all_tokens()
        for e in self.eng:
            for t in toks:
                if t[2] == e and e in self.inorder:
                    continue
                self._wait(e, t)


def dram_bcast(ap, nparts):
    n = 1
    for s in ap.shape:
        n *= s
    return bass.AP(ap.tensor, ap.offset, [[0, nparts], [1, n]])


def tok_blocks():
    blks = [(0, 256, 256)]
    for i in range(8):
        blks.append((256 + 512 * i, 512, 64))
    return blks


class Builder:
    def __init__(self, n_layers=DEPTH, debug=False, stop_after=None, force_last=False):
        self.force_last = force_last
        self.n_layers = n_layers
        self.debug = debug
        self.stop_after = stop_after
        nc = self.nc = bass.Bass("TRN2", target_bir_lowering=False)
        self.K = Ctx(nc)
        self.gpe = getattr(Builder, 'GPE', 'gp')
        self.gpv = nc.gpsimd if self.gpe == 'gp' else nc.vector
        di = lambda name, shape: nc.dram_tensor(name, shape, F32, kind="ExternalInput").ap()
        self.h0 = di("h0", [T, D])
        self.c2t = di("c2t", [128, 32])
        self.pv = di("pv", [DEPTH, 128, NPV])
        self.consts = di("consts", [128, NCONST])
        self.w_ada = di("w_ada", [DEPTH, D, 3 * D])
        self.b_ada = di("b_ada", [DEPTH, 3 * D])
        self.post_w = di("post_norm_w", [DEPTH, D])
        self.w_in = di("w_in", [DEPTH, D, IN_DIM])
        self.ssd_a_log = di("ssd_a_log", [DEPTH, 32])
        self.ssd_dt_bias = di("ssd_dt_bias", [DEPTH, 32])
        self.ssd_d = di("ssd_d", [DEPTH, 16])
        self.ssd_norm_w = di("ssd_norm_w", [DEPTH, 1024])
        self.gdn_a_log = di("gdn_a_log", [DEPTH, 16])
        self.gdn_dt_bias = di("gdn_dt_bias", [DEPTH, 16])
        self.gdn_norm_w = di("gdn_norm_w", [DEPTH, 128])
        self.w_out = di("w_out", [DEPTH, D, D])
        self.out = nc.dram_tensor("out", [4096, D], F32, kind="ExternalOutput").ap()
        self.ext_set = getattr(Builder, 'EXT_SET', ())
        ds = lambda name, shape, dt: nc.dram_tensor(name, shape, dt, kind=("ExternalOutput" if (debug or name in self.ext_set) else "Internal")).ap()
        self.modv = ds("modv", [2, 3 * D], F32)
        self.FM = ds("FM", [FM_ROWS, T], BF16)
        self.TM = ds("TM", [T, TM_COLS], BF16)
        self.ZS = ds("ZS", [T, D], BF16)
        self.SMALL = ds("SMALL", [T, 64], F32)
        self.YO = ds("YO", [T, D], F32)
        self.YM = ds("YM", [T, D], BF16)
        self.H = ds("H", [T, D], F32)
        if debug:
            self.UT = ds("UT", [D, T], BF16)

    def sb(self, es, name, shape, dt=F32):
        self.uid = getattr(self, 'uid', 0) + 1
        name = "%s_%d" % (name, self.uid)
        return Buf(name, es.enter_context(self.nc.sbuf_tensor(name, shape, dt)))

    def load(self, q, dst, dst_ap, src_ap, **kw):
        eng = self.K.eng[q]
        return self.K.op(q, lambda: eng.dma_start(out=dst_ap, in_=src_ap, **kw), writes=[dst])

    def store(self, q, src, dst_ap, src_ap, **kw):
        eng = self.K.eng[q]
        return self.K.op(q, lambda: eng.dma_start(out=dst_ap, in_=src_ap, **kw), reads=[src])

    def build(self):
        nc, K = self.nc, self.K
        with ExitStack() as es:
            self.ps = [Buf("ps%d" % i, es.enter_context(nc.psum_tensor("ps%d" % i, [128, 1024], F32))) for i in range(4)]
            self.psi = 0
            self.psh = [HalfBuf("psh%d" % i, self.ps[i // 2].t, (i % 2) * 512) for i in range(8)]
            self.psfree = list(self.psh)
            self.cst = self.sb(es, "cst", [128, NCONST])
            self.load('sp', self.cst, self.cst[:], self.consts)
            self.idb = self.sb(es, "idb", [128, 128], BF16)
            K.op('dve', lambda: nc.vector.tensor_copy(out=self.idb[:], in_=self.cst[:, C_ID:C_ID + 128]), reads=[self.cst], writes=[self.idb])
            self.onesb = self.sb(es, "onesb", [128, 128], BF16)
            K.op('dve', lambda: nc.vector.tensor_copy(out=self.onesb[:], in_=self.cst[:, C_ONES:C_ONES + 128]), reads=[self.cst], writes=[self.onesb])
            self.epsb = self.sb(es, "epsb", [128, 1])
            K.op('dve', lambda: nc.vector.memset(self.epsb[:], EPS), writes=[self.epsb])
            self.lnqs = self.sb(es, "lnqs", [128, 1])
            K.op('dve', lambda: nc.vector.memset(self.lnqs[:], float(np.log(128.0 ** -0.5))), writes=[self.lnqs])
            for l in range(self.n_layers):
                self.layer(l)
            K.barrier()
        return nc

    def next_ps(self):
        p = self.ps[self.psi % 4]
        self.psi += 1
        return p

    def next_half(self):
        assert self.psfree, "PSUM half-slots exhausted"
        return self.psfree.pop(0)

    def free_half(self, p):
        self.psfree.append(p)

    def layer(self, l):
        K = self.K
        last = (l == DEPTH - 1) or (self.force_last and l == self.n_layers - 1)
        hsrc = self.h0 if l == 0 else self.H
        with ExitStack() as es:
            self.pvt = self.sb(es, "pvt", [128, NPV])
            self.load('sp', self.pvt, self.pvt[:], self.pv[l])
            self.phase0(l)
            K.barrier()
            if self.stop_after == 'p0':
                return
            self.phaseA(l, hsrc)
            K.barrier()
            if self.stop_after == 'A':
                return
            self.phaseB(l, last)
            K.barrier()
            if self.stop_after == 'B':
                return
            self.phaseC(l, hsrc, last)
            K.barrier()

    def phase0(self, l):
        nc, K = self.nc, self.K
        with ExitStack() as es:
            sct = self.sb(es, "sct", [128, 32])
            self.load('sp', sct, sct[:], self.c2t)
            K.op('act', lambda: nc.scalar.activation(out=sct[:], in_=sct[:], func=AF.Silu), reads=[sct], writes=[sct])
            bia = self.sb(es, "bia", [2, 3 * D])
            self.load('sp', bia, bia[:], dram_bcast(self.b_ada[l], 2))
            modsb = self.sb(es, "modsb", [2, 3 * D])
            wts = [self.sb(es, "wada%d" % i, [128, 16, 512]) for i in range(2)]
            sct3 = sct[:].rearrange("p (k r) -> p k r", r=2)
            for cb in range(12):
                wt = wts[cb % 2]
                src = self.w_ada[l][:, cb * 512:(cb + 1) * 512].rearrange("(k p) f -> p k f", p=128)
                self.load('sp', wt, wt[:], src)
                ps = self.next_ps()
                fns = []
                for kc in range(16):
                    fns.append(lambda kc=kc, ps=ps, wt=wt: nc.tensor.matmul(ps[0:2, 0:512], lhsT=sct3[:, kc, :], rhs=wt[:, kc, :], start=(kc == 0), stop=(kc == 15)))
                K.op('pe', fns, reads=[sct, wt], writes=[ps])
                K.op('dve', lambda cb=cb, ps=ps: nc.vector.tensor_tensor(out=modsb[:, cb * 512:(cb + 1) * 512], in0=ps[0:2, 0:512], in1=bia[:, cb * 512:(cb + 1) * 512], op=ALU.add),
                     reads=[ps, bia], writes=[modsb])
            self.store('sp', modsb, self.modv, modsb[:])

    def phaseA(self, l, hsrc):
        nc, K = self.nc, self.K
        with ExitStack() as es:
            uT = self.sb(es, "uT", [128, 16, T], BF16)
            mraw = self.sb(es, "mraw", [128, 2, 2, 16])
            for r in range(2):
                for w in range(2):
                    src = bass.AP(self.modv.tensor, self.modv.offset + r * 3 * D + w * D, [[1, 128], [128, 16]])
                    self.load('sp', mraw, mraw[:, r, w, :], src, allow_slow_non_contiguous=True)
            mA = self.sb(es, "mA", [128, 2, 16])
            for r in range(2):
                K.op('dve', lambda r=r: nc.vector.scalar_tensor_tensor(out=mA[:, r, :], in0=mraw[:, r, 1, :], scalar=1.0, in1=self.pvt[:, 0:16], op0=ALU.add, op1=ALU.mult),
                     reads=[mraw, self.pvt], writes=[mA])
            with ExitStack() as es0:
                NH = 3
                hx = [self.sb(es0, "hx%d" % i, [128, D]) for i in range(NH)]
                junk = self.sb(es0, "junk", [128, D], BF16)
                xn = [self.sb(es0, "xn%d" % i, [128, D], BF16) for i in range(NH)]
                st = [self.sb(es0, "st%d" % i, [128, 4]) for i in range(NH)]

                def a0_load(c):
                    h_ = hx[c % NH]
                    self.load('sp', h_, h_[:], hsrc[c * 128:(c + 1) * 128, :])

                def a0_iter(c):
                    r = 1 if c < 2 else 0
                    h_, x_, s_ = hx[c % NH], xn[c % NH], st[c % NH]
                    K.op('act', lambda: nc.scalar.activation(out=junk[:], in_=h_[:], func=AF.Square, accum_out=s_[:, 0:1]), reads=[h_], writes=[junk, s_])
                    yield
                    K.op('act', lambda: nc.scalar.activation(out=s_[:, 1:2], in_=s_[:, 0:1], func=AF.Ln, bias=self.epsb[:], scale=1.0 / D), reads=[s_, self.epsb], writes=[s_])
                    K.op('act', lambda: nc.scalar.activation(out=s_[:, 2:3], in_=s_[:, 1:2], func=AF.Exp, scale=-0.5), reads=[s_], writes=[s_])
                    yield
                    K.op('dve', lambda: nc.vector.tensor_scalar(out=x_[:], in0=h_[:], scalar1=s_[:, 2:3], scalar2=None, op0=ALU.mult), reads=[h_, s_], writes=[x_])
                    if c + NH < NCH:
                        a0_load(c + NH)
                    yield
                    for half in range(2):
                        ps = self.next_half()
                        pb = ps[:].bitcast(BF16)
                        fns = [lambda j=j, pb=pb, half=half: nc.tensor.transpose(out=pb[:, j * 128:(j + 1) * 128], in_=x_[:, (half * 8 + j) * 128:(half * 8 + j + 1) * 128], identity=self.idb[:]) for j in range(8)]
                        K.op('pe', fns, reads=[x_, self.idb], writes=[ps])
                        for j in range(8):
                            kc = half * 8 + j
                            K.op('act', lambda j=j, kc=kc, pb=pb: nc.scalar.activation(out=uT[:, kc, c * 128:(c + 1) * 128], in_=pb[:, j * 128:(j + 1) * 128], func=AF.Identity,
                                                                                     bias=mraw[:, r, 0, kc:kc + 1], scale=mA[:, r, kc:kc + 1]),
                                 reads=[ps, mraw, mA], writes=[uT])
                        self.free_half(ps)
                        yield

                for c in range(NH):
                    a0_load(c)
                run_pipeline((a0_iter(c) for c in range(NCH)), 2)
                K.barrier()
            if self.debug:
                for kc in range(16):
                    self.store('pool', uT, self.UT[kc * 128:(kc + 1) * 128, :], uT[:, kc, :])
            with ExitStack() as es1:
                NB = 6
                wf = [self.sb(es1, "wf%d" % i, [128, 16, 128], BF16) for i in range(2)]
                acc = [self.sb(es1, "acc%d" % i, [128, 512]) for i in range(NB)]
                cv = [self.sb(es1, "cv%d" % i, [128, 512]) for i in range(NB)]
                ob = [self.sb(es1, "ob%d" % i, [128, 512], BF16) for i in range(NB)]
                sq = [self.sb(es1, "sq%d" % i, [128, 512], BF16) for i in range(NB)]
                lnv = [self.sb(es1, "lnv%d" % i, [128, 512]) for i in range(NB)]
                tmo = [self.sb(es1, "tmo%d" % i, [128, 4, 128], BF16) for i in range(NB)]
                blks = tok_blocks()

                def fm_iter(it, fc, w_, kind, fm_row, tm_col, t0, nt, rl):
                    a_, c_, o_, s_, l_, m_ = acc[it % NB], cv[it % NB], ob[it % NB], sq[it % NB], lnv[it % NB], tmo[it % NB]
                    ps = self.next_half()
                    fns = [lambda kc=kc: nc.tensor.matmul(ps[:, 0:nt], lhsT=w_[:, kc, :], rhs=uT[:, kc, t0:t0 + nt], start=(kc == 0), stop=(kc == 15)) for kc in range(16)]
                    K.op('pe', fns, reads=[w_, uT], writes=[ps])
                    yield
                    cw = lambda k: self.pvt[:, 16 + fc * 5 + k:16 + fc * 5 + k + 1]
                    if kind in ('q', 'k') or (it % 2 == 0):
                        K.op('dve', lambda: nc.vector.tensor_scalar(out=a_[:, 0:nt], in0=ps[:, 0:nt], scalar1=cw(2), scalar2=None, op0=ALU.mult), reads=[ps, self.pvt], writes=[a_])
                    else:
                        K.op('act', lambda: nc.scalar.activation(out=a_[:, 0:nt], in_=ps[:, 0:nt], func=AF.Copy, scale=cw(2)), reads=[ps, self.pvt], writes=[a_])
                    pv3 = ps[:, 0:nt].rearrange("p (r j) -> p r j", j=rl)
                    av3 = a_[:, 0:nt].rearrange("p (r j) -> p r j", j=rl)
                    for k in (0, 1, 3, 4):
                        sft = k - 2
                        j0, j1 = max(0, -sft), rl - max(0, sft)
                        K.op('dve', lambda k=k, sft=sft, j0=j0, j1=j1: nc.vector.scalar_tensor_tensor(out=av3[:, :, j0:j1], in0=pv3[:, :, j0 + sft:j1 + sft], scalar=cw(k), in1=av3[:, :, j0:j1], op0=ALU.mult, op1=ALU.add),
                             reads=[ps, a_, self.pvt], writes=[a_])
                        if k == 1:
                            yield
                    self.free_half(ps)
                    yield
                    if kind in ('q', 'k'):
                        K.op('act', lambda: nc.scalar.activation(out=c_[:, 0:nt], in_=a_[:, 0:nt], func=AF.Silu), reads=[a_], writes=[c_])
                        K.op('dve', lambda: nc.vector.tensor_tensor(out=s_[:, 0:nt], in0=c_[:, 0:nt], in1=c_[:, 0:nt], op=ALU.mult), reads=[c_], writes=[s_])
                        yield
                        ps2 = self.next_half()
                        K.op('pe', lambda: nc.tensor.matmul(ps2[:, 0:nt], lhsT=self.onesb[:], rhs=s_[:, 0:nt], start=True, stop=True), reads=[s_, self.onesb], writes=[ps2])
                        K.op('act', lambda: nc.scalar.activation(out=l_[:, 0:nt], in_=ps2[:, 0:nt], func=AF.Ln, bias=self.epsb[:], scale=1.0), reads=[ps2, self.epsb], writes=[l_])
                        self.free_half(ps2)
                        yield
                        if kind == 'q':
                            K.op('act', lambda: nc.scalar.activation(out=l_[:, 0:nt], in_=l_[:, 0:nt], func=AF.Exp, bias=self.lnqs[:], scale=-0.5), reads=[l_, self.lnqs], writes=[l_])
                        else:
                            K.op('act', lambda: nc.scalar.activation(out=l_[:, 0:nt], in_=l_[:, 0:nt], func=AF.Exp, scale=-0.5), reads=[l_], writes=[l_])
                        yield
                        K.op('dve', lambda: nc.vector.tensor_tensor(out=o_[:, 0:nt], in0=c_[:, 0:nt], in1=l_[:, 0:nt], op=ALU.mult), reads=[c_, l_], writes=[o_])
                    elif fc < 12:
                        K.op('act', lambda: nc.scalar.activation(out=o_[:, 0:nt], in_=a_[:, 0:nt], func=AF.Silu, bias=self.pvt[:, 196 + fc:197 + fc], scale=1.0), reads=[a_, self.pvt], writes=[o_])
                    else:
                        K.op('act', lambda: nc.scalar.activation(out=o_[:, 0:nt], in_=a_[:, 0:nt], func=AF.Silu), reads=[a_], writes=[o_])
                    yield
                    if fm_row is not None:
                        self.store('pool', o_, self.FM[fm_row:fm_row + 128, t0:t0 + nt], o_[:, 0:nt])
                    if tm_col is not None:
                        nj = nt // 128
                        ps3 = self.next_half()
                        pb = ps3[:].bitcast(BF16)
                        fns = [lambda j=j: nc.tensor.transpose(out=pb[:, j * 128:(j + 1) * 128], in_=o_[:, j * 128:(j + 1) * 128], identity=self.idb[:]) for j in range(nj)]
                        K.op('pe', fns, reads=[o_, self.idb], writes=[ps3])
                        K.op('act', lambda: nc.scalar.copy(out=m_[:, 0:nj, :], in_=pb[:, 0:nj * 128].rearrange("p (j f) -> p j f", f=128)), reads=[ps3], writes=[m_])
                        self.free_half(ps3)
                        dst = self.TM[t0:t0 + nt, tm_col:tm_col + 128].rearrange("(j p) f -> p j f", p=128)
                        self.store('pool', m_, dst, m_[:, 0:nj, :])

                def fm_gens():
                    it = 0
                    for fc in range(36):
                        col0 = (2048 + fc * 128) if fc < 12 else (3616 + (fc - 12) * 128)
                        w_ = wf[fc % 2]
                        src = self.w_in[l][:, col0:col0 + 128].rearrange("(k p) f -> p k f", p=128)
                        self.load('pool', w_, w_[:], src)
                        if fc < 8:
                            kind, fm_row, tm_col = 'x', None, TM_X + fc * 128
                        elif fc < 10:
                            kind, fm_row, tm_col = 'B', FM_B + (fc - 8) * 128, TM_B + (fc - 8) * 128
                        elif fc < 12:
                            kind, fm_row, tm_col = 'C', FM_C + (fc - 10) * 128, None
                        elif fc < 20:
                            kind, fm_row, tm_col = 'q', FM_Q + (fc - 12) * 128, None
                        elif fc < 28:
                            kind, fm_row, tm_col = 'k', FM_K + (fc - 20) * 128, TM_K + (fc - 20) * 128
                        else:
                            kind, fm_row, tm_col = 'v', None, TM_V + (fc - 28) * 128
                        for (t0, nt, rl) in blks:
                            yield fm_iter(it, fc, w_, kind, fm_row, tm_col, t0, nt, rl)
                            it += 1

                run_pipeline(fm_gens(), 6)
                K.barrier()
            with ExitStack() as es2:
                NZ = 6
                wz = [self.sb(es2, "wz%d" % i, [128, 16, 256], BF16) for i in range(2)]
                zo = [self.sb(es2, "zo%d" % i, [128, 256], BF16) for i in range(NZ)]
                so = [self.sb(es2, "so%d" % i, [128, 64]) for i in range(NZ)]

                def z_iter(it, cb, c, w_, ncol):
                    ps = self.next_half()
                    fns = [lambda kc=kc: nc.tensor.matmul(ps[:, 0:ncol], lhsT=uT[:, kc, c * 128:(c + 1) * 128], rhs=w_[:, kc, 0:ncol], start=(kc == 0), stop=(kc == 15)) for kc in range(16)]
                    K.op('pe', fns, reads=[w_, uT], writes=[ps])
                    yield
                    if cb < 8:
                        z_ = zo[it % NZ]
                        K.op('act', lambda: nc.scalar.activation(out=z_[:], in_=ps[:, 0:256], func=AF.Silu), reads=[ps], writes=[z_])
                        self.free_half(ps)
                        self.store('sp', z_, self.ZS[c * 128:(c + 1) * 128, cb * 256:(cb + 1) * 256], z_[:])
                    else:
                        s_ = so[it % NZ]
                        K.op('act', lambda: nc.scalar.copy(out=s_[:], in_=ps[:, 0:64]), reads=[ps], writes=[s_])
                        self.free_half(ps)
                        self.store('sp', s_, self.SMALL[c * 128:(c + 1) * 128, :], s_[:])

                def z_gens():
                    it = 0
                    for cb in range(9):
                        w_ = wz[cb % 2]
                        if cb < 8:
                            src = self.w_in[l][:, cb * 256:(cb + 1) * 256].rearrange("(k p) f -> p k f", p=128)
                            self.load('pool', w_, w_[:], src)
                            ncol = 256
                        else:
                            self.load('pool', w_, w_[:, :, 0:32], self.w_in[l][:, 3584:3616].rearrange("(k p) f -> p k f", p=128))
                            self.load('pool', w_, w_[:, :, 32:64], self.w_in[l][:, 6688:6720].rearrange("(k p) f -> p k f", p=128))
                            ncol = 64
                        for c in range(NCH):
                            yield z_iter(it, cb, c, w_, ncol)
                            it += 1

                run_pipeline(z_gens(), 5)

    def phaseS(self, l, es):
        nc, K = self.nc, self.K
        sb = lambda name, shape, dt=F32: self.sb(es, name, shape, dt)
        self.dt_all = sb("dt_all", [128, NCH, 32])
        self.loga_all = sb("loga_all", [128, NCH, 32])
        self.beta_all = sb("beta_all", [128, NCH, 16])
        self.g_all = sb("g_all", [128, NCH, 16])
        self.dsk = sb("dsk", [128, 16])
        self.onecol = sb("onecol", [128, 1])
        K.op('dve', lambda: nc.vector.memset(self.onecol[:], 1.0), writes=[self.onecol])
        self.load('sp', self.dsk, self.dsk[:], dram_bcast(self.ssd_d[l], 128))
        with ExitStack() as e2:
            sm = self.sb(e2, "sm", [128, NCH, 64])
            self.load('sp', sm, sm[:], self.SMALL.rearrange("(c p) f -> p c f", p=128))
            bias = self.sb(e2, "sbias", [128, 48])
            alog = self.sb(e2, "salog", [128, 48])
            self.load('sp', bias, bias[:, 0:32], dram_bcast(self.ssd_dt_bias[l], 128))
            self.load('sp', bias, bias[:, 32:48], dram_bcast(self.gdn_dt_bias[l], 128))
            self.load('sp', alog, alog[:, 0:32], dram_bcast(self.ssd_a_log[l], 128))
            self.load('sp', alog, alog[:, 32:48], dram_bcast(self.gdn_a_log[l], 128))
            K.op('act', lambda: nc.scalar.activation(out=alog[:], in_=alog[:], func=AF.Exp), reads=[alog], writes=[alog])
            K.op('dve', lambda: nc.vector.tensor_scalar(out=alog[:], in0=alog[:], scalar1=-1.0, scalar2=None, op0=ALU.mult), reads=[alog], writes=[alog])
            tmp = self.sb(e2, "stmp", [128, NCH, 48])
            K.op('dve', lambda: nc.vector.tensor_tensor(out=tmp[:, :, 0:32], in0=sm[:, :, 0:32], in1=bias[:, 0:32].unsqueeze(1).broadcast_to([128, NCH, 32]), op=ALU.add), reads=[sm, bias], writes=[tmp])
            K.op('dve', lambda: nc.vector.tensor_tensor(out=tmp[:, :, 32:48], in0=sm[:, :, 48:64], in1=bias[:, 32:48].unsqueeze(1).broadcast_to([128, NCH, 16]), op=ALU.add), reads=[sm, bias], writes=[tmp])
            K.op('act', lambda: nc.scalar.activation(out=tmp[:], in_=tmp[:], func=AF.Exp), reads=[tmp], writes=[tmp])
            K.op('act', lambda: nc.scalar.activation(out=tmp[:], in_=tmp[:], func=AF.Ln, bias=self.onecol[:], scale=1.0), reads=[tmp, self.onecol], writes=[tmp])
            K.op('dve', lambda: nc.vector.tensor_copy(out=self.dt_all[:], in_=tmp[:, :, 0:32]), reads=[tmp], writes=[self.dt_all])
            K.op('dve', lambda: nc.vector.tensor_tensor(out=self.loga_all[:], in0=tmp[:, :, 0:32], in1=alog[:, 0:32].unsqueeze(1).broadcast_to([128, NCH, 32]), op=ALU.mult), reads=[tmp, alog], writes=[self.loga_all])
            K.op('dve', lambda: nc.vector.tensor_tensor(out=self.g_all[:], in0=tmp[:, :, 32:48], in1=alog[:, 32:48].unsqueeze(1).broadcast_to([128, NCH, 16]), op=ALU.mult), reads=[tmp, alog], writes=[self.g_all])
            K.op('act', lambda: nc.scalar.activation(out=self.beta_all[:], in_=sm[:, :, 32:48], func=AF.Sigmoid), reads=[sm], writes=[self.beta_all])
            K.barrier()

    def phaseB(self, l, last):
        nc, K = self.nc, self.K
        with ExitStack() as es:
            self.phaseS(l, es)
            sb = lambda name, shape, dt=F32: self.sb(es, name, shape, dt)
            B = self.Bt = type('T', (), {})()
            B.tm = [sb("tm%d" % i, [128, TM_COLS], BF16) for i in range(3)]
            B.fm = [sb("fm%d" % i, [128, 20, 128], BF16) for i in range(3)]
            B.yo1 = sb("yo1", [128, D])
            B.zs = sb("zs", [128, D], BF16)
            B.ex = sb("ex", [128, 48]); B.rhsL = sb("rhsL", [128, 8, 128]); B.E = sb("E", [128, 16, 128], BF16)
            B.cbm = sb("cbm", [128, 2, 128], BF16); B.MT = sb("MT", [128, 16, 128], BF16)
            B.xdt = sb("xdt", [128, 16, 64], BF16); B.xdd = sb("xdd", [128, 16, 64], BF16)
            B.ytmp = sb("ytmp", [128, 1024]); B.ysum = sb("ysum", [128, 1024]); B.yfin = sb("yfin", [128, 1024])
            B.hT = sb("hT", [128, 16, 64]); B.hTb = sb("hTb", [128, 16, 64], BF16)
            B.gex = [sb("gex%d" % i, [128, 3, 8]) for i in range(2)]; B.rhsG = sb("rhsG", [128, 8, 128]); B.DT = sb("DT", [128, 8, 128])
            B.tA = sb("tA", [128, 8, 128]); B.tB = sb("tB", [128, 8, 128])
            B.nm = sb("nm", [128, 128], BF16); B.bdS = sb("bdS", [128, 128])
            g16 = lambda name: sb(name, [128, 8, 128], BF16)
            B.UD = g16("UD"); B.UO = g16("UO"); B.attnT = [g16("attnT0"), g16("attnT1")]; B.AD = g16("AD")
            B.P = [g16("P0"), g16("P1")]; B.X = [g16("X0"), g16("X1")]; B.XT = [g16("XT0"), g16("XT1")]
            B.TD = g16("TD"); B.VT = g16("VT"); B.V = g16("V"); B.V2T = g16("V2T"); B.Z = g16("Z"); B.Sf = g16("Sf")
            B.kg = g16("kg"); B.kendk = [g16("kendk0"), g16("kendk1")]; B.wT = [g16("wT0"), g16("wT1")]; B.vnew = g16("vnew")
            B.u_sb = [sb("u_sb%d" % i, [128, 8, 128]) for i in range(2)]; B.o = sb("o", [128, 8, 128]); B.ofin = sb("ofin", [128, 8, 128])
            B.Sg = sb("Sg", [128, 8, 128]); B.Sgb = g16("Sgb")
            NG = self.NG = 1
            self.PREW = getattr(Builder, "PREW", 20)
            for t_ in [B.rhsG, B.DT, B.tA, B.UD, B.UO, B.AD, B.TD, B.VT, B.V, B.V2T, B.Z, B.Sf, B.kg] + B.P + B.X + B.XT + B.gex + B.attnT + B.u_sb + B.wT + B.kendk:
                t_.g = [Buf(t_.name + "_g%d" % j, t_.t) for j in range(NG)]
            B.mT = sb("mT", [128, 128]); B.nbd = sb("nbd", [128, 128])
            K.op('dve', lambda: nc.vector.tensor_scalar(out=B.nbd[:], in0=self.cst[:, C_BD:C_BD + 128], scalar1=-1.0, scalar2=1.0, op0=ALU.mult, op1=ALU.add), reads=[self.cst], writes=[B.nbd])
            B.snw = sb("snw", [128, 1024]); B.gnw = sb("gnw", [128, 128])
            self.load('sp', B.snw, B.snw[:], dram_bcast(self.ssd_norm_w[l], 128))
            self.load('sp', B.gnw, B.gnw[:], dram_bcast(self.gdn_norm_w[l], 128))
            B.yg = sb("yg", [128, 1024]); B.ejunk = sb("ejunk", [128, 1024], BF16); B.est = sb("est", [128, 16]); B.est2 = sb("est2", [128, 16])
            B.ym = sb("ym", [128, D], BF16); B.wz = sb("wzg", [128, 8, 128]); B.wz2 = sb("wzg2", [128, 8, 128], BF16)
            for d in range(2):
                self.scan_pass(l, d, last)
                K.barrier()

    def scan_pass(self, l, d, last):
        nc, K, B = self.nc, self.K, self.Bt
        order = list(range(NCH)) if d == 0 else [1, 0] + list(range(NCH - 1, 1, -1))
        B.inc = self.cst[:, (C_INC0 if d == 0 else C_INC1):(C_INC0 if d == 0 else C_INC1) + 128]
        B.exc = self.cst[:, (C_EXC0 if d == 0 else C_EXC1):(C_EXC0 if d == 0 else C_EXC1) + 128]
        B.strictT = self.cst[:, (C_EXC1 if d == 0 else C_EXC0):(C_EXC1 if d == 0 else C_EXC0) + 128]
        B.ones = self.cst[:, C_ONES:C_ONES + 128]
        B.bd = self.cst[:, C_BD:C_BD + 128]
        K.op('dve', lambda: nc.vector.memset(B.hT[:], 0.0), writes=[B.hT])
        K.op('dve', lambda: nc.vector.memset(B.hTb[:], 0.0), writes=[B.hTb])
        K.op('dve', lambda: nc.vector.memset(B.Sg[:], 0.0), writes=[B.Sg])
        K.op('dve', lambda: nc.vector.memset(B.Sgb[:], 0.0), writes=[B.Sgb])

        K.op('dve', lambda: nc.vector.tensor_scalar(out=B.nm[:], in0=B.inc, scalar1=30000.0, scalar2=-30000.0, op0=ALU.mult, op1=ALU.add), reads=[self.cst], writes=[B.nm])
        K.op('dve', lambda: nc.vector.tensor_tensor(out=B.bdS[:], in0=B.strictT, in1=B.bd, op=ALU.mult), reads=[self.cst], writes=[B.bdS])

        def issue_loads(i):
            c = order[i]
            tm_, fm_ = B.tm[i % 3], B.fm[i % 3]
            self.load('sp', tm_, tm_[:], self.TM[c * 128:(c + 1) * 128, :])
            self.load('sp', fm_, fm_[:], self.FM[:, c * 128:(c + 1) * 128].rearrange("(f p) t -> p f t", p=128))

        def run_weighted(gens):
            st = [[g, n, 0] for g, n in gens]
            while st:
                st.sort(key=lambda x: x[2] / float(x[1]))
                g = st[0]
                try:
                    next(g[0])
                    g[2] += 1
                except StopIteration:
                    st.remove(g)

        n = len(order)
        issue_loads(0)
        if n > 1:
            issue_loads(1)
        run_weighted([(self.gdn_pre(order[0], d, B.tm[0], B.fm[0], 0, hg, self.NG), 20) for hg in range(self.NG)])
        for i, c in enumerate(order):
            if i + 2 < n:
                issue_loads(i + 2)
            tm_, fm_ = B.tm[i % 3], B.fm[i % 3]
            need_out = not (last and c < 2)
            if d == 1 and need_out:
                self.load('sp', B.yo1, B.yo1[:], self.YO[c * 128:(c + 1) * 128, :])
                self.load('sp', B.zs, B.zs[:], self.ZS[c * 128:(c + 1) * 128, :])
            gens = [(self.ssd_gen(c, d, tm_, fm_, need_out), 7), (self.gdn_rec(c, d, fm_, need_out, i % 2), 3)]
            if i + 1 < n:
                for hg in range(self.NG):
                    gens.append((self.gdn_pre(order[i + 1], d, B.tm[(i + 1) % 3], B.fm[(i + 1) % 3], (i + 1) % 2, hg, self.NG), self.PREW))
            run_weighted(gens)
            if d == 1 and need_out:
                self.epilogue(c)

    def ssd_gen(self, c, d, tm_, fm_, need_out):
        nc, K, B = self.nc, self.K, self.Bt
        cst = self.cst
        loga = self.loga_all[:, c, d * 16:(d + 1) * 16]
        dtc = self.dt_all[:, c, d * 16:(d + 1) * 16]
        xs3 = tm_[:, TM_X:TM_X + 1024].rearrange("p (h q) -> p h q", q=64)
        ps = self.next_ps()
        K.op('pe', [lambda: nc.tensor.matmul(ps[:, 0:16], lhsT=B.inc, rhs=loga, start=True, stop=True),
                    lambda: nc.tensor.matmul(ps[:, 16:32], lhsT=B.exc, rhs=loga, start=True, stop=True),
                    lambda: nc.tensor.matmul(ps[:, 32:48], lhsT=B.ones, rhs=loga, start=True, stop=True)], reads=[cst, self.loga_all], writes=[ps])
        K.op('act', lambda: nc.scalar.activation(out=B.ex[:], in_=ps[:, 0:48], func=AF.Exp), reads=[ps], writes=[B.ex])
        expA, dend, cd = B.ex[:, 0:16], B.ex[:, 16:32], B.ex[:, 32:48]
        yield
        for hh in range(2):
            K.op(self.gpe, lambda hh=hh: self.gpv.tensor_tensor(out=B.rhsL[:], in0=B.inc.unsqueeze(1).broadcast_to([128, 8, 128]),
                                                             in1=loga[:, hh * 8:(hh + 1) * 8].unsqueeze(2).broadcast_to([128, 8, 128]), op=ALU.mult),
                 reads=[cst, self.loga_all], writes=[B.rhsL])
            ps = self.next_ps()
            rl2 = B.rhsL[:].rearrange("p h q -> p (h q)")
            nm4 = B.nm[:].unsqueeze(1).broadcast_to([128, 4, 128])
            fns = []
            for j in range(2):
                fns.append(lambda ps=ps, rl2=rl2, j=j: nc.tensor.matmul(ps[:, j * 512:(j + 1) * 512], lhsT=B.exc, rhs=rl2[:, j * 512:(j + 1) * 512], start=True, stop=False))
                fns.append(lambda ps=ps, j=j: nc.tensor.matmul(ps[:, j * 512:(j + 1) * 512].rearrange("p (h q) -> p h q", h=4), lhsT=self.idb[:], rhs=nm4, start=False, stop=True))
            K.op('pe', fns, reads=[cst, B.rhsL, B.nm, self.idb], writes=[ps])
            K.op('act', lambda ps=ps, hh=hh: nc.scalar.activation(out=B.E[:, hh * 8:(hh + 1) * 8, :].rearrange("p h q -> p (h q)"), in_=ps[:], func=AF.Exp), reads=[ps], writes=[B.E])
            yield
        ps = self.next_ps()
        K.op('pe', [lambda ps=ps, g=g: nc.tensor.matmul(ps[:, g * 128:(g + 1) * 128], lhsT=fm_[:, FM_B // 128 + g, :], rhs=fm_[:, FM_C // 128 + g, :], start=True, stop=True) for g in range(2)],
             reads=[fm_], writes=[ps])
        K.op('act', lambda ps=ps: nc.scalar.copy(out=B.cbm[:].rearrange("p g q -> p (g q)"), in_=ps[:, 0:256]), reads=[ps], writes=[B.cbm])
        yield
        K.op('dve', lambda: nc.vector.tensor_tensor(out=B.MT[:].rearrange("p (g r) q -> p g r q", g=2), in0=B.E[:].rearrange("p (g r) q -> p g r q", g=2),
                                                    in1=B.cbm[:].unsqueeze(2).broadcast_to([128, 2, 8, 128]), op=ALU.mult), reads=[B.E, B.cbm], writes=[B.MT])
        K.op(self.gpe, lambda: self.gpv.tensor_tensor(out=B.xdt[:], in0=xs3, in1=dtc.unsqueeze(2).broadcast_to([128, 16, 64]), op=ALU.mult), reads=[tm_, self.dt_all], writes=[B.xdt])
        K.op(self.gpe, lambda: self.gpv.tensor_tensor(out=B.xdd[:], in0=B.xdt[:], in1=dend.unsqueeze(2).broadcast_to([128, 16, 64]), op=ALU.mult), reads=[B.xdt, B.ex], writes=[B.xdd])
        yield
        psd = self.next_ps()
        K.op('pe', [lambda h=h: nc.tensor.matmul(psd[:, h * 64:(h + 1) * 64], lhsT=B.MT[:, h, :], rhs=B.xdt[:, h, :], start=True, stop=True) for h in range(16)], reads=[B.MT, B.xdt], writes=[psd])
        pso = self.next_ps()
        K.op('pe', [lambda g=g: nc.tensor.matmul(pso[:, g * 512:(g + 1) * 512], lhsT=fm_[:, FM_C // 128 + g, :], rhs=B.hTb[:, g * 8:(g + 1) * 8, :].rearrange("p h q -> p (h q)"), start=True, stop=True) for g in range(2)],
             reads=[fm_, B.hTb], writes=[pso])
        K.op('dve', lambda: nc.vector.tensor_tensor(out=B.ytmp[:].rearrange("p (h q) -> p h q", q=64), in0=pso[:].rearrange("p (h q) -> p h q", q=64), in1=expA.unsqueeze(2).broadcast_to([128, 16, 64]), op=ALU.mult),
             reads=[pso, B.ex], writes=[B.ytmp])
        K.op('dve', lambda: nc.vector.tensor_tensor(out=B.ysum[:], in0=B.ytmp[:], in1=psd[:], op=ALU.add), reads=[B.ytmp, psd], writes=[B.ysum])
        if need_out:
            if d == 0:
                K.op(self.gpe, lambda: self.gpv.tensor_tensor(out=B.ytmp[:].rearrange("p (h q) -> p h q", q=64), in0=xs3, in1=self.dsk[:].unsqueeze(2).broadcast_to([128, 16, 64]), op=ALU.mult),
                     reads=[tm_, self.dsk], writes=[B.ytmp])
                K.op('dve', lambda: nc.vector.tensor_tensor(out=B.yfin[:], in0=B.ytmp[:], in1=B.ysum[:], op=ALU.add), reads=[B.ytmp, B.ysum], writes=[B.yfin])
                self.store('sp', B.yfin, self.YO[c * 128:(c + 1) * 128, 0:1024], B.yfin[:])
            else:
                K.op('dve', lambda: nc.vector.tensor_tensor(out=B.yfin[:], in0=B.yo1[:, 0:1024], in1=B.ysum[:], op=ALU.add), reads=[B.yo1, B.ysum], writes=[B.yfin])
        yield
        pss = self.next_ps()
        K.op('pe', [lambda g=g: nc.tensor.matmul(pss[:, g * 512:(g + 1) * 512], lhsT=tm_[:, TM_B + g * 128:TM_B + (g + 1) * 128], rhs=B.xdd[:, g * 8:(g + 1) * 8, :].rearrange("p h q -> p (h q)"), start=True, stop=True) for g in range(2)],
             reads=[tm_, B.xdd], writes=[pss])
        K.op(self.gpe, lambda: self.gpv.tensor_tensor(out=B.hT[:], in0=B.hT[:], in1=cd.unsqueeze(2).broadcast_to([128, 16, 64]), op=ALU.mult), reads=[B.hT, B.ex], writes=[B.hT])
        K.op('dve', lambda: nc.vector.tensor_tensor(out=B.hT[:].rearrange("p h q -> p (h q)"), in0=B.hT[:].rearrange("p h q -> p (h q)"), in1=pss[:], op=ALU.add), reads=[B.hT, pss], writes=[B.hT])
        K.op('act', lambda: nc.scalar.copy(out=B.hTb[:], in_=B.hT[:]), reads=[B.hT], writes=[B.hTb])
        yield

    def mm8(self, ps, lhs, rhs, lhs_bufs, rhs_bufs, bf16_out=False, transpose=False, acc=None, hs=None):
        nc, K = self.nc, self.K
        if hs is None:
            hs = range(8)
        if bf16_out:
            pv = ps[:].bitcast(BF16)
        else:
            pv = ps[:]
        if transpose:
            fns = [lambda j=j, H=H: nc.tensor.transpose(out=pv[:, j * 128:(j + 1) * 128], in_=lhs(H), identity=self.idb[:]) for j, H in enumerate(hs)]
        elif acc is None:
            fns = [lambda j=j, H=H: nc.tensor.matmul(pv[:, j * 128:(j + 1) * 128], lhsT=lhs(H), rhs=rhs(H), start=True, stop=True) for j, H in enumerate(hs)]
        else:
            al, ar = acc
            fns = []
            for j, H in enumerate(hs):
                fns.append(lambda j=j, H=H: nc.tensor.matmul(pv[:, j * 128:(j + 1) * 128], lhsT=lhs(H), rhs=rhs(H), start=True, stop=False))
                fns.append(lambda j=j, H=H: nc.tensor.matmul(pv[:, j * 128:(j + 1) * 128], lhsT=al(H), rhs=ar(H), start=False, stop=True))
        K.op('pe', fns, reads=list(lhs_bufs) + list(rhs_bufs), writes=[ps])
        return pv

    def gdn_pre(self, c, d, tm_, fm_, slot, hg, ng):
        nc, K, B = self.nc, self.K, self.Bt
        cst = self.cst
        nh = 8 // ng
        H0 = hg * nh
        hs = range(H0, H0 + nh)
        W = nh * 128
        gc = self.g_all[:, c, d * 8 + H0:d * 8 + H0 + nh]
        bc = self.beta_all[:, c, d * 8 + H0:d * 8 + H0 + nh]
        k3 = tm_[:, TM_K + H0 * 128:TM_K + (H0 + nh) * 128].rearrange("p (h q) -> p h q", q=128)
        v3 = tm_[:, TM_V:TM_V + 1024].rearrange("p (h q) -> p h q", q=128)
        kT = lambda H: fm_[:, FM_K // 128 + H, :]
        qT = lambda H: fm_[:, FM_Q // 128 + H, :]
        G = lambda buf: buf.g[hg]
        hd = lambda buf: (lambda H: buf[:, H, :])
        idH = lambda H: self.idb[:]
        sl = lambda buf: buf[:, H0:H0 + nh, :]
        fl = lambda buf: buf[:, H0:H0 + nh, :].rearrange("p h q -> p (h q)")
        bcl = lambda ap2: ap2.unsqueeze(2).broadcast_to([128, nh, 128])
        bcm = lambda ap2: ap2.unsqueeze(1).broadcast_to([128, nh, 128])
        gex, attnT, u_sb, wT, kendk = B.gex[slot], B.attnT[slot], B.u_sb[slot], B.wT[slot], B.kendk[slot]
        ps = self.next_ps()
        K.op('pe', [lambda: nc.tensor.matmul(ps[:, 0:nh], lhsT=B.inc, rhs=gc, start=True, stop=True),
                    lambda: nc.tensor.matmul(ps[:, 8:8 + nh], lhsT=B.exc, rhs=gc, start=True, stop=True),
                    lambda: nc.tensor.matmul(ps[:, 16:16 + nh], lhsT=B.ones, rhs=gc, start=True, stop=True)], reads=[cst, self.g_all], writes=[ps])
        K.op('act', lambda: nc.scalar.activation(out=gex[:, :, H0:H0 + nh], in_=ps[:, 0:24].rearrange("p (a h) -> p a h", a=3)[:, :, 0:nh], func=AF.Exp), reads=[ps], writes=[G(gex)])
        expG, kend = gex[:, 0, H0:H0 + nh], gex[:, 1, H0:H0 + nh]
        yield
        K.op(self.gpe, lambda: self.gpv.tensor_tensor(out=sl(B.rhsG), in0=bcm(B.inc), in1=bcl(gc), op=ALU.mult), reads=[cst, self.g_all], writes=[G(B.rhsG)])
        ps = self.next_ps()
        rg2 = fl(B.rhsG)
        nm4 = B.nm[:].unsqueeze(1).broadcast_to([128, 4, 128])
        fns = []
        for j in range(W // 512):
            fns.append(lambda j=j: nc.tensor.matmul(ps[:, j * 512:(j + 1) * 512], lhsT=B.exc, rhs=rg2[:, j * 512:(j + 1) * 512], start=True, stop=False))
            fns.append(lambda j=j: nc.tensor.matmul(ps[:, j * 512:(j + 1) * 512].rearrange("p (h q) -> p h q", h=4), lhsT=self.idb[:], rhs=nm4, start=False, stop=True))
        K.op('pe', fns, reads=[cst, G(B.rhsG), B.nm, self.idb], writes=[ps])
        K.op('act', lambda: nc.scalar.activation(out=fl(B.DT), in_=ps[:, 0:W], func=AF.Exp), reads=[ps], writes=[G(B.DT)])
        yield
        ps = self.next_ps()
        self.mm8(ps, kT, kT, [fm_], [], hs=hs)
        K.op('dve', lambda: nc.vector.tensor_tensor(out=fl(B.tA), in0=ps[:, 0:W], in1=fl(B.DT), op=ALU.mult), reads=[ps, G(B.DT)], writes=[G(B.tA)])
        K.op('dve', lambda: nc.vector.tensor_tensor(out=sl(B.tA), in0=sl(B.tA), in1=bcl(bc), op=ALU.mult), reads=[G(B.tA), self.beta_all], writes=[G(B.tA)])
        yield
        K.op('dve', lambda: nc.vector.tensor_tensor(out=sl(B.UD), in0=sl(B.tA), in1=bcm(B.bdS[:]), op=ALU.mult), reads=[G(B.tA), B.bdS], writes=[G(B.UD)])
        K.op('dve', lambda: nc.vector.tensor_tensor(out=sl(B.UO), in0=sl(B.tA), in1=bcm(B.nbd[:]), op=ALU.mult), reads=[G(B.tA), B.nbd], writes=[G(B.UO)])
        yield
        ps = self.next_ps()
        self.mm8(ps, kT, qT, [fm_], [], hs=hs)
        K.op('dve', lambda: nc.vector.tensor_tensor(out=fl(attnT), in0=ps[:, 0:W], in1=fl(B.DT), op=ALU.mult), reads=[ps, G(B.DT)], writes=[G(attnT)])
        yield
        ps = self.next_ps()
        pv = self.mm8(ps, hd(B.UD), None, [G(B.UD), self.idb], [], bf16_out=True, transpose=True, hs=hs)
        K.op('act', lambda: nc.scalar.copy(out=fl(B.AD), in_=pv[:, 0:W]), reads=[ps], writes=[G(B.AD)])
        yield
        X, XT = B.UD, B.AD
        P = B.P[0]
        K.op(self.gpe, lambda: self.gpv.tensor_tensor(out=sl(P), in0=bcm(self.idb[:]), in1=sl(X), op=ALU.subtract), reads=[self.idb, G(X)], writes=[G(P)])

        def square(k, X, XT):
            XTn = B.XT[k % 2]
            ps = self.next_ps()
            self.mm8(ps, hd(X), hd(XT), [G(X)], [G(XT)], hs=hs)
            K.op('act', lambda: nc.scalar.copy(out=fl(XTn), in_=ps[:, 0:W]), reads=[ps], writes=[G(XTn)])
            Xn = None
            if k < 4:
                Xn = B.X[k % 2]
                ps2 = self.next_ps()
                self.mm8(ps2, hd(XT), hd(X), [G(XT)], [G(X)], hs=hs)
                K.op('act', lambda: nc.scalar.copy(out=fl(Xn), in_=ps2[:, 0:W]), reads=[ps2], writes=[G(Xn)])
            return Xn, XTn

        Xn, XTn = square(1, X, XT)
        yield
        for k in range(1, 5):
            ps3 = self.next_ps()
            self.mm8(ps3, hd(XTn), hd(P), [G(XTn), self.idb], [G(P)], acc=(idH, hd(P)), hs=hs)
            Pn = B.P[k % 2]
            K.op('act', lambda ps3=ps3, Pn=Pn: nc.scalar.copy(out=fl(Pn), in_=ps3[:, 0:W]), reads=[ps3], writes=[G(Pn)])
            P = Pn
            if k < 4:
                yield
                Xn, XTn = square(k + 1, Xn, XTn)
            yield
        SD = P
        ps = self.next_ps()
        self.mm8(ps, hd(SD), idH, [G(SD)], [self.idb], hs=hs)
        K.op('act', lambda ps=ps: nc.scalar.copy(out=fl(B.TD), in_=ps[:, 0:W]), reads=[ps], writes=[G(B.TD)])
        yield
        ps = self.next_ps()
        self.mm8(ps, hd(B.UO), hd(B.TD), [G(B.UO)], [G(B.TD)], hs=hs)
        K.op('act', lambda ps=ps: nc.scalar.activation(out=fl(B.VT), in_=ps[:, 0:W], func=AF.Copy, scale=-1.0), reads=[ps], writes=[G(B.VT)])
        ps2 = self.next_ps()
        self.mm8(ps2, hd(B.TD), hd(B.UO), [G(B.TD)], [G(B.UO)], hs=hs)
        K.op('act', lambda ps2=ps2: nc.scalar.copy(out=fl(B.V), in_=ps2[:, 0:W]), reads=[ps2], writes=[G(B.V)])
        yield
        ps = self.next_ps()
        self.mm8(ps, hd(B.V), hd(B.VT), [G(B.V)], [G(B.VT)], hs=hs)
        K.op('act', lambda ps=ps: nc.scalar.activation(out=fl(B.V2T), in_=ps[:, 0:W], func=AF.Copy, scale=-1.0), reads=[ps], writes=[G(B.V2T)])
        ps2 = self.next_ps()
        self.mm8(ps2, hd(B.VT), hd(SD), [G(B.VT), self.idb], [G(SD)], acc=(idH, hd(SD)), hs=hs)
        K.op('act', lambda ps2=ps2: nc.scalar.copy(out=fl(B.Z), in_=ps2[:, 0:W]), reads=[ps2], writes=[G(B.Z)])
        yield
        ps = self.next_ps()
        self.mm8(ps, hd(B.V2T), hd(B.Z), [G(B.V2T), self.idb], [G(B.Z)], acc=(idH, hd(B.Z)), hs=hs)
        K.op('act', lambda ps=ps: nc.scalar.copy(out=fl(B.Sf), in_=ps[:, 0:W]), reads=[ps], writes=[G(B.Sf)])
        yield
        K.op(self.gpe, lambda: self.gpv.tensor_tensor(out=sl(B.kg), in0=k3, in1=bcl(expG), op=ALU.mult), reads=[tm_, G(gex)], writes=[G(B.kg)])
        K.op(self.gpe, lambda: self.gpv.tensor_tensor(out=sl(kendk), in0=k3, in1=bcl(kend), op=ALU.mult), reads=[tm_, G(gex)], writes=[G(kendk)])
        yield
        ps = self.next_ps()
        self.mm8(ps, hd(B.Sf), lambda H: v3[:, H, :], [G(B.Sf)], [tm_], hs=hs)
        K.op('act', lambda ps=ps: nc.scalar.copy(out=fl(u_sb), in_=ps[:, 0:W]), reads=[ps], writes=[G(u_sb)])
        ps2 = self.next_ps()
        self.mm8(ps2, hd(B.kg), hd(B.Sf), [G(B.kg)], [G(B.Sf)], hs=hs)
        K.op('act', lambda ps2=ps2: nc.scalar.copy(out=fl(wT), in_=ps2[:, 0:W]), reads=[ps2], writes=[G(wT)])
        yield

    def gdn_rec(self, c, d, fm_, need_out, slot):
        nc, K, B = self.nc, self.K, self.Bt
        bc = self.beta_all[:, c, d * 8:(d + 1) * 8]
        qT = lambda H: fm_[:, FM_Q // 128 + H, :]
        hd = lambda buf: (lambda H: buf[:, H, :])
        fl = lambda buf: buf[:].rearrange("p h q -> p (h q)")
        f3 = lambda ap: ap.rearrange("p (h q) -> p h q", q=128)
        gex_, attnT_, u_sb_, wT_, kendk_ = B.gex[slot], B.attnT[slot], B.u_sb[slot], B.wT[slot], B.kendk[slot]
        expG, cdG = gex_[:, 0, :], gex_[:, 2, :]

        class _Multi:
            def __init__(self, buf):
                self.buf = buf
            def __getitem__(self, k):
                return self.buf[k]
        gex, attnT, u_sb, wT, kendk = gex_, attnT_, u_sb_, wT_, kendk_
        GG = lambda buf: list(buf.g)
        ps = self.next_ps()
        self.mm8(ps, hd(wT), hd(B.Sgb), GG(wT), [B.Sgb])
        K.op('dve', lambda: nc.vector.tensor_tensor(out=fl(B.tB), in0=fl(u_sb), in1=ps[:], op=ALU.subtract), reads=GG(u_sb) + [ps], writes=[B.tB])
        K.op('dve', lambda: nc.vector.tensor_tensor(out=B.vnew[:], in0=B.tB[:], in1=bc.unsqueeze(2).broadcast_to([128, 8, 128]), op=ALU.mult), reads=[B.tB, self.beta_all], writes=[B.vnew])
        yield
        ps = self.next_ps()
        self.mm8(ps, qT, hd(B.Sgb), [fm_], [B.Sgb])
        K.op('dve', lambda: nc.vector.tensor_tensor(out=B.tB[:], in0=f3(ps[:]), in1=expG.unsqueeze(2).broadcast_to([128, 8, 128]), op=ALU.mult), reads=[ps] + GG(gex), writes=[B.tB])
        ps2 = self.next_ps()
        self.mm8(ps2, hd(attnT), hd(B.vnew), GG(attnT), [B.vnew])
        K.op('dve', lambda: nc.vector.tensor_tensor(out=fl(B.o), in0=fl(B.tB), in1=ps2[:], op=ALU.add), reads=[B.tB, ps2], writes=[B.o])
        if need_out:
            if d == 0:
                self.store('sp', B.o, self.YO[c * 128:(c + 1) * 128, 1024:2048], fl(B.o))
            else:
                K.op('dve', lambda: nc.vector.tensor_tensor(out=fl(B.ofin), in0=fl(B.o), in1=B.yo1[:, 1024:2048], op=ALU.add), reads=[B.o, B.yo1], writes=[B.ofin])
        yield
        ps = self.next_ps()
        self.mm8(ps, hd(kendk), hd(B.vnew), GG(kendk), [B.vnew])
        K.op(self.gpe, lambda: self.gpv.tensor_tensor(out=B.Sg[:], in0=B.Sg[:], in1=cdG.unsqueeze(2).broadcast_to([128, 8, 128]), op=ALU.mult), reads=[B.Sg] + GG(gex), writes=[B.Sg])
        K.op('dve', lambda: nc.vector.tensor_tensor(out=fl(B.Sg), in0=fl(B.Sg), in1=ps[:], op=ALU.add), reads=[B.Sg, ps], writes=[B.Sg])
        K.op('act', lambda: nc.scalar.copy(out=B.Sgb[:], in_=B.Sg[:]), reads=[B.Sg], writes=[B.Sgb])
        yield

    def epilogue(self, c):
        nc, K, B = self.nc, self.K, self.Bt
        K.op(self.gpe, lambda: self.gpv.tensor_tensor(out=B.yg[:], in0=B.yfin[:], in1=B.zs[:, 0:1024], op=ALU.mult), reads=[B.yfin, B.zs], writes=[B.yg])
        for g in range(2):
            K.op('act', lambda g=g: nc.scalar.activation(out=B.ejunk[:, 0:512], in_=B.yg[:, g * 512:(g + 1) * 512], func=AF.Square, accum_out=B.est[:, g:g + 1]), reads=[B.yg], writes=[B.ejunk, B.est])
        K.op('dve', lambda: nc.vector.tensor_tensor(out=B.wz[:], in0=B.ofin[:], in1=B.ofin[:], op=ALU.mult), reads=[B.ofin], writes=[B.wz])
        K.op('dve', lambda: nc.vector.tensor_reduce(out=B.est[:, 2:10], in_=B.wz[:], axis=AX.X, op=ALU.add), reads=[B.wz], writes=[B.est])
        K.op('act', lambda: nc.scalar.activation(out=B.est2[:, 0:2], in_=B.est[:, 0:2], func=AF.Ln, bias=self.epsb[:], scale=1.0 / 512), reads=[B.est, self.epsb], writes=[B.est2])
        K.op('act', lambda: nc.scalar.activation(out=B.est2[:, 2:10], in_=B.est[:, 2:10], func=AF.Ln, bias=self.epsb[:], scale=1.0 / 128), reads=[B.est, self.epsb], writes=[B.est2])
        K.op('act', lambda: nc.scalar.activation(out=B.est2[:, 0:10], in_=B.est2[:, 0:10], func=AF.Exp, scale=-0.5), reads=[B.est2], writes=[B.est2])
        for g in range(2):
            K.op('dve', lambda g=g: nc.vector.scalar_tensor_tensor(out=B.ym[:, g * 512:(g + 1) * 512], in0=B.yg[:, g * 512:(g + 1) * 512], scalar=B.est2[:, g:g + 1], in1=B.snw[:, g * 512:(g + 1) * 512], op0=ALU.mult, op1=ALU.mult),
                 reads=[B.yg, B.est2, B.snw], writes=[B.ym])
        K.op(self.gpe, lambda: self.gpv.tensor_tensor(out=B.wz2[:], in0=B.zs[:, 1024:2048].rearrange("p (h q) -> p h q", q=128), in1=B.gnw[:].unsqueeze(1).broadcast_to([128, 8, 128]), op=ALU.mult), reads=[B.zs, B.gnw], writes=[B.wz2])
        K.op('dve', lambda: nc.vector.tensor_tensor(out=B.ofin[:], in0=B.ofin[:], in1=B.est2[:, 2:10].unsqueeze(2).broadcast_to([128, 8, 128]), op=ALU.mult), reads=[B.ofin, B.est2], writes=[B.ofin])
        K.op('dve', lambda: nc.vector.tensor_tensor(out=B.ym[:, 1024:2048].rearrange("p (h q) -> p h q", q=128), in0=B.ofin[:], in1=B.wz2[:], op=ALU.mult), reads=[B.ofin, B.wz2], writes=[B.ym])
        self.store('sp', B.ym, self.YM[c * 128:(c + 1) * 128, :], B.ym[:])

    def phaseC(self, l, hsrc, last):
        nc, K = self.nc, self.K
        with ExitStack() as es:
            sb = lambda name, shape, dt=F32: self.sb(es, name, shape, dt)
            wo = sb("wo", [128, 16, D], BF16)
            for kc in range(16):
                self.load('pool', wo, wo[:, kc, :], self.w_out[l][kc * 128:(kc + 1) * 128, :])
            gp = [sb("gp%d" % r, [128, D]) for r in range(2)]
            pw = sb("pwb", [128, D])
            self.load('sp', pw, pw[:], dram_bcast(self.post_w[l], 128))
            for r in range(2):
                self.load('sp', gp[r], gp[r][:], dram_bcast(self.modv[r, 2 * D:3 * D], 128))
                K.op('dve', lambda r=r: nc.vector.tensor_tensor(out=gp[r][:], in0=gp[r][:], in1=pw[:], op=ALU.mult), reads=[gp[r], pw], writes=[gp[r]])
            ymc = [sb("ymc%d" % i, [128, D], BF16) for i in range(2)]
            hc = [sb("hc%d" % i, [128, D]) for i in range(2)]
            ymT = sb("ymT", [128, 16, 128], BF16)
            cj = sb("cjunk", [128, D], BF16)
            cst_ = sb("cst_", [128, 8])
            res = [sb("res%d" % i, [128, D]) for i in range(2)]
            chunks = list(range(2, NCH)) if last else list(range(NCH))
            ymTs = [ymT, sb("ymT2", [128, 16, 128], BF16)]
            csts = [cst_, sb("cst2_", [128, 8])]

            def issue(i):
                c = chunks[i]
                self.load('sp', ymc[i % 2], ymc[i % 2][:], self.YM[c * 128:(c + 1) * 128, :])
                self.load('sp', hc[i % 2], hc[i % 2][:], hsrc[c * 128:(c + 1) * 128, :])

            pos = [sb("po%d" % i, [128, D]) for i in range(2)]

            def c_iter(i, c):
                y_, h_, r_, yT, st_, po = ymc[i % 2], hc[i % 2], res[i % 2], ymTs[i % 2], csts[i % 2], pos[i % 2]
                r = 1 if c < 2 else 0
                for half in range(2):
                    ps = self.next_half()
                    pb = ps[:].bitcast(BF16)
                    K.op('pe', [lambda j=j, pb=pb, half=half: nc.tensor.transpose(out=pb[:, j * 128:(j + 1) * 128], in_=y_[:, (half * 8 + j) * 128:(half * 8 + j + 1) * 128], identity=self.idb[:]) for j in range(8)],
                         reads=[y_, self.idb], writes=[ps])
                    K.op('act', lambda pb=pb, half=half: nc.scalar.copy(out=yT[:, half * 8:(half + 1) * 8, :].rearrange("p k t -> p (k t)"), in_=pb[:, 0:1024]), reads=[ps], writes=[yT])
                    self.free_half(ps)
                yield
                for cb in range(4):
                    ps = self.next_half()
                    fns = [lambda ps=ps, cb=cb, kc=kc: nc.tensor.matmul(ps[:, 0:512], lhsT=yT[:, kc, :], rhs=wo[:, kc, cb * 512:(cb + 1) * 512], start=(kc == 0), stop=(kc == 15)) for kc in range(16)]
                    K.op('pe', fns, reads=[yT, wo], writes=[ps])
                    K.op('act', lambda ps=ps, cb=cb: nc.scalar.activation(out=cj[:, cb * 512:(cb + 1) * 512], in_=ps[:, 0:512], func=AF.Square, accum_out=st_[:, cb:cb + 1]), reads=[ps], writes=[cj, st_])
                    K.op('act', lambda ps=ps, cb=cb: nc.scalar.copy(out=po[:, cb * 512:(cb + 1) * 512], in_=ps[:, 0:512]), reads=[ps], writes=[po])
                    self.free_half(ps)
                    yield
                K.op('dve', lambda: nc.vector.tensor_reduce(out=st_[:, 4:5], in_=st_[:, 0:4], axis=AX.X, op=ALU.add), reads=[st_], writes=[st_])
                K.op('act', lambda: nc.scalar.activation(out=st_[:, 5:6], in_=st_[:, 4:5], func=AF.Ln, bias=self.epsb[:], scale=1.0 / D), reads=[st_, self.epsb], writes=[st_])
                K.op('act', lambda: nc.scalar.activation(out=st_[:, 6:7], in_=st_[:, 5:6], func=AF.Exp, scale=-0.5), reads=[st_], writes=[st_])
                yield
                K.op('dve', lambda: nc.vector.scalar_tensor_tensor(out=r_[:], in0=po[:], scalar=st_[:, 6:7], in1=gp[r][:], op0=ALU.mult, op1=ALU.mult), reads=[po, st_, gp[r]], writes=[r_])
                yield
                K.op('dve', lambda: nc.vector.tensor_tensor(out=r_[:], in0=r_[:], in1=h_[:], op=ALU.add), reads=[r_, h_], writes=[r_])
                if last:
                    dst = self.out[(c - 2) * 128:(c - 1) * 128, :]
                else:
                    dst = self.H[c * 128:(c + 1) * 128, :]
                self.store('pool', r_, dst, r_[:])
                if i + 2 < len(chunks):
                    issue(i + 2)

            def c_gens():
                for i, c in enumerate(chunks):
                    yield c_iter(i, c)

            issue(0)
            issue(1)
            run_pipeline(c_gens(), 2)


def make_consts():
    i = np.arange(128)
    c = np.zeros((128, NCONST), np.float32)
    c[:, C_ID:C_ID + 128] = np.eye(128)
    c[:, C_INC0:C_INC0 + 128] = (i[:, None] <= i[None, :])
    c[:, C_INC1:C_INC1 + 128] = (i[:, None] >= i[None, :])
    c[:, C_EXC0:C_EXC0 + 128] = (i[:, None] > i[None, :])
    c[:, C_EXC1:C_EXC1 + 128] = (i[:, None] < i[None, :])
    c[:, C_ONES:C_ONES + 128] = 1.0
    c[:, C_BD:C_BD + 128] = (i[:, None] // 32 == i[None, :] // 32)
    return c


def make_pv(inp):
    pv = np.zeros((DEPTH, 128, NPV), np.float32)
    for l in range(DEPTH):
        pv[l, :, 0:16] = inp['pre_norm_w'][l].reshape(16, 128).T
        cw = np.concatenate([inp['conv_ssd_w'][l], inp['conv_gdn_w'][l]], axis=1)
        pv[l, :, 16:196] = cw.reshape(5, 36, 128).transpose(2, 1, 0).reshape(128, 180)
        pv[l, :, 196:208] = inp['conv_ssd_b'][l].reshape(12, 128).T
    return pv


def make_in_maps(inp, cores):
    shared = {
        'pv': make_pv(inp), 'consts': make_consts(),
        'w_ada': np.ascontiguousarray(inp['w_ada']), 'b_ada': np.ascontiguousarray(inp['b_ada']),
        'post_norm_w': np.ascontiguousarray(inp['post_norm_w']), 'w_in': np.ascontiguousarray(inp['w_in']),
        'ssd_a_log': np.ascontiguousarray(inp['ssd_a_log']).reshape(DEPTH, 32),
        'ssd_dt_bias': np.ascontiguousarray(inp['ssd_dt_bias']).reshape(DEPTH, 32),
        'ssd_d': np.ascontiguousarray(inp['ssd_d']), 'ssd_norm_w': np.ascontiguousarray(inp['ssd_norm_w']),
        'gdn_a_log': np.ascontiguousarray(inp['gdn_a_log']).reshape(DEPTH, 16),
        'gdn_dt_bias': np.ascontiguousarray(inp['gdn_dt_bias']).reshape(DEPTH, 16),
        'gdn_norm_w': np.ascontiguousarray(inp['gdn_norm_w']), 'w_out': np.ascontiguousarray(inp['w_out']),
    }
    maps = []
    for b in cores:
        m = dict(shared)
        m['h0'] = np.ascontiguousarray(np.concatenate([inp['ctx'][b], inp['x'][b]], axis=0))
        c2 = np.stack([inp['c'][b], inp['c_ctx']])
        m['c2t'] = np.ascontiguousarray(c2.reshape(2, 16, 128).transpose(2, 1, 0).reshape(128, 32))
        maps.append(m)
    return maps


def kernel(**inputs):
    inp = {k: np.asarray(v) for k, v in inputs.items()}
    nc = Builder().build()
    maps = make_in_maps(inp, list(range(8)))
    res = run_bass_kernel_spmd(nc, maps, core_ids=list(range(8)))
    return np.stack([r['out'] for r in res.results], axis=0).astype(np.float32)
```

```python
import numpy as np
from contextlib import ExitStack
import concourse.bass as bass
import concourse.mybir as mybir
from concourse.bass_utils import run_bass_kernel_spmd

F32, BF16 = mybir.dt.float32, mybir.dt.bfloat16
AF = mybir.ActivationFunctionType
ALU = mybir.AluOpType
AX = mybir.AxisListType

D = 2048
T = 4352
NCH = 34
DEPTH = 4
IN_DIM = 6720
EPS = 1e-6
NPV = 208
C_ID, C_INC0, C_INC1, C_EXC0, C_EXC1, C_ONES, C_BD = 0, 128, 256, 384, 512, 640, 768
NCONST = 896
FM_B, FM_C, FM_Q, FM_K, FM_ROWS = 0, 256, 512, 1536, 2560
TM_X, TM_B, TM_K, TM_V, TM_COLS = 0, 1024, 1280, 2304, 3328


class Buf:
    __slots__ = ('name', 'w', 'r', 't', 'g')

    def __init__(self, name, t=None):
        self.name = name
        self.w = None
        self.r = {}
        self.t = t

    def __getitem__(self, k):
        return self.t[k]


class HalfBuf(Buf):
    __slots__ = ('off',)

    def __init__(self, name, t, off):
        Buf.__init__(self, name, t)
        self.off = off

    def __getitem__(self, k):
        if isinstance(k, tuple):
            p, c = k[0], k[1]
            start = (c.start or 0) + self.off
            stop = (c.stop if c.stop is not None else 512) + self.off
            return self.t[p, start:stop]
        return self.t[:, self.off:self.off + 512]


def run_pipeline(gens, depth):
    active = []
    it = iter(gens)
    done = False
    while True:
        if not done and len(active) < depth:
            try:
                active.append(next(it))
            except StopIteration:
                done = True
        if not active:
            if done:
                break
            continue
        for g in list(active):
            try:
                next(g)
            except StopIteration:
                active.remove(g)


class Ctx:
    NS = 8

    def __init__(self, nc):
        self.nc = nc
        self.eng = {'pe': nc.tensor, 'act': nc.scalar, 'dve': nc.vector, 'gp': nc.gpsimd, 'sp': nc.sync, 'pool': nc.gpsimd}
        self.inorder = ('pe', 'act', 'dve', 'gp')
        self.inorder_skip = ('pe',)
        self.dmaq = ('sp', 'pool')
        self.sem = {e: nc.alloc_semaphore('s_' + e) for e in self.inorder}
        self.cnt = {e: 0 for e in self.inorder}
        self.dsem = {q: [nc.alloc_semaphore('d_%s%d' % (q, i)) for i in range(self.NS)] for q in self.dmaq}
        self.dcnt = {q: 0 for q in self.dmaq}
        self.waited = {}
        self.nins = 0

    def _wait(self, e, tok):
        if tok is None:
            return
        sem, val, owner = tok
        if owner == e and e in self.inorder_skip:
            return
        key = (e, id(sem))
        if self.waited.get(key, 0) >= val:
            return
        self.eng[e].wait_ge(sem, val)
        self.waited[key] = val

    def op(self, e, fns, reads=(), writes=()):
        if not isinstance(fns, (list, tuple)):
            fns = [fns]
        for b in reads:
            self._wait(e, b.w)
        for b in writes:
            self._wait(e, b.w)
            for t in b.r.values():
                self._wait(e, t)
        self.nins += len(fns)
        if e in self.dmaq:
            n = self.dcnt[e]
            slot = n % self.NS
            rnd = n // self.NS
            sem = self.dsem[e][slot]
            if rnd > 0:
                self._wait(e, (sem, 16 * rnd, None))
            for f in fns[:-1]:
                f()
            fns[-1]().then_inc(sem, 16)
            tok = (sem, 16 * (rnd + 1), e)
            self.dcnt[e] = n + 1
        else:
            for f in fns[:-1]:
                f()
            fns[-1]().then_inc(self.sem[e], 1)
            self.cnt[e] += 1
            tok = (self.sem[e], self.cnt[e], e)
        for b in reads:
            old = b.r.get(id(tok[0]))
            if old is None or old[1] < tok[1]:
                b.r[id(tok[0])] = tok
        for b in writes:
            b.w = tok
            b.r = {}
        return tok

    def all_tokens(self):
        toks = [(self.sem[e], self.cnt[e], e) for e in self.inorder if self.cnt[e] > 0]
        for q in self.dmaq:
            n = self.dcnt[q]
            for slot in range(self.NS):
                uses = (n - slot + self.NS - 1) // self.NS if n > slot else 0
                if uses > 0:
                    toks.append((self.dsem[q][slot], 16 * uses, q))
        return toks

    def barrier(self):
        toks = self.all_tokens()
        for e in self.eng:
            for t in toks:
                if t[2] == e and e in self.inorder:
                    continue
                self._wait(e, t)


def dram_bcast(ap, nparts):
    n = 1
    for s in ap.shape:
        n *= s
    return bass.AP(ap.tensor, ap.offset, [[0, nparts], [1, n]])


def tok_blocks():
    blks = [(0, 256, 256)]
    for i in range(8):
        blks.append((256 + 512 * i, 512, 64))
    return blks


class Builder:
    def __init__(self, n_layers=DEPTH, debug=False, stop_after=None, force_last=False):
        self.force_last = force_last
        self.n_layers = n_layers
        self.debug = debug
        self.stop_after = stop_after
        nc = self.nc = bass.Bass("TRN2", target_bir_lowering=False)
        self.K = Ctx(nc)
        self.gpe = getattr(Builder, 'GPE', 'gp')
        self.gpv = nc.gpsimd if self.gpe == 'gp' else nc.vector
        di = lambda name, shape: nc.dram_tensor(name, shape, F32, kind="ExternalInput").ap()
        self.h0 = di("h0", [T, D])
        self.c2t = di("c2t", [128, 32])
        self.pv = di("pv", [DEPTH, 128, NPV])
        self.consts = di("consts", [128, NCONST])
        self.w_ada = di("w_ada", [DEPTH, D, 3 * D])
        self.b_ada = di("b_ada", [DEPTH, 3 * D])
        self.post_w = di("post_norm_w", [DEPTH, D])
        self.w_in = di("w_in", [DEPTH, D, IN_DIM])
        self.ssd_a_log = di("ssd_a_log", [DEPTH, 32])
        self.ssd_dt_bias = di("ssd_dt_bias", [DEPTH, 32])
        self.ssd_d = di("ssd_d", [DEPTH, 16])
        self.ssd_norm_w = di("ssd_norm_w", [DEPTH, 1024])
        self.gdn_a_log = di("gdn_a_log", [DEPTH, 16])
        self.gdn_dt_bias = di("gdn_dt_bias", [DEPTH, 16])
        self.gdn_norm_w = di("gdn_norm_w", [DEPTH, 128])
        self.w_out = di("w_out", [DEPTH, D, D])
        self.out = nc.dram_tensor("out", [4096, D], F32, kind="ExternalOutput").ap()
        self.ext_set = getattr(Builder, 'EXT_SET', ())
        ds = lambda name, shape, dt: nc.dram_tensor(name, shape, dt, kind=("ExternalOutput" if (debug or name in self.ext_set) else "Internal")).ap()
        self.modv = ds("modv", [2, 3 * D], F32)
        self.FM = ds("FM", [FM_ROWS, T], BF16)
        self.TM = ds("TM", [T, TM_COLS], BF16)
        self.ZS = ds("ZS", [T, D], BF16)
        self.SMALL = ds("SMALL", [T, 64], F32)
        self.YO = ds("YO", [T, D], F32)
        self.YM = ds("YM", [T, D], BF16)
        self.H = ds("H", [T, D], F32)
        if debug:
            self.UT = ds("UT", [D, T], BF16)

    def sb(self, es, name, shape, dt=F32):
        self.uid = getattr(self, 'uid', 0) + 1
        name = "%s_%d" % (name, self.uid)
        return Buf(name, es.enter_context(self.nc.sbuf_tensor(name, shape, dt)))

    def load(self, q, dst, dst_ap, src_ap, **kw):
        eng = self.K.eng[q]
        return self.K.op(q, lambda: eng.dma_start(out=dst_ap, in_=src_ap, **kw), writes=[dst])

    def store(self, q, src, dst_ap, src_ap, **kw):
        eng = self.K.eng[q]
        return self.K.op(q, lambda: eng.dma_start(out=dst_ap, in_=src_ap, **kw), reads=[src])

    def build(self):
        nc, K = self.nc, self.K
        with ExitStack() as es:
            self.ps = [Buf("ps%d" % i, es.enter_context(nc.psum_tensor("ps%d" % i, [128, 1024], F32))) for i in range(4)]
            self.psi = 0
            self.psh = [HalfBuf("psh%d" % i, self.ps[i // 2].t, (i % 2) * 512) for i in range(8)]
            self.psfree = list(self.psh)
            self.cst = self.sb(es, "cst", [128, NCONST])
            self.load('sp', self.cst, self.cst[:], self.consts)
            self.idb = self.sb(es, "idb", [128, 128], BF16)
            K.op('dve', lambda: nc.vector.tensor_copy(out=self.idb[:], in_=self.cst[:, C_ID:C_ID + 128]), reads=[self.cst], writes=[self.idb])
            self.onesb = self.sb(es, "onesb", [128, 128], BF16)
            K.op('dve', lambda: nc.vector.tensor_copy(out=self.onesb[:], in_=self.cst[:, C_ONES:C_ONES + 128]), reads=[self.cst], writes=[self.onesb])
            self.epsb = self.sb(es, "epsb", [128, 1])
            K.op('dve', lambda: nc.vector.memset(self.epsb[:], EPS), writes=[self.epsb])
            self.lnqs = self.sb(es, "lnqs", [128, 1])
            K.op('dve', lambda: nc.vector.memset(self.lnqs[:], float(np.log(128.0 ** -0.5))), writes=[self.lnqs])
            for l in range(self.n_layers):
                self.layer(l)
            K.barrier()
        return nc

    def next_ps(self):
        p = self.ps[self.psi % 4]
        self.psi += 1
        return p

    def next_half(self):
        assert self.psfree, "PSUM half-slots exhausted"
        return self.psfree.pop(0)

    def free_half(self, p):
        self.psfree.append(p)

    def layer(self, l):
        K = self.K
        last = (l == DEPTH - 1) or (self.force_last and l == self.n_layers - 1)
        hsrc = self.h0 if l == 0 else self.H
        with ExitStack() as es:
            self.pvt = self.sb(es, "pvt", [128, NPV])
            self.load('sp', self.pvt, self.pvt[:], self.pv[l])
            self.phase0(l)
            K.barrier()
            if self.stop_after == 'p0':
                return
            self.phaseA(l, hsrc)
            K.barrier()
            if self.stop_after == 'A':
                return
            self.phaseB(l, last)
            K.barrier()
            if self.stop_after == 'B':
                return
            self.phaseC(l, hsrc, last)
            K.barrier()

    def phase0(self, l):
        nc, K = self.nc, self.K
        with ExitStack() as es:
            sct = self.sb(es, "sct", [128, 32])
            self.load('sp', sct, sct[:], self.c2t)
            K.op('act', lambda: nc.scalar.activation(out=sct[:], in_=sct[:], func=AF.Silu), reads=[sct], writes=[sct])
            bia = self.sb(es, "bia", [2, 3 * D])
            self.load('sp', bia, bia[:], dram_bcast(self.b_ada[l], 2))
            modsb = self.sb(es, "modsb", [2, 3 * D])
            wts = [self.sb(es, "wada%d" % i, [128, 16, 512]) for i in range(2)]
            sct3 = sct[:].rearrange("p (k r) -> p k r", r=2)
            for cb in range(12):
                wt = wts[cb % 2]
                src = self.w_ada[l][:, cb * 512:(cb + 1) * 512].rearrange("(k p) f -> p k f", p=128)
                self.load('sp', wt, wt[:], src)
                ps = self.next_ps()
                fns = []
                for kc in range(16):
                    fns.append(lambda kc=kc, ps=ps, wt=wt: nc.tensor.matmul(ps[0:2, 0:512], lhsT=sct3[:, kc, :], rhs=wt[:, kc, :], start=(kc == 0), stop=(kc == 15)))
                K.op('pe', fns, reads=[sct, wt], writes=[ps])
                K.op('dve', lambda cb=cb, ps=ps: nc.vector.tensor_tensor(out=modsb[:, cb * 512:(cb + 1) * 512], in0=ps[0:2, 0:512], in1=bia[:, cb * 512:(cb + 1) * 512], op=ALU.add),
                     reads=[ps, bia], writes=[modsb])
            self.store('sp', modsb, self.modv, modsb[:])

    def phaseA(self, l, hsrc):
        nc, K = self.nc, self.K
        with ExitStack() as es:
            uT = self.sb(es, "uT", [128, 16, T], BF16)
            mraw = self.sb(es, "mraw", [128, 2, 2, 16])
            for r in range(2):
                for w in range(2):
                    src = bass.AP(self.modv.tensor, self.modv.offset + r * 3 * D + w * D, [[1, 128], [128, 16]])
                    self.load('sp', mraw, mraw[:, r, w, :], src, allow_slow_non_contiguous=True)
            mA = self.sb(es, "mA", [128, 2, 16])
            for r in range(2):
                K.op('dve', lambda r=r: nc.vector.scalar_tensor_tensor(out=mA[:, r, :], in0=mraw[:, r, 1, :], scalar=1.0, in1=self.pvt[:, 0:16], op0=ALU.add, op1=ALU.mult),
                     reads=[mraw, self.pvt], writes=[mA])
            with ExitStack() as es0:
                NH = 3
                hx = [self.sb(es0, "hx%d" % i, [128, D]) for i in range(NH)]
                junk = self.sb(es0, "junk", [128, D], BF16)
                xn = [self.sb(es0, "xn%d" % i, [128, D], BF16) for i in range(NH)]
                st = [self.sb(es0, "st%d" % i, [128, 4]) for i in range(NH)]

                def a0_load(c):
                    h_ = hx[c % NH]
                    self.load('sp', h_, h_[:], hsrc[c * 128:(c + 1) * 128, :])

                def a0_iter(c):
                    r = 1 if c < 2 else 0
                    h_, x_, s_ = hx[c % NH], xn[c % NH], st[c % NH]
                    K.op('act', lambda: nc.scalar.activation(out=junk[:], in_=h_[:], func=AF.Square, accum_out=s_[:, 0:1]), reads=[h_], writes=[junk, s_])
                    yield
                    K.op('act', lambda: nc.scalar.activation(out=s_[:, 1:2], in_=s_[:, 0:1], func=AF.Ln, bias=self.epsb[:], scale=1.0 / D), reads=[s_, self.epsb], writes=[s_])
                    K.op('act', lambda: nc.scalar.activation(out=s_[:, 2:3], in_=s_[:, 1:2], func=AF.Exp, scale=-0.5), reads=[s_], writes=[s_])
                    yield
                    K.op('dve', lambda: nc.vector.tensor_scalar(out=x_[:], in0=h_[:], scalar1=s_[:, 2:3], scalar2=None, op0=ALU.mult), reads=[h_, s_], writes=[x_])
                    if c + NH < NCH:
                        a0_load(c + NH)
                    yield
                    for half in range(2):
                        ps = self.next_half()
                        pb = ps[:].bitcast(BF16)
                        fns = [lambda j=j, pb=pb, half=half: nc.tensor.transpose(out=pb[:, j * 128:(j + 1) * 128], in_=x_[:, (half * 8 + j) * 128:(half * 8 + j + 1) * 128], identity=self.idb[:]) for j in range(8)]
                        K.op('pe', fns, reads=[x_, self.idb], writes=[ps])
                        for j in range(8):
                            kc = half * 8 + j
                            K.op('act', lambda j=j, kc=kc, pb=pb: nc.scalar.activation(out=uT[:, kc, c * 128:(c + 1) * 128], in_=pb[:, j * 128:(j + 1) * 128], func=AF.Identity,
                                                                                     bias=mraw[:, r, 0, kc:kc + 1], scale=mA[:, r, kc:kc + 1]),
                                 reads=[ps, mraw, mA], writes=[uT])
                        self.free_half(ps)
                        yield

                for c in range(NH):
                    a0_load(c)
                run_pipeline((a0_iter(c) for c in range(NCH)), 2)
                K.barrier()
            if self.debug:
                for kc in range(16):
                    self.store('pool', uT, self.UT[kc * 128:(kc + 1) * 128, :], uT[:, kc, :])
            with ExitStack() as es1:
                NB = 6
                wf = [self.sb(es1, "wf%d" % i, [128, 16, 128], BF16) for i in range(2)]
                acc = [self.sb(es1, "acc%d" % i, [128, 512]) for i in range(NB)]
                cv = [self.sb(es1, "cv%d" % i, [128, 512]) for i in range(NB)]
                ob = [self.sb(es1, "ob%d" % i, [128, 512], BF16) for i in range(NB)]
                sq = [self.sb(es1, "sq%d" % i, [128, 512], BF16) for i in range(NB)]
                lnv = [self.sb(es1, "lnv%d" % i, [128, 512]) for i in range(NB)]
                tmo = [self.sb(es1, "tmo%d" % i, [128, 4, 128], BF16) for i in range(NB)]
                blks = tok_blocks()

                def fm_iter(it, fc, w_, kind, fm_row, tm_col, t0, nt, rl):
                    a_, c_, o_, s_, l_, m_ = acc[it % NB], cv[it % NB], ob[it % NB], sq[it % NB], lnv[it % NB], tmo[it % NB]
                    ps = self.next_half()
                    fns = [lambda kc=kc: nc.tensor.matmul(ps[:, 0:nt], lhsT=w_[:, kc, :], rhs=uT[:, kc, t0:t0 + nt], start=(kc == 0), stop=(kc == 15)) for kc in range(16)]
                    K.op('pe', fns, reads=[w_, uT], writes=[ps])
                    yield
                    cw = lambda k: self.pvt[:, 16 + fc * 5 + k:16 + fc * 5 + k + 1]
                    if kind in ('q', 'k') or (it % 2 == 0):
                        K.op('dve', lambda: nc.vector.tensor_scalar(out=a_[:, 0:nt], in0=ps[:, 0:nt], scalar1=cw(2), scalar2=None, op0=ALU.mult), reads=[ps, self.pvt], writes=[a_])
                    else:
                        K.op('act', lambda: nc.scalar.activation(out=a_[:, 0:nt], in_=ps[:, 0:nt], func=AF.Copy, scale=cw(2)), reads=[ps, self.pvt], writes=[a_])
                    pv3 = ps[:, 0:nt].rearrange("p (r j) -> p r j", j=rl)
                    av3 = a_[:, 0:nt].rearrange("p (r j) -> p r j", j=rl)
                    for k in (0, 1, 3, 4):
                        sft = k - 2
                        j0, j1 = max(0, -sft), rl - max(0, sft)
                        K.op('dve', lambda k=k, sft=sft, j0=j0, j1=j1: nc.vector.scalar_tensor_tensor(out=av3[:, :, j0:j1], in0=pv3[:, :, j0 + sft:j1 + sft], scalar=cw(k), in1=av3[:, :, j0:j1], op0=ALU.mult, op1=ALU.add),
                             reads=[ps, a_, self.pvt], writes=[a_])
                        if k == 1:
                            yield
                    self.free_half(ps)
                    yield
                    if kind in ('q', 'k'):
                        K.op('act', lambda: nc.scalar.activation(out=c_[:, 0:nt], in_=a_[:, 0:nt], func=AF.Silu), reads=[a_], writes=[c_])
                        K.op('dve', lambda: nc.vector.tensor_tensor(out=s_[:, 0:nt], in0=c_[:, 0:nt], in1=c_[:, 0:nt], op=ALU.mult), reads=[c_], writes=[s_])
                        yield
                        ps2 = self.next_half()
                        K.op('pe', lambda: nc.tensor.matmul(ps2[:, 0:nt], lhsT=self.onesb[:], rhs=s_[:, 0:nt], start=True, stop=True), reads=[s_, self.onesb], writes=[ps2])
                        K.op('act', lambda: nc.scalar.activation(out=l_[:, 0:nt], in_=ps2[:, 0:nt], func=AF.Ln, bias=self.epsb[:], scale=1.0), reads=[ps2, self.epsb], writes=[l_])
                        self.free_half(ps2)
                        yield
                        if kind == 'q':
                            K.op('act', lambda: nc.scalar.activation(out=l_[:, 0:nt], in_=l_[:, 0:nt], func=AF.Exp, bias=self.lnqs[:], scale=-0.5), reads=[l_, self.lnqs], writes=[l_])
                        else:
                            K.op('act', lambda: nc.scalar.activation(out=l_[:, 0:nt], in_=l_[:, 0:nt], func=AF.Exp, scale=-0.5), reads=[l_], writes=[l_])
                        yield
                        K.op('dve', lambda: nc.vector.tensor_tensor(out=o_[:, 0:nt], in0=c_[:, 0:nt], in1=l_[:, 0:nt], op=ALU.mult), reads=[c_, l_], writes=[o_])
                    elif fc < 12:
                        K.op('act', lambda: nc.scalar.activation(out=o_[:, 0:nt], in_=a_[:, 0:nt], func=AF.Silu, bias=self.pvt[:, 196 + fc:197 + fc], scale=1.0), reads=[a_, self.pvt], writes=[o_])
                    else:
                        K.op('act', lambda: nc.scalar.activation(out=o_[:, 0:nt], in_=a_[:, 0:nt], func=AF.Silu), reads=[a_], writes=[o_])
                    yield
                    if fm_row is not None:
                        self.store('pool', o_, self.FM[fm_row:fm_row + 128, t0:t0 + nt], o_[:, 0:nt])
                    if tm_col is not None:
                        nj = nt // 128
                        ps3 = self.next_half()
                        pb = ps3[:].bitcast(BF16)
                        fns = [lambda j=j: nc.tensor.transpose(out=pb[:, j * 128:(j + 1) * 128], in_=o_[:, j * 128:(j + 1) * 128], identity=self.idb[:]) for j in range(nj)]
                        K.op('pe', fns, reads=[o_, self.idb], writes=[ps3])
                        K.op('act', lambda: nc.scalar.copy(out=m_[:, 0:nj, :], in_=pb[:, 0:nj * 128].rearrange("p (j f) -> p j f", f=128)), reads=[ps3], writes=[m_])
                        self.free_half(ps3)
                        dst = self.TM[t0:t0 + nt, tm_col:tm_col + 128].rearrange("(j p) f -> p j f", p=128)
                        self.store('pool', m_, dst, m_[:, 0:nj, :])

                def fm_gens():
                    it = 0
                    for fc in range(36):
                        col0 = (2048 + fc * 128) if fc < 12 else (3616 + (fc - 12) * 128)
                        w_ = wf[fc % 2]
                        src = self.w_in[l][:, col0:col0 + 128].rearrange("(k p) f -> p k f", p=128)
                        self.load('pool', w_, w_[:], src)
                        if fc < 8:
                            kind, fm_row, tm_col = 'x', None, TM_X + fc * 128
                        elif fc < 10:
                            kind, fm_row, tm_col = 'B', FM_B + (fc - 8) * 128, TM_B + (fc - 8) * 128
                        elif fc < 12:
                            kind, fm_row, tm_col = 'C', FM_C + (fc - 10) * 128, None
                        elif fc < 20:
                            kind, fm_row, tm_col = 'q', FM_Q + (fc - 12) * 128, None
                        elif fc < 28:
                            kind, fm_row, tm_col = 'k', FM_K + (fc - 20) * 128, TM_K + (fc - 20) * 128
                        else:
                            kind, fm_row, tm_col = 'v', None, TM_V + (fc - 28) * 128
                        for (t0, nt, rl) in blks:
                            yield fm_iter(it, fc, w_, kind, fm_row, tm_col, t0, nt, rl)
                            it += 1

                run_pipeline(fm_gens(), 6)
                K.barrier()
            with ExitStack() as es2:
                NZ = 6
                wz = [self.sb(es2, "wz%d" % i, [128, 16, 256], BF16) for i in range(2)]
                zo = [self.sb(es2, "zo%d" % i, [128, 256], BF16) for i in range(NZ)]
                so = [self.sb(es2, "so%d" % i, [128, 64]) for i in range(NZ)]

                def z_iter(it, cb, c, w_, ncol):
                    ps = self.next_half()
                    fns = [lambda kc=kc: nc.tensor.matmul(ps[:, 0:ncol], lhsT=uT[:, kc, c * 128:(c + 1) * 128], rhs=w_[:, kc, 0:ncol], start=(kc == 0), stop=(kc == 15)) for kc in range(16)]
                    K.op('pe', fns, reads=[w_, uT], writes=[ps])
                    yield
                    if cb < 8:
                        z_ = zo[it % NZ]
                        K.op('act', lambda: nc.scalar.activation(out=z_[:], in_=ps[:, 0:256], func=AF.Silu), reads=[ps], writes=[z_])
                        self.free_half(ps)
                        self.store('sp', z_, self.ZS[c * 128:(c + 1) * 128, cb * 256:(cb + 1) * 256], z_[:])
                    else:
                        s_ = so[it % NZ]
                        K.op('act', lambda: nc.scalar.copy(out=s_[:], in_=ps[:, 0:64]), reads=[ps], writes=[s_])
                        self.free_half(ps)
                        self.store('sp', s_, self.SMALL[c * 128:(c + 1) * 128, :], s_[:])

                def z_gens():
                    it = 0
                    for cb in range(9):
                        w_ = wz[cb % 2]
                        if cb < 8:
                            src = self.w_in[l][:, cb * 256:(cb + 1) * 256].rearrange("(k p) f -> p k f", p=128)
                            self.load('pool', w_, w_[:], src)
                            ncol = 256
                        else:
                            self.load('pool', w_, w_[:, :, 0:32], self.w_in[l][:, 3584:3616].rearrange("(k p) f -> p k f", p=128))
                            self.load('pool', w_, w_[:, :, 32:64], self.w_in[l][:, 6688:6720].rearrange("(k p) f -> p k f", p=128))
                            ncol = 64
                        for c in range(NCH):
                            yield z_iter(it, cb, c, w_, ncol)
                            it += 1

                run_pipeline(z_gens(), 5)

    def phaseS(self, l, es):
        nc, K = self.nc, self.K
        sb = lambda name, shape, dt=F32: self.sb(es, name, shape, dt)
        self.dt_all = sb("dt_all", [128, NCH, 32])
        self.loga_all = sb("loga_all", [128, NCH, 32])
        self.beta_all = sb("beta_all", [128, NCH, 16])
        self.g_all = sb("g_all", [128, NCH, 16])
        self.dsk = sb("dsk", [128, 16])
        self.onecol = sb("onecol", [128, 1])
        K.op('dve', lambda: nc.vector.memset(self.onecol[:], 1.0), writes=[self.onecol])
        self.load('sp', self.dsk, self.dsk[:], dram_bcast(self.ssd_d[l], 128))
        with ExitStack() as e2:
            sm = self.sb(e2, "sm", [128, NCH, 64])
            self.load('sp', sm, sm[:], self.SMALL.rearrange("(c p) f -> p c f", p=128))
            bias = self.sb(e2, "sbias", [128, 48])
            alog = self.sb(e2, "salog", [128, 48])
            self.load('sp', bias, bias[:, 0:32], dram_bcast(self.ssd_dt_bias[l], 128))
            self.load('sp', bias, bias[:, 32:48], dram_bcast(self.gdn_dt_bias[l], 128))
            self.load('sp', alog, alog[:, 0:32], dram_bcast(self.ssd_a_log[l], 128))
            self.load('sp', alog, alog[:, 32:48], dram_bcast(self.gdn_a_log[l], 128))
            K.op('act', lambda: nc.scalar.activation(out=alog[:], in_=alog[:], func=AF.Exp), reads=[alog], writes=[alog])
            K.op('dve', lambda: nc.vector.tensor_scalar(out=alog[:], in0=alog[:], scalar1=-1.0, scalar2=None, op0=ALU.mult), reads=[alog], writes=[alog])
            tmp = self.sb(e2, "stmp", [128, NCH, 48])
            K.op('dve', lambda: nc.vector.tensor_tensor(out=tmp[:, :, 0:32], in0=sm[:, :, 0:32], in1=bias[:, 0:32].unsqueeze(1).broadcast_to([128, NCH, 32]), op=ALU.add), reads=[sm, bias], writes=[tmp])
            K.op('dve', lambda: nc.vector.tensor_tensor(out=tmp[:, :, 32:48], in0=sm[:, :, 48:64], in1=bias[:, 32:48].unsqueeze(1).broadcast_to([128, NCH, 16]), op=ALU.add), reads=[sm, bias], writes=[tmp])
            K.op('act', lambda: nc.scalar.activation(out=tmp[:], in_=tmp[:], func=AF.Exp), reads=[tmp], writes=[tmp])
            K.op('act', lambda: nc.scalar.activation(out=tmp[:], in_=tmp[:], func=AF.Ln, bias=self.onecol[:], scale=1.0), reads=[tmp, self.onecol], writes=[tmp])
            K.op('dve', lambda: nc.vector.tensor_copy(out=self.dt_all[:], in_=tmp[:, :, 0:32]), reads=[tmp], writes=[self.dt_all])
            K.op('dve', lambda: nc.vector.tensor_tensor(out=self.loga_all[:], in0=tmp[:, :, 0:32], in1=alog[:, 0:32].unsqueeze(1).broadcast_to([128, NCH, 32]), op=ALU.mult), reads=[tmp, alog], writes=[self.loga_all])
            K.op('dve', lambda: nc.vector.tensor_tensor(out=self.g_all[:], in0=tmp[:, :, 32:48], in1=alog[:, 32:48].unsqueeze(1).broadcast_to([128, NCH, 16]), op=ALU.mult), reads=[tmp, alog], writes=[self.g_all])
            K.op('act', lambda: nc.scalar.activation(out=self.beta_all[:], in_=sm[:, :, 32:48], func=AF.Sigmoid), reads=[sm], writes=[self.beta_all])
            K.barrier()

    def phaseB(self, l, last):
        nc, K = self.nc, self.K
        with ExitStack() as es:
            self.phaseS(l, es)
            sb = lambda name, shape, dt=F32: self.sb(es, name, shape, dt)
            B = self.Bt = type('T', (), {})()
            B.tm = [sb("tm%d" % i, [128, TM_COLS], BF16) for i in range(3)]
            B.fm = [sb("fm%d" % i, [128, 20, 128], BF16) for i in range(3)]
            B.yo1 = sb("yo1", [128, D])
            B.zs = sb("zs", [128, D], BF16)
            B.ex = sb("ex", [128, 48]); B.rhsL = sb("rhsL", [128, 8, 128], BF16); B.excb = sb("excb", [128, 128], BF16); B.E = sb("E", [128, 16, 128], BF16)
            B.cbm = sb("cbm", [128, 2, 128], BF16); B.MT = sb("MT", [128, 16, 128], BF16)
            B.xdt = sb("xdt", [128, 16, 64], BF16); B.xdd = sb("xdd", [128, 16, 64], BF16)
            B.ytmp = sb("ytmp", [128, 1024]); B.ysum = sb("ysum", [128, 1024]); B.yfin = sb("yfin", [128, 1024])
            B.hT = sb("hT", [128, 16, 64]); B.hTb = sb("hTb", [128, 16, 64], BF16)
            B.gex = [sb("gex%d" % i, [128, 3, 8]) for i in range(2)]; B.rhsG = sb("rhsG", [128, 8, 128], BF16); B.DT = sb("DT", [128, 8, 128])
            B.tA = sb("tA", [128, 8, 128]); B.tB = sb("tB", [128, 8, 128])
            B.nm = sb("nm", [128, 128], BF16); B.bdS = sb("bdS", [128, 128])
            g16 = lambda name: sb(name, [128, 8, 128], BF16)
            B.UD = g16("UD"); B.UO = g16("UO"); B.attnT = [g16("attnT0"), g16("attnT1")]; B.AD = g16("AD")
            B.P = [g16("P0"), g16("P1")]; B.X = [g16("X0"), g16("X1")]; B.XT = [g16("XT0"), g16("XT1")]
            B.TD = g16("TD"); B.VT = g16("VT"); B.V = g16("V"); B.V2T = g16("V2T"); B.Z = g16("Z"); B.Sf = g16("Sf")
            B.kg = g16("kg"); B.kendk = [g16("kendk0"), g16("kendk1")]; B.wT = [g16("wT0"), g16("wT1")]; B.vnew = g16("vnew")
            B.u_sb = [sb("u_sb%d" % i, [128, 8, 128]) for i in range(2)]; B.o = sb("o", [128, 8, 128]); B.ofin = sb("ofin", [128, 8, 128])
            B.Sg = sb("Sg", [128, 8, 128]); B.Sgb = g16("Sgb")
            NG = self.NG = 1
            self.PREW = getattr(Builder, "PREW", 20)
            for t_ in [B.rhsG, B.DT, B.tA, B.UD, B.UO, B.AD, B.TD, B.VT, B.V, B.V2T, B.Z, B.Sf, B.kg] + B.P + B.X + B.XT + B.gex + B.attnT + B.u_sb + B.wT + B.kendk:
                t_.g = [Buf(t_.name + "_g%d" % j, t_.t) for j in range(NG)]
            B.mT = sb("mT", [128, 128]); B.nbd = sb("nbd", [128, 128])
            K.op('dve', lambda: nc.vector.tensor_scalar(out=B.nbd[:], in0=self.cst[:, C_BD:C_BD + 128], scalar1=-1.0, scalar2=1.0, op0=ALU.mult, op1=ALU.add), reads=[self.cst], writes=[B.nbd])
            B.snw = sb("snw", [128, 1024]); B.gnw = sb("gnw", [128, 128])
            self.load('sp', B.snw, B.snw[:], dram_bcast(self.ssd_norm_w[l], 128))
            self.load('sp', B.gnw, B.gnw[:], dram_bcast(self.gdn_norm_w[l], 128))
            B.yg = sb("yg", [128, 1024]); B.ejunk = sb("ejunk", [128, 1024], BF16); B.est = sb("est", [128, 16]); B.est2 = sb("est2", [128, 16])
            B.ym = sb("ym", [128, D], BF16); B.wz = sb("wzg", [128, 8, 128]); B.wz2 = sb("wzg2", [128, 8, 128], BF16)
            for d in range(2):
                self.scan_pass(l, d, last)
                K.barrier()

    def scan_pass(self, l, d, last):
        nc, K, B = self.nc, self.K, self.Bt
        order = list(range(NCH)) if d == 0 else [1, 0] + list(range(NCH - 1, 1, -1))
        B.inc = self.cst[:, (C_INC0 if d == 0 else C_INC1):(C_INC0 if d == 0 else C_INC1) + 128]
        B.exc = self.cst[:, (C_EXC0 if d == 0 else C_EXC1):(C_EXC0 if d == 0 else C_EXC1) + 128]
        B.strictT = self.cst[:, (C_EXC1 if d == 0 else C_EXC0):(C_EXC1 if d == 0 else C_EXC0) + 128]
        B.ones = self.cst[:, C_ONES:C_ONES + 128]
        B.bd = self.cst[:, C_BD:C_BD + 128]
        K.op('dve', lambda: nc.vector.memset(B.hT[:], 0.0), writes=[B.hT])
        K.op('dve', lambda: nc.vector.memset(B.hTb[:], 0.0), writes=[B.hTb])
        K.op('dve', lambda: nc.vector.memset(B.Sg[:], 0.0), writes=[B.Sg])
        K.op('dve', lambda: nc.vector.memset(B.Sgb[:], 0.0), writes=[B.Sgb])

        K.op('dve', lambda: nc.vector.tensor_scalar(out=B.nm[:], in0=B.inc, scalar1=30000.0, scalar2=-30000.0, op0=ALU.mult, op1=ALU.add), reads=[self.cst], writes=[B.nm])
        K.op('dve', lambda: nc.vector.tensor_tensor(out=B.bdS[:], in0=B.strictT, in1=B.bd, op=ALU.mult), reads=[self.cst], writes=[B.bdS])
        K.op('dve', lambda: nc.vector.tensor_copy(out=B.excb[:], in_=B.exc), reads=[self.cst], writes=[B.excb])

        def issue_loads(i):
            c = order[i]
            tm_, fm_ = B.tm[i % 3], B.fm[i % 3]
            self.load('sp', tm_, tm_[:], self.TM[c * 128:(c + 1) * 128, :])
            self.load('sp', fm_, fm_[:], self.FM[:, c * 128:(c + 1) * 128].rearrange("(f p) t -> p f t", p=128))

        def run_weighted(gens):
            st = [[g, n, 0] for g, n in gens]
            while st:
                st.sort(key=lambda x: x[2] / float(x[1]))
                g = st[0]
                try:
                    next(g[0])
                    g[2] += 1
                except StopIteration:
                    st.remove(g)

        n = len(order)
        issue_loads(0)
        if n > 1:
            issue_loads(1)
        run_weighted([(self.gdn_pre(order[0], d, B.tm[0], B.fm[0], 0, hg, self.NG), 20) for hg in range(self.NG)])
        for i, c in enumerate(order):
            if i + 2 < n:
                issue_loads(i + 2)
            tm_, fm_ = B.tm[i % 3], B.fm[i % 3]
            need_out = not (last and c < 2)
            if d == 1 and need_out:
                self.load('sp', B.yo1, B.yo1[:], self.YO[c * 128:(c + 1) * 128, :])
                self.load('sp', B.zs, B.zs[:], self.ZS[c * 128:(c + 1) * 128, :])
            gens = [(self.ssd_gen(c, d, tm_, fm_, need_out), 7), (self.gdn_rec(c, d, fm_, need_out, i % 2), 3)]
            if i + 1 < n:
                for hg in range(self.NG):
                    gens.append((self.gdn_pre(order[i + 1], d, B.tm[(i + 1) % 3], B.fm[(i + 1) % 3], (i + 1) % 2, hg, self.NG), self.PREW))
            run_weighted(gens)
            if d == 1 and need_out:
                self.epilogue(c)

    def ssd_gen(self, c, d, tm_, fm_, need_out):
        nc, K, B = self.nc, self.K, self.Bt
        cst = self.cst
        loga = self.loga_all[:, c, d * 16:(d + 1) * 16]
        dtc = self.dt_all[:, c, d * 16:(d + 1) * 16]
        xs3 = tm_[:, TM_X:TM_X + 1024].rearrange("p (h q) -> p h q", q=64)
        ps = self.next_ps()
        K.op('pe', [lambda: nc.tensor.matmul(ps[:, 0:16], lhsT=B.inc, rhs=loga, start=True, stop=True),
                    lambda: nc.tensor.matmul(ps[:, 16:32], lhsT=B.exc, rhs=loga, start=True, stop=True),
                    lambda: nc.tensor.matmul(ps[:, 32:48], lhsT=B.ones, rhs=loga, start=True, stop=True)], reads=[cst, self.loga_all], writes=[ps])
        K.op('act', lambda: nc.scalar.activation(out=B.ex[:], in_=ps[:, 0:48], func=AF.Exp), reads=[ps], writes=[B.ex])
        expA, dend, cd = B.ex[:, 0:16], B.ex[:, 16:32], B.ex[:, 32:48]
        yield
        for hh in range(2):
            K.op(self.gpe, lambda hh=hh: self.gpv.tensor_tensor(out=B.rhsL[:], in0=B.inc.unsqueeze(1).broadcast_to([128, 8, 128]),
                                                             in1=loga[:, hh * 8:(hh + 1) * 8].unsqueeze(2).broadcast_to([128, 8, 128]), op=ALU.mult),
                 reads=[cst, self.loga_all], writes=[B.rhsL])
            ps = self.next_ps()
            rl2 = B.rhsL[:].rearrange("p h q -> p (h q)")
            nm4 = B.nm[:].unsqueeze(1).broadcast_to([128, 4, 128])
            fns = []
            for j in range(2):
                fns.append(lambda ps=ps, rl2=rl2, j=j: nc.tensor.matmul(ps[:, j * 512:(j + 1) * 512], lhsT=B.excb[:], rhs=rl2[:, j * 512:(j + 1) * 512], start=True, stop=False))
                fns.append(lambda ps=ps, j=j: nc.tensor.matmul(ps[:, j * 512:(j + 1) * 512].rearrange("p (h q) -> p h q", h=4), lhsT=self.idb[:], rhs=nm4, start=False, stop=True))
            K.op('pe', fns, reads=[B.excb, B.rhsL, B.nm, self.idb], writes=[ps])
            K.op('act', lambda ps=ps, hh=hh: nc.scalar.activation(out=B.E[:, hh * 8:(hh + 1) * 8, :].rearrange("p h q -> p (h q)"), in_=ps[:], func=AF.Exp), reads=[ps], writes=[B.E])
            yield
        ps = self.next_ps()
        K.op('pe', [lambda ps=ps, g=g: nc.tensor.matmul(ps[:, g * 128:(g + 1) * 128], lhsT=fm_[:, FM_B // 128 + g, :], rhs=fm_[:, FM_C // 128 + g, :], start=True, stop=True) for g in range(2)],
             reads=[fm_], writes=[ps])
        K.op('act', lambda ps=ps: nc.scalar.copy(out=B.cbm[:].rearrange("p g q -> p (g q)"), in_=ps[:, 0:256]), reads=[ps], writes=[B.cbm])
        yield
        K.op('dve', lambda: nc.vector.tensor_tensor(out=B.MT[:].rearrange("p (g r) q -> p g r q", g=2), in0=B.E[:].rearrange("p (g r) q -> p g r q", g=2),
                                                    in1=B.cbm[:].unsqueeze(2).broadcast_to([128, 2, 8, 128]), op=ALU.mult), reads=[B.E, B.cbm], writes=[B.MT])
        K.op(self.gpe, lambda: self.gpv.tensor_tensor(out=B.xdt[:], in0=xs3, in1=dtc.unsqueeze(2).broadcast_to([128, 16, 64]), op=ALU.mult), reads=[tm_, self.dt_all], writes=[B.xdt])
        K.op(self.gpe, lambda: self.gpv.tensor_tensor(out=B.xdd[:], in0=B.xdt[:], in1=dend.unsqueeze(2).broadcast_to([128, 16, 64]), op=ALU.mult), reads=[B.xdt, B.ex], writes=[B.xdd])
        yield
        psd = self.next_ps()
        K.op('pe', [lambda h=h: nc.tensor.matmul(psd[:, h * 64:(h + 1) * 64], lhsT=B.MT[:, h, :], rhs=B.xdt[:, h, :], start=True, stop=True) for h in range(16)], reads=[B.MT, B.xdt], writes=[psd])
        pso = self.next_ps()
        K.op('pe', [lambda g=g: nc.tensor.matmul(pso[:, g * 512:(g + 1) * 512], lhsT=fm_[:, FM_C // 128 + g, :], rhs=B.hTb[:, g * 8:(g + 1) * 8, :].rearrange("p h q -> p (h q)"), start=True, stop=True) for g in range(2)],
             reads=[fm_, B.hTb], writes=[pso])
        K.op('dve', lambda: nc.vector.tensor_tensor(out=B.ytmp[:].rearrange("p (h q) -> p h q", q=64), in0=pso[:].rearrange("p (h q) -> p h q", q=64), in1=expA.unsqueeze(2).broadcast_to([128, 16, 64]), op=ALU.mult),
             reads=[pso, B.ex], writes=[B.ytmp])
        K.op('dve', lambda: nc.vector.tensor_tensor(out=B.ysum[:], in0=B.ytmp[:], in1=psd[:], op=ALU.add), reads=[B.ytmp, psd], writes=[B.ysum])
        if need_out:
            if d == 0:
                K.op(self.gpe, lambda: self.gpv.tensor_tensor(out=B.ytmp[:].rearrange("p (h q) -> p h q", q=64), in0=xs3, in1=self.dsk[:].unsqueeze(2).broadcast_to([128, 16, 64]), op=ALU.mult),
                     reads=[tm_, self.dsk], writes=[B.ytmp])
                K.op('dve', lambda: nc.vector.tensor_tensor(out=B.yfin[:], in0=B.ytmp[:], in1=B.ysum[:], op=ALU.add), reads=[B.ytmp, B.ysum], writes=[B.yfin])
                self.store('sp', B.yfin, self.YO[c * 128:(c + 1) * 128, 0:1024], B.yfin[:])
            else:
                K.op('dve', lambda: nc.vector.tensor_tensor(out=B.yfin[:], in0=B.yo1[:, 0:1024], in1=B.ysum[:], op=ALU.add), reads=[B.yo1, B.ysum], writes=[B.yfin])
        yield
        pss = self.next_ps()
        K.op('pe', [lambda g=g: nc.tensor.matmul(pss[:, g * 512:(g + 1) * 512], lhsT=tm_[:, TM_B + g * 128:TM_B + (g + 1) * 128], rhs=B.xdd[:, g * 8:(g + 1) * 8, :].rearrange("p h q -> p (h q)"), start=True, stop=True) for g in range(2)],
             reads=[tm_, B.xdd], writes=[pss])
        K.op(self.gpe, lambda: self.gpv.tensor_tensor(out=B.hT[:], in0=B.hT[:], in1=cd.unsqueeze(2).broadcast_to([128, 16, 64]), op=ALU.mult), reads=[B.hT, B.ex], writes=[B.hT])
        K.op('dve', lambda: nc.vector.tensor_tensor(out=B.hT[:].rearrange("p h q -> p (h q)"), in0=B.hT[:].rearrange("p h q -> p (h q)"), in1=pss[:], op=ALU.add), reads=[B.hT, pss], writes=[B.hT])
        K.op('act', lambda: nc.scalar.copy(out=B.hTb[:], in_=B.hT[:]), reads=[B.hT], writes=[B.hTb])
        yield

    def mm8(self, ps, lhs, rhs, lhs_bufs, rhs_bufs, bf16_out=False, transpose=False, acc=None, hs=None):
        nc, K = self.nc, self.K
        if hs is None:
            hs = range(8)
        if bf16_out:
            pv = ps[:].bitcast(BF16)
        else:
            pv = ps[:]
        if transpose:
            fns = [lambda j=j, H=H: nc.tensor.transpose(out=pv[:, j * 128:(j + 1) * 128], in_=lhs(H), identity=self.idb[:]) for j, H in enumerate(hs)]
        elif acc is None:
            fns = [lambda j=j, H=H: nc.tensor.matmul(pv[:, j * 128:(j + 1) * 128], lhsT=lhs(H), rhs=rhs(H), start=True, stop=True) for j, H in enumerate(hs)]
        else:
            al, ar = acc
            fns = []
            for j, H in enumerate(hs):
                fns.append(lambda j=j, H=H: nc.tensor.matmul(pv[:, j * 128:(j + 1) * 128], lhsT=lhs(H), rhs=rhs(H), start=True, stop=False))
                fns.append(lambda j=j, H=H: nc.tensor.matmul(pv[:, j * 128:(j + 1) * 128], lhsT=al(H), rhs=ar(H), start=False, stop=True))
        K.op('pe', fns, reads=list(lhs_bufs) + list(rhs_bufs), writes=[ps])
        return pv

    def gdn_pre(self, c, d, tm_, fm_, slot, hg, ng):
        nc, K, B = self.nc, self.K, self.Bt
        cst = self.cst
        nh = 8 // ng
        H0 = hg * nh
        hs = range(H0, H0 + nh)
        W = nh * 128
        gc = self.g_all[:, c, d * 8 + H0:d * 8 + H0 + nh]
        bc = self.beta_all[:, c, d * 8 + H0:d * 8 + H0 + nh]
        k3 = tm_[:, TM_K + H0 * 128:TM_K + (H0 + nh) * 128].rearrange("p (h q) -> p h q", q=128)
        v3 = tm_[:, TM_V:TM_V + 1024].rearrange("p (h q) -> p h q", q=128)
        kT = lambda H: fm_[:, FM_K // 128 + H, :]
        qT = lambda H: fm_[:, FM_Q // 128 + H, :]
        G = lambda buf: buf.g[hg]
        hd = lambda buf: (lambda H: buf[:, H, :])
        idH = lambda H: self.idb[:]
        sl = lambda buf: buf[:, H0:H0 + nh, :]
        fl = lambda buf: buf[:, H0:H0 + nh, :].rearrange("p h q -> p (h q)")
        bcl = lambda ap2: ap2.unsqueeze(2).broadcast_to([128, nh, 128])
        bcm = lambda ap2: ap2.unsqueeze(1).broadcast_to([128, nh, 128])
        gex, attnT, u_sb, wT, kendk = B.gex[slot], B.attnT[slot], B.u_sb[slot], B.wT[slot], B.kendk[slot]
        ps = self.next_ps()
        K.op('pe', [lambda: nc.tensor.matmul(ps[:, 0:nh], lhsT=B.inc, rhs=gc, start=True, stop=True),
                    lambda: nc.tensor.matmul(ps[:, 8:8 + nh], lhsT=B.exc, rhs=gc, start=True, stop=True),
                    lambda: nc.tensor.matmul(ps[:, 16:16 + nh], lhsT=B.ones, rhs=gc, start=True, stop=True)], reads=[cst, self.g_all], writes=[ps])
        K.op('act', lambda: nc.scalar.activation(out=gex[:, :, H0:H0 + nh], in_=ps[:, 0:24].rearrange("p (a h) -> p a h", a=3)[:, :, 0:nh], func=AF.Exp), reads=[ps], writes=[G(gex)])
        expG, kend = gex[:, 0, H0:H0 + nh], gex[:, 1, H0:H0 + nh]
        yield
        K.op(self.gpe, lambda: self.gpv.tensor_tensor(out=sl(B.rhsG), in0=bcm(B.inc), in1=bcl(gc), op=ALU.mult), reads=[cst, self.g_all], writes=[G(B.rhsG)])
        ps = self.next_ps()
        rg2 = fl(B.rhsG)
        nm4 = B.nm[:].unsqueeze(1).broadcast_to([128, 4, 128])
        fns = []
        for j in range(W // 512):
            fns.append(lambda j=j: nc.tensor.matmul(ps[:, j * 512:(j + 1) * 512], lhsT=B.excb[:], rhs=rg2[:, j * 512:(j + 1) * 512], start=True, stop=False))
            fns.append(lambda j=j: nc.tensor.matmul(ps[:, j * 512:(j + 1) * 512].rearrange("p (h q) -> p h q", h=4), lhsT=self.idb[:], rhs=nm4, start=False, stop=True))
        K.op('pe', fns, reads=[B.excb, G(B.rhsG), B.nm, self.idb], writes=[ps])
        K.op('act', lambda: nc.scalar.activation(out=fl(B.DT), in_=ps[:, 0:W], func=AF.Exp), reads=[ps], writes=[G(B.DT)])
        yield
        ps = self.next_ps()
        self.mm8(ps, kT, kT, [fm_], [], hs=hs)
        K.op('dve', lambda: nc.vector.tensor_tensor(out=fl(B.tA), in0=ps[:, 0:W], in1=fl(B.DT), op=ALU.mult), reads=[ps, G(B.DT)], writes=[G(B.tA)])
        K.op('dve', lambda: nc.vector.tensor_tensor(out=sl(B.tA), in0=sl(B.tA), in1=bcl(bc), op=ALU.mult), reads=[G(B.tA), self.beta_all], writes=[G(B.tA)])
        yield
        K.op('dve', lambda: nc.vector.tensor_tensor(out=sl(B.UD), in0=sl(B.tA), in1=bcm(B.bdS[:]), op=ALU.mult), reads=[G(B.tA), B.bdS], writes=[G(B.UD)])
        K.op('dve', lambda: nc.vector.tensor_tensor(out=sl(B.UO), in0=sl(B.tA), in1=bcm(B.nbd[:]), op=ALU.mult), reads=[G(B.tA), B.nbd], writes=[G(B.UO)])
        yield
        ps = self.next_ps()
        self.mm8(ps, kT, qT, [fm_], [], hs=hs)
        K.op('dve', lambda: nc.vector.tensor_tensor(out=fl(attnT), in0=ps[:, 0:W], in1=fl(B.DT), op=ALU.mult), reads=[ps, G(B.DT)], writes=[G(attnT)])
        yield
        ps = self.next_ps()
        pv = self.mm8(ps, hd(B.UD), None, [G(B.UD), self.idb], [], bf16_out=True, transpose=True, hs=hs)
        K.op('act', lambda: nc.scalar.copy(out=fl(B.AD), in_=pv[:, 0:W]), reads=[ps], writes=[G(B.AD)])
        yield
        X, XT = B.UD, B.AD
        P = B.P[0]
        K.op(self.gpe, lambda: self.gpv.tensor_tensor(out=sl(P), in0=bcm(self.idb[:]), in1=sl(X), op=ALU.subtract), reads=[self.idb, G(X)], writes=[G(P)])

        def square(k, X, XT):
            XTn = B.XT[k % 2]
            ps = self.next_ps()
            self.mm8(ps, hd(X), hd(XT), [G(X)], [G(XT)], hs=hs)
            K.op('act', lambda: nc.scalar.copy(out=fl(XTn), in_=ps[:, 0:W]), reads=[ps], writes=[G(XTn)])
            Xn = None
            if k < 4:
                Xn = B.X[k % 2]
                ps2 = self.next_ps()
                self.mm8(ps2, hd(XT), hd(X), [G(XT)], [G(X)], hs=hs)
                K.op('act', lambda: nc.scalar.copy(out=fl(Xn), in_=ps2[:, 0:W]), reads=[ps2], writes=[G(Xn)])
            return Xn, XTn

        Xn, XTn = square(1, X, XT)
        yield
        for k in range(1, 5):
            ps3 = self.next_ps()
            self.mm8(ps3, hd(XTn), hd(P), [G(XTn), self.idb], [G(P)], acc=(idH, hd(P)), hs=hs)
            Pn = B.P[k % 2]
            K.op('act', lambda ps3=ps3, Pn=Pn: nc.scalar.copy(out=fl(Pn), in_=ps3[:, 0:W]), reads=[ps3], writes=[G(Pn)])
            P = Pn
            if k < 4:
                yield
                Xn, XTn = square(k + 1, Xn, XTn)
            yield
        SD = P
        ps = self.next_ps()
        self.mm8(ps, hd(SD), idH, [G(SD)], [self.idb], hs=hs)
        K.op('act', lambda ps=ps: nc.scalar.copy(out=fl(B.TD), in_=ps[:, 0:W]), reads=[ps], writes=[G(B.TD)])
        yield
        ps = self.next_ps()
        self.mm8(ps, hd(B.UO), hd(B.TD), [G(B.UO)], [G(B.TD)], hs=hs)
        K.op('act', lambda ps=ps: nc.scalar.activation(out=fl(B.VT), in_=ps[:, 0:W], func=AF.Copy, scale=-1.0), reads=[ps], writes=[G(B.VT)])
        ps2 = self.next_ps()
        self.mm8(ps2, hd(B.TD), hd(B.UO), [G(B.TD)], [G(B.UO)], hs=hs)
        K.op('act', lambda ps2=ps2: nc.scalar.copy(out=fl(B.V), in_=ps2[:, 0:W]), reads=[ps2], writes=[G(B.V)])
        yield
        ps = self.next_ps()
        self.mm8(ps, hd(B.V), hd(B.VT), [G(B.V)], [G(B.VT)], hs=hs)
        K.op('act', lambda ps=ps: nc.scalar.activation(out=fl(B.V2T), in_=ps[:, 0:W], func=AF.Copy, scale=-1.0), reads=[ps], writes=[G(B.V2T)])
        ps2 = self.next_ps()
        self.mm8(ps2, hd(B.VT), hd(SD), [G(B.VT), self.idb], [G(SD)], acc=(idH, hd(SD)), hs=hs)
        K.op('act', lambda ps2=ps2: nc.scalar.copy(out=fl(B.Z), in_=ps2[:, 0:W]), reads=[ps2], writes=[G(B.Z)])
        yield
        ps = self.next_ps()
        self.mm8(ps, hd(B.V2T), hd(B.Z), [G(B.V2T), self.idb], [G(B.Z)], acc=(idH, hd(B.Z)), hs=hs)
        K.op('act', lambda ps=ps: nc.scalar.copy(out=fl(B.Sf), in_=ps[:, 0:W]), reads=[ps], writes=[G(B.Sf)])
        yield
        K.op(self.gpe, lambda: self.gpv.tensor_tensor(out=sl(B.kg), in0=k3, in1=bcl(expG), op=ALU.mult), reads=[tm_, G(gex)], writes=[G(B.kg)])
        K.op(self.gpe, lambda: self.gpv.tensor_tensor(out=sl(kendk), in0=k3, in1=bcl(kend), op=ALU.mult), reads=[tm_, G(gex)], writes=[G(kendk)])
        yield
        ps = self.next_ps()
        self.mm8(ps, hd(B.Sf), lambda H: v3[:, H, :], [G(B.Sf)], [tm_], hs=hs)
        K.op('act', lambda ps=ps: nc.scalar.copy(out=fl(u_sb), in_=ps[:, 0:W]), reads=[ps], writes=[G(u_sb)])
        ps2 = self.next_ps()
        self.mm8(ps2, hd(B.kg), hd(B.Sf), [G(B.kg)], [G(B.Sf)], hs=hs)
        K.op('act', lambda ps2=ps2: nc.scalar.copy(out=fl(wT), in_=ps2[:, 0:W]), reads=[ps2], writes=[G(wT)])
        yield

    def gdn_rec(self, c, d, fm_, need_out, slot):
        nc, K, B = self.nc, self.K, self.Bt
        bc = self.beta_all[:, c, d * 8:(d + 1) * 8]
        qT = lambda H: fm_[:, FM_Q // 128 + H, :]
        hd = lambda buf: (lambda H: buf[:, H, :])
        fl = lambda buf: buf[:].rearrange("p h q -> p (h q)")
        f3 = lambda ap: ap.rearrange("p (h q) -> p h q", q=128)
        gex_, attnT_, u_sb_, wT_, kendk_ = B.gex[slot], B.attnT[slot], B.u_sb[slot], B.wT[slot], B.kendk[slot]
        expG, cdG = gex_[:, 0, :], gex_[:, 2, :]

        class _Multi:
            def __init__(self, buf):
                self.buf = buf
            def __getitem__(self, k):
                return self.buf[k]
        gex, attnT, u_sb, wT, kendk = gex_, attnT_, u_sb_, wT_, kendk_
        GG = lambda buf: list(buf.g)
        ps = self.next_ps()
        self.mm8(ps, hd(wT), hd(B.Sgb), GG(wT), [B.Sgb])
        K.op('dve', lambda: nc.vector.tensor_tensor(out=fl(B.tB), in0=fl(u_sb), in1=ps[:], op=ALU.subtract), reads=GG(u_sb) + [ps], writes=[B.tB])
        K.op('dve', lambda: nc.vector.tensor_tensor(out=B.vnew[:], in0=B.tB[:], in1=bc.unsqueeze(2).broadcast_to([128, 8, 128]), op=ALU.mult), reads=[B.tB, self.beta_all], writes=[B.vnew])
        yield
        ps = self.next_ps()
        self.mm8(ps, qT, hd(B.Sgb), [fm_], [B.Sgb])
        K.op('dve', lambda: nc.vector.tensor_tensor(out=B.tB[:], in0=f3(ps[:]), in1=expG.unsqueeze(2).broadcast_to([128, 8, 128]), op=ALU.mult), reads=[ps] + GG(gex), writes=[B.tB])
        ps2 = self.next_ps()
        self.mm8(ps2, hd(attnT), hd(B.vnew), GG(attnT), [B.vnew])
        K.op('dve', lambda: nc.vector.tensor_tensor(out=fl(B.o), in0=fl(B.tB), in1=ps2[:], op=ALU.add), reads=[B.tB, ps2], writes=[B.o])
        if need_out:
            if d == 0:
                self.store('sp', B.o, self.YO[c * 128:(c + 1) * 128, 1024:2048], fl(B.o))
            else:
                K.op('dve', lambda: nc.vector.tensor_tensor(out=fl(B.ofin), in0=fl(B.o), in1=B.yo1[:, 1024:2048], op=ALU.add), reads=[B.o, B.yo1], writes=[B.ofin])
        yield
        ps = self.next_ps()
        self.mm8(ps, hd(kendk), hd(B.vnew), GG(kendk), [B.vnew])
        K.op(self.gpe, lambda: self.gpv.tensor_tensor(out=B.Sg[:], in0=B.Sg[:], in1=cdG.unsqueeze(2).broadcast_to([128, 8, 128]), op=ALU.mult), reads=[B.Sg] + GG(gex), writes=[B.Sg])
        K.op('dve', lambda: nc.vector.tensor_tensor(out=fl(B.Sg), in0=fl(B.Sg), in1=ps[:], op=ALU.add), reads=[B.Sg, ps], writes=[B.Sg])
        K.op('act', lambda: nc.scalar.copy(out=B.Sgb[:], in_=B.Sg[:]), reads=[B.Sg], writes=[B.Sgb])
        yield

    def epilogue(self, c):
        nc, K, B = self.nc, self.K, self.Bt
        K.op(self.gpe, lambda: self.gpv.tensor_tensor(out=B.yg[:], in0=B.yfin[:], in1=B.zs[:, 0:1024], op=ALU.mult), reads=[B.yfin, B.zs], writes=[B.yg])
        for g in range(2):
            K.op('act', lambda g=g: nc.scalar.activation(out=B.ejunk[:, 0:512], in_=B.yg[:, g * 512:(g + 1) * 512], func=AF.Square, accum_out=B.est[:, g:g + 1]), reads=[B.yg], writes=[B.ejunk, B.est])
        K.op('dve', lambda: nc.vector.tensor_tensor(out=B.wz[:], in0=B.ofin[:], in1=B.ofin[:], op=ALU.mult), reads=[B.ofin], writes=[B.wz])
        K.op('dve', lambda: nc.vector.tensor_reduce(out=B.est[:, 2:10], in_=B.wz[:], axis=AX.X, op=ALU.add), reads=[B.wz], writes=[B.est])
        K.op('act', lambda: nc.scalar.activation(out=B.est2[:, 0:2], in_=B.est[:, 0:2], func=AF.Ln, bias=self.epsb[:], scale=1.0 / 512), reads=[B.est, self.epsb], writes=[B.est2])
        K.op('act', lambda: nc.scalar.activation(out=B.est2[:, 2:10], in_=B.est[:, 2:10], func=AF.Ln, bias=self.epsb[:], scale=1.0 / 128), reads=[B.est, self.epsb], writes=[B.est2])
        K.op('act', lambda: nc.scalar.activation(out=B.est2[:, 0:10], in_=B.est2[:, 0:10], func=AF.Exp, scale=-0.5), reads=[B.est2], writes=[B.est2])
        for g in range(2):
            K.op('dve', lambda g=g: nc.vector.scalar_tensor_tensor(out=B.ym[:, g * 512:(g + 1) * 512], in0=B.yg[:, g * 512:(g + 1) * 512], scalar=B.est2[:, g:g + 1], in1=B.snw[:, g * 512:(g + 1) * 512], op0=ALU.mult, op1=ALU.mult),
                 reads=[B.yg, B.est2, B.snw], writes=[B.ym])
        K.op(self.gpe, lambda: self.gpv.tensor_tensor(out=B.wz2[:], in0=B.zs[:, 1024:2048].rearrange("p (h q) -> p h q", q=128), in1=B.gnw[:].unsqueeze(1).broadcast_to([128, 8, 128]), op=ALU.mult), reads=[B.zs, B.gnw], writes=[B.wz2])
        K.op('dve', lambda: nc.vector.tensor_tensor(out=B.ofin[:], in0=B.ofin[:], in1=B.est2[:, 2:10].unsqueeze(2).broadcast_to([128, 8, 128]), op=ALU.mult), reads=[B.ofin, B.est2], writes=[B.ofin])
        K.op('dve', lambda: nc.vector.tensor_tensor(out=B.ym[:, 1024:2048].rearrange("p (h q) -> p h q", q=128), in0=B.ofin[:], in1=B.wz2[:], op=ALU.mult), reads=[B.ofin, B.wz2], writes=[B.ym])
        self.store('sp', B.ym, self.YM[c * 128:(c + 1) * 128, :], B.ym[:])

    def phaseC(self, l, hsrc, last):
        nc, K = self.nc, self.K
        with ExitStack() as es:
            sb = lambda name, shape, dt=F32: self.sb(es, name, shape, dt)
            wo = sb("wo", [128, 16, D], BF16)
            for kc in range(16):
                self.load('pool', wo, wo[:, kc, :], self.w_out[l][kc * 128:(kc + 1) * 128, :])
            gp = [sb("gp%d" % r, [128, D]) for r in range(2)]
            pw = sb("pwb", [128, D])
            self.load('sp', pw, pw[:], dram_bcast(self.post_w[l], 128))
            for r in range(2):
                self.load('sp', gp[r], gp[r][:], dram_bcast(self.modv[r, 2 * D:3 * D], 128))
                K.op('dve', lambda r=r: nc.vector.tensor_tensor(out=gp[r][:], in0=gp[r][:], in1=pw[:], op=ALU.mult), reads=[gp[r], pw], writes=[gp[r]])
            ymc = [sb("ymc%d" % i, [128, D], BF16) for i in range(2)]
            hc = [sb("hc%d" % i, [128, D]) for i in range(2)]
            ymT = sb("ymT", [128, 16, 128], BF16)
            cj = sb("cjunk", [128, D], BF16)
            cst_ = sb("cst_", [128, 8])
            res = [sb("res%d" % i, [128, D]) for i in range(2)]
            chunks = list(range(2, NCH)) if last else list(range(NCH))
            ymTs = [ymT, sb("ymT2", [128, 16, 128], BF16)]
            csts = [cst_, sb("cst2_", [128, 8])]

            def issue(i):
                c = chunks[i]
                self.load('sp', ymc[i % 2], ymc[i % 2][:], self.YM[c * 128:(c + 1) * 128, :])
                self.load('sp', hc[i % 2], hc[i % 2][:], hsrc[c * 128:(c + 1) * 128, :])

            pos = [sb("po%d" % i, [128, D]) for i in range(2)]

            def c_iter(i, c):
                y_, h_, r_, yT, st_, po = ymc[i % 2], hc[i % 2], res[i % 2], ymTs[i % 2], csts[i % 2], pos[i % 2]
                r = 1 if c < 2 else 0
                for half in range(2):
                    ps = self.next_half()
                    pb = ps[:].bitcast(BF16)
                    K.op('pe', [lambda j=j, pb=pb, half=half: nc.tensor.transpose(out=pb[:, j * 128:(j + 1) * 128], in_=y_[:, (half * 8 + j) * 128:(half * 8 + j + 1) * 128], identity=self.idb[:]) for j in range(8)],
                         reads=[y_, self.idb], writes=[ps])
                    K.op('act', lambda pb=pb, half=half: nc.scalar.copy(out=yT[:, half * 8:(half + 1) * 8, :].rearrange("p k t -> p (k t)"), in_=pb[:, 0:1024]), reads=[ps], writes=[yT])
                    self.free_half(ps)
                yield
                for cb in range(4):
                    ps = self.next_half()
                    fns = [lambda ps=ps, cb=cb, kc=kc: nc.tensor.matmul(ps[:, 0:512], lhsT=yT[:, kc, :], rhs=wo[:, kc, cb * 512:(cb + 1) * 512], start=(kc == 0), stop=(kc == 15)) for kc in range(16)]
                    K.op('pe', fns, reads=[yT, wo], writes=[ps])
                    K.op('act', lambda ps=ps, cb=cb: nc.scalar.activation(out=cj[:, cb * 512:(cb + 1) * 512], in_=ps[:, 0:512], func=AF.Square, accum_out=st_[:, cb:cb + 1]), reads=[ps], writes=[cj, st_])
                    K.op('act', lambda ps=ps, cb=cb: nc.scalar.copy(out=po[:, cb * 512:(cb + 1) * 512], in_=ps[:, 0:512]), reads=[ps], writes=[po])
                    self.free_half(ps)
                    yield
                K.op('dve', lambda: nc.vector.tensor_reduce(out=st_[:, 4:5], in_=st_[:, 0:4], axis=AX.X, op=ALU.add), reads=[st_], writes=[st_])
                K.op('act', lambda: nc.scalar.activation(out=st_[:, 5:6], in_=st_[:, 4:5], func=AF.Ln, bias=self.epsb[:], scale=1.0 / D), reads=[st_, self.epsb], writes=[st_])
                K.op('act', lambda: nc.scalar.activation(out=st_[:, 6:7], in_=st_[:, 5:6], func=AF.Exp, scale=-0.5), reads=[st_], writes=[st_])
                yield
                K.op('dve', lambda: nc.vector.scalar_tensor_tensor(out=r_[:], in0=po[:], scalar=st_[:, 6:7], in1=gp[r][:], op0=ALU.mult, op1=ALU.mult), reads=[po, st_, gp[r]], writes=[r_])
                yield
                K.op('dve', lambda: nc.vector.tensor_tensor(out=r_[:], in0=r_[:], in1=h_[:], op=ALU.add), reads=[r_, h_], writes=[r_])
                if last:
                    dst = self.out[(c - 2) * 128:(c - 1) * 128, :]
                else:
                    dst = self.H[c * 128:(c + 1) * 128, :]
                self.store('pool', r_, dst, r_[:])
                if i + 2 < len(chunks):
                    issue(i + 2)

            def c_gens():
                for i, c in enumerate(chunks):
                    yield c_iter(i, c)

            issue(0)
            issue(1)
            run_pipeline(c_gens(), 2)


def make_consts():
    i = np.arange(128)
    c = np.zeros((128, NCONST), np.float32)
    c[:, C_ID:C_ID + 128] = np.eye(128)
    c[:, C_INC0:C_INC0 + 128] = (i[:, None] <= i[None, :])
    c[:, C_INC1:C_INC1 + 128] = (i[:, None] >= i[None, :])
    c[:, C_EXC0:C_EXC0 + 128] = (i[:, None] > i[None, :])
    c[:, C_EXC1:C_EXC1 + 128] = (i[:, None] < i[None, :])
    c[:, C_ONES:C_ONES + 128] = 1.0
    c[:, C_BD:C_BD + 128] = (i[:, None] // 32 == i[None, :] // 32)
    return c


def make_pv(inp):
    pv = np.zeros((DEPTH, 128, NPV), np.float32)
    for l in range(DEPTH):
        pv[l, :, 0:16] = inp['pre_norm_w'][l].reshape(16, 128).T
        cw = np.concatenate([inp['conv_ssd_w'][l], inp['conv_gdn_w'][l]], axis=1)
        pv[l, :, 16:196] = cw.reshape(5, 36, 128).transpose(2, 1, 0).reshape(128, 180)
        pv[l, :, 196:208] = inp['conv_ssd_b'][l].reshape(12, 128).T
    return pv


def make_in_maps(inp, cores):
    shared = {
        'pv': make_pv(inp), 'consts': make_consts(),
        'w_ada': np.ascontiguousarray(inp['w_ada']), 'b_ada': np.ascontiguousarray(inp['b_ada']),
        'post_norm_w': np.ascontiguousarray(inp['post_norm_w']), 'w_in': np.ascontiguousarray(inp['w_in']),
        'ssd_a_log': np.ascontiguousarray(inp['ssd_a_log']).reshape(DEPTH, 32),
        'ssd_dt_bias': np.ascontiguousarray(inp['ssd_dt_bias']).reshape(DEPTH, 32),
        'ssd_d': np.ascontiguousarray(inp['ssd_d']), 'ssd_norm_w': np.ascontiguousarray(inp['ssd_norm_w']),
        'gdn_a_log': np.ascontiguousarray(inp['gdn_a_log']).reshape(DEPTH, 16),
        'gdn_dt_bias': np.ascontiguousarray(inp['gdn_dt_bias']).reshape(DEPTH, 16),
        'gdn_norm_w': np.ascontiguousarray(inp['gdn_norm_w']), 'w_out': np.ascontiguousarray(inp['w_out']),
    }
    maps = []
    for b in cores:
        m = dict(shared)
        m['h0'] = np.ascontiguousarray(np.concatenate([inp['ctx'][b], inp['x'][b]], axis=0))
        c2 = np.stack([inp['c'][b], inp['c_ctx']])
        m['c2t'] = np.ascontiguousarray(c2.reshape(2, 16, 128).transpose(2, 1, 0).reshape(128, 32))
        maps.append(m)
    return maps


def kernel(**inputs):
    inp = {k: np.asarray(v) for k, v in inputs.items()}
    nc = Builder().build()
    maps = make_in_maps(inp, list(range(8)))
    res = run_bass_kernel_spmd(nc, maps, core_ids=list(range(8)))
    return np.stack([r['out'] for r in res.results], axis=0).astype(np.float32)
```

```python
import numpy as np
from contextlib import ExitStack
import concourse.bass as bass
import concourse.mybir as mybir
from concourse.bass_utils import run_bass_kernel_spmd

F32, BF16 = mybir.dt.float32, mybir.dt.bfloat16
AF = mybir.ActivationFunctionType
ALU = mybir.AluOpType
AX = mybir.AxisListType

D = 2048
T = 4352
NCH = 34
DEPTH = 4
IN_DIM = 6720
EPS = 1e-6
NPV = 208
C_ID, C_INC0, C_INC1, C_EXC0, C_EXC1, C_ONES, C_BD = 0, 128, 256, 384, 512, 640, 768
NCONST = 896
FM_B, FM_C, FM_Q, FM_K, FM_ROWS = 0, 256, 512, 1536, 2560
TM_X, TM_B, TM_K, TM_V, TM_COLS = 0, 1024, 1280, 2304, 3328


class Buf:
    __slots__ = ('name', 'w', 'r', 't', 'g')

    def __init__(self, name, t=None):
        self.name = name
        self.w = None
        self.r = {}
        self.t = t

    def __getitem__(self, k):
        return self.t[k]


class HalfBuf(Buf):
    __slots__ = ('off',)

    def __init__(self, name, t, off):
        Buf.__init__(self, name, t)
        self.off = off

    def __getitem__(self, k):
        if isinstance(k, tuple):
            p, c = k[0], k[1]
            start = (c.start or 0) + self.off
            stop = (c.stop if c.stop is not None else 512) + self.off
            return self.t[p, start:stop]
        return self.t[:, self.off:self.off + 512]


def run_pipeline(gens, depth):
    active = []
    it = iter(gens)
    done = False
    while True:
        if not done and len(active) < depth:
            try:
                active.append(next(it))
            except StopIteration:
                done = True
        if not active:
            if done:
                break
            continue
        for g in list(active):
            try:
                next(g)
            except StopIteration:
                active.remove(g)


class Ctx:
    NS = 8

    def __init__(self, nc):
        self.nc = nc
        self.eng = {'pe': nc.tensor, 'act': nc.scalar, 'dve': nc.vector, 'gp': nc.gpsimd, 'sp': nc.sync, 'pool': nc.gpsimd}
        self.inorder = ('pe', 'act', 'dve', 'gp')
        self.inorder_skip = ('pe',)
        self.dmaq = ('sp', 'pool')
        self.sem = {e: nc.alloc_semaphore('s_' + e) for e in self.inorder}
        self.cnt = {e: 0 for e in self.inorder}
        self.dsem = {q: [nc.alloc_semaphore('d_%s%d' % (q, i)) for i in range(self.NS)] for q in self.dmaq}
        self.dcnt = {q: 0 for q in self.dmaq}
        self.waited = {}
        self.nins = 0

    def _wait(self, e, tok):
        if tok is None:
            return
        sem, val, owner = tok
        if owner == e and e in self.inorder_skip:
            return
        key = (e, id(sem))
        if self.waited.get(key, 0) >= val:
            return
        self.eng[e].wait_ge(sem, val)
        self.waited[key] = val

    def op(self, e, fns, reads=(), writes=()):
        if not isinstance(fns, (list, tuple)):
            fns = [fns]
        for b in reads:
            self._wait(e, b.w)
        for b in writes:
            self._wait(e, b.w)
            for t in b.r.values():
                self._wait(e, t)
        self.nins += len(fns)
        if e in self.dmaq:
            n = self.dcnt[e]
            slot = n % self.NS
            rnd = n // self.NS
            sem = self.dsem[e][slot]
            if rnd > 0:
                self._wait(e, (sem, 16 * rnd, None))
            for f in fns[:-1]:
                f()
            fns[-1]().then_inc(sem, 16)
            tok = (sem, 16 * (rnd + 1), e)
            self.dcnt[e] = n + 1
        else:
            for f in fns[:-1]:
                f()
            fns[-1]().then_inc(self.sem[e], 1)
            self.cnt[e] += 1
            tok = (self.sem[e], self.cnt[e], e)
        for b in reads:
            old = b.r.get(id(tok[0]))
            if old is None or old[1] < tok[1]:
                b.r[id(tok[0])] = tok
        for b in writes:
            b.w = tok
            b.r = {}
        return tok

    def all_tokens(self):
        toks = [(self.sem[e], self.cnt[e], e) for e in self.inorder if self.cnt[e] > 0]
        for q in self.dmaq:
            n = self.dcnt[q]
            for slot in range(self.NS):
                uses = (n - slot + self.NS - 1) // self.NS if n > slot else 0
                if uses > 0:
                    toks.append((self.dsem[q][slot], 16 * uses, q))
        return toks

    def barrier(self):
        toks = self.all_tokens()
        for e in self.eng:
            for t in toks:
                if t[2] == e and e in self.inorder:
                    continue
                self._wait(e, t)


def dram_bcast(ap, nparts):
    n = 1
    for s in ap.shape:
        n *= s
    return bass.AP(ap.tensor, ap.offset, [[0, nparts], [1, n]])


def tok_blocks():
    blks = [(0, 256, 256)]
    for i in range(8):
        blks.append((256 + 512 * i, 512, 64))
    return blks


class Builder:
    def __init__(self, n_layers=DEPTH, debug=False, stop_after=None, force_last=False):
        self.force_last = force_last
        self.n_layers = n_layers
        self.debug = debug
        self.stop_after = stop_after
        nc = self.nc = bass.Bass("TRN2", target_bir_lowering=False)
        self.K = Ctx(nc)
        self.gpe = getattr(Builder, 'GPE', 'gp')
        self.gpv = nc.gpsimd if self.gpe == 'gp' else nc.vector
        di = lambda name, shape: nc.dram_tensor(name, shape, F32, kind="ExternalInput").ap()
        self.h0 = di("h0", [T, D])
        self.c2t = di("c2t", [128, 32])
        self.pv = di("pv", [DEPTH, 128, NPV])
        self.consts = di("consts", [128, NCONST])
        self.w_ada = di("w_ada", [DEPTH, D, 3 * D])
        self.b_ada = di("b_ada", [DEPTH, 3 * D])
        self.post_w = di("post_norm_w", [DEPTH, D])
        self.w_in = di("w_in", [DEPTH, D, IN_DIM])
        self.ssd_a_log = di("ssd_a_log", [DEPTH, 32])
        self.ssd_dt_bias = di("ssd_dt_bias", [DEPTH, 32])
        self.ssd_d = di("ssd_d", [DEPTH, 16])
        self.ssd_norm_w = di("ssd_norm_w", [DEPTH, 1024])
        self.gdn_a_log = di("gdn_a_log", [DEPTH, 16])
        self.gdn_dt_bias = di("gdn_dt_bias", [DEPTH, 16])
        self.gdn_norm_w = di("gdn_norm_w", [DEPTH, 128])
        self.w_out = di("w_out", [DEPTH, D, D])
        self.out = nc.dram_tensor("out", [4096, D], F32, kind="ExternalOutput").ap()
        self.ext_set = getattr(Builder, 'EXT_SET', ())
        ds = lambda name, shape, dt: nc.dram_tensor(name, shape, dt, kind=("ExternalOutput" if (debug or name in self.ext_set) else "Internal")).ap()
        self.modv = ds("modv", [2, 3 * D], F32)
        self.FM = ds("FM", [FM_ROWS, T], BF16)
        self.TM = ds("TM", [T, TM_COLS], BF16)
        self.ZS = ds("ZS", [T, D], BF16)
        self.SMALL = ds("SMALL", [T, 64], F32)
        self.YO = ds("YO", [T, D], F32)
        self.YM = ds("YM", [T, D], BF16)
        self.H = ds("H", [T, D], F32)
        if debug:
            self.UT = ds("UT", [D, T], BF16)

    def sb(self, es, name, shape, dt=F32):
        self.uid = getattr(self, 'uid', 0) + 1
        name = "%s_%d" % (name, self.uid)
        return Buf(name, es.enter_context(self.nc.sbuf_tensor(name, shape, dt)))

    def load(self, q, dst, dst_ap, src_ap, **kw):
        eng = self.K.eng[q]
        return self.K.op(q, lambda: eng.dma_start(out=dst_ap, in_=src_ap, **kw), writes=[dst])

    def store(self, q, src, dst_ap, src_ap, **kw):
        eng = self.K.eng[q]
        return self.K.op(q, lambda: eng.dma_start(out=dst_ap, in_=src_ap, **kw), reads=[src])

    def build(self):
        nc, K = self.nc, self.K
        with ExitStack() as es:
            self.ps = [Buf("ps%d" % i, es.enter_context(nc.psum_tensor("ps%d" % i, [128, 1024], F32))) for i in range(4)]
            self.psi = 0
            self.psh = [HalfBuf("psh%d" % i, self.ps[i // 2].t, (i % 2) * 512) for i in range(8)]
            self.psfree = list(self.psh)
            self.cst = self.sb(es, "cst", [128, NCONST])
            self.load('sp', self.cst, self.cst[:], self.consts)
            self.idb = self.sb(es, "idb", [128, 128], BF16)
            K.op('dve', lambda: nc.vector.tensor_copy(out=self.idb[:], in_=self.cst[:, C_ID:C_ID + 128]), reads=[self.cst], writes=[self.idb])
            self.onesb = self.sb(es, "onesb", [128, 128], BF16)
            K.op('dve', lambda: nc.vector.tensor_copy(out=self.onesb[:], in_=self.cst[:, C_ONES:C_ONES + 128]), reads=[self.cst], writes=[self.onesb])
            self.epsb = self.sb(es, "epsb", [128, 1])
            K.op('dve', lambda: nc.vector.memset(self.epsb[:], EPS), writes=[self.epsb])
            self.lnqs = self.sb(es, "lnqs", [128, 1])
            K.op('dve', lambda: nc.vector.memset(self.lnqs[:], float(np.log(128.0 ** -0.5))), writes=[self.lnqs])
            for l in range(self.n_layers):
                self.layer(l)
            K.barrier()
        return nc

    def next_ps(self):
        p = self.ps[self.psi % 4]
        self.psi += 1
        return p

    def next_half(self):
        assert self.psfree, "PSUM half-slots exhausted"
        return self.psfree.pop(0)

    def free_half(self, p):
        self.psfree.append(p)

    def layer(self, l):
        K = self.K
        last = (l == DEPTH - 1) or (self.force_last and l == self.n_layers - 1)
        hsrc = self.h0 if l == 0 else self.H
        with ExitStack() as es:
            self.pvt = self.sb(es, "pvt", [128, NPV])
            self.load('sp', self.pvt, self.pvt[:], self.pv[l])
            self.phase0(l)
            K.barrier()
            if self.stop_after == 'p0':
                return
            self.phaseA(l, hsrc)
            K.barrier()
            if self.stop_after == 'A':
                return
            self.phaseB(l, last)
            K.barrier()
            if self.stop_after == 'B':
                return
            self.phaseC(l, hsrc, last)
            K.barrier()

    def phase0(self, l):
        nc, K = self.nc, self.K
        with ExitStack() as es:
            sct = self.sb(es, "sct", [128, 32])
            self.load('sp', sct, sct[:], self.c2t)
            K.op('act', lambda: nc.scalar.activation(out=sct[:], in_=sct[:], func=AF.Silu), reads=[sct], writes=[sct])
            bia = self.sb(es, "bia", [2, 3 * D])
            self.load('sp', bia, bia[:], dram_bcast(self.b_ada[l], 2))
            modsb = self.sb(es, "modsb", [2, 3 * D])
            wts = [self.sb(es, "wada%d" % i, [128, 16, 512]) for i in range(2)]
            sct3 = sct[:].rearrange("p (k r) -> p k r", r=2)
            for cb in range(12):
                wt = wts[cb % 2]
                src = self.w_ada[l][:, cb * 512:(cb + 1) * 512].rearrange("(k p) f -> p k f", p=128)
                self.load('sp', wt, wt[:], src)
                ps = self.next_ps()
                fns = []
                for kc in range(16):
                    fns.append(lambda kc=kc, ps=ps, wt=wt: nc.tensor.matmul(ps[0:2, 0:512], lhsT=sct3[:, kc, :], rhs=wt[:, kc, :], start=(kc == 0), stop=(kc == 15)))
                K.op('pe', fns, reads=[sct, wt], writes=[ps])
                K.op('dve', lambda cb=cb, ps=ps: nc.vector.tensor_tensor(out=modsb[:, cb * 512:(cb + 1) * 512], in0=ps[0:2, 0:512], in1=bia[:, cb * 512:(cb + 1) * 512], op=ALU.add),
                     reads=[ps, bia], writes=[modsb])
            self.store('sp', modsb, self.modv, modsb[:])

    def phaseA(self, l, hsrc):
        nc, K = self.nc, self.K
        with ExitStack() as es:
            uT = self.sb(es, "uT", [128, 16, T], BF16)
            mraw = self.sb(es, "mraw", [128, 2, 2, 16])
            for r in range(2):
                for w in range(2):
                    src = bass.AP(self.modv.tensor, self.modv.offset + r * 3 * D + w * D, [[1, 128], [128, 16]])
                    self.load('sp', mraw, mraw[:, r, w, :], src, allow_slow_non_contiguous=True)
            mA = self.sb(es, "mA", [128, 2, 16])
            for r in range(2):
                K.op('dve', lambda r=r: nc.vector.scalar_tensor_tensor(out=mA[:, r, :], in0=mraw[:, r, 1, :], scalar=1.0, in1=self.pvt[:, 0:16], op0=ALU.add, op1=ALU.mult),
                     reads=[mraw, self.pvt], writes=[mA])
            with ExitStack() as es0:
                NH = 3
                hx = [self.sb(es0, "hx%d" % i, [128, D]) for i in range(NH)]
                junk = self.sb(es0, "junk", [128, D], BF16)
                xn = [self.sb(es0, "xn%d" % i, [128, D], BF16) for i in range(NH)]
                st = [self.sb(es0, "st%d" % i, [128, 4]) for i in range(NH)]

                def a0_load(c):
                    h_ = hx[c % NH]
                    self.load('sp', h_, h_[:], hsrc[c * 128:(c + 1) * 128, :])

                def a0_iter(c):
                    r = 1 if c < 2 else 0
                    h_, x_, s_ = hx[c % NH], xn[c % NH], st[c % NH]
                    K.op('act', lambda: nc.scalar.activation(out=junk[:], in_=h_[:], func=AF.Square, accum_out=s_[:, 0:1]), reads=[h_], writes=[junk, s_])
                    yield
                    K.op('act', lambda: nc.scalar.activation(out=s_[:, 1:2], in_=s_[:, 0:1], func=AF.Ln, bias=self.epsb[:], scale=1.0 / D), reads=[s_, self.epsb], writes=[s_])
                    K.op('act', lambda: nc.scalar.activation(out=s_[:, 2:3], in_=s_[:, 1:2], func=AF.Exp, scale=-0.5), reads=[s_], writes=[s_])
                    yield
                    K.op('dve', lambda: nc.vector.tensor_scalar(out=x_[:], in0=h_[:], scalar1=s_[:, 2:3], scalar2=None, op0=ALU.mult), reads=[h_, s_], writes=[x_])
                    if c + NH < NCH:
                        a0_load(c + NH)
                    yield
                    for half in range(2):
                        ps = self.next_half()
                        pb = ps[:].bitcast(BF16)
                        fns = [lambda j=j, pb=pb, half=half: nc.tensor.transpose(out=pb[:, j * 128:(j + 1) * 128], in_=x_[:, (half * 8 + j) * 128:(half * 8 + j + 1) * 128], identity=self.idb[:]) for j in range(8)]
                        K.op('pe', fns, reads=[x_, self.idb], writes=[ps])
                        for j in range(8):
                            kc = half * 8 + j
                            if j % 2 == 0:
                                K.op('act', lambda j=j, kc=kc, pb=pb: nc.scalar.activation(out=uT[:, kc, c * 128:(c + 1) * 128], in_=pb[:, j * 128:(j + 1) * 128], func=AF.Identity,
                                                                                         bias=mraw[:, r, 0, kc:kc + 1], scale=mA[:, r, kc:kc + 1]),
                                     reads=[ps, mraw, mA], writes=[uT])
                            else:
                                K.op('dve', lambda j=j, kc=kc, pb=pb: nc.vector.tensor_scalar(out=uT[:, kc, c * 128:(c + 1) * 128], in0=pb[:, j * 128:(j + 1) * 128], scalar1=mA[:, r, kc:kc + 1],
                                                                                          scalar2=mraw[:, r, 0, kc:kc + 1], op0=ALU.mult, op1=ALU.add),
                                     reads=[ps, mraw, mA], writes=[uT])
                        self.free_half(ps)
                        yield

                for c in range(NH):
                    a0_load(c)
                run_pipeline((a0_iter(c) for c in range(NCH)), 2)
                K.barrier()
            if self.debug:
                for kc in range(16):
                    self.store('pool', uT, self.UT[kc * 128:(kc + 1) * 128, :], uT[:, kc, :])
            with ExitStack() as es1:
                NB = 6
                wf = [self.sb(es1, "wf%d" % i, [128, 16, 128], BF16) for i in range(2)]
                acc = [self.sb(es1, "acc%d" % i, [128, 512]) for i in range(NB)]
                cv = [self.sb(es1, "cv%d" % i, [128, 512]) for i in range(NB)]
                ob = [self.sb(es1, "ob%d" % i, [128, 512], BF16) for i in range(NB)]
                sq = [self.sb(es1, "sq%d" % i, [128, 512], BF16) for i in range(NB)]
                lnv = [self.sb(es1, "lnv%d" % i, [128, 512]) for i in range(NB)]
                tmo = [self.sb(es1, "tmo%d" % i, [128, 4, 128], BF16) for i in range(NB)]
                blks = tok_blocks()

                def fm_iter(it, fc, w_, kind, fm_row, tm_col, t0, nt, rl):
                    a_, c_, o_, s_, l_, m_ = acc[it % NB], cv[it % NB], ob[it % NB], sq[it % NB], lnv[it % NB], tmo[it % NB]
                    ps = self.next_half()
                    fns = [lambda kc=kc: nc.tensor.matmul(ps[:, 0:nt], lhsT=w_[:, kc, :], rhs=uT[:, kc, t0:t0 + nt], start=(kc == 0), stop=(kc == 15)) for kc in range(16)]
                    K.op('pe', fns, reads=[w_, uT], writes=[ps])
                    yield
                    cw = lambda k: self.pvt[:, 16 + fc * 5 + k:16 + fc * 5 + k + 1]
                    if kind in ('q', 'k') or (it % 2 == 0):
                        K.op('dve', lambda: nc.vector.tensor_scalar(out=a_[:, 0:nt], in0=ps[:, 0:nt], scalar1=cw(2), scalar2=None, op0=ALU.mult), reads=[ps, self.pvt], writes=[a_])
                    else:
                        K.op('dve', lambda: nc.vector.tensor_scalar(out=a_[:, 0:nt], in0=ps[:, 0:nt], scalar1=cw(2), scalar2=None, op0=ALU.mult), reads=[ps, self.pvt], writes=[a_])
                    pv3 = ps[:, 0:nt].rearrange("p (r j) -> p r j", j=rl)
                    av3 = a_[:, 0:nt].rearrange("p (r j) -> p r j", j=rl)
                    for k in (0, 1, 3, 4):
                        sft = k - 2
                        j0, j1 = max(0, -sft), rl - max(0, sft)
                        K.op('dve', lambda k=k, sft=sft, j0=j0, j1=j1: nc.vector.scalar_tensor_tensor(out=av3[:, :, j0:j1], in0=pv3[:, :, j0 + sft:j1 + sft], scalar=cw(k), in1=av3[:, :, j0:j1], op0=ALU.mult, op1=ALU.add),
                             reads=[ps, a_, self.pvt], writes=[a_])
                        if k == 1:
                            yield
                    self.free_half(ps)
                    yield
                    if kind in ('q', 'k'):
                        K.op('act', lambda: nc.scalar.activation(out=c_[:, 0:nt], in_=a_[:, 0:nt], func=AF.Silu), reads=[a_], writes=[c_])
                        K.op(self.gpe, lambda: self.gpv.tensor_tensor(out=s_[:, 0:nt], in0=c_[:, 0:nt], in1=c_[:, 0:nt], op=ALU.mult), reads=[c_], writes=[s_])
                        yield
                        ps2 = self.next_half()
                        K.op('pe', lambda: nc.tensor.matmul(ps2[:, 0:nt], lhsT=self.onesb[:], rhs=s_[:, 0:nt], start=True, stop=True), reads=[s_, self.onesb], writes=[ps2])
                        K.op('act', lambda: nc.scalar.activation(out=l_[:, 0:nt], in_=ps2[:, 0:nt], func=AF.Ln, bias=self.epsb[:], scale=1.0), reads=[ps2, self.epsb], writes=[l_])
                        self.free_half(ps2)
                        yield
                        if kind == 'q':
                            K.op('act', lambda: nc.scalar.activation(out=l_[:, 0:nt], in_=l_[:, 0:nt], func=AF.Exp, bias=self.lnqs[:], scale=-0.5), reads=[l_, self.lnqs], writes=[l_])
                        else:
                            K.op('act', lambda: nc.scalar.activation(out=l_[:, 0:nt], in_=l_[:, 0:nt], func=AF.Exp, scale=-0.5), reads=[l_], writes=[l_])
                        yield
                        K.op(self.gpe, lambda: self.gpv.tensor_tensor(out=o_[:, 0:nt], in0=c_[:, 0:nt], in1=l_[:, 0:nt], op=ALU.mult), reads=[c_, l_], writes=[o_])
                    elif fc < 12:
                        K.op('act', lambda: nc.scalar.activation(out=o_[:, 0:nt], in_=a_[:, 0:nt], func=AF.Silu, bias=self.pvt[:, 196 + fc:197 + fc], scale=1.0), reads=[a_, self.pvt], writes=[o_])
                    else:
                        K.op('act', lambda: nc.scalar.activation(out=o_[:, 0:nt], in_=a_[:, 0:nt], func=AF.Silu), reads=[a_], writes=[o_])
                    yield
                    if fm_row is not None:
                        self.store('sp', o_, self.FM[fm_row:fm_row + 128, t0:t0 + nt], o_[:, 0:nt])
                    if tm_col is not None:
                        nj = nt // 128
                        ps3 = self.next_half()
                        pb = ps3[:].bitcast(BF16)
                        fns = [lambda j=j: nc.tensor.transpose(out=pb[:, j * 128:(j + 1) * 128], in_=o_[:, j * 128:(j + 1) * 128], identity=self.idb[:]) for j in range(nj)]
                        K.op('pe', fns, reads=[o_, self.idb], writes=[ps3])
                        K.op('act', lambda: nc.scalar.copy(out=m_[:, 0:nj, :], in_=pb[:, 0:nj * 128].rearrange("p (j f) -> p j f", f=128)), reads=[ps3], writes=[m_])
                        self.free_half(ps3)
                        dst = self.TM[t0:t0 + nt, tm_col:tm_col + 128].rearrange("(j p) f -> p j f", p=128)
                        self.store('sp', m_, dst, m_[:, 0:nj, :])

                def fm_gens():
                    it = 0
                    for fc in range(36):
                        col0 = (2048 + fc * 128) if fc < 12 else (3616 + (fc - 12) * 128)
                        w_ = wf[fc % 2]
                        src = self.w_in[l][:, col0:col0 + 128].rearrange("(k p) f -> p k f", p=128)
                        self.load('pool', w_, w_[:], src)
                        if fc < 8:
                            kind, fm_row, tm_col = 'x', None, TM_X + fc * 128
                        elif fc < 10:
                            kind, fm_row, tm_col = 'B', FM_B + (fc - 8) * 128, TM_B + (fc - 8) * 128
                        elif fc < 12:
                            kind, fm_row, tm_col = 'C', FM_C + (fc - 10) * 128, None
                        elif fc < 20:
                            kind, fm_row, tm_col = 'q', FM_Q + (fc - 12) * 128, None
                        elif fc < 28:
                            kind, fm_row, tm_col = 'k', FM_K + (fc - 20) * 128, TM_K + (fc - 20) * 128
                        else:
                            kind, fm_row, tm_col = 'v', None, TM_V + (fc - 28) * 128
                        for (t0, nt, rl) in blks:
                            yield fm_iter(it, fc, w_, kind, fm_row, tm_col, t0, nt, rl)
                            it += 1

                run_pipeline(fm_gens(), 6)
                K.barrier()
            with ExitStack() as es2:
                NZ = 6
                wz = [self.sb(es2, "wz%d" % i, [128, 16, 256], BF16) for i in range(2)]
                zo = [self.sb(es2, "zo%d" % i, [128, 256], BF16) for i in range(NZ)]
                so = [self.sb(es2, "so%d" % i, [128, 64]) for i in range(NZ)]

                def z_iter(it, cb, c, w_, ncol):
                    ps = self.next_half()
                    fns = [lambda kc=kc: nc.tensor.matmul(ps[:, 0:ncol], lhsT=uT[:, kc, c * 128:(c + 1) * 128], rhs=w_[:, kc, 0:ncol], start=(kc == 0), stop=(kc == 15)) for kc in range(16)]
                    K.op('pe', fns, reads=[w_, uT], writes=[ps])
                    yield
                    if cb < 8:
                        z_ = zo[it % NZ]
                        K.op('act', lambda: nc.scalar.activation(out=z_[:], in_=ps[:, 0:256], func=AF.Silu), reads=[ps], writes=[z_])
                        self.free_half(ps)
                        self.store('sp', z_, self.ZS[c * 128:(c + 1) * 128, cb * 256:(cb + 1) * 256], z_[:])
                    else:
                        s_ = so[it % NZ]
                        K.op('act', lambda: nc.scalar.copy(out=s_[:], in_=ps[:, 0:64]), reads=[ps], writes=[s_])
                        self.free_half(ps)
                        self.store('sp', s_, self.SMALL[c * 128:(c + 1) * 128, :], s_[:])

                def z_gens():
                    it = 0
                    for cb in range(9):
                        w_ = wz[cb % 2]
                        if cb < 8:
                            src = self.w_in[l][:, cb * 256:(cb + 1) * 256].rearrange("(k p) f -> p k f", p=128)
                            self.load('pool', w_, w_[:], src)
                            ncol = 256
                        else:
                            self.load('pool', w_, w_[:, :, 0:32], self.w_in[l][:, 3584:3616].rearrange("(k p) f -> p k f", p=128))
                            self.load('pool', w_, w_[:, :, 32:64], self.w_in[l][:, 6688:6720].rearrange("(k p) f -> p k f", p=128))
                            ncol = 64
                        for c in range(NCH):
                            yield z_iter(it, cb, c, w_, ncol)
                            it += 1

                run_pipeline(z_gens(), 5)

    def phaseS(self, l, es):
        nc, K = self.nc, self.K
        sb = lambda name, shape, dt=F32: self.sb(es, name, shape, dt)
        self.dt_all = sb("dt_all", [128, NCH, 32])
        self.loga_all = sb("loga_all", [128, NCH, 32])
        self.beta_all = sb("beta_all", [128, NCH, 16])
        self.g_all = sb("g_all", [128, NCH, 16])
        self.dsk = sb("dsk", [128, 16])
        self.onecol = sb("onecol", [128, 1])
        K.op('dve', lambda: nc.vector.memset(self.onecol[:], 1.0), writes=[self.onecol])
        self.load('sp', self.dsk, self.dsk[:], dram_bcast(self.ssd_d[l], 128))
        with ExitStack() as e2:
            sm = self.sb(e2, "sm", [128, NCH, 64])
            self.load('sp', sm, sm[:], self.SMALL.rearrange("(c p) f -> p c f", p=128))
            bias = self.sb(e2, "sbias", [128, 48])
            alog = self.sb(e2, "salog", [128, 48])
            self.load('sp', bias, bias[:, 0:32], dram_bcast(self.ssd_dt_bias[l], 128))
            self.load('sp', bias, bias[:, 32:48], dram_bcast(self.gdn_dt_bias[l], 128))
            self.load('sp', alog, alog[:, 0:32], dram_bcast(self.ssd_a_log[l], 128))
            self.load('sp', alog, alog[:, 32:48], dram_bcast(self.gdn_a_log[l], 128))
            K.op('act', lambda: nc.scalar.activation(out=alog[:], in_=alog[:], func=AF.Exp), reads=[alog], writes=[alog])
            K.op('dve', lambda: nc.vector.tensor_scalar(out=alog[:], in0=alog[:], scalar1=-1.0, scalar2=None, op0=ALU.mult), reads=[alog], writes=[alog])
            tmp = self.sb(e2, "stmp", [128, NCH, 48])
            K.op('dve', lambda: nc.vector.tensor_tensor(out=tmp[:, :, 0:32], in0=sm[:, :, 0:32], in1=bias[:, 0:32].unsqueeze(1).broadcast_to([128, NCH, 32]), op=ALU.add), reads=[sm, bias], writes=[tmp])
            K.op('dve', lambda: nc.vector.tensor_tensor(out=tmp[:, :, 32:48], in0=sm[:, :, 48:64], in1=bias[:, 32:48].unsqueeze(1).broadcast_to([128, NCH, 16]), op=ALU.add), reads=[sm, bias], writes=[tmp])
            K.op('act', lambda: nc.scalar.activation(out=tmp[:], in_=tmp[:], func=AF.Exp), reads=[tmp], writes=[tmp])
            K.op('act', lambda: nc.scalar.activation(out=tmp[:], in_=tmp[:], func=AF.Ln, bias=self.onecol[:], scale=1.0), reads=[tmp, self.onecol], writes=[tmp])
            K.op('dve', lambda: nc.vector.tensor_copy(out=self.dt_all[:], in_=tmp[:, :, 0:32]), reads=[tmp], writes=[self.dt_all])
            K.op('dve', lambda: nc.vector.tensor_tensor(out=self.loga_all[:], in0=tmp[:, :, 0:32], in1=alog[:, 0:32].unsqueeze(1).broadcast_to([128, NCH, 32]), op=ALU.mult), reads=[tmp, alog], writes=[self.loga_all])
            K.op('dve', lambda: nc.vector.tensor_tensor(out=self.g_all[:], in0=tmp[:, :, 32:48], in1=alog[:, 32:48].unsqueeze(1).broadcast_to([128, NCH, 16]), op=ALU.mult), reads=[tmp, alog], writes=[self.g_all])
            K.op('act', lambda: nc.scalar.activation(out=self.beta_all[:], in_=sm[:, :, 32:48], func=AF.Sigmoid), reads=[sm], writes=[self.beta_all])
            K.barrier()

    def phaseB(self, l, last):
        nc, K = self.nc, self.K
        with ExitStack() as es:
            self.phaseS(l, es)
            sb = lambda name, shape, dt=F32: self.sb(es, name, shape, dt)
            B = self.Bt = type('T', (), {})()
            B.tm = [sb("tm%d" % i, [128, TM_COLS], BF16) for i in range(3)]
            B.fm = [sb("fm%d" % i, [128, 20, 128], BF16) for i in range(3)]
            B.yo1 = sb("yo1", [128, D])
            B.zs = sb("zs", [128, D], BF16)
            B.ex = sb("ex", [128, 48]); B.rhsL = sb("rhsL", [128, 8, 128], BF16); B.excb = sb("excb", [128, 128], BF16); B.E = sb("E", [128, 16, 128], BF16)
            B.cbm = sb("cbm", [128, 2, 128], BF16); B.MT = sb("MT", [128, 16, 128], BF16)
            B.xdt = sb("xdt", [128, 16, 64], BF16); B.xdd = sb("xdd", [128, 16, 64], BF16)
            B.ytmp = sb("ytmp", [128, 1024]); B.ysum = sb("ysum", [128, 1024]); B.yfin = sb("yfin", [128, 1024])
            B.hT = sb("hT", [128, 16, 64]); B.hTb = sb("hTb", [128, 16, 64], BF16)
            B.gex = [sb("gex%d" % i, [128, 3, 8]) for i in range(2)]; B.rhsG = sb("rhsG", [128, 8, 128], BF16); B.DT = sb("DT", [128, 8, 128])
            B.tA = sb("tA", [128, 8, 128]); B.tB = sb("tB", [128, 8, 128])
            B.nm = sb("nm", [128, 128], BF16); B.bdS = sb("bdS", [128, 128])
            g16 = lambda name: sb(name, [128, 8, 128], BF16)
            B.UD = g16("UD"); B.UO = g16("UO"); B.attnT = [g16("attnT0"), g16("attnT1")]; B.AD = g16("AD")
            B.P = [g16("P0"), g16("P1")]; B.X = [g16("X0"), g16("X1")]; B.XT = [g16("XT0"), g16("XT1")]
            B.TD = g16("TD"); B.VT = g16("VT"); B.V = g16("V"); B.V2T = g16("V2T"); B.Z = g16("Z"); B.Sf = g16("Sf")
            B.kg = g16("kg"); B.kendk = [g16("kendk0"), g16("kendk1")]; B.wT = [g16("wT0"), g16("wT1")]; B.vnew = g16("vnew")
            B.u_sb = [sb("u_sb%d" % i, [128, 8, 128]) for i in range(2)]; B.o = sb("o", [128, 8, 128]); B.ofin = sb("ofin", [128, 8, 128])
            B.Sg = sb("Sg", [128, 8, 128]); B.Sgb = g16("Sgb")
            NG = self.NG = 1
            self.PREW = getattr(Builder, "PREW", 20)
            for t_ in [B.rhsG, B.DT, B.tA, B.UD, B.UO, B.AD, B.TD, B.VT, B.V, B.V2T, B.Z, B.Sf, B.kg] + B.P + B.X + B.XT + B.gex + B.attnT + B.u_sb + B.wT + B.kendk:
                t_.g = [Buf(t_.name + "_g%d" % j, t_.t) for j in range(NG)]
            B.mT = sb("mT", [128, 128]); B.nbd = sb("nbd", [128, 128])
            K.op('dve', lambda: nc.vector.tensor_scalar(out=B.nbd[:], in0=self.cst[:, C_BD:C_BD + 128], scalar1=-1.0, scalar2=1.0, op0=ALU.mult, op1=ALU.add), reads=[self.cst], writes=[B.nbd])
            B.snw = sb("snw", [128, 1024]); B.gnw = sb("gnw", [128, 128])
            self.load('sp', B.snw, B.snw[:], dram_bcast(self.ssd_norm_w[l], 128))
            self.load('sp', B.gnw, B.gnw[:], dram_bcast(self.gdn_norm_w[l], 128))
            B.yg = sb("yg", [128, 1024]); B.ejunk = sb("ejunk", [128, 1024], BF16); B.est = sb("est", [128, 16]); B.est2 = sb("est2", [128, 16])
            B.ym = sb("ym", [128, D], BF16); B.wz = sb("wzg", [128, 8, 128]); B.wz2 = sb("wzg2", [128, 8, 128], BF16)
            for d in range(2):
                self.scan_pass(l, d, last)
                K.barrier()

    def scan_pass(self, l, d, last):
        nc, K, B = self.nc, self.K, self.Bt
        order = list(range(NCH)) if d == 0 else [1, 0] + list(range(NCH - 1, 1, -1))
        B.inc = self.cst[:, (C_INC0 if d == 0 else C_INC1):(C_INC0 if d == 0 else C_INC1) + 128]
        B.exc = self.cst[:, (C_EXC0 if d == 0 else C_EXC1):(C_EXC0 if d == 0 else C_EXC1) + 128]
        B.strictT = self.cst[:, (C_EXC1 if d == 0 else C_EXC0):(C_EXC1 if d == 0 else C_EXC0) + 128]
        B.ones = self.cst[:, C_ONES:C_ONES + 128]
        B.bd = self.cst[:, C_BD:C_BD + 128]
        K.op('dve', lambda: nc.vector.memset(B.hT[:], 0.0), writes=[B.hT])
        K.op('dve', lambda: nc.vector.memset(B.hTb[:], 0.0), writes=[B.hTb])
        K.op('dve', lambda: nc.vector.memset(B.Sg[:], 0.0), writes=[B.Sg])
        K.op('dve', lambda: nc.vector.memset(B.Sgb[:], 0.0), writes=[B.Sgb])

        K.op('dve', lambda: nc.vector.tensor_scalar(out=B.nm[:], in0=B.inc, scalar1=30000.0, scalar2=-30000.0, op0=ALU.mult, op1=ALU.add), reads=[self.cst], writes=[B.nm])
        K.op('dve', lambda: nc.vector.tensor_tensor(out=B.bdS[:], in0=B.strictT, in1=B.bd, op=ALU.mult), reads=[self.cst], writes=[B.bdS])
        K.op('dve', lambda: nc.vector.tensor_copy(out=B.excb[:], in_=B.exc), reads=[self.cst], writes=[B.excb])

        def issue_loads(i):
            c = order[i]
            tm_, fm_ = B.tm[i % 3], B.fm[i % 3]
            self.load('sp', tm_, tm_[:], self.TM[c * 128:(c + 1) * 128, :])
            self.load('sp', fm_, fm_[:], self.FM[:, c * 128:(c + 1) * 128].rearrange("(f p) t -> p f t", p=128))

        def run_weighted(gens):
            st = [[g, n, 0] for g, n in gens]
            while st:
                st.sort(key=lambda x: x[2] / float(x[1]))
                g = st[0]
                try:
                    next(g[0])
                    g[2] += 1
                except StopIteration:
                    st.remove(g)

        n = len(order)
        issue_loads(0)
        if n > 1:
            issue_loads(1)
        run_weighted([(self.gdn_pre(order[0], d, B.tm[0], B.fm[0], 0, hg, self.NG), 20) for hg in range(self.NG)])
        for i, c in enumerate(order):
            if i + 2 < n:
                issue_loads(i + 2)
            tm_, fm_ = B.tm[i % 3], B.fm[i % 3]
            need_out = not (last and c < 2)
            if d == 1 and need_out:
                self.load('sp', B.yo1, B.yo1[:], self.YO[c * 128:(c + 1) * 128, :])
                self.load('sp', B.zs, B.zs[:], self.ZS[c * 128:(c + 1) * 128, :])
            gens = [(self.ssd_gen(c, d, tm_, fm_, need_out), 7), (self.gdn_rec(c, d, fm_, need_out, i % 2), 3)]
            if i + 1 < n:
                for hg in range(self.NG):
                    gens.append((self.gdn_pre(order[i + 1], d, B.tm[(i + 1) % 3], B.fm[(i + 1) % 3], (i + 1) % 2, hg, self.NG), self.PREW))
            run_weighted(gens)
            if d == 1 and need_out:
                self.epilogue(c)

    def ssd_gen(self, c, d, tm_, fm_, need_out):
        nc, K, B = self.nc, self.K, self.Bt
        cst = self.cst
        loga = self.loga_all[:, c, d * 16:(d + 1) * 16]
        dtc = self.dt_all[:, c, d * 16:(d + 1) * 16]
        xs3 = tm_[:, TM_X:TM_X + 1024].rearrange("p (h q) -> p h q", q=64)
        ps = self.next_ps()
        K.op('pe', [lambda: nc.tensor.matmul(ps[:, 0:16], lhsT=B.inc, rhs=loga, start=True, stop=True),
                    lambda: nc.tensor.matmul(ps[:, 16:32], lhsT=B.exc, rhs=loga, start=True, stop=True),
                    lambda: nc.tensor.matmul(ps[:, 32:48], lhsT=B.ones, rhs=loga, start=True, stop=True)], reads=[cst, self.loga_all], writes=[ps])
        K.op('act', lambda: nc.scalar.activation(out=B.ex[:], in_=ps[:, 0:48], func=AF.Exp), reads=[ps], writes=[B.ex])
        expA, dend, cd = B.ex[:, 0:16], B.ex[:, 16:32], B.ex[:, 32:48]
        yield
        for hh in range(2):
            K.op(self.gpe, lambda hh=hh: self.gpv.tensor_tensor(out=B.rhsL[:], in0=B.inc.unsqueeze(1).broadcast_to([128, 8, 128]),
                                                             in1=loga[:, hh * 8:(hh + 1) * 8].unsqueeze(2).broadcast_to([128, 8, 128]), op=ALU.mult),
                 reads=[cst, self.loga_all], writes=[B.rhsL])
            ps = self.next_ps()
            rl2 = B.rhsL[:].rearrange("p h q -> p (h q)")
            nm4 = B.nm[:].unsqueeze(1).broadcast_to([128, 4, 128])
            fns = []
            for j in range(2):
                fns.append(lambda ps=ps, rl2=rl2, j=j: nc.tensor.matmul(ps[:, j * 512:(j + 1) * 512], lhsT=B.excb[:], rhs=rl2[:, j * 512:(j + 1) * 512], start=True, stop=False))
                fns.append(lambda ps=ps, j=j: nc.tensor.matmul(ps[:, j * 512:(j + 1) * 512].rearrange("p (h q) -> p h q", h=4), lhsT=self.idb[:], rhs=nm4, start=False, stop=True))
            K.op('pe', fns, reads=[B.excb, B.rhsL, B.nm, self.idb], writes=[ps])
            K.op('act', lambda ps=ps, hh=hh: nc.scalar.activation(out=B.E[:, hh * 8:(hh + 1) * 8, :].rearrange("p h q -> p (h q)"), in_=ps[:], func=AF.Exp), reads=[ps], writes=[B.E])
            yield
        ps = self.next_ps()
        K.op('pe', [lambda ps=ps, g=g: nc.tensor.matmul(ps[:, g * 128:(g + 1) * 128], lhsT=fm_[:, FM_B // 128 + g, :], rhs=fm_[:, FM_C // 128 + g, :], start=True, stop=True) for g in range(2)],
             reads=[fm_], writes=[ps])
        K.op('act', lambda ps=ps: nc.scalar.copy(out=B.cbm[:].rearrange("p g q -> p (g q)"), in_=ps[:, 0:256]), reads=[ps], writes=[B.cbm])
        yield
        K.op('dve', lambda: nc.vector.tensor_tensor(out=B.MT[:].rearrange("p (g r) q -> p g r q", g=2), in0=B.E[:].rearrange("p (g r) q -> p g r q", g=2),
                                                    in1=B.cbm[:].unsqueeze(2).broadcast_to([128, 2, 8, 128]), op=ALU.mult), reads=[B.E, B.cbm], writes=[B.MT])
        K.op(self.gpe, lambda: self.gpv.tensor_tensor(out=B.xdt[:], in0=xs3, in1=dtc.unsqueeze(2).broadcast_to([128, 16, 64]), op=ALU.mult), reads=[tm_, self.dt_all], writes=[B.xdt])
        K.op(self.gpe, lambda: self.gpv.tensor_tensor(out=B.xdd[:], in0=B.xdt[:], in1=dend.unsqueeze(2).broadcast_to([128, 16, 64]), op=ALU.mult), reads=[B.xdt, B.ex], writes=[B.xdd])
        yield
        psd = self.next_ps()
        K.op('pe', [lambda h=h: nc.tensor.matmul(psd[:, h * 64:(h + 1) * 64], lhsT=B.MT[:, h, :], rhs=B.xdt[:, h, :], start=True, stop=True) for h in range(16)], reads=[B.MT, B.xdt], writes=[psd])
        pso = self.next_ps()
        K.op('pe', [lambda g=g: nc.tensor.matmul(pso[:, g * 512:(g + 1) * 512], lhsT=fm_[:, FM_C // 128 + g, :], rhs=B.hTb[:, g * 8:(g + 1) * 8, :].rearrange("p h q -> p (h q)"), start=True, stop=True) for g in range(2)],
             reads=[fm_, B.hTb], writes=[pso])
        K.op('dve', lambda: nc.vector.tensor_tensor(out=B.ytmp[:].rearrange("p (h q) -> p h q", q=64), in0=pso[:].rearrange("p (h q) -> p h q", q=64), in1=expA.unsqueeze(2).broadcast_to([128, 16, 64]), op=ALU.mult),
             reads=[pso, B.ex], writes=[B.ytmp])
        K.op('dve', lambda: nc.vector.tensor_tensor(out=B.ysum[:], in0=B.ytmp[:], in1=psd[:], op=ALU.add), reads=[B.ytmp, psd], writes=[B.ysum])
        if need_out:
            if d == 0:
                K.op(self.gpe, lambda: self.gpv.tensor_tensor(out=B.ytmp[:].rearrange("p (h q) -> p h q", q=64), in0=xs3, in1=self.dsk[:].unsqueeze(2).broadcast_to([128, 16, 64]), op=ALU.mult),
                     reads=[tm_, self.dsk], writes=[B.ytmp])
                K.op('dve', lambda: nc.vector.tensor_tensor(out=B.yfin[:], in0=B.ytmp[:], in1=B.ysum[:], op=ALU.add), reads=[B.ytmp, B.ysum], writes=[B.yfin])
                self.store('sp', B.yfin, self.YO[c * 128:(c + 1) * 128, 0:1024], B.yfin[:])
            else:
                K.op('dve', lambda: nc.vector.tensor_tensor(out=B.yfin[:], in0=B.yo1[:, 0:1024], in1=B.ysum[:], op=ALU.add), reads=[B.yo1, B.ysum], writes=[B.yfin])
        yield
        pss = self.next_ps()
        K.op('pe', [lambda g=g: nc.tensor.matmul(pss[:, g * 512:(g + 1) * 512], lhsT=tm_[:, TM_B + g * 128:TM_B + (g + 1) * 128], rhs=B.xdd[:, g * 8:(g + 1) * 8, :].rearrange("p h q -> p (h q)"), start=True, stop=True) for g in range(2)],
             reads=[tm_, B.xdd], writes=[pss])
        K.op(self.gpe, lambda: self.gpv.tensor_tensor(out=B.hT[:], in0=B.hT[:], in1=cd.unsqueeze(2).broadcast_to([128, 16, 64]), op=ALU.mult), reads=[B.hT, B.ex], writes=[B.hT])
        K.op('dve', lambda: nc.vector.tensor_tensor(out=B.hT[:].rearrange("p h q -> p (h q)"), in0=B.hT[:].rearrange("p h q -> p (h q)"), in1=pss[:], op=ALU.add), reads=[B.hT, pss], writes=[B.hT])
        K.op('act', lambda: nc.scalar.copy(out=B.hTb[:], in_=B.hT[:]), reads=[B.hT], writes=[B.hTb])
        yield

    def mm8(self, ps, lhs, rhs, lhs_bufs, rhs_bufs, bf16_out=False, transpose=False, acc=None, hs=None):
        nc, K = self.nc, self.K
        if hs is None:
            hs = range(8)
        if bf16_out:
            pv = ps[:].bitcast(BF16)
        else:
            pv = ps[:]
        if transpose:
            fns = [lambda j=j, H=H: nc.tensor.transpose(out=pv[:, j * 128:(j + 1) * 128], in_=lhs(H), identity=self.idb[:]) for j, H in enumerate(hs)]
        elif acc is None:
            fns = [lambda j=j, H=H: nc.tensor.matmul(pv[:, j * 128:(j + 1) * 128], lhsT=lhs(H), rhs=rhs(H), start=True, stop=True) for j, H in enumerate(hs)]
        else:
            al, ar = acc
            fns = []
            for j, H in enumerate(hs):
                fns.append(lambda j=j, H=H: nc.tensor.matmul(pv[:, j * 128:(j + 1) * 128], lhsT=lhs(H), rhs=rhs(H), start=True, stop=False))
                fns.append(lambda j=j, H=H: nc.tensor.matmul(pv[:, j * 128:(j + 1) * 128], lhsT=al(H), rhs=ar(H), start=False, stop=True))
        K.op('pe', fns, reads=list(lhs_bufs) + list(rhs_bufs), writes=[ps])
        return pv

    def gdn_pre(self, c, d, tm_, fm_, slot, hg, ng):
        nc, K, B = self.nc, self.K, self.Bt
        cst = self.cst
        nh = 8 // ng
        H0 = hg * nh
        hs = range(H0, H0 + nh)
        W = nh * 128
        gc = self.g_all[:, c, d * 8 + H0:d * 8 + H0 + nh]
        bc = self.beta_all[:, c, d * 8 + H0:d * 8 + H0 + nh]
        k3 = tm_[:, TM_K + H0 * 128:TM_K + (H0 + nh) * 128].rearrange("p (h q) -> p h q", q=128)
        v3 = tm_[:, TM_V:TM_V + 1024].rearrange("p (h q) -> p h q", q=128)
        kT = lambda H: fm_[:, FM_K // 128 + H, :]
        qT = lambda H: fm_[:, FM_Q // 128 + H, :]
        G = lambda buf: buf.g[hg]
        hd = lambda buf: (lambda H: buf[:, H, :])
        idH = lambda H: self.idb[:]
        sl = lambda buf: buf[:, H0:H0 + nh, :]
        fl = lambda buf: buf[:, H0:H0 + nh, :].rearrange("p h q -> p (h q)")
        bcl = lambda ap2: ap2.unsqueeze(2).broadcast_to([128, nh, 128])
        bcm = lambda ap2: ap2.unsqueeze(1).broadcast_to([128, nh, 128])
        gex, attnT, u_sb, wT, kendk = B.gex[slot], B.attnT[slot], B.u_sb[slot], B.wT[slot], B.kendk[slot]
        ps = self.next_ps()
        K.op('pe', [lambda: nc.tensor.matmul(ps[:, 0:nh], lhsT=B.inc, rhs=gc, start=True, stop=True),
                    lambda: nc.tensor.matmul(ps[:, 8:8 + nh], lhsT=B.exc, rhs=gc, start=True, stop=True),
                    lambda: nc.tensor.matmul(ps[:, 16:16 + nh], lhsT=B.ones, rhs=gc, start=True, stop=True)], reads=[cst, self.g_all], writes=[ps])
        K.op('act', lambda: nc.scalar.activation(out=gex[:, :, H0:H0 + nh], in_=ps[:, 0:24].rearrange("p (a h) -> p a h", a=3)[:, :, 0:nh], func=AF.Exp), reads=[ps], writes=[G(gex)])
        expG, kend = gex[:, 0, H0:H0 + nh], gex[:, 1, H0:H0 + nh]
        yield
        K.op(self.gpe, lambda: self.gpv.tensor_tensor(out=sl(B.rhsG), in0=bcm(B.inc), in1=bcl(gc), op=ALU.mult), reads=[cst, self.g_all], writes=[G(B.rhsG)])
        ps = self.next_ps()
        rg2 = fl(B.rhsG)
        nm4 = B.nm[:].unsqueeze(1).broadcast_to([128, 4, 128])
        fns = []
        for j in range(W // 512):
            fns.append(lambda j=j: nc.tensor.matmul(ps[:, j * 512:(j + 1) * 512], lhsT=B.excb[:], rhs=rg2[:, j * 512:(j + 1) * 512], start=True, stop=False))
            fns.append(lambda j=j: nc.tensor.matmul(ps[:, j * 512:(j + 1) * 512].rearrange("p (h q) -> p h q", h=4), lhsT=self.idb[:], rhs=nm4, start=False, stop=True))
        K.op('pe', fns, reads=[B.excb, G(B.rhsG), B.nm, self.idb], writes=[ps])
        K.op('act', lambda: nc.scalar.activation(out=fl(B.DT), in_=ps[:, 0:W], func=AF.Exp), reads=[ps], writes=[G(B.DT)])
        yield
        ps = self.next_ps()
        self.mm8(ps, kT, kT, [fm_], [], hs=hs)
        K.op('dve', lambda: nc.vector.tensor_tensor(out=fl(B.tA), in0=ps[:, 0:W], in1=fl(B.DT), op=ALU.mult), reads=[ps, G(B.DT)], writes=[G(B.tA)])
        K.op('dve', lambda: nc.vector.tensor_tensor(out=sl(B.tA), in0=sl(B.tA), in1=bcl(bc), op=ALU.mult), reads=[G(B.tA), self.beta_all], writes=[G(B.tA)])
        yield
        K.op('dve', lambda: nc.vector.tensor_tensor(out=sl(B.UD), in0=sl(B.tA), in1=bcm(B.bdS[:]), op=ALU.mult), reads=[G(B.tA), B.bdS], writes=[G(B.UD)])
        K.op('dve', lambda: nc.vector.tensor_tensor(out=sl(B.UO), in0=sl(B.tA), in1=bcm(B.nbd[:]), op=ALU.mult), reads=[G(B.tA), B.nbd], writes=[G(B.UO)])
        yield
        ps = self.next_ps()
        self.mm8(ps, kT, qT, [fm_], [], hs=hs)
        K.op('dve', lambda: nc.vector.tensor_tensor(out=fl(attnT), in0=ps[:, 0:W], in1=fl(B.DT), op=ALU.mult), reads=[ps, G(B.DT)], writes=[G(attnT)])
        yield
        ps = self.next_ps()
        pv = self.mm8(ps, hd(B.UD), None, [G(B.UD), self.idb], [], bf16_out=True, transpose=True, hs=hs)
        K.op('act', lambda: nc.scalar.copy(out=fl(B.AD), in_=pv[:, 0:W]), reads=[ps], writes=[G(B.AD)])
        yield
        X, XT = B.UD, B.AD
        P = B.P[0]
        K.op(self.gpe, lambda: self.gpv.tensor_tensor(out=sl(P), in0=bcm(self.idb[:]), in1=sl(X), op=ALU.subtract), reads=[self.idb, G(X)], writes=[G(P)])

        def square(k, X, XT):
            XTn = B.XT[k % 2]
            ps = self.next_ps()
            self.mm8(ps, hd(X), hd(XT), [G(X)], [G(XT)], hs=hs)
            K.op('act', lambda: nc.scalar.copy(out=fl(XTn), in_=ps[:, 0:W]), reads=[ps], writes=[G(XTn)])
            Xn = None
            if k < 4:
                Xn = B.X[k % 2]
                ps2 = self.next_ps()
                self.mm8(ps2, hd(XT), hd(X), [G(XT)], [G(X)], hs=hs)
                K.op('act', lambda: nc.scalar.copy(out=fl(Xn), in_=ps2[:, 0:W]), reads=[ps2], writes=[G(Xn)])
            return Xn, XTn

        Xn, XTn = square(1, X, XT)
        yield
        for k in range(1, 5):
            ps3 = self.next_ps()
            self.mm8(ps3, hd(XTn), hd(P), [G(XTn), self.idb], [G(P)], acc=(idH, hd(P)), hs=hs)
            Pn = B.P[k % 2]
            K.op('act', lambda ps3=ps3, Pn=Pn: nc.scalar.copy(out=fl(Pn), in_=ps3[:, 0:W]), reads=[ps3], writes=[G(Pn)])
            P = Pn
            if k < 4:
                yield
                Xn, XTn = square(k + 1, Xn, XTn)
            yield
        SD = P
        ps = self.next_ps()
        self.mm8(ps, hd(SD), idH, [G(SD)], [self.idb], hs=hs)
        K.op('act', lambda ps=ps: nc.scalar.copy(out=fl(B.TD), in_=ps[:, 0:W]), reads=[ps], writes=[G(B.TD)])
        yield
        ps = self.next_ps()
        self.mm8(ps, hd(B.UO), hd(B.TD), [G(B.UO)], [G(B.TD)], hs=hs)
        K.op('act', lambda ps=ps: nc.scalar.activation(out=fl(B.VT), in_=ps[:, 0:W], func=AF.Copy, scale=-1.0), reads=[ps], writes=[G(B.VT)])
        ps2 = self.next_ps()
        self.mm8(ps2, hd(B.TD), hd(B.UO), [G(B.TD)], [G(B.UO)], hs=hs)
        K.op('act', lambda ps2=ps2: nc.scalar.copy(out=fl(B.V), in_=ps2[:, 0:W]), reads=[ps2], writes=[G(B.V)])
        yield
        ps = self.next_ps()
        self.mm8(ps, hd(B.V), hd(B.VT), [G(B.V)], [G(B.VT)], hs=hs)
        K.op('act', lambda ps=ps: nc.scalar.activation(out=fl(B.V2T), in_=ps[:, 0:W], func=AF.Copy, scale=-1.0), reads=[ps], writes=[G(B.V2T)])
        ps2 = self.next_ps()
        self.mm8(ps2, hd(B.VT), hd(SD), [G(B.VT), self.idb], [G(SD)], acc=(idH, hd(SD)), hs=hs)
        K.op('act', lambda ps2=ps2: nc.scalar.copy(out=fl(B.Z), in_=ps2[:, 0:W]), reads=[ps2], writes=[G(B.Z)])
        yield
        ps = self.next_ps()
        self.mm8(ps, hd(B.V2T), hd(B.Z), [G(B.V2T), self.idb], [G(B.Z)], acc=(idH, hd(B.Z)), hs=hs)
        K.op('act', lambda ps=ps: nc.scalar.copy(out=fl(B.Sf), in_=ps[:, 0:W]), reads=[ps], writes=[G(B.Sf)])
        yield
        K.op(self.gpe, lambda: self.gpv.tensor_tensor(out=sl(B.kg), in0=k3, in1=bcl(expG), op=ALU.mult), reads=[tm_, G(gex)], writes=[G(B.kg)])
        K.op(self.gpe, lambda: self.gpv.tensor_tensor(out=sl(kendk), in0=k3, in1=bcl(kend), op=ALU.mult), reads=[tm_, G(gex)], writes=[G(kendk)])
        yield
        ps = self.next_ps()
        self.mm8(ps, hd(B.Sf), lambda H: v3[:, H, :], [G(B.Sf)], [tm_], hs=hs)
        K.op('act', lambda ps=ps: nc.scalar.copy(out=fl(u_sb), in_=ps[:, 0:W]), reads=[ps], writes=[G(u_sb)])
        ps2 = self.next_ps()
        self.mm8(ps2, hd(B.kg), hd(B.Sf), [G(B.kg)], [G(B.Sf)], hs=hs)
        K.op('act', lambda ps2=ps2: nc.scalar.copy(out=fl(wT), in_=ps2[:, 0:W]), reads=[ps2], writes=[G(wT)])
        yield

    def gdn_rec(self, c, d, fm_, need_out, slot):
        nc, K, B = self.nc, self.K, self.Bt
        bc = self.beta_all[:, c, d * 8:(d + 1) * 8]
        qT = lambda H: fm_[:, FM_Q // 128 + H, :]
        hd = lambda buf: (lambda H: buf[:, H, :])
        fl = lambda buf: buf[:].rearrange("p h q -> p (h q)")
        f3 = lambda ap: ap.rearrange("p (h q) -> p h q", q=128)
        gex_, attnT_, u_sb_, wT_, kendk_ = B.gex[slot], B.attnT[slot], B.u_sb[slot], B.wT[slot], B.kendk[slot]
        expG, cdG = gex_[:, 0, :], gex_[:, 2, :]

        class _Multi:
            def __init__(self, buf):
                self.buf = buf
            def __getitem__(self, k):
                return self.buf[k]
        gex, attnT, u_sb, wT, kendk = gex_, attnT_, u_sb_, wT_, kendk_
        GG = lambda buf: list(buf.g)
        ps = self.next_ps()
        self.mm8(ps, hd(wT), hd(B.Sgb), GG(wT), [B.Sgb])
        K.op('dve', lambda: nc.vector.tensor_tensor(out=fl(B.tB), in0=fl(u_sb), in1=ps[:], op=ALU.subtract), reads=GG(u_sb) + [ps], writes=[B.tB])
        K.op('dve', lambda: nc.vector.tensor_tensor(out=B.vnew[:], in0=B.tB[:], in1=bc.unsqueeze(2).broadcast_to([128, 8, 128]), op=ALU.mult), reads=[B.tB, self.beta_all], writes=[B.vnew])
        yield
        ps = self.next_ps()
        self.mm8(ps, qT, hd(B.Sgb), [fm_], [B.Sgb])
        K.op('dve', lambda: nc.vector.tensor_tensor(out=B.tB[:], in0=f3(ps[:]), in1=expG.unsqueeze(2).broadcast_to([128, 8, 128]), op=ALU.mult), reads=[ps] + GG(gex), writes=[B.tB])
        ps2 = self.next_ps()
        self.mm8(ps2, hd(attnT), hd(B.vnew), GG(attnT), [B.vnew])
        K.op('dve', lambda: nc.vector.tensor_tensor(out=fl(B.o), in0=fl(B.tB), in1=ps2[:], op=ALU.add), reads=[B.tB, ps2], writes=[B.o])
        if need_out:
            if d == 0:
                self.store('sp', B.o, self.YO[c * 128:(c + 1) * 128, 1024:2048], fl(B.o))
            else:
                K.op('dve', lambda: nc.vector.tensor_tensor(out=fl(B.ofin), in0=fl(B.o), in1=B.yo1[:, 1024:2048], op=ALU.add), reads=[B.o, B.yo1], writes=[B.ofin])
        yield
        ps = self.next_ps()
        self.mm8(ps, hd(kendk), hd(B.vnew), GG(kendk), [B.vnew])
        K.op(self.gpe, lambda: self.gpv.tensor_tensor(out=B.Sg[:], in0=B.Sg[:], in1=cdG.unsqueeze(2).broadcast_to([128, 8, 128]), op=ALU.mult), reads=[B.Sg] + GG(gex), writes=[B.Sg])
        K.op('dve', lambda: nc.vector.tensor_tensor(out=fl(B.Sg), in0=fl(B.Sg), in1=ps[:], op=ALU.add), reads=[B.Sg, ps], writes=[B.Sg])
        K.op('act', lambda: nc.scalar.copy(out=B.Sgb[:], in_=B.Sg[:]), reads=[B.Sg], writes=[B.Sgb])
        yield

    def epilogue(self, c):
        nc, K, B = self.nc, self.K, self.Bt
        K.op(self.gpe, lambda: self.gpv.tensor_tensor(out=B.yg[:], in0=B.yfin[:], in1=B.zs[:, 0:1024], op=ALU.mult), reads=[B.yfin, B.zs], writes=[B.yg])
        for g in range(2):
            K.op('act', lambda g=g: nc.scalar.activation(out=B.ejunk[:, 0:512], in_=B.yg[:, g * 512:(g + 1) * 512], func=AF.Square, accum_out=B.est[:, g:g + 1]), reads=[B.yg], writes=[B.ejunk, B.est])
        K.op('dve', lambda: nc.vector.tensor_tensor(out=B.wz[:], in0=B.ofin[:], in1=B.ofin[:], op=ALU.mult), reads=[B.ofin], writes=[B.wz])
        K.op('dve', lambda: nc.vector.tensor_reduce(out=B.est[:, 2:10], in_=B.wz[:], axis=AX.X, op=ALU.add), reads=[B.wz], writes=[B.est])
        K.op('act', lambda: nc.scalar.activation(out=B.est2[:, 0:2], in_=B.est[:, 0:2], func=AF.Ln, bias=self.epsb[:], scale=1.0 / 512), reads=[B.est, self.epsb], writes=[B.est2])
        K.op('act', lambda: nc.scalar.activation(out=B.est2[:, 2:10], in_=B.est[:, 2:10], func=AF.Ln, bias=self.epsb[:], scale=1.0 / 128), reads=[B.est, self.epsb], writes=[B.est2])
        K.op('act', lambda: nc.scalar.activation(out=B.est2[:, 0:10], in_=B.est2[:, 0:10], func=AF.Exp, scale=-0.5), reads=[B.est2], writes=[B.est2])
        for g in range(2):
            K.op('dve', lambda g=g: nc.vector.scalar_tensor_tensor(out=B.ym[:, g * 512:(g + 1) * 512], in0=B.yg[:, g * 512:(g + 1) * 512], scalar=B.est2[:, g:g + 1], in1=B.snw[:, g * 512:(g + 1) * 512], op0=ALU.mult, op1=ALU.mult),
                 reads=[B.yg, B.est2, B.snw], writes=[B.ym])
        K.op(self.gpe, lambda: self.gpv.tensor_tensor(out=B.wz2[:], in0=B.zs[:, 1024:2048].rearrange("p (h q) -> p h q", q=128), in1=B.gnw[:].unsqueeze(1).broadcast_to([128, 8, 128]), op=ALU.mult), reads=[B.zs, B.gnw], writes=[B.wz2])
        K.op('dve', lambda: nc.vector.tensor_tensor(out=B.ofin[:], in0=B.ofin[:], in1=B.est2[:, 2:10].unsqueeze(2).broadcast_to([128, 8, 128]), op=ALU.mult), reads=[B.ofin, B.est2], writes=[B.ofin])
        K.op('dve', lambda: nc.vector.tensor_tensor(out=B.ym[:, 1024:2048].rearrange("p (h q) -> p h q", q=128), in0=B.ofin[:], in1=B.wz2[:], op=ALU.mult), reads=[B.ofin, B.wz2], writes=[B.ym])
        self.store('sp', B.ym, self.YM[c * 128:(c + 1) * 128, :], B.ym[:])

    def phaseC(self, l, hsrc, last):
        nc, K = self.nc, self.K
        with ExitStack() as es:
            sb = lambda name, shape, dt=F32: self.sb(es, name, shape, dt)
            wo = sb("wo", [128, 16, D], BF16)
            for kc in range(16):
                self.load('pool', wo, wo[:, kc, :], self.w_out[l][kc * 128:(kc + 1) * 128, :])
            gp = [sb("gp%d" % r, [128, D]) for r in range(2)]
            pw = sb("pwb", [128, D])
            self.load('sp', pw, pw[:], dram_bcast(self.post_w[l], 128))
            for r in range(2):
                self.load('sp', gp[r], gp[r][:], dram_bcast(self.modv[r, 2 * D:3 * D], 128))
                K.op('dve', lambda r=r: nc.vector.tensor_tensor(out=gp[r][:], in0=gp[r][:], in1=pw[:], op=ALU.mult), reads=[gp[r], pw], writes=[gp[r]])
            ymc = [sb("ymc%d" % i, [128, D], BF16) for i in range(2)]
            hc = [sb("hc%d" % i, [128, D]) for i in range(2)]
            ymT = sb("ymT", [128, 16, 128], BF16)
            cj = sb("cjunk", [128, D], BF16)
            cst_ = sb("cst_", [128, 8])
            res = [sb("res%d" % i, [128, D]) for i in range(2)]
            chunks = list(range(2, NCH)) if last else list(range(NCH))
            ymTs = [ymT, sb("ymT2", [128, 16, 128], BF16)]
            csts = [cst_, sb("cst2_", [128, 8])]

            def issue(i):
                c = chunks[i]
                self.load('sp', ymc[i % 2], ymc[i % 2][:], self.YM[c * 128:(c + 1) * 128, :])
                self.load('sp', hc[i % 2], hc[i % 2][:], hsrc[c * 128:(c + 1) * 128, :])

            pos = [sb("po%d" % i, [128, D]) for i in range(2)]

            def c_iter(i, c):
                y_, h_, r_, yT, st_, po = ymc[i % 2], hc[i % 2], res[i % 2], ymTs[i % 2], csts[i % 2], pos[i % 2]
                r = 1 if c < 2 else 0
                for half in range(2):
                    ps = self.next_half()
                    pb = ps[:].bitcast(BF16)
                    K.op('pe', [lambda j=j, pb=pb, half=half: nc.tensor.transpose(out=pb[:, j * 128:(j + 1) * 128], in_=y_[:, (half * 8 + j) * 128:(half * 8 + j + 1) * 128], identity=self.idb[:]) for j in range(8)],
                         reads=[y_, self.idb], writes=[ps])
                    K.op('act', lambda pb=pb, half=half: nc.scalar.copy(out=yT[:, half * 8:(half + 1) * 8, :].rearrange("p k t -> p (k t)"), in_=pb[:, 0:1024]), reads=[ps], writes=[yT])
                    self.free_half(ps)
                yield
                for cb in range(4):
                    ps = self.next_half()
                    fns = [lambda ps=ps, cb=cb, kc=kc: nc.tensor.matmul(ps[:, 0:512], lhsT=yT[:, kc, :], rhs=wo[:, kc, cb * 512:(cb + 1) * 512], start=(kc == 0), stop=(kc == 15)) for kc in range(16)]
                    K.op('pe', fns, reads=[yT, wo], writes=[ps])
                    K.op('act', lambda ps=ps, cb=cb: nc.scalar.activation(out=cj[:, cb * 512:(cb + 1) * 512], in_=ps[:, 0:512], func=AF.Square, accum_out=st_[:, cb:cb + 1]), reads=[ps], writes=[cj, st_])
                    K.op('act', lambda ps=ps, cb=cb: nc.scalar.copy(out=po[:, cb * 512:(cb + 1) * 512], in_=ps[:, 0:512]), reads=[ps], writes=[po])
                    self.free_half(ps)
                    yield
                K.op('dve', lambda: nc.vector.tensor_reduce(out=st_[:, 4:5], in_=st_[:, 0:4], axis=AX.X, op=ALU.add), reads=[st_], writes=[st_])
                K.op('act', lambda: nc.scalar.activation(out=st_[:, 5:6], in_=st_[:, 4:5], func=AF.Ln, bias=self.epsb[:], scale=1.0 / D), reads=[st_, self.epsb], writes=[st_])
                K.op('act', lambda: nc.scalar.activation(out=st_[:, 6:7], in_=st_[:, 5:6], func=AF.Exp, scale=-0.5), reads=[st_], writes=[st_])
                yield
                K.op('dve', lambda: nc.vector.scalar_tensor_tensor(out=r_[:], in0=po[:], scalar=st_[:, 6:7], in1=gp[r][:], op0=ALU.mult, op1=ALU.mult), reads=[po, st_, gp[r]], writes=[r_])
                yield
                K.op('dve', lambda: nc.vector.tensor_tensor(out=r_[:], in0=r_[:], in1=h_[:], op=ALU.add), reads=[r_, h_], writes=[r_])
                if last:
                    dst = self.out[(c - 2) * 128:(c - 1) * 128, :]
                else:
                    dst = self.H[c * 128:(c + 1) * 128, :]
                self.store('pool', r_, dst, r_[:])
                if i + 2 < len(chunks):
                    issue(i + 2)

            def c_gens():
                for i, c in enumerate(chunks):
                    yield c_iter(i, c)

            issue(0)
            issue(1)
            run_pipeline(c_gens(), 2)


def make_consts():
    i = np.arange(128)
    c = np.zeros((128, NCONST), np.float32)
    c[:, C_ID:C_ID + 128] = np.eye(128)
    c[:, C_INC0:C_INC0 + 128] = (i[:, None] <= i[None, :])
    c[:, C_INC1:C_INC1 + 128] = (i[:, None] >= i[None, :])
    c[:, C_EXC0:C_EXC0 + 128] = (i[:, None] > i[None, :])
    c[:, C_EXC1:C_EXC1 + 128] = (i[:, None] < i[None, :])
    c[:, C_ONES:C_ONES + 128] = 1.0
    c[:, C_BD:C_BD + 128] = (i[:, None] // 32 == i[None, :] // 32)
    return c


def make_pv(inp):
    pv = np.zeros((DEPTH, 128, NPV), np.float32)
    for l in range(DEPTH):
        pv[l, :, 0:16] = inp['pre_norm_w'][l].reshape(16, 128).T
        cw = np.concatenate([inp['conv_ssd_w'][l], inp['conv_gdn_w'][l]], axis=1)
        pv[l, :, 16:196] = cw.reshape(5, 36, 128).transpose(2, 1, 0).reshape(128, 180)
        pv[l, :, 196:208] = inp['conv_ssd_b'][l].reshape(12, 128).T
    return pv


def make_in_maps(inp, cores):
    shared = {
        'pv': make_pv(inp), 'consts': make_consts(),
        'w_ada': np.ascontiguousarray(inp['w_ada']), 'b_ada': np.ascontiguousarray(inp['b_ada']),
        'post_norm_w': np.ascontiguousarray(inp['post_norm_w']), 'w_in': np.ascontiguousarray(inp['w_in']),
        'ssd_a_log': np.ascontiguousarray(inp['ssd_a_log']).reshape(DEPTH, 32),
        'ssd_dt_bias': np.ascontiguousarray(inp['ssd_dt_bias']).reshape(DEPTH, 32),
        'ssd_d': np.ascontiguousarray(inp['ssd_d']), 'ssd_norm_w': np.ascontiguousarray(inp['ssd_norm_w']),
        'gdn_a_log': np.ascontiguousarray(inp['gdn_a_log']).reshape(DEPTH, 16),
        'gdn_dt_bias': np.ascontiguousarray(inp['gdn_dt_bias']).reshape(DEPTH, 16),
        'gdn_norm_w': np.ascontiguousarray(inp['gdn_norm_w']), 'w_out': np.ascontiguousarray(inp['w_out']),
    }
    maps = []
    for b in cores:
        m = dict(shared)
        m['h0'] = np.ascontiguousarray(np.concatenate([inp['ctx'][b], inp['x'][b]], axis=0))
        c2 = np.stack([inp['c'][b], inp['c_ctx']])
        m['c2t'] = np.ascontiguousarray(c2.reshape(2, 16, 128).transpose(2, 1, 0).reshape(128, 32))
        maps.append(m)
    return maps


def kernel(**inputs):
    inp = {k: np.asarray(v) for k, v in inputs.items()}
    nc = Builder().build()
    maps = make_in_maps(inp, list(range(8)))
    res = run_bass_kernel_spmd(nc, maps, core_ids=list(range(8)))
    return np.stack([r['out'] for r in res.results], axis=0).astype(np.float32)
```

```python
import numpy as np
from contextlib import ExitStack
import concourse.bass as bass
import concourse.mybir as mybir
from concourse.bass_utils import run_bass_kernel_spmd

F32, BF16 = mybir.dt.float32, mybir.dt.bfloat16
AF = mybir.ActivationFunctionType
ALU = mybir.AluOpType
AX = mybir.AxisListType

D = 2048
T = 4352
NCH = 34
DEPTH = 4
IN_DIM = 6720
EPS = 1e-6
NPV = 208
C_ID, C_INC0, C_INC1, C_EXC0, C_EXC1, C_ONES, C_BD = 0, 128, 256, 384, 512, 640, 768
NCONST = 896
FM_B, FM_C, FM_Q, FM_K, FM_ROWS = 0, 256, 512, 1536, 2560
TM_X, TM_B, TM_K, TM_V, TM_COLS = 0, 1024, 1280, 2304, 3328


class Buf:
    __slots__ = ('name', 'w', 'r', 't', 'g')

    def __init__(self, name, t=None):
        self.name = name
        self.w = None
        self.r = {}
        self.t = t

    def __getitem__(self, k):
        return self.t[k]


class HalfBuf(Buf):
    __slots__ = ('off',)

    def __init__(self, name, t, off):
        Buf.__init__(self, name, t)
        self.off = off

    def __getitem__(self, k):
        if isinstance(k, tuple):
            p, c = k[0], k[1]
            start = (c.start or 0) + self.off
            stop = (c.stop if c.stop is not None else 512) + self.off
            return self.t[p, start:stop]
        return self.t[:, self.off:self.off + 512]


def run_pipeline(gens, depth):
    active = []
    it = iter(gens)
    done = False
    while True:
        if not done and len(active) < depth:
            try:
                active.append(next(it))
            except StopIteration:
                done = True
        if not active:
            if done:
                break
            continue
        for g in list(active):
            try:
                next(g)
            except StopIteration:
                active.remove(g)


class Ctx:
    NS = 8

    def __init__(self, nc):
        self.nc = nc
        self.eng = {'pe': nc.tensor, 'act': nc.scalar, 'dve': nc.vector, 'gp': nc.gpsimd, 'sp': nc.sync, 'pool': nc.gpsimd}
        self.inorder = ('pe', 'act', 'dve', 'gp')
        self.inorder_skip = ('pe',)
        self.dmaq = ('sp', 'pool')
        self.sem = {e: nc.alloc_semaphore('s_' + e) for e in self.inorder}
        self.cnt = {e: 0 for e in self.inorder}
        self.dsem = {q: [nc.alloc_semaphore('d_%s%d' % (q, i)) for i in range(self.NS)] for q in self.dmaq}
        self.dcnt = {q: 0 for q in self.dmaq}
        self.waited = {}
        self.nins = 0

    def _wait(self, e, tok):
        if tok is None:
            return
        sem, val, owner = tok
        if owner == e and e in self.inorder_skip:
            return
        key = (e, id(sem))
        if self.waited.get(key, 0) >= val:
            return
        self.eng[e].wait_ge(sem, val)
        self.waited[key] = val

    def op(self, e, fns, reads=(), writes=()):
        if not isinstance(fns, (list, tuple)):
            fns = [fns]
        for b in reads:
            self._wait(e, b.w)
        for b in writes:
            self._wait(e, b.w)
            for t in b.r.values():
                self._wait(e, t)
        self.nins += len(fns)
        if e in self.dmaq:
            n = self.dcnt[e]
            slot = n % self.NS
            rnd = n // self.NS
            sem = self.dsem[e][slot]
            if rnd > 0:
                self._wait(e, (sem, 16 * rnd, None))
            for f in fns[:-1]:
                f()
            fns[-1]().then_inc(sem, 16)
            tok = (sem, 16 * (rnd + 1), e)
            self.dcnt[e] = n + 1
        else:
            for f in fns[:-1]:
                f()
            fns[-1]().then_inc(self.sem[e], 1)
            self.cnt[e] += 1
            tok = (self.sem[e], self.cnt[e], e)
        for b in reads:
            old = b.r.get(id(tok[0]))
            if old is None or old[1] < tok[1]:
                b.r[id(tok[0])] = tok
        for b in writes:
            b.w = tok
            b.r = {}
        return tok

    def all_tokens(self):
        toks = [(self.sem[e], self.cnt[e], e) for e in self.inorder if self.cnt[e] > 0]
        for q in self.dmaq:
            n = self.dcnt[q]
            for slot in range(self.NS):
                uses = (n - slot + self.NS - 1) // self.NS if n > slot else 0
                if uses > 0:
                    toks.append((self.dsem[q][slot], 16 * uses, q))
        return toks

    def barrier(self):
        toks = self.all_tokens()
        for e in self.eng:
            for t in toks:
                if t[2] == e and e in self.inorder:
                    continue
                self._wait(e, t)


def dram_bcast(ap, nparts):
    n = 1
    for s in ap.shape:
        n *= s
    return bass.AP(ap.tensor, ap.offset, [[0, nparts], [1, n]])


def tok_blocks():
    blks = [(0, 256, 256)]
    for i in range(8):
        blks.append((256 + 512 * i, 512, 64))
    return blks


class Builder:
    def __init__(self, n_layers=DEPTH, debug=False, stop_after=None, force_last=False):
        self.force_last = force_last
        self.n_layers = n_layers
        self.debug = debug
        self.stop_after = stop_after
        nc = self.nc = bass.Bass("TRN2", target_bir_lowering=False)
        self.K = Ctx(nc)
        self.gpe = getattr(Builder, 'GPE', 'gp')
        self.gpv = nc.gpsimd if self.gpe == 'gp' else nc.vector
        di = lambda name, shape: nc.dram_tensor(name, shape, F32, kind="ExternalInput").ap()
        self.h0 = di("h0", [T, D])
        self.c2t = di("c2t", [128, 32])
        self.pv = di("pv", [DEPTH, 128, NPV])
        self.consts = di("consts", [128, NCONST])
        self.w_ada = di("w_ada", [DEPTH, D, 3 * D])
        self.b_ada = di("b_ada", [DEPTH, 3 * D])
        self.post_w = di("post_norm_w", [DEPTH, D])
        self.w_in = di("w_in", [DEPTH, D, IN_DIM])
        self.ssd_a_log = di("ssd_a_log", [DEPTH, 32])
        self.ssd_dt_bias = di("ssd_dt_bias", [DEPTH, 32])
        self.ssd_d = di("ssd_d", [DEPTH, 16])
        self.ssd_norm_w = di("ssd_norm_w", [DEPTH, 1024])
        self.gdn_a_log = di("gdn_a_log", [DEPTH, 16])
        self.gdn_dt_bias = di("gdn_dt_bias", [DEPTH, 16])
        self.gdn_norm_w = di("gdn_norm_w", [DEPTH, 128])
        self.w_out = di("w_out", [DEPTH, D, D])
        self.out = nc.dram_tensor("out", [4096, D], F32, kind="ExternalOutput").ap()
        self.ext_set = getattr(Builder, 'EXT_SET', ())
        ds = lambda name, shape, dt: nc.dram_tensor(name, shape, dt, kind=("ExternalOutput" if (debug or name in self.ext_set) else "Internal")).ap()
        self.modv = ds("modv", [2, 3 * D], F32)
        self.FM = ds("FM", [FM_ROWS, T], BF16)
        self.TM = ds("TM", [T, TM_COLS], BF16)
        self.ZS = ds("ZS", [T, D], BF16)
        self.SMALL = ds("SMALL", [T, 64], F32)
        self.YO = ds("YO", [T, D], F32)
        self.YM = ds("YM", [T, D], BF16)
        self.H = ds("H", [T, D], F32)
        if debug:
            self.UT = ds("UT", [D, T], BF16)

    def sb(self, es, name, shape, dt=F32):
        self.uid = getattr(self, 'uid', 0) + 1
        name = "%s_%d" % (name, self.uid)
        return Buf(name, es.enter_context(self.nc.sbuf_tensor(name, shape, dt)))

    def load(self, q, dst, dst_ap, src_ap, **kw):
        eng = self.K.eng[q]
        return self.K.op(q, lambda: eng.dma_start(out=dst_ap, in_=src_ap, **kw), writes=[dst])

    def store(self, q, src, dst_ap, src_ap, **kw):
        eng = self.K.eng[q]
        return self.K.op(q, lambda: eng.dma_start(out=dst_ap, in_=src_ap, **kw), reads=[src])

    def build(self):
        nc, K = self.nc, self.K
        with ExitStack() as es:
            self.ps = [Buf("ps%d" % i, es.enter_context(nc.psum_tensor("ps%d" % i, [128, 1024], F32))) for i in range(4)]
            self.psi = 0
            self.psh = [HalfBuf("psh%d" % i, self.ps[i // 2].t, (i % 2) * 512) for i in range(8)]
            self.psfree = list(self.psh)
            self.cst = self.sb(es, "cst", [128, NCONST])
            self.load('sp', self.cst, self.cst[:], self.consts)
            self.idb = self.sb(es, "idb", [128, 128], BF16)
            K.op('dve', lambda: nc.vector.tensor_copy(out=self.idb[:], in_=self.cst[:, C_ID:C_ID + 128]), reads=[self.cst], writes=[self.idb])
            self.onesb = self.sb(es, "onesb", [128, 128], BF16)
            K.op('dve', lambda: nc.vector.tensor_copy(out=self.onesb[:], in_=self.cst[:, C_ONES:C_ONES + 128]), reads=[self.cst], writes=[self.onesb])
            self.epsb = self.sb(es, "epsb", [128, 1])
            K.op('dve', lambda: nc.vector.memset(self.epsb[:], EPS), writes=[self.epsb])
            self.lnqs = self.sb(es, "lnqs", [128, 1])
            K.op('dve', lambda: nc.vector.memset(self.lnqs[:], float(np.log(128.0 ** -0.5))), writes=[self.lnqs])
            for l in range(self.n_layers):
                self.layer(l)
            K.barrier()
        return nc

    def next_ps(self):
        p = self.ps[self.psi % 4]
        self.psi += 1
        return p

    def next_half(self):
        assert self.psfree, "PSUM half-slots exhausted"
        return self.psfree.pop(0)

    def free_half(self, p):
        self.psfree.append(p)

    def layer(self, l):
        K = self.K
        last = (l == DEPTH - 1) or (self.force_last and l == self.n_layers - 1)
        hsrc = self.h0 if l == 0 else self.H
        with ExitStack() as es:
            self.pvt = self.sb(es, "pvt", [128, NPV])
            self.load('sp', self.pvt, self.pvt[:], self.pv[l])
            self.phase0(l)
            K.barrier()
            if self.stop_after == 'p0':
                return
            self.phaseA(l, hsrc)
            K.barrier()
            if self.stop_after == 'A':
                return
            self.phaseB(l, last)
            K.barrier()
            if self.stop_after == 'B':
                return
            self.phaseC(l, hsrc, last)
            K.barrier()

    def phase0(self, l):
        nc, K = self.nc, self.K
        with ExitStack() as es:
            sct = self.sb(es, "sct", [128, 32])
            self.load('sp', sct, sct[:], self.c2t)
            K.op('act', lambda: nc.scalar.activation(out=sct[:], in_=sct[:], func=AF.Silu), reads=[sct], writes=[sct])
            bia = self.sb(es, "bia", [2, 3 * D])
            self.load('sp', bia, bia[:], dram_bcast(self.b_ada[l], 2))
            modsb = self.sb(es, "modsb", [2, 3 * D])
            wts = [self.sb(es, "wada%d" % i, [128, 16, 512]) for i in range(2)]
            sct3 = sct[:].rearrange("p (k r) -> p k r", r=2)
            for cb in range(12):
                wt = wts[cb % 2]
                src = self.w_ada[l][:, cb * 512:(cb + 1) * 512].rearrange("(k p) f -> p k f", p=128)
                self.load('sp', wt, wt[:], src)
                ps = self.next_ps()
                fns = []
                for kc in range(16):
                    fns.append(lambda kc=kc, ps=ps, wt=wt: nc.tensor.matmul(ps[0:2, 0:512], lhsT=sct3[:, kc, :], rhs=wt[:, kc, :], start=(kc == 0), stop=(kc == 15)))
                K.op('pe', fns, reads=[sct, wt], writes=[ps])
                K.op('dve', lambda cb=cb, ps=ps: nc.vector.tensor_tensor(out=modsb[:, cb * 512:(cb + 1) * 512], in0=ps[0:2, 0:512], in1=bia[:, cb * 512:(cb + 1) * 512], op=ALU.add),
                     reads=[ps, bia], writes=[modsb])
            self.store('sp', modsb, self.modv, modsb[:])

    def phaseA(self, l, hsrc):
        nc, K = self.nc, self.K
        with ExitStack() as es:
            uT = self.sb(es, "uT", [128, 16, T], BF16)
            mraw = self.sb(es, "mraw", [128, 2, 2, 16])
            for r in range(2):
                for w in range(2):
                    src = bass.AP(self.modv.tensor, self.modv.offset + r * 3 * D + w * D, [[1, 128], [128, 16]])
                    self.load('sp', mraw, mraw[:, r, w, :], src, allow_slow_non_contiguous=True)
            mA = self.sb(es, "mA", [128, 2, 16])
            for r in range(2):
                K.op('dve', lambda r=r: nc.vector.scalar_tensor_tensor(out=mA[:, r, :], in0=mraw[:, r, 1, :], scalar=1.0, in1=self.pvt[:, 0:16], op0=ALU.add, op1=ALU.mult),
                     reads=[mraw, self.pvt], writes=[mA])
            with ExitStack() as es0:
                NH = 3
                hx = [self.sb(es0, "hx%d" % i, [128, D]) for i in range(NH)]
                junk = self.sb(es0, "junk", [128, D], BF16)
                xn = [self.sb(es0, "xn%d" % i, [128, D], BF16) for i in range(NH)]
                st = [self.sb(es0, "st%d" % i, [128, 4]) for i in range(NH)]

                def a0_load(c):
                    h_ = hx[c % NH]
                    self.load('sp', h_, h_[:], hsrc[c * 128:(c + 1) * 128, :])

                def a0_iter(c):
                    r = 1 if c < 2 else 0
                    h_, x_, s_ = hx[c % NH], xn[c % NH], st[c % NH]
                    K.op('act', lambda: nc.scalar.activation(out=junk[:], in_=h_[:], func=AF.Square, accum_out=s_[:, 0:1]), reads=[h_], writes=[junk, s_])
                    yield
                    K.op('act', lambda: nc.scalar.activation(out=s_[:, 1:2], in_=s_[:, 0:1], func=AF.Ln, bias=self.epsb[:], scale=1.0 / D), reads=[s_, self.epsb], writes=[s_])
                    K.op('act', lambda: nc.scalar.activation(out=s_[:, 2:3], in_=s_[:, 1:2], func=AF.Exp, scale=-0.5), reads=[s_], writes=[s_])
                    yield
                    K.op('dve', lambda: nc.vector.tensor_scalar(out=x_[:], in0=h_[:], scalar1=s_[:, 2:3], scalar2=None, op0=ALU.mult), reads=[h_, s_], writes=[x_])
                    if c + NH < NCH:
                        a0_load(c + NH)
                    yield
                    for half in range(2):
                        ps = self.next_half()
                        pb = ps[:].bitcast(BF16)
                        fns = [lambda j=j, pb=pb, half=half: nc.tensor.transpose(out=pb[:, j * 128:(j + 1) * 128], in_=x_[:, (half * 8 + j) * 128:(half * 8 + j + 1) * 128], identity=self.idb[:]) for j in range(8)]
                        K.op('pe', fns, reads=[x_, self.idb], writes=[ps])
                        for j in range(8):
                            kc = half * 8 + j
                            if j % 2 == 0:
                                K.op('act', lambda j=j, kc=kc, pb=pb: nc.scalar.activation(out=uT[:, kc, c * 128:(c + 1) * 128], in_=pb[:, j * 128:(j + 1) * 128], func=AF.Identity,
                                                                                         bias=mraw[:, r, 0, kc:kc + 1], scale=mA[:, r, kc:kc + 1]),
                                     reads=[ps, mraw, mA], writes=[uT])
                            else:
                                K.op('dve', lambda j=j, kc=kc, pb=pb: nc.vector.tensor_scalar(out=uT[:, kc, c * 128:(c + 1) * 128], in0=pb[:, j * 128:(j + 1) * 128], scalar1=mA[:, r, kc:kc + 1],
                                                                                          scalar2=mraw[:, r, 0, kc:kc + 1], op0=ALU.mult, op1=ALU.add),
                                     reads=[ps, mraw, mA], writes=[uT])
                        self.free_half(ps)
                        yield

                for c in range(NH):
                    a0_load(c)
                run_pipeline((a0_iter(c) for c in range(NCH)), 2)
                K.barrier()
            if self.debug:
                for kc in range(16):
                    self.store('pool', uT, self.UT[kc * 128:(kc + 1) * 128, :], uT[:, kc, :])
            with ExitStack() as es1:
                NB = 6
                wf = [self.sb(es1, "wf%d" % i, [128, 16, 128], BF16) for i in range(2)]
                acc = [self.sb(es1, "acc%d" % i, [128, 512]) for i in range(NB)]
                cv = [self.sb(es1, "cv%d" % i, [128, 512]) for i in range(NB)]
                ob = [self.sb(es1, "ob%d" % i, [128, 512], BF16) for i in range(NB)]
                sq = [self.sb(es1, "sq%d" % i, [128, 512], BF16) for i in range(NB)]
                lnv = [self.sb(es1, "lnv%d" % i, [128, 512]) for i in range(NB)]
                tmo = [self.sb(es1, "tmo%d" % i, [128, 4, 128], BF16) for i in range(NB)]
                blks = tok_blocks()

                def fm_iter(it, fc, w_, kind, fm_row, tm_col, t0, nt, rl):
                    a_, c_, o_, s_, l_, m_ = acc[it % NB], cv[it % NB], ob[it % NB], sq[it % NB], lnv[it % NB], tmo[it % NB]
                    ps = self.next_half()
                    fns = [lambda kc=kc: nc.tensor.matmul(ps[:, 0:nt], lhsT=w_[:, kc, :], rhs=uT[:, kc, t0:t0 + nt], start=(kc == 0), stop=(kc == 15)) for kc in range(16)]
                    K.op('pe', fns, reads=[w_, uT], writes=[ps])
                    yield
                    cw = lambda k: self.pvt[:, 16 + fc * 5 + k:16 + fc * 5 + k + 1]
                    if kind in ('q', 'k') or (it % 2 == 0):
                        K.op('dve', lambda: nc.vector.tensor_scalar(out=a_[:, 0:nt], in0=ps[:, 0:nt], scalar1=cw(2), scalar2=None, op0=ALU.mult), reads=[ps, self.pvt], writes=[a_])
                    else:
                        K.op('dve', lambda: nc.vector.tensor_scalar(out=a_[:, 0:nt], in0=ps[:, 0:nt], scalar1=cw(2), scalar2=None, op0=ALU.mult), reads=[ps, self.pvt], writes=[a_])
                    pv3 = ps[:, 0:nt].rearrange("p (r j) -> p r j", j=rl)
                    av3 = a_[:, 0:nt].rearrange("p (r j) -> p r j", j=rl)
                    for k in (0, 1, 3, 4):
                        sft = k - 2
                        j0, j1 = max(0, -sft), rl - max(0, sft)
                        K.op('dve', lambda k=k, sft=sft, j0=j0, j1=j1: nc.vector.scalar_tensor_tensor(out=av3[:, :, j0:j1], in0=pv3[:, :, j0 + sft:j1 + sft], scalar=cw(k), in1=av3[:, :, j0:j1], op0=ALU.mult, op1=ALU.add),
                             reads=[ps, a_, self.pvt], writes=[a_])
                        if k == 1:
                            yield
                    self.free_half(ps)
                    yield
                    if kind in ('q', 'k'):
                        K.op('act', lambda: nc.scalar.activation(out=c_[:, 0:nt], in_=a_[:, 0:nt], func=AF.Silu), reads=[a_], writes=[c_])
                        K.op(self.gpe, lambda: self.gpv.tensor_tensor(out=s_[:, 0:nt], in0=c_[:, 0:nt], in1=c_[:, 0:nt], op=ALU.mult), reads=[c_], writes=[s_])
                        yield
                        ps2 = self.next_half()
                        K.op('pe', lambda: nc.tensor.matmul(ps2[:, 0:nt], lhsT=self.onesb[:], rhs=s_[:, 0:nt], start=True, stop=True), reads=[s_, self.onesb], writes=[ps2])
                        K.op('act', lambda: nc.scalar.activation(out=l_[:, 0:nt], in_=ps2[:, 0:nt], func=AF.Ln, bias=self.epsb[:], scale=1.0), reads=[ps2, self.epsb], writes=[l_])
                        self.free_half(ps2)
                        yield
                        if kind == 'q':
                            K.op('act', lambda: nc.scalar.activation(out=l_[:, 0:nt], in_=l_[:, 0:nt], func=AF.Exp, bias=self.lnqs[:], scale=-0.5), reads=[l_, self.lnqs], writes=[l_])
                        else:
                            K.op('act', lambda: nc.scalar.activation(out=l_[:, 0:nt], in_=l_[:, 0:nt], func=AF.Exp, scale=-0.5), reads=[l_], writes=[l_])
                        yield
                        K.op(self.gpe, lambda: self.gpv.tensor_tensor(out=o_[:, 0:nt], in0=c_[:, 0:nt], in1=l_[:, 0:nt], op=ALU.mult), reads=[c_, l_], writes=[o_])
                    elif fc < 12:
                        K.op('act', lambda: nc.scalar.activation(out=o_[:, 0:nt], in_=a_[:, 0:nt], func=AF.Silu, bias=self.pvt[:, 196 + fc:197 + fc], scale=1.0), reads=[a_, self.pvt], writes=[o_])
                    else:
                        K.op('act', lambda: nc.scalar.activation(out=o_[:, 0:nt], in_=a_[:, 0:nt], func=AF.Silu), reads=[a_], writes=[o_])
                    yield
                    if fm_row is not None:
                        self.store('sp', o_, self.FM[fm_row:fm_row + 128, t0:t0 + nt], o_[:, 0:nt])
                    if tm_col is not None:
                        nj = nt // 128
                        ps3 = self.next_half()
                        pb = ps3[:].bitcast(BF16)
                        fns = [lambda j=j: nc.tensor.transpose(out=pb[:, j * 128:(j + 1) * 128], in_=o_[:, j * 128:(j + 1) * 128], identity=self.idb[:]) for j in range(nj)]
                        K.op('pe', fns, reads=[o_, self.idb], writes=[ps3])
                        K.op('act', lambda: nc.scalar.copy(out=m_[:, 0:nj, :], in_=pb[:, 0:nj * 128].rearrange("p (j f) -> p j f", f=128)), reads=[ps3], writes=[m_])
                        self.free_half(ps3)
                        dst = self.TM[t0:t0 + nt, tm_col:tm_col + 128].rearrange("(j p) f -> p j f", p=128)
                        self.store('sp', m_, dst, m_[:, 0:nj, :])

                def fm_gens():
                    it = 0
                    for fc in range(36):
                        col0 = (2048 + fc * 128) if fc < 12 else (3616 + (fc - 12) * 128)
                        w_ = wf[fc % 2]
                        src = self.w_in[l][:, col0:col0 + 128].rearrange("(k p) f -> p k f", p=128)
                        self.load('pool', w_, w_[:], src)
                        if fc < 8:
                            kind, fm_row, tm_col = 'x', None, TM_X + fc * 128
                        elif fc < 10:
                            kind, fm_row, tm_col = 'B', FM_B + (fc - 8) * 128, TM_B + (fc - 8) * 128
                        elif fc < 12:
                            kind, fm_row, tm_col = 'C', FM_C + (fc - 10) * 128, None
                        elif fc < 20:
                            kind, fm_row, tm_col = 'q', FM_Q + (fc - 12) * 128, None
                        elif fc < 28:
                            kind, fm_row, tm_col = 'k', FM_K + (fc - 20) * 128, TM_K + (fc - 20) * 128
                        else:
                            kind, fm_row, tm_col = 'v', None, TM_V + (fc - 28) * 128
                        for (t0, nt, rl) in blks:
                            yield fm_iter(it, fc, w_, kind, fm_row, tm_col, t0, nt, rl)
                            it += 1

                run_pipeline(fm_gens(), 6)
                K.barrier()
            with ExitStack() as es2:
                NZ = 6
                wz = [self.sb(es2, "wz%d" % i, [128, 16, 256], BF16) for i in range(2)]
                zo = [self.sb(es2, "zo%d" % i, [128, 256], BF16) for i in range(NZ)]
                so = [self.sb(es2, "so%d" % i, [128, 64]) for i in range(NZ)]

                def z_iter(it, cb, c, w_, ncol):
                    ps = self.next_half()
                    fns = [lambda kc=kc: nc.tensor.matmul(ps[:, 0:ncol], lhsT=uT[:, kc, c * 128:(c + 1) * 128], rhs=w_[:, kc, 0:ncol], start=(kc == 0), stop=(kc == 15)) for kc in range(16)]
                    K.op('pe', fns, reads=[w_, uT], writes=[ps])
                    yield
                    if cb < 8:
                        z_ = zo[it % NZ]
                        K.op('act', lambda: nc.scalar.activation(out=z_[:], in_=ps[:, 0:256], func=AF.Silu), reads=[ps], writes=[z_])
                        self.free_half(ps)
                        self.store('sp', z_, self.ZS[c * 128:(c + 1) * 128, cb * 256:(cb + 1) * 256], z_[:])
                    else:
                        s_ = so[it % NZ]
                        K.op('act', lambda: nc.scalar.copy(out=s_[:], in_=ps[:, 0:64]), reads=[ps], writes=[s_])
                        self.free_half(ps)
                        self.store('sp', s_, self.SMALL[c * 128:(c + 1) * 128, :], s_[:])

                def z_gens():
                    it = 0
                    for cb in range(9):
                        w_ = wz[cb % 2]
                        if cb < 8:
                            src = self.w_in[l][:, cb * 256:(cb + 1) * 256].rearrange("(k p) f -> p k f", p=128)
                            self.load('pool', w_, w_[:], src)
                            ncol = 256
                        else:
                            self.load('pool', w_, w_[:, :, 0:32], self.w_in[l][:, 3584:3616].rearrange("(k p) f -> p k f", p=128))
                            self.load('pool', w_, w_[:, :, 32:64], self.w_in[l][:, 6688:6720].rearrange("(k p) f -> p k f", p=128))
                            ncol = 64
                        for c in range(NCH):
                            yield z_iter(it, cb, c, w_, ncol)
                            it += 1

                run_pipeline(z_gens(), 5)

    def phaseS(self, l, es):
        nc, K = self.nc, self.K
        sb = lambda name, shape, dt=F32: self.sb(es, name, shape, dt)
        self.dt_all = sb("dt_all", [128, NCH, 32])
        self.loga_all = sb("loga_all", [128, NCH, 32])
        self.beta_all = sb("beta_all", [128, NCH, 16])
        self.g_all = sb("g_all", [128, NCH, 16])
        self.dsk = sb("dsk", [128, 16])
        self.onecol = sb("onecol", [128, 1])
        K.op('dve', lambda: nc.vector.memset(self.onecol[:], 1.0), writes=[self.onecol])
        self.load('sp', self.dsk, self.dsk[:], dram_bcast(self.ssd_d[l], 128))
        with ExitStack() as e2:
            sm = self.sb(e2, "sm", [128, NCH, 64])
            self.load('sp', sm, sm[:], self.SMALL.rearrange("(c p) f -> p c f", p=128))
            bias = self.sb(e2, "sbias", [128, 48])
            alog = self.sb(e2, "salog", [128, 48])
            self.load('sp', bias, bias[:, 0:32], dram_bcast(self.ssd_dt_bias[l], 128))
            self.load('sp', bias, bias[:, 32:48], dram_bcast(self.gdn_dt_bias[l], 128))
            self.load('sp', alog, alog[:, 0:32], dram_bcast(self.ssd_a_log[l], 128))
            self.load('sp', alog, alog[:, 32:48], dram_bcast(self.gdn_a_log[l], 128))
            K.op('act', lambda: nc.scalar.activation(out=alog[:], in_=alog[:], func=AF.Exp), reads=[alog], writes=[alog])
            K.op('dve', lambda: nc.vector.tensor_scalar(out=alog[:], in0=alog[:], scalar1=-1.0, scalar2=None, op0=ALU.mult), reads=[alog], writes=[alog])
            tmp = self.sb(e2, "stmp", [128, NCH, 48])
            K.op('dve', lambda: nc.vector.tensor_tensor(out=tmp[:, :, 0:32], in0=sm[:, :, 0:32], in1=bias[:, 0:32].unsqueeze(1).broadcast_to([128, NCH, 32]), op=ALU.add), reads=[sm, bias], writes=[tmp])
            K.op('dve', lambda: nc.vector.tensor_tensor(out=tmp[:, :, 32:48], in0=sm[:, :, 48:64], in1=bias[:, 32:48].unsqueeze(1).broadcast_to([128, NCH, 16]), op=ALU.add), reads=[sm, bias], writes=[tmp])
            K.op('act', lambda: nc.scalar.activation(out=tmp[:], in_=tmp[:], func=AF.Exp), reads=[tmp], writes=[tmp])
            K.op('act', lambda: nc.scalar.activation(out=tmp[:], in_=tmp[:], func=AF.Ln, bias=self.onecol[:], scale=1.0), reads=[tmp, self.onecol], writes=[tmp])
            K.op('dve', lambda: nc.vector.tensor_copy(out=self.dt_all[:], in_=tmp[:, :, 0:32]), reads=[tmp], writes=[self.dt_all])
            K.op('dve', lambda: nc.vector.tensor_tensor(out=self.loga_all[:], in0=tmp[:, :, 0:32], in1=alog[:, 0:32].unsqueeze(1).broadcast_to([128, NCH, 32]), op=ALU.mult), reads=[tmp, alog], writes=[self.loga_all])
            K.op('dve', lambda: nc.vector.tensor_tensor(out=self.g_all[:], in0=tmp[:, :, 32:48], in1=alog[:, 32:48].unsqueeze(1).broadcast_to([128, NCH, 16]), op=ALU.mult), reads=[tmp, alog], writes=[self.g_all])
            K.op('act', lambda: nc.scalar.activation(out=self.beta_all[:], in_=sm[:, :, 32:48], func=AF.Sigmoid), reads=[sm], writes=[self.beta_all])
            K.barrier()

    def phaseB(self, l, last):
        nc, K = self.nc, self.K
        with ExitStack() as es:
            self.phaseS(l, es)
            sb = lambda name, shape, dt=F32: self.sb(es, name, shape, dt)
            B = self.Bt = type('T', (), {})()
            B.tm = [sb("tm%d" % i, [128, TM_COLS], BF16) for i in range(3)]
            B.fm = [sb("fm%d" % i, [128, 20, 128], BF16) for i in range(3)]
            B.yo1 = sb("yo1", [128, D])
            B.zs = sb("zs", [128, D], BF16)
            B.ex = sb("ex", [128, 48]); B.rhsL = sb("rhsL", [128, 8, 128], BF16); B.excb = sb("excb", [128, 128], BF16); B.E = sb("E", [128, 16, 128], BF16)
            B.cbm = sb("cbm", [128, 2, 128], BF16); B.MT = sb("MT", [128, 16, 128], BF16)
            B.xdt = sb("xdt", [128, 16, 64], BF16); B.xdd = sb("xdd", [128, 16, 64], BF16)
            B.ytmp = sb("ytmp", [128, 1024]); B.ysum = sb("ysum", [128, 1024]); B.yfin = sb("yfin", [128, 1024])
            B.hT = sb("hT", [128, 16, 64]); B.hTb = sb("hTb", [128, 16, 64], BF16)
            B.gex = [sb("gex%d" % i, [128, 3, 8]) for i in range(2)]; B.rhsG = sb("rhsG", [128, 8, 128], BF16); B.DT = sb("DT", [128, 8, 128])
            B.tA = sb("tA", [128, 8, 128]); B.tB = sb("tB", [128, 8, 128])
            B.nm = sb("nm", [128, 128], BF16); B.bdS = sb("bdS", [128, 128])
            g16 = lambda name: sb(name, [128, 8, 128], BF16)
            B.UD = g16("UD"); B.UO = g16("UO"); B.attnT = [g16("attnT0"), g16("attnT1")]; B.AD = g16("AD")
            B.P = [g16("P0"), g16("P1")]; B.X = [g16("X0"), g16("X1")]; B.XT = [g16("XT0"), g16("XT1")]
            B.TD = g16("TD"); B.VT = g16("VT"); B.V = g16("V"); B.V2T = g16("V2T"); B.Z = g16("Z"); B.Sf = g16("Sf")
            B.kg = g16("kg"); B.kendk = [g16("kendk0"), g16("kendk1")]; B.wT = [g16("wT0"), g16("wT1")]; B.vnew = g16("vnew")
            B.u_sb = [sb("u_sb%d" % i, [128, 8, 128]) for i in range(2)]; B.o = sb("o", [128, 8, 128]); B.ofin = sb("ofin", [128, 8, 128])
            B.Sg = sb("Sg", [128, 8, 128]); B.Sgb = g16("Sgb")
            NG = self.NG = 1
            self.PREW = getattr(Builder, "PREW", 20)
            for t_ in [B.rhsG, B.DT, B.tA, B.UD, B.UO, B.AD, B.TD, B.VT, B.V, B.V2T, B.Z, B.Sf, B.kg] + B.P + B.X + B.XT + B.gex + B.attnT + B.u_sb + B.wT + B.kendk:
                t_.g = [Buf(t_.name + "_g%d" % j, t_.t) for j in range(NG)]
            B.mT = sb("mT", [128, 128]); B.nbd = sb("nbd", [128, 128])
            K.op('dve', lambda: nc.vector.tensor_scalar(out=B.nbd[:], in0=self.cst[:, C_BD:C_BD + 128], scalar1=-1.0, scalar2=1.0, op0=ALU.mult, op1=ALU.add), reads=[self.cst], writes=[B.nbd])
            B.snw = sb("snw", [128, 1024]); B.gnw = sb("gnw", [128, 128])
            self.load('sp', B.snw, B.snw[:], dram_bcast(self.ssd_norm_w[l], 128))
            self.load('sp', B.gnw, B.gnw[:], dram_bcast(self.gdn_norm_w[l], 128))
            B.yg = sb("yg", [128, 1024]); B.ejunk = sb("ejunk", [128, 1024], BF16); B.est = sb("est", [128, 16]); B.est2 = sb("est2", [128, 16])
            B.ym = sb("ym", [128, D], BF16); B.wz = sb("wzg", [128, 8, 128]); B.wz2 = sb("wzg2", [128, 8, 128], BF16)
            for d in range(2):
                self.scan_pass(l, d, last)
                K.barrier()

    def scan_pass(self, l, d, last):
        nc, K, B = self.nc, self.K, self.Bt
        order = list(range(NCH)) if d == 0 else [1, 0] + list(range(NCH - 1, 1, -1))
        B.inc = self.cst[:, (C_INC0 if d == 0 else C_INC1):(C_INC0 if d == 0 else C_INC1) + 128]
        B.exc = self.cst[:, (C_EXC0 if d == 0 else C_EXC1):(C_EXC0 if d == 0 else C_EXC1) + 128]
        B.strictT = self.cst[:, (C_EXC1 if d == 0 else C_EXC0):(C_EXC1 if d == 0 else C_EXC0) + 128]
        B.ones = self.cst[:, C_ONES:C_ONES + 128]
        B.bd = self.cst[:, C_BD:C_BD + 128]
        K.op('dve', lambda: nc.vector.memset(B.hT[:], 0.0), writes=[B.hT])
        K.op('dve', lambda: nc.vector.memset(B.hTb[:], 0.0), writes=[B.hTb])
        K.op('dve', lambda: nc.vector.memset(B.Sg[:], 0.0), writes=[B.Sg])
        K.op('dve', lambda: nc.vector.memset(B.Sgb[:], 0.0), writes=[B.Sgb])

        K.op('dve', lambda: nc.vector.tensor_scalar(out=B.nm[:], in0=B.inc, scalar1=30000.0, scalar2=-30000.0, op0=ALU.mult, op1=ALU.add), reads=[self.cst], writes=[B.nm])
        K.op('dve', lambda: nc.vector.tensor_tensor(out=B.bdS[:], in0=B.strictT, in1=B.bd, op=ALU.mult), reads=[self.cst], writes=[B.bdS])
        K.op('dve', lambda: nc.vector.tensor_copy(out=B.excb[:], in_=B.exc), reads=[self.cst], writes=[B.excb])

        def issue_loads(i):
            c = order[i]
            tm_, fm_ = B.tm[i % 3], B.fm[i % 3]
            self.load('sp', tm_, tm_[:], self.TM[c * 128:(c + 1) * 128, :])
            self.load('sp', fm_, fm_[:], self.FM[:, c * 128:(c + 1) * 128].rearrange("(f p) t -> p f t", p=128))

        def run_weighted(gens):
            st = [[g, n, 0] for g, n in gens]
            while st:
                st.sort(key=lambda x: x[2] / float(x[1]))
                g = st[0]
                try:
                    next(g[0])
                    g[2] += 1
                except StopIteration:
                    st.remove(g)

        n = len(order)
        issue_loads(0)
        if n > 1:
            issue_loads(1)
        run_weighted([(self.gdn_pre(order[0], d, B.tm[0], B.fm[0], 0, hg, self.NG), 20) for hg in range(self.NG)])
        for i, c in enumerate(order):
            if i + 2 < n:
                issue_loads(i + 2)
            tm_, fm_ = B.tm[i % 3], B.fm[i % 3]
            need_out = not (last and c < 2)
            if d == 1 and need_out:
                self.load('sp', B.yo1, B.yo1[:], self.YO[c * 128:(c + 1) * 128, :])
                self.load('sp', B.zs, B.zs[:], self.ZS[c * 128:(c + 1) * 128, :])
            gens = [(self.ssd_gen(c, d, tm_, fm_, need_out), 7), (self.gdn_rec(c, d, fm_, need_out, i % 2), 3)]
            if i + 1 < n:
                for hg in range(self.NG):
                    gens.append((self.gdn_pre(order[i + 1], d, B.tm[(i + 1) % 3], B.fm[(i + 1) % 3], (i + 1) % 2, hg, self.NG), self.PREW))
            run_weighted(gens)
            if d == 1 and need_out:
                self.epilogue(c)

    def ssd_gen(self, c, d, tm_, fm_, need_out):
        nc, K, B = self.nc, self.K, self.Bt
        cst = self.cst
        loga = self.loga_all[:, c, d * 16:(d + 1) * 16]
        dtc = self.dt_all[:, c, d * 16:(d + 1) * 16]
        xs3 = tm_[:, TM_X:TM_X + 1024].rearrange("p (h q) -> p h q", q=64)
        ps = self.next_ps()
        K.op('pe', [lambda: nc.tensor.matmul(ps[:, 0:16], lhsT=B.inc, rhs=loga, start=True, stop=True),
                    lambda: nc.tensor.matmul(ps[:, 16:32], lhsT=B.exc, rhs=loga, start=True, stop=True),
                    lambda: nc.tensor.matmul(ps[:, 32:48], lhsT=B.ones, rhs=loga, start=True, stop=True)], reads=[cst, self.loga_all], writes=[ps])
        K.op('act', lambda: nc.scalar.activation(out=B.ex[:], in_=ps[:, 0:48], func=AF.Exp), reads=[ps], writes=[B.ex])
        expA, dend, cd = B.ex[:, 0:16], B.ex[:, 16:32], B.ex[:, 32:48]
        yield
        for hh in range(2):
            K.op(self.gpe, lambda hh=hh: self.gpv.tensor_tensor(out=B.rhsL[:], in0=B.inc.unsqueeze(1).broadcast_to([128, 8, 128]),
                                                             in1=loga[:, hh * 8:(hh + 1) * 8].unsqueeze(2).broadcast_to([128, 8, 128]), op=ALU.mult),
                 reads=[cst, self.loga_all], writes=[B.rhsL])
            ps = self.next_ps()
            rl2 = B.rhsL[:].rearrange("p h q -> p (h q)")
            nm4 = B.nm[:].unsqueeze(1).broadcast_to([128, 4, 128])
            fns = []
            for j in range(2):
                fns.append(lambda ps=ps, rl2=rl2, j=j: nc.tensor.matmul(ps[:, j * 512:(j + 1) * 512], lhsT=B.excb[:], rhs=rl2[:, j * 512:(j + 1) * 512], start=True, stop=False))
                fns.append(lambda ps=ps, j=j: nc.tensor.matmul(ps[:, j * 512:(j + 1) * 512].rearrange("p (h q) -> p h q", h=4), lhsT=self.idb[:], rhs=nm4, start=False, stop=True))
            K.op('pe', fns, reads=[B.excb, B.rhsL, B.nm, self.idb], writes=[ps])
            K.op('act', lambda ps=ps, hh=hh: nc.scalar.activation(out=B.E[:, hh * 8:(hh + 1) * 8, :].rearrange("p h q -> p (h q)"), in_=ps[:], func=AF.Exp), reads=[ps], writes=[B.E])
            yield
        ps = self.next_ps()
        K.op('pe', [lambda ps=ps, g=g: nc.tensor.matmul(ps[:, g * 128:(g + 1) * 128], lhsT=fm_[:, FM_B // 128 + g, :], rhs=fm_[:, FM_C // 128 + g, :], start=True, stop=True) for g in range(2)],
             reads=[fm_], writes=[ps])
        K.op('act', lambda ps=ps: nc.scalar.copy(out=B.cbm[:].rearrange("p g q -> p (g q)"), in_=ps[:, 0:256]), reads=[ps], writes=[B.cbm])
        yield
        K.op('dve', lambda: nc.vector.tensor_tensor(out=B.MT[:].rearrange("p (g r) q -> p g r q", g=2), in0=B.E[:].rearrange("p (g r) q -> p g r q", g=2),
                                                    in1=B.cbm[:].unsqueeze(2).broadcast_to([128, 2, 8, 128]), op=ALU.mult), reads=[B.E, B.cbm], writes=[B.MT])
        K.op(self.gpe, lambda: self.gpv.tensor_tensor(out=B.xdt[:], in0=xs3, in1=dtc.unsqueeze(2).broadcast_to([128, 16, 64]), op=ALU.mult), reads=[tm_, self.dt_all], writes=[B.xdt])
        K.op(self.gpe, lambda: self.gpv.tensor_tensor(out=B.xdd[:], in0=B.xdt[:], in1=dend.unsqueeze(2).broadcast_to([128, 16, 64]), op=ALU.mult), reads=[B.xdt, B.ex], writes=[B.xdd])
        yield
        psd = self.next_ps()
        K.op('pe', [lambda h=h: nc.tensor.matmul(psd[:, h * 64:(h + 1) * 64], lhsT=B.MT[:, h, :], rhs=B.xdt[:, h, :], start=True, stop=True) for h in range(16)], reads=[B.MT, B.xdt], writes=[psd])
        pso = self.next_ps()
        K.op('pe', [lambda g=g: nc.tensor.matmul(pso[:, g * 512:(g + 1) * 512], lhsT=fm_[:, FM_C // 128 + g, :], rhs=B.hTb[:, g * 8:(g + 1) * 8, :].rearrange("p h q -> p (h q)"), start=True, stop=True) for g in range(2)],
             reads=[fm_, B.hTb], writes=[pso])
        K.op('dve', lambda: nc.vector.tensor_tensor(out=B.ytmp[:].rearrange("p (h q) -> p h q", q=64), in0=pso[:].rearrange("p (h q) -> p h q", q=64), in1=expA.unsqueeze(2).broadcast_to([128, 16, 64]), op=ALU.mult),
             reads=[pso, B.ex], writes=[B.ytmp])
        K.op('dve', lambda: nc.vector.tensor_tensor(out=B.ysum[:], in0=B.ytmp[:], in1=psd[:], op=ALU.add), reads=[B.ytmp, psd], writes=[B.ysum])
        if need_out:
            if d == 0:
                K.op(self.gpe, lambda: self.gpv.tensor_tensor(out=B.ytmp[:].rearrange("p (h q) -> p h q", q=64), in0=xs3, in1=self.dsk[:].unsqueeze(2).broadcast_to([128, 16, 64]), op=ALU.mult),
                     reads=[tm_, self.dsk], writes=[B.ytmp])
                K.op('dve', lambda: nc.vector.tensor_tensor(out=B.yfin[:], in0=B.ytmp[:], in1=B.ysum[:], op=ALU.add), reads=[B.ytmp, B.ysum], writes=[B.yfin])
                self.store('sp', B.yfin, self.YO[c * 128:(c + 1) * 128, 0:1024], B.yfin[:])
            else:
                K.op('dve', lambda: nc.vector.tensor_tensor(out=B.yfin[:], in0=B.yo1[:, 0:1024], in1=B.ysum[:], op=ALU.add), reads=[B.yo1, B.ysum], writes=[B.yfin])
        yield
        pss = self.next_ps()
        K.op('pe', [lambda g=g: nc.tensor.matmul(pss[:, g * 512:(g + 1) * 512], lhsT=tm_[:, TM_B + g * 128:TM_B + (g + 1) * 128], rhs=B.xdd[:, g * 8:(g + 1) * 8, :].rearrange("p h q -> p (h q)"), start=True, stop=True) for g in range(2)],
             reads=[tm_, B.xdd], writes=[pss])
        K.op(self.gpe, lambda: self.gpv.tensor_tensor(out=B.hT[:], in0=B.hT[:], in1=cd.unsqueeze(2).broadcast_to([128, 16, 64]), op=ALU.mult), reads=[B.hT, B.ex], writes=[B.hT])
        K.op('dve', lambda: nc.vector.tensor_tensor(out=B.hT[:].rearrange("p h q -> p (h q)"), in0=B.hT[:].rearrange("p h q -> p (h q)"), in1=pss[:], op=ALU.add), reads=[B.hT, pss], writes=[B.hT])
        K.op('act', lambda: nc.scalar.copy(out=B.hTb[:], in_=B.hT[:]), reads=[B.hT], writes=[B.hTb])
        yield

    def mm8(self, ps, lhs, rhs, lhs_bufs, rhs_bufs, bf16_out=False, transpose=False, acc=None, hs=None):
        nc, K = self.nc, self.K
        if hs is None:
            hs = range(8)
        if bf16_out:
            pv = ps[:].bitcast(BF16)
        else:
            pv = ps[:]
        if transpose:
            fns = [lambda j=j, H=H: nc.tensor.transpose(out=pv[:, j * 128:(j + 1) * 128], in_=lhs(H), identity=self.idb[:]) for j, H in enumerate(hs)]
        elif acc is None:
            fns = [lambda j=j, H=H: nc.tensor.matmul(pv[:, j * 128:(j + 1) * 128], lhsT=lhs(H), rhs=rhs(H), start=True, stop=True) for j, H in enumerate(hs)]
        else:
            al, ar = acc
            fns = []
            for j, H in enumerate(hs):
                fns.append(lambda j=j, H=H: nc.tensor.matmul(pv[:, j * 128:(j + 1) * 128], lhsT=lhs(H), rhs=rhs(H), start=True, stop=False))
                fns.append(lambda j=j, H=H: nc.tensor.matmul(pv[:, j * 128:(j + 1) * 128], lhsT=al(H), rhs=ar(H), start=False, stop=True))
        K.op('pe', fns, reads=list(lhs_bufs) + list(rhs_bufs), writes=[ps])
        return pv

    def gdn_pre(self, c, d, tm_, fm_, slot, hg, ng):
        nc, K, B = self.nc, self.K, self.Bt
        cst = self.cst
        nh = 8 // ng
        H0 = hg * nh
        hs = range(H0, H0 + nh)
        W = nh * 128
        gc = self.g_all[:, c, d * 8 + H0:d * 8 + H0 + nh]
        bc = self.beta_all[:, c, d * 8 + H0:d * 8 + H0 + nh]
        k3 = tm_[:, TM_K + H0 * 128:TM_K + (H0 + nh) * 128].rearrange("p (h q) -> p h q", q=128)
        v3 = tm_[:, TM_V:TM_V + 1024].rearrange("p (h q) -> p h q", q=128)
        kT = lambda H: fm_[:, FM_K // 128 + H, :]
        qT = lambda H: fm_[:, FM_Q // 128 + H, :]
        G = lambda buf: buf.g[hg]
        hd = lambda buf: (lambda H: buf[:, H, :])
        idH = lambda H: self.idb[:]
        sl = lambda buf: buf[:, H0:H0 + nh, :]
        fl = lambda buf: buf[:, H0:H0 + nh, :].rearrange("p h q -> p (h q)")
        bcl = lambda ap2: ap2.unsqueeze(2).broadcast_to([128, nh, 128])
        bcm = lambda ap2: ap2.unsqueeze(1).broadcast_to([128, nh, 128])
        gex, attnT, u_sb, wT, kendk = B.gex[slot], B.attnT[slot], B.u_sb[slot], B.wT[slot], B.kendk[slot]
        ps = self.next_ps()
        K.op('pe', [lambda: nc.tensor.matmul(ps[:, 0:nh], lhsT=B.inc, rhs=gc, start=True, stop=True),
                    lambda: nc.tensor.matmul(ps[:, 8:8 + nh], lhsT=B.exc, rhs=gc, start=True, stop=True),
                    lambda: nc.tensor.matmul(ps[:, 16:16 + nh], lhsT=B.ones, rhs=gc, start=True, stop=True)], reads=[cst, self.g_all], writes=[ps])
        K.op('act', lambda: nc.scalar.activation(out=gex[:, :, H0:H0 + nh], in_=ps[:, 0:24].rearrange("p (a h) -> p a h", a=3)[:, :, 0:nh], func=AF.Exp), reads=[ps], writes=[G(gex)])
        expG, kend = gex[:, 0, H0:H0 + nh], gex[:, 1, H0:H0 + nh]
        yield
        K.op(self.gpe, lambda: self.gpv.tensor_tensor(out=sl(B.rhsG), in0=bcm(B.inc), in1=bcl(gc), op=ALU.mult), reads=[cst, self.g_all], writes=[G(B.rhsG)])
        ps = self.next_ps()
        rg2 = fl(B.rhsG)
        nm4 = B.nm[:].unsqueeze(1).broadcast_to([128, 4, 128])
        fns = []
        for j in range(W // 512):
            fns.append(lambda j=j: nc.tensor.matmul(ps[:, j * 512:(j + 1) * 512], lhsT=B.excb[:], rhs=rg2[:, j * 512:(j + 1) * 512], start=True, stop=False))
            fns.append(lambda j=j: nc.tensor.matmul(ps[:, j * 512:(j + 1) * 512].rearrange("p (h q) -> p h q", h=4), lhsT=self.idb[:], rhs=nm4, start=False, stop=True))
        K.op('pe', fns, reads=[B.excb, G(B.rhsG), B.nm, self.idb], writes=[ps])
        K.op('act', lambda: nc.scalar.activation(out=fl(B.DT), in_=ps[:, 0:W], func=AF.Exp), reads=[ps], writes=[G(B.DT)])
        yield
        ps = self.next_ps()
        self.mm8(ps, kT, kT, [fm_], [], hs=hs)
        K.op('dve', lambda: nc.vector.tensor_tensor(out=fl(B.tA), in0=ps[:, 0:W], in1=fl(B.DT), op=ALU.mult), reads=[ps, G(B.DT)], writes=[G(B.tA)])
        K.op('dve', lambda: nc.vector.tensor_tensor(out=sl(B.tA), in0=sl(B.tA), in1=bcl(bc), op=ALU.mult), reads=[G(B.tA), self.beta_all], writes=[G(B.tA)])
        yield
        K.op('dve', lambda: nc.vector.tensor_tensor(out=sl(B.UD), in0=sl(B.tA), in1=bcm(B.bdS[:]), op=ALU.mult), reads=[G(B.tA), B.bdS], writes=[G(B.UD)])
        K.op('dve', lambda: nc.vector.tensor_tensor(out=sl(B.UO), in0=sl(B.tA), in1=bcm(B.nbd[:]), op=ALU.mult), reads=[G(B.tA), B.nbd], writes=[G(B.UO)])
        yield
        ps = self.next_ps()
        self.mm8(ps, kT, qT, [fm_], [], hs=hs)
        K.op('dve', lambda: nc.vector.tensor_tensor(out=fl(attnT), in0=ps[:, 0:W], in1=fl(B.DT), op=ALU.mult), reads=[ps, G(B.DT)], writes=[G(attnT)])
        yield
        ps = self.next_ps()
        pv = self.mm8(ps, hd(B.UD), None, [G(B.UD), self.idb], [], bf16_out=True, transpose=True, hs=hs)
        K.op('act', lambda: nc.scalar.copy(out=fl(B.AD), in_=pv[:, 0:W]), reads=[ps], writes=[G(B.AD)])
        yield
        X, XT = B.UD, B.AD
        P = B.P[0]
        K.op(self.gpe, lambda: self.gpv.tensor_tensor(out=sl(P), in0=bcm(self.idb[:]), in1=sl(X), op=ALU.subtract), reads=[self.idb, G(X)], writes=[G(P)])

        def square(k, X, XT):
            XTn = B.XT[k % 2]
            ps = self.next_ps()
            self.mm8(ps, hd(X), hd(XT), [G(X)], [G(XT)], hs=hs)
            K.op('act', lambda: nc.scalar.copy(out=fl(XTn), in_=ps[:, 0:W]), reads=[ps], writes=[G(XTn)])
            Xn = None
            if k < 4:
                Xn = B.X[k % 2]
                ps2 = self.next_ps()
                self.mm8(ps2, hd(XT), hd(X), [G(XT)], [G(X)], hs=hs)
                K.op('act', lambda: nc.scalar.copy(out=fl(Xn), in_=ps2[:, 0:W]), reads=[ps2], writes=[G(Xn)])
            return Xn, XTn

        Xn, XTn = square(1, X, XT)
        yield
        for k in range(1, 5):
            ps3 = self.next_ps()
            self.mm8(ps3, hd(XTn), hd(P), [G(XTn), self.idb], [G(P)], acc=(idH, hd(P)), hs=hs)
            Pn = B.P[k % 2]
            K.op('act', lambda ps3=ps3, Pn=Pn: nc.scalar.copy(out=fl(Pn), in_=ps3[:, 0:W]), reads=[ps3], writes=[G(Pn)])
            P = Pn
            if k < 4:
                yield
                Xn, XTn = square(k + 1, Xn, XTn)
            yield
        SD = P
        ps = self.next_ps()
        self.mm8(ps, hd(SD), idH, [G(SD)], [self.idb], hs=hs)
        K.op('act', lambda ps=ps: nc.scalar.copy(out=fl(B.TD), in_=ps[:, 0:W]), reads=[ps], writes=[G(B.TD)])
        yield
        ps = self.next_ps()
        self.mm8(ps, hd(B.UO), hd(B.TD), [G(B.UO)], [G(B.TD)], hs=hs)
        K.op('act', lambda ps=ps: nc.scalar.activation(out=fl(B.VT), in_=ps[:, 0:W], func=AF.Copy, scale=-1.0), reads=[ps], writes=[G(B.VT)])
        ps2 = self.next_ps()
        self.mm8(ps2, hd(B.TD), hd(B.UO), [G(B.TD)], [G(B.UO)], hs=hs)
        K.op('act', lambda ps2=ps2: nc.scalar.copy(out=fl(B.V), in_=ps2[:, 0:W]), reads=[ps2], writes=[G(B.V)])
        yield
        ps = self.next_ps()
        self.mm8(ps, hd(B.V), hd(B.VT), [G(B.V)], [G(B.VT)], hs=hs)
        K.op('act', lambda ps=ps: nc.scalar.activation(out=fl(B.V2T), in_=ps[:, 0:W], func=AF.Copy, scale=-1.0), reads=[ps], writes=[G(B.V2T)])
        ps2 = self.next_ps()
        self.mm8(ps2, hd(B.VT), hd(SD), [G(B.VT), self.idb], [G(SD)], acc=(idH, hd(SD)), hs=hs)
        K.op('act', lambda ps2=ps2: nc.scalar.copy(out=fl(B.Z), in_=ps2[:, 0:W]), reads=[ps2], writes=[G(B.Z)])
        yield
        ps = self.next_ps()
        self.mm8(ps, hd(B.V2T), hd(B.Z), [G(B.V2T), self.idb], [G(B.Z)], acc=(idH, hd(B.Z)), hs=hs)
        K.op('act', lambda ps=ps: nc.scalar.copy(out=fl(B.Sf), in_=ps[:, 0:W]), reads=[ps], writes=[G(B.Sf)])
        yield
        K.op(self.gpe, lambda: self.gpv.tensor_tensor(out=sl(B.kg), in0=k3, in1=bcl(expG), op=ALU.mult), reads=[tm_, G(gex)], writes=[G(B.kg)])
        K.op(self.gpe, lambda: self.gpv.tensor_tensor(out=sl(kendk), in0=k3, in1=bcl(kend), op=ALU.mult), reads=[tm_, G(gex)], writes=[G(kendk)])
        yield
        ps = self.next_ps()
        self.mm8(ps, hd(B.Sf), lambda H: v3[:, H, :], [G(B.Sf)], [tm_], hs=hs)
        K.op('act', lambda ps=ps: nc.scalar.copy(out=fl(u_sb), in_=ps[:, 0:W]), reads=[ps], writes=[G(u_sb)])
        ps2 = self.next_ps()
        self.mm8(ps2, hd(B.kg), hd(B.Sf), [G(B.kg)], [G(B.Sf)], hs=hs)
        K.op('act', lambda ps2=ps2: nc.scalar.copy(out=fl(wT), in_=ps2[:, 0:W]), reads=[ps2], writes=[G(wT)])
        yield

    def gdn_rec(self, c, d, fm_, need_out, slot):
        nc, K, B = self.nc, self.K, self.Bt
        bc = self.beta_all[:, c, d * 8:(d + 1) * 8]
        qT = lambda H: fm_[:, FM_Q // 128 + H, :]
        hd = lambda buf: (lambda H: buf[:, H, :])
        fl = lambda buf: buf[:].rearrange("p h q -> p (h q)")
        f3 = lambda ap: ap.rearrange("p (h q) -> p h q", q=128)
        gex_, attnT_, u_sb_, wT_, kendk_ = B.gex[slot], B.attnT[slot], B.u_sb[slot], B.wT[slot], B.kendk[slot]
        expG, cdG = gex_[:, 0, :], gex_[:, 2, :]

        class _Multi:
            def __init__(self, buf):
                self.buf = buf
            def __getitem__(self, k):
                return self.buf[k]
        gex, attnT, u_sb, wT, kendk = gex_, attnT_, u_sb_, wT_, kendk_
        GG = lambda buf: list(buf.g)
        ps = self.next_ps()
        self.mm8(ps, hd(wT), hd(B.Sgb), GG(wT), [B.Sgb])
        K.op('dve', lambda: nc.vector.tensor_tensor(out=fl(B.tB), in0=fl(u_sb), in1=ps[:], op=ALU.subtract), reads=GG(u_sb) + [ps], writes=[B.tB])
        K.op('dve', lambda: nc.vector.tensor_tensor(out=B.vnew[:], in0=B.tB[:], in1=bc.unsqueeze(2).broadcast_to([128, 8, 128]), op=ALU.mult), reads=[B.tB, self.beta_all], writes=[B.vnew])
        yield
        ps = self.next_ps()
        self.mm8(ps, qT, hd(B.Sgb), [fm_], [B.Sgb])
        K.op('dve', lambda: nc.vector.tensor_tensor(out=B.tB[:], in0=f3(ps[:]), in1=expG.unsqueeze(2).broadcast_to([128, 8, 128]), op=ALU.mult), reads=[ps] + GG(gex), writes=[B.tB])
        ps2 = self.next_ps()
        self.mm8(ps2, hd(attnT), hd(B.vnew), GG(attnT), [B.vnew])
        K.op('dve', lambda: nc.vector.tensor_tensor(out=fl(B.o), in0=fl(B.tB), in1=ps2[:], op=ALU.add), reads=[B.tB, ps2], writes=[B.o])
        if need_out:
            if d == 0:
                self.store('sp', B.o, self.YO[c * 128:(c + 1) * 128, 1024:2048], fl(B.o))
            else:
                K.op('dve', lambda: nc.vector.tensor_tensor(out=fl(B.ofin), in0=fl(B.o), in1=B.yo1[:, 1024:2048], op=ALU.add), reads=[B.o, B.yo1], writes=[B.ofin])
        yield
        ps = self.next_ps()
        self.mm8(ps, hd(kendk), hd(B.vnew), GG(kendk), [B.vnew])
        K.op(self.gpe, lambda: self.gpv.tensor_tensor(out=B.Sg[:], in0=B.Sg[:], in1=cdG.unsqueeze(2).broadcast_to([128, 8, 128]), op=ALU.mult), reads=[B.Sg] + GG(gex), writes=[B.Sg])
        K.op('dve', lambda: nc.vector.tensor_tensor(out=fl(B.Sg), in0=fl(B.Sg), in1=ps[:], op=ALU.add), reads=[B.Sg, ps], writes=[B.Sg])
        K.op('act', lambda: nc.scalar.copy(out=B.Sgb[:], in_=B.Sg[:]), reads=[B.Sg], writes=[B.Sgb])
        yield

    def epilogue(self, c):
        nc, K, B = self.nc, self.K, self.Bt
        K.op(self.gpe, lambda: self.gpv.tensor_tensor(out=B.yg[:], in0=B.yfin[:], in1=B.zs[:, 0:1024], op=ALU.mult), reads=[B.yfin, B.zs], writes=[B.yg])
        for g in range(2):
            K.op('act', lambda g=g: nc.scalar.activation(out=B.ejunk[:, 0:512], in_=B.yg[:, g * 512:(g + 1) * 512], func=AF.Square, accum_out=B.est[:, g:g + 1]), reads=[B.yg], writes=[B.ejunk, B.est])
        K.op('dve', lambda: nc.vector.tensor_tensor(out=B.wz[:], in0=B.ofin[:], in1=B.ofin[:], op=ALU.mult), reads=[B.ofin], writes=[B.wz])
        K.op('dve', lambda: nc.vector.tensor_reduce(out=B.est[:, 2:10], in_=B.wz[:], axis=AX.X, op=ALU.add), reads=[B.wz], writes=[B.est])
        K.op('act', lambda: nc.scalar.activation(out=B.est2[:, 0:2], in_=B.est[:, 0:2], func=AF.Ln, bias=self.epsb[:], scale=1.0 / 512), reads=[B.est, self.epsb], writes=[B.est2])
        K.op('act', lambda: nc.scalar.activation(out=B.est2[:, 2:10], in_=B.est[:, 2:10], func=AF.Ln, bias=self.epsb[:], scale=1.0 / 128), reads=[B.est, self.epsb], writes=[B.est2])
        K.op('act', lambda: nc.scalar.activation(out=B.est2[:, 0:10], in_=B.est2[:, 0:10], func=AF.Exp, scale=-0.5), reads=[B.est2], writes=[B.est2])
        for g in range(2):
            K.op('dve', lambda g=g: nc.vector.scalar_tensor_tensor(out=B.ym[:, g * 512:(g + 1) * 512], in0=B.yg[:, g * 512:(g + 1) * 512], scalar=B.est2[:, g:g + 1], in1=B.snw[:, g * 512:(g + 1) * 512], op0=ALU.mult, op1=ALU.mult),
                 reads=[B.yg, B.est2, B.snw], writes=[B.ym])
        K.op(self.gpe, lambda: self.gpv.tensor_tensor(out=B.wz2[:], in0=B.zs[:, 1024:2048].rearrange("p (h q) -> p h q", q=128), in1=B.gnw[:].unsqueeze(1).broadcast_to([128, 8, 128]), op=ALU.mult), reads=[B.zs, B.gnw], writes=[B.wz2])
        K.op('dve', lambda: nc.vector.tensor_tensor(out=B.ofin[:], in0=B.ofin[:], in1=B.est2[:, 2:10].unsqueeze(2).broadcast_to([128, 8, 128]), op=ALU.mult), reads=[B.ofin, B.est2], writes=[B.ofin])
        K.op('dve', lambda: nc.vector.tensor_tensor(out=B.ym[:, 1024:2048].rearrange("p (h q) -> p h q", q=128), in0=B.ofin[:], in1=B.wz2[:], op=ALU.mult), reads=[B.ofin, B.wz2], writes=[B.ym])
        self.store('sp', B.ym, self.YM[c * 128:(c + 1) * 128, :], B.ym[:])

    def phaseC(self, l, hsrc, last):
        nc, K = self.nc, self.K
        with ExitStack() as es:
            sb = lambda name, shape, dt=F32: self.sb(es, name, shape, dt)
            wo = sb("wo", [128, 16, D], BF16)
            for kc in range(16):
                self.load('pool', wo, wo[:, kc, :], self.w_out[l][kc * 128:(kc + 1) * 128, :])
            gp = [sb("gp%d" % r, [128, D]) for r in range(2)]
            pw = sb("pwb", [128, D])
            self.load('sp', pw, pw[:], dram_bcast(self.post_w[l], 128))
            for r in range(2):
                self.load('sp', gp[r], gp[r][:], dram_bcast(self.modv[r, 2 * D:3 * D], 128))
                K.op('dve', lambda r=r: nc.vector.tensor_tensor(out=gp[r][:], in0=gp[r][:], in1=pw[:], op=ALU.mult), reads=[gp[r], pw], writes=[gp[r]])
            ymc = [sb("ymc%d" % i, [128, D], BF16) for i in range(3)]
            hc = [sb("hc%d" % i, [128, D]) for i in range(3)]
            ymT = sb("ymT", [128, 16, 128], BF16)
            cj = sb("cjunk", [128, D], BF16)
            cst_ = sb("cst_", [128, 8])
            res = [sb("res%d" % i, [128, D]) for i in range(3)]
            chunks = list(range(2, NCH)) if last else list(range(NCH))
            ymTs = [ymT, sb("ymT2", [128, 16, 128], BF16), sb("ymT3", [128, 16, 128], BF16)]
            csts = [cst_, sb("cst2_", [128, 8]), sb("cst3_", [128, 8])]

            def issue(i):
                c = chunks[i]
                self.load('sp', ymc[i % 3], ymc[i % 3][:], self.YM[c * 128:(c + 1) * 128, :])
                self.load('sp', hc[i % 3], hc[i % 3][:], hsrc[c * 128:(c + 1) * 128, :])

            pos = [sb("po%d" % i, [128, D]) for i in range(3)]

            def c_iter(i, c):
                y_, h_, r_, yT, st_, po = ymc[i % 3], hc[i % 3], res[i % 3], ymTs[i % 3], csts[i % 3], pos[i % 3]
                r = 1 if c < 2 else 0
                for half in range(2):
                    ps = self.next_half()
                    pb = ps[:].bitcast(BF16)
                    K.op('pe', [lambda j=j, pb=pb, half=half: nc.tensor.transpose(out=pb[:, j * 128:(j + 1) * 128], in_=y_[:, (half * 8 + j) * 128:(half * 8 + j + 1) * 128], identity=self.idb[:]) for j in range(8)],
                         reads=[y_, self.idb], writes=[ps])
                    K.op('act', lambda pb=pb, half=half: nc.scalar.copy(out=yT[:, half * 8:(half + 1) * 8, :].rearrange("p k t -> p (k t)"), in_=pb[:, 0:1024]), reads=[ps], writes=[yT])
                    self.free_half(ps)
                yield
                for cb in range(4):
                    ps = self.next_half()
                    fns = [lambda ps=ps, cb=cb, kc=kc: nc.tensor.matmul(ps[:, 0:512], lhsT=yT[:, kc, :], rhs=wo[:, kc, cb * 512:(cb + 1) * 512], start=(kc == 0), stop=(kc == 15)) for kc in range(16)]
                    K.op('pe', fns, reads=[yT, wo], writes=[ps])
                    K.op('act', lambda ps=ps, cb=cb: nc.scalar.activation(out=cj[:, cb * 512:(cb + 1) * 512], in_=ps[:, 0:512], func=AF.Square, accum_out=st_[:, cb:cb + 1]), reads=[ps], writes=[cj, st_])
                    K.op('act', lambda ps=ps, cb=cb: nc.scalar.copy(out=po[:, cb * 512:(cb + 1) * 512], in_=ps[:, 0:512]), reads=[ps], writes=[po])
                    self.free_half(ps)
                    yield
                K.op('dve', lambda: nc.vector.tensor_reduce(out=st_[:, 4:5], in_=st_[:, 0:4], axis=AX.X, op=ALU.add), reads=[st_], writes=[st_])
                K.op('act', lambda: nc.scalar.activation(out=st_[:, 5:6], in_=st_[:, 4:5], func=AF.Ln, bias=self.epsb[:], scale=1.0 / D), reads=[st_, self.epsb], writes=[st_])
                K.op('act', lambda: nc.scalar.activation(out=st_[:, 6:7], in_=st_[:, 5:6], func=AF.Exp, scale=-0.5), reads=[st_], writes=[st_])
                yield
                K.op('dve', lambda: nc.vector.scalar_tensor_tensor(out=r_[:], in0=po[:], scalar=st_[:, 6:7], in1=gp[r][:], op0=ALU.mult, op1=ALU.mult), reads=[po, st_, gp[r]], writes=[r_])
                yield
                K.op('dve', lambda: nc.vector.tensor_tensor(out=r_[:], in0=r_[:], in1=h_[:], op=ALU.add), reads=[r_, h_], writes=[r_])
                if last:
                    dst = self.out[(c - 2) * 128:(c - 1) * 128, :]
                else:
                    dst = self.H[c * 128:(c + 1) * 128, :]
                self.store('pool', r_, dst, r_[:])
                if i + 3 < len(chunks):
                    issue(i + 3)

            def c_gens():
                for i, c in enumerate(chunks):
                    yield c_iter(i, c)

            issue(0)
            issue(1)
            issue(2)
            run_pipeline(c_gens(), 3)


def make_consts():
    i = np.arange(128)
    c = np.zeros((128, NCONST), np.float32)
    c[:, C_ID:C_ID + 128] = np.eye(128)
    c[:, C_INC0:C_INC0 + 128] = (i[:, None] <= i[None, :])
    c[:, C_INC1:C_INC1 + 128] = (i[:, None] >= i[None, :])
    c[:, C_EXC0:C_EXC0 + 128] = (i[:, None] > i[None, :])
    c[:, C_EXC1:C_EXC1 + 128] = (i[:, None] < i[None, :])
    c[:, C_ONES:C_ONES + 128] = 1.0
    c[:, C_BD:C_BD + 128] = (i[:, None] // 32 == i[None, :] // 32)
    return c


def make_pv(inp):
    pv = np.zeros((DEPTH, 128, NPV), np.float32)
    for l in range(DEPTH):
        pv[l, :, 0:16] = inp['pre_norm_w'][l].reshape(16, 128).T
        cw = np.concatenate([inp['conv_ssd_w'][l], inp['conv_gdn_w'][l]], axis=1)
        pv[l, :, 16:196] = cw.reshape(5, 36, 128).transpose(2, 1, 0).reshape(128, 180)
        pv[l, :, 196:208] = inp['conv_ssd_b'][l].reshape(12, 128).T
    return pv


def make_in_maps(inp, cores):
    shared = {
        'pv': make_pv(inp), 'consts': make_consts(),
        'w_ada': np.ascontiguousarray(inp['w_ada']), 'b_ada': np.ascontiguousarray(inp['b_ada']),
        'post_norm_w': np.ascontiguousarray(inp['post_norm_w']), 'w_in': np.ascontiguousarray(inp['w_in']),
        'ssd_a_log': np.ascontiguousarray(inp['ssd_a_log']).reshape(DEPTH, 32),
        'ssd_dt_bias': np.ascontiguousarray(inp['ssd_dt_bias']).reshape(DEPTH, 32),
        'ssd_d': np.ascontiguousarray(inp['ssd_d']), 'ssd_norm_w': np.ascontiguousarray(inp['ssd_norm_w']),
        'gdn_a_log': np.ascontiguousarray(inp['gdn_a_log']).reshape(DEPTH, 16),
        'gdn_dt_bias': np.ascontiguousarray(inp['gdn_dt_bias']).reshape(DEPTH, 16),
        'gdn_norm_w': np.ascontiguousarray(inp['gdn_norm_w']), 'w_out': np.ascontiguousarray(inp['w_out']),
    }
    maps = []
    for b in cores:
        m = dict(shared)
        m['h0'] = np.ascontiguousarray(np.concatenate([inp['ctx'][b], inp['x'][b]], axis=0))
        c2 = np.stack([inp['c'][b], inp['c_ctx']])
        m['c2t'] = np.ascontiguousarray(c2.reshape(2, 16, 128).transpose(2, 1, 0).reshape(128, 32))
        maps.append(m)
    return maps


def kernel(**inputs):
    inp = {k: np.asarray(v) for k, v in inputs.items()}
    nc = Builder().build()
    maps = make_in_maps(inp, list(range(8)))
    res = run_bass_kernel_spmd(nc, maps, core_ids=list(range(8)))
    return np.stack([r['out'] for r in res.results], axis=0).astype(np.float32)
```
